# Optimizing a Trainium2 kernel written in Bass

```python
import math
import jax, jax.numpy as jnp
from jax import lax
import numpy as np

D_MODEL = 1024
BATCH = 8
SEQ = 4096
DEPTH = 1

D_FF = 2816
EPS = 1e-6
N_HEADS_MLA = 16
QK_NOPE = 64
QK_ROPE = 32
QK_HEAD = QK_NOPE + QK_ROPE
V_HEAD = 64
Q_LORA = 384
KV_LORA = 256
ROPE_BASE = 10000.0
Q_BLOCK = 128
D_INNER = 2 * D_MODEL
SSM_HEAD_DIM = 64
SSM_HEADS = D_INNER // SSM_HEAD_DIM
SSM_GROUPS = 4
D_STATE = 128
CONV_WIDTH = 5
CHUNK = 128
XBC_DIM = D_INNER + 2 * SSM_GROUPS * D_STATE
IN_SPLITS = (Q_LORA, KV_LORA, QK_ROPE, D_INNER, XBC_DIM, SSM_HEADS, SSM_HEADS, D_MODEL, D_MODEL)
IN_DIM = sum(IN_SPLITS)

kernel_name = "hybrid_mla_bissd_macaron_block"


def rmsnorm(x, g):
    xf = x.astype(jnp.float32)
    y = xf * lax.rsqrt(jnp.mean(xf * xf, axis=-1, keepdims=True) + EPS)
    return (y * g).astype(x.dtype)


def swiglu(h, w_gate, w_up, w_down):
    return (jax.nn.silu(h @ w_gate) * (h @ w_up)) @ w_down


def rope_tables(positions):
    inv_freq = 1.0 / (ROPE_BASE ** (jnp.arange(0, QK_ROPE, 2, dtype=jnp.float32) / QK_ROPE))
    ang = positions.astype(jnp.float32)[..., None] * inv_freq
    return jnp.cos(ang)[:, :, None, :], jnp.sin(ang)[:, :, None, :]


def apply_rope(t, cos, sin):
    tf = t.astype(jnp.float32)
    t1, t2 = tf[..., : QK_ROPE // 2], tf[..., QK_ROPE // 2:]
    return jnp.concatenate([t1 * cos - t2 * sin, t2 * cos + t1 * sin], axis=-1).astype(t.dtype)


def mla(c_q, c_kv, k_pe, positions, q_a_norm, w_q_b, kv_a_norm, w_kv_b, q_head_norm, k_head_norm):
    b, s, _ = c_q.shape
    q = (rmsnorm(c_q, q_a_norm) @ w_q_b).reshape(b, s, N_HEADS_MLA, QK_HEAD)
    kv = (rmsnorm(c_kv, kv_a_norm) @ w_kv_b).reshape(b, s, N_HEADS_MLA, QK_NOPE + V_HEAD)
    k_nope, v = kv[..., :QK_NOPE], kv[..., QK_NOPE:]
    k_pe_h = jnp.broadcast_to(k_pe[:, :, None, :], (b, s, N_HEADS_MLA, QK_ROPE))
    k = jnp.concatenate([k_nope, k_pe_h], axis=-1)
    q = rmsnorm(q, q_head_norm)
    k = rmsnorm(k, k_head_norm)
    cos, sin = rope_tables(positions)
    q = jnp.concatenate([q[..., :QK_NOPE], apply_rope(q[..., QK_NOPE:], cos, sin)], axis=-1)
    k = jnp.concatenate([k[..., :QK_NOPE], apply_rope(k[..., QK_NOPE:], cos, sin)], axis=-1)
    scale = 1.0 / math.sqrt(QK_HEAD)
    n_blk = s // Q_BLOCK
    qb = q.reshape(b, n_blk, Q_BLOCK, N_HEADS_MLA, QK_HEAD).transpose(1, 0, 2, 3, 4)

    def attend(q_blk):
        sc = jnp.einsum('bqhd,bkhd->bhqk', q_blk, k).astype(jnp.float32) * scale
        p = jax.nn.softmax(sc, axis=-1).astype(v.dtype)
        return jnp.einsum('bhqk,bkhd->bqhd', p, v)

    o = lax.map(attend, qb)
    return o.transpose(1, 0, 2, 3, 4).reshape(b, s, N_HEADS_MLA * V_HEAD)


def ssd(x, dt, a, bm, cm):
    b, l, h, p = x.shape
    g, n = bm.shape[2], bm.shape[3]
    hg = h // g
    nc = l // CHUNK
    f32 = jnp.float32
    xdt = (x.astype(f32) * dt[..., None]).reshape(b, nc, CHUNK, g, hg, p)
    da = jnp.moveaxis((dt * a).reshape(b, nc, CHUNK, g, hg), 2, -1)
    a_cs = jnp.cumsum(da, axis=-1)
    bc = bm.astype(f32).reshape(b, nc, CHUNK, g, n)
    cc = cm.astype(f32).reshape(b, nc, CHUNK, g, n)
    tril = jnp.tril(jnp.ones((CHUNK, CHUNK), dtype=bool))
    seg = a_cs[..., :, None] - a_cs[..., None, :]
    decay = jnp.exp(jnp.where(tril, seg, -jnp.inf))
    cb = jnp.einsum('bclgn,bcsgn->bcgls', cc, bc)
    y_diag = jnp.einsum('bcgls,bcghls,bcsghp->bclghp', cb, decay, xdt)
    decay_states = jnp.exp(a_cs[..., -1:] - a_cs)
    states = jnp.einsum('bclgn,bcghl,bclghp->bcghpn', bc, decay_states, xdt)
    chunk_decay = jnp.exp(a_cs[..., -1])

    def step(carry, inp):
        st, dec = inp
        return carry * dec[..., None, None] + st, carry

    init = jnp.zeros((b, g, hg, p, n), f32)
    _, prev = lax.scan(step, init, (jnp.moveaxis(states, 1, 0), jnp.moveaxis(chunk_decay, 1, 0)))
    prev = jnp.moveaxis(prev, 0, 1)
    y_off = jnp.einsum('bclgn,bcghpn,bcghl->bclghp', cc, prev, jnp.exp(a_cs))
    return (y_diag + y_off).reshape(b, l, h, p)


def bi_mamba2(xbc, z, dt_f_raw, dt_b_raw, conv_w, conv_b, a_log_fwd, a_log_bwd,
              dt_bias_fwd, dt_bias_bwd, d_skip, ssm_norm):
    b, s, _ = xbc.shape
    pad = CONV_WIDTH // 2
    xbc = lax.conv_general_dilated(xbc, conv_w, window_strides=(1,), padding=[(pad, pad)],
                                   dimension_numbers=('NWC', 'WIO', 'NWC'),
                                   feature_group_count=XBC_DIM)
    xbc = jax.nn.silu(xbc + conv_b)
    xs, bm, cm = jnp.split(xbc, [D_INNER, D_INNER + SSM_GROUPS * D_STATE], axis=-1)
    xs = xs.reshape(b, s, SSM_HEADS, SSM_HEAD_DIM)
    bm = bm.reshape(b, s, SSM_GROUPS, D_STATE)
    cm = cm.reshape(b, s, SSM_GROUPS, D_STATE)
    dt_f = jax.nn.softplus(dt_f_raw.astype(jnp.float32) + dt_bias_fwd)
    dt_b = jax.nn.softplus(dt_b_raw.astype(jnp.float32) + dt_bias_bwd)
    a_f = -jnp.exp(a_log_fwd.astype(jnp.float32))
    a_b = -jnp.exp(a_log_bwd.astype(jnp.float32))
    rev = lambda t: jnp.flip(t, axis=1)
    y_f = ssd(xs, dt_f, a_f, bm, cm)
    y_b = rev(ssd(rev(xs), rev(dt_b), a_b, rev(bm), rev(cm)))
    y = y_f + y_b + d_skip.astype(jnp.float32)[:, None] * xs.astype(jnp.float32)
    y = y.reshape(b, s, D_INNER) * jax.nn.silu(z.astype(jnp.float32))
    yg = y.reshape(b, s, SSM_GROUPS, D_INNER // SSM_GROUPS)
    yg = yg * lax.rsqrt(jnp.mean(yg * yg, axis=-1, keepdims=True) + EPS)
    return (yg.reshape(b, s, D_INNER) * ssm_norm).astype(xs.dtype)


def setup_inputs(seed: int = 0) -> dict:
    key = jax.random.key(seed)
    ks = iter(jax.random.split(key, 40))
    f32 = jnp.float32

    def nrm(shape, scale):
        return jax.random.normal(next(ks), (DEPTH,) + shape, f32) * scale

    def gain(n):
        return 1.0 + 0.01 * jax.random.normal(next(ks), (DEPTH, n), f32)

    x = jax.random.normal(next(ks), (BATCH, SEQ, D_MODEL), f32)
    positions = jnp.broadcast_to(jnp.arange(SEQ, dtype=jnp.int32)[None, :], (BATCH, SEQ))
    d = {}
    d['x'] = x
    d['positions'] = positions
    d['ffn1_norm'] = gain(D_MODEL)
    d['ffn1_w_gate'] = nrm((D_MODEL, D_FF), D_MODEL ** -0.5)
    d['ffn1_w_up'] = nrm((D_MODEL, D_FF), D_MODEL ** -0.5)
    d['ffn1_w_down'] = nrm((D_FF, D_MODEL), D_FF ** -0.5)
    d['mix_norm'] = gain(D_MODEL)
    d['w_in'] = nrm((D_MODEL, IN_DIM), D_MODEL ** -0.5)
    d['q_a_norm'] = gain(Q_LORA)
    d['w_q_b'] = nrm((Q_LORA, N_HEADS_MLA * QK_HEAD), Q_LORA ** -0.5)
    d['kv_a_norm'] = gain(KV_LORA)
    d['w_kv_b'] = nrm((KV_LORA, N_HEADS_MLA * (QK_NOPE + V_HEAD)), KV_LORA ** -0.5)
    d['q_head_norm'] = gain(QK_HEAD)
    d['k_head_norm'] = gain(QK_HEAD)
    d['conv_w'] = nrm((CONV_WIDTH, 1, XBC_DIM), CONV_WIDTH ** -0.5)
    d['conv_b'] = nrm((XBC_DIM,), 0.01)
    d['a_log_fwd'] = jnp.log(jax.random.uniform(next(ks), (DEPTH, SSM_HEADS), f32, 1.0, 16.0))
    d['a_log_bwd'] = jnp.log(jax.random.uniform(next(ks), (DEPTH, SSM_HEADS), f32, 1.0, 16.0))
    dt0_f = jnp.exp(jax.random.uniform(next(ks), (DEPTH, SSM_HEADS), f32, math.log(1e-3), math.log(1e-1)))
    dt0_b = jnp.exp(jax.random.uniform(next(ks), (DEPTH, SSM_HEADS), f32, math.log(1e-3), math.log(1e-1)))
    d['dt_bias_fwd'] = dt0_f + jnp.log(-jnp.expm1(-dt0_f))
    d['dt_bias_bwd'] = dt0_b + jnp.log(-jnp.expm1(-dt0_b))
    d['d_skip'] = gain(SSM_HEADS)
    d['ssm_norm'] = gain(D_INNER)
    d['w_attn_branch'] = nrm((N_HEADS_MLA * V_HEAD, D_MODEL), (N_HEADS_MLA * V_HEAD) ** -0.5)
    d['w_ssm_branch'] = nrm((D_INNER, D_MODEL), D_INNER ** -0.5)
    d['w_out'] = nrm((D_MODEL, D_MODEL), D_MODEL ** -0.5)
    d['ffn2_norm'] = gain(D_MODEL)
    d['ffn2_w_gate'] = nrm((D_MODEL, D_FF), D_MODEL ** -0.5)
    d['ffn2_w_up'] = nrm((D_MODEL, D_FF), D_MODEL ** -0.5)
    d['ffn2_w_down'] = nrm((D_FF, D_MODEL), D_FF ** -0.5)
    return d


def reference(x, positions, ffn1_norm, ffn1_w_gate, ffn1_w_up, ffn1_w_down, mix_norm, w_in,
              q_a_norm, w_q_b, kv_a_norm, w_kv_b, q_head_norm, k_head_norm,
              conv_w, conv_b, a_log_fwd, a_log_bwd, dt_bias_fwd, dt_bias_bwd, d_skip, ssm_norm,
              w_attn_branch, w_ssm_branch, w_out,
              ffn2_norm, ffn2_w_gate, ffn2_w_up, ffn2_w_down):
    split_idx = list(np.cumsum(IN_SPLITS)[:-1])
    for l in range(DEPTH):
        x = x + 0.5 * swiglu(rmsnorm(x, ffn1_norm[l]), ffn1_w_gate[l], ffn1_w_up[l], ffn1_w_down[l])
        h = rmsnorm(x, mix_norm[l])
        u = h @ w_in[l]
        c_q, c_kv, k_pe, z, xbc, dt_f, dt_b, g_a, g_b = jnp.split(u, split_idx, axis=-1)
        a = mla(c_q, c_kv, k_pe, positions, q_a_norm[l], w_q_b[l], kv_a_norm[l], w_kv_b[l],
                q_head_norm[l], k_head_norm[l])
        m = bi_mamba2(xbc, z, dt_f, dt_b, conv_w[l], conv_b[l], a_log_fwd[l], a_log_bwd[l],
                      dt_bias_fwd[l], dt_bias_bwd[l], d_skip[l], ssm_norm[l])
        merged = jax.nn.sigmoid(g_a) * (a @ w_attn_branch[l]) + jax.nn.sigmoid(g_b) * (m @ w_ssm_branch[l])
        x = x + merged @ w_out[l]
        x = x + 0.5 * swiglu(rmsnorm(x, ffn2_norm[l]), ffn2_w_gate[l], ffn2_w_up[l], ffn2_w_down[l])
    return x
```

```python
import math
from contextlib import ExitStack
import numpy as np
import ml_dtypes
import concourse.bass as bass
import concourse.mybir as mybir
from concourse.bass_utils import run_bass_kernel_spmd

F32 = mybir.dt.float32
BF16 = mybir.dt.bfloat16
I32 = mybir.dt.int32
AF = mybir.ActivationFunctionType
ALU = mybir.AluOpType
AX = mybir.AxisListType

S_ = 4096
D_ = 1024
FF = 2816
NFF = 22
NH = 16
EPS = 1e-6
C_Q, C_KV, C_PE, C_Z, C_XBC, C_DTF, C_DTB, C_GA, C_GB = 0, 384, 640, 672, 2720, 5792, 5824, 5856, 6880
IN_DIM = 7904
ENGS = ("pe", "act", "dve", "pool", "sp")
FUSE_WAITS = True


class DSem:
    def __init__(self):
        self.count = 0
        self.handle = None


class Buf:
    __slots__ = ("name", "lw", "rd", "dsem", "ep")

    def __init__(self, name=""):
        self.name = name
        self.lw = None
        self.rd = []
        self.dsem = None
        self.ep = -1


class Op:
    __slots__ = ("eng", "fn", "idx", "waits", "dwaits", "inc", "dsem", "know", "seq", "multi")


class Sched:
    def __init__(self):
        self.ops = {e: [] for e in ENGS}
        self.know = {e: {} for e in ENGS}
        self.dsems = []
        self.free = []
        self.epoch = 0

    def _add(self, eng, fn, reads, writes, dsem=None, extra=(), extra_ds=()):
        op = Op()
        op.eng = eng
        op.fn = fn
        op.idx = len(self.ops[eng])
        op.waits = {}
        op.dwaits = {}
        op.inc = False
        op.dsem = dsem
        op.seq = None
        op.multi = False
        know = self.know[eng]
        deps = list(extra)
        for b in reads:
            if b.lw is not None:
                deps.append(b.lw)
        for b in writes:
            if b.lw is not None:
                deps.append(b.lw)
            deps.extend(b.rd)
        for a in deps:
            if a is op:
                continue
            if a.dsem is None:
                if a.eng == "pe" and eng == "pe":
                    continue
                if know.get(a.eng, -1) >= a.idx:
                    continue
                a.inc = True
                cur = op.waits.get(a.eng)
                if cur is None or cur.idx < a.idx:
                    op.waits[a.eng] = a
                for k, v in a.know.items():
                    if know.get(k, -1) < v:
                        know[k] = v
                know[a.eng] = max(know.get(a.eng, -1), a.idx)
            else:
                ds = a.dsem
                v = ds.count
                if know.get(ds, -1) >= v:
                    continue
                op.dwaits[ds] = v
                for k, vv in a.know.items():
                    if know.get(k, -1) < vv:
                        know[k] = vv
                know[ds] = v
        for ds in extra_ds:
            v = ds.count
            if know.get(ds, -1) < v:
                op.dwaits[ds] = v
                know[ds] = v
        if dsem is not None:
            dsem.count += 16
        op.know = dict(know)
        for b in reads:
            b.rd.append(op)
        for b in writes:
            b.lw = op
            b.rd = []
        self.ops[eng].append(op)
        return op

    def op(self, eng, fn, reads=(), writes=(), multi=False):
        o = self._add(eng, fn, reads, writes)
        o.multi = multi
        return o

    def dma(self, fn, reads=(), writes=(), sem_buf=None, eng="sp"):
        if sem_buf.dsem is None or sem_buf.ep != self.epoch:
            if self.free:
                sem_buf.dsem = self.free.pop()
            else:
                sem_buf.dsem = DSem()
                self.dsems.append(sem_buf.dsem)
            sem_buf.ep = self.epoch
        return self._add(eng, fn, reads, writes, dsem=sem_buf.dsem)

    def barrier(self):
        lasts = []
        for e in ENGS:
            if e == "sp":
                continue
            for o in reversed(self.ops[e]):
                if o.dsem is None:
                    lasts.append(o)
                    break
        spop = self._add("sp", lambda e: e.nop(), (), (), extra=lasts, extra_ds=list(self.dsems))
        self.epoch += 1
        self.free = list(self.dsems)
        for e in ENGS:
            if e == "sp":
                continue
            self._add(e, lambda eh: eh.nop(), (), (), extra=[spop])

    def emit(self, nc):
        with ExitStack() as st:
            esem = {e: st.enter_context(nc.semaphore("es_" + e)) for e in ENGS}
            for i, d in enumerate(self.dsems):
                d.handle = st.enter_context(nc.semaphore("ds%d" % i))
            for e in ENGS:
                c = 0
                for o in self.ops[e]:
                    if o.dsem is None and o.inc:
                        c += 1
                        o.seq = c
            block = st.enter_context(nc.Block())

            def run(e, eh):
                for o in self.ops[e]:
                    wl = [(esem[se], a.seq) for se, a in o.waits.items()] + [(ds.handle, v) for ds, v in o.dwaits.items()]
                    attach = None
                    if wl and o.dsem is None and not o.multi and e != "sp" and FUSE_WAITS:
                        attach = wl.pop()
                    for hh_, vv_ in wl:
                        eh.wait_ge(hh_, vv_)
                    n0 = nc.n_instructions()
                    ins = o.fn(eh)
                    if attach is not None:
                        if nc.n_instructions() - n0 != 1:
                            raise RuntimeError("multi-instruction op with fused wait on %s (%d)" % (e, nc.n_instructions() - n0))
                        ins._wait_ge(attach[0], attach[1])
                    if o.dsem is not None:
                        ins.then_inc(o.dsem.handle, 16)
                    elif o.inc:
                        ins.then_inc(esem[e], 1)
                if e == "sp":
                    for ds in self.dsems:
                        eh.wait_ge(ds.handle, ds.count)

            @block.tensor
            def _(eh):
                run("pe", eh)

            @block.scalar
            def _(eh):
                run("act", eh)

            @block.vector
            def _(eh):
                run("dve", eh)

            @block.gpsimd
            def _(eh):
                run("pool", eh)

            @block.sync
            def _(eh):
                run("sp", eh)


class Ring:
    def __init__(self, nc, st, name, shape, dtype, n, psum=False):
        self.items = []
        for i in range(n):
            if psum:
                t = st.enter_context(nc.psum_tensor("%s%d" % (name, i), shape, dtype))
            else:
                t = st.enter_context(nc.sbuf_tensor("%s%d" % (name, i), shape, dtype))
            self.items.append((t, Buf("%s%d" % (name, i))))
        self.i = 0

    def next(self):
        r = self.items[self.i % len(self.items)]
        self.i += 1
        return r


def pipeline(n, load_fn, compute_fn, depth):
    hs = {}
    for i in range(n + depth):
        if i < n:
            hs[i] = load_fn(i)
        if i >= depth:
            compute_fn(i - depth, hs.pop(i - depth))


class K:
    pass


def build(dbg=False, stop_after=99):
    nc = bass.Bass("TRN2", target_bir_lowering=False)
    S = Sched()
    uid = [0]

    def un(p):
        uid[0] += 1
        return "%s_%d" % (p, uid[0])

    def inp(name, shape, dt=F32):
        return nc.dram_tensor(name, shape, dt, kind="ExternalInput").ap()

    def scratch(name, shape, dt, out=False):
        kind = "ExternalOutput" if (out or dbg) else "Internal"
        return nc.dram_tensor(name, shape, dt, kind=kind).ap(), Buf(name)

    x = inp("x", [S_, D_])
    pos = inp("pos", [128, 32], I32)
    gfm = inp("gfm", [128, 24])
    gqa = inp("gqa", [128, 3])
    gkva = inp("gkva", [128, 2])
    gssm = inp("gssm", [128, 16])
    w_g1 = inp("w_g1", [D_, FF]); w_u1 = inp("w_u1", [D_, FF]); w_d1 = inp("w_d1", [FF, D_])
    w_g2 = inp("w_g2", [D_, FF]); w_u2 = inp("w_u2", [D_, FF]); w_d2 = inp("w_d2", [FF, D_])
    w_in = inp("w_in", [D_, IN_DIM])
    w_qb = inp("w_qb", [384, 1536]); w_kvb = inp("w_kvb", [256, 2048])
    hn = inp("hn", [1, 192])
    convw = inp("convw", [128, 24, 5]); convb = inp("convb", [128, 24])
    ssp = inp("ssp", [1, 160])
    w_pa = inp("w_pa", [D_, D_]); w_pb = inp("w_pb", [2048, D_]); w_o = inp("w_o", [D_, D_])
    c_identb = inp("c_identb", [128, 128], BF16)
    c_mats = inp("c_mats", [128, 4, 128])
    c_neg = inp("c_neg", [128, 2, 128], BF16)
    c_invf = inp("c_invf", [1, 16])

    y_out, b_yout = scratch("y", [S_, D_], F32, out=True)
    x1s, b_x1s = scratch("x1s", [S_, D_], F32)
    x2s, b_x2s = scratch("x2s", [S_, D_], F32)
    hmid, b_hmid = scratch("hmid", [FF, S_], BF16)
    QT, b_QT = scratch("QT", [NH, 96, S_], BF16)
    KT, b_KT = scratch("KT", [NH, 96, S_], BF16)
    Vs, b_Vs = scratch("Vs", [NH, 128, 32, 65], BF16)
    zs, b_zs = scratch("zs", [S_, 2048], BF16)
    xcs, b_xcs = scratch("xcs", [32, 128, 24, 128], BF16)
    gts, b_gts = scratch("gts", [2048, S_], BF16)
    yfs, b_yfs = scratch("yfs", [S_, 2048], F32)
    mTs, b_mTs = scratch("mTs", [2048, S_], BF16)
    aTs, b_aTs = scratch("aTs", [D_, S_], BF16)
    if dbg:
        ybs, b_ybs = scratch("ybs", [S_, 2048], F32)

    top = ExitStack()
    A = lambda name, shape, dt: top.enter_context(nc.sbuf_tensor(name, shape, dt))
    identb = A("identb", [128, 128], BF16); b_identb = Buf()
    mats = A("mats", [128, 4, 128], F32); b_mats = Buf()
    negm = A("negm", [128, 2, 128], BF16); b_negm = Buf()
    gfm_t = A("gfm_t", [128, 24], F32); b_gfm = Buf()
    gqa_t = A("gqa_t", [128, 3], F32); b_gqa = Buf()
    gkva_t = A("gkva_t", [128, 2], F32); b_gkva = Buf()
    gssm_t = A("gssm_t", [128, 16], F32); b_gssm = Buf()
    hn_t = A("hn_t", [128, 192], F32); b_hn = Buf()
    ssp_t = A("ssp_t", [128, 160], F32); b_ssp = Buf()
    convw_t = A("convw_t", [128, 24, 5], F32); b_convw = Buf()
    convb_t = A("convb_t", [128, 24], F32); b_convb = Buf()
    cos_t = A("cos_t", [128, 32, 16], F32); b_cos = Buf()
    sin_t = A("sin_t", [128, 32, 16], F32); b_sin = Buf()
    dtraw = A("dtraw", [128, 32, 64], F32); b_dtraw = Buf()

    def ld(dst, src, b):
        S.dma(lambda e: e.dma_start(out=dst, in_=src), writes=[b], sem_buf=b)

    ld(identb[:], c_identb[:, :], b_identb)
    ld(mats[:], c_mats[:, :, :], b_mats)
    ld(negm[:], c_neg[:, :, :], b_negm)
    ld(gfm_t[:], gfm[:, :], b_gfm)
    ld(gqa_t[:], gqa[:, :], b_gqa)
    ld(gkva_t[:], gkva[:, :], b_gkva)
    ld(gssm_t[:], gssm[:, :], b_gssm)
    ld(hn_t[:], hn.partition_broadcast(128), b_hn)
    ld(ssp_t[:], ssp.partition_broadcast(128), b_ssp)
    ld(convw_t[:], convw[:, :, :], b_convw)
    ld(convb_t[:], convb[:, :], b_convb)
    Umat = mats[:, 0, :]
    Ustr = mats[:, 1, :]
    onesf = mats[:, 2, :]
    identf = mats[:, 3, :]

    with ExitStack() as st:
        post = st.enter_context(nc.sbuf_tensor("post", [128, 32], I32)); b_post = Buf()
        posf = st.enter_context(nc.sbuf_tensor("posf", [128, 32], F32)); b_posf = Buf()
        invf = st.enter_context(nc.sbuf_tensor("invf", [128, 16], F32)); b_invf = Buf()
        ang = st.enter_context(nc.sbuf_tensor("ang", [128, 32, 16], F32)); b_ang = Buf()
        ang2 = st.enter_context(nc.sbuf_tensor("ang2", [128, 32, 16], F32)); b_ang2 = Buf()
        ld(post[:], pos[:, :], b_post)
        ld(invf[:], c_invf.partition_broadcast(128), b_invf)
        S.op("dve", lambda e: e.tensor_copy(out=posf[:], in_=post[:]), reads=[b_post], writes=[b_posf])
        S.op("dve", lambda e: e.tensor_tensor(out=ang[:], in0=posf[:].unsqueeze(2).to_broadcast([128, 32, 16]),
                                              in1=invf[:].unsqueeze(1).to_broadcast([128, 32, 16]), op=ALU.mult),
             reads=[b_posf, b_invf], writes=[b_ang])
        PI = math.pi
        angi = st.enter_context(nc.sbuf_tensor("angi", [128, 32, 16], I32)); b_angi = Buf()
        ang3 = st.enter_context(nc.sbuf_tensor("ang3", [128, 32, 16], F32)); b_ang3 = Buf()

        def rr(shift, dst, b_dst):
            S.op("dve", lambda e: e.tensor_scalar(out=ang2[:], in0=ang[:], scalar1=shift, scalar2=None, op0=ALU.add),
                 reads=[b_ang], writes=[b_ang2])
            S.op("dve", lambda e: e.tensor_scalar(out=ang3[:], in0=ang2[:], scalar1=1.0 / (2 * PI), scalar2=None,
                                                  op0=ALU.mult), reads=[b_ang2], writes=[b_ang3])
            S.op("dve", lambda e: e.tensor_copy(out=angi[:], in_=ang3[:]), reads=[b_ang3], writes=[b_angi])
            S.op("dve", lambda e: e.tensor_copy(out=ang3[:], in_=angi[:]), reads=[b_angi], writes=[b_ang3])
            S.op("dve", lambda e: e.scalar_tensor_tensor(out=ang2[:], in0=ang3[:], scalar=-2 * PI, in1=ang2[:],
                                                         op0=ALU.mult, op1=ALU.add), reads=[b_ang3, b_ang2], writes=[b_ang2])
            S.op("dve", lambda e: e.tensor_scalar(out=ang3[:], in0=ang2[:], scalar1=-PI, scalar2=1e9,
                                                  op0=ALU.add, op1=ALU.mult), reads=[b_ang2], writes=[b_ang3])
            S.op("dve", lambda e: e.tensor_scalar(out=ang3[:], in0=ang3[:], scalar1=0.0, scalar2=1.0,
                                                  op0=ALU.max, op1=ALU.min), reads=[b_ang3], writes=[b_ang3])
            S.op("dve", lambda e: e.scalar_tensor_tensor(out=ang2[:], in0=ang3[:], scalar=-2 * PI, in1=ang2[:],
                                                         op0=ALU.mult, op1=ALU.add), reads=[b_ang3, b_ang2], writes=[b_ang2])
            S.op("dve", lambda e: e.tensor_scalar(out=ang3[:], in0=ang2[:], scalar1=PI, scalar2=-1e9,
                                                  op0=ALU.add, op1=ALU.mult), reads=[b_ang2], writes=[b_ang3])
            S.op("dve", lambda e: e.tensor_scalar(out=ang3[:], in0=ang3[:], scalar1=0.0, scalar2=1.0,
                                                  op0=ALU.max, op1=ALU.min), reads=[b_ang3], writes=[b_ang3])
            S.op("dve", lambda e: e.scalar_tensor_tensor(out=ang2[:], in0=ang3[:], scalar=2 * PI, in1=ang2[:],
                                                         op0=ALU.mult, op1=ALU.add), reads=[b_ang3, b_ang2], writes=[b_ang2])
            S.op("dve", lambda e: e.tensor_scalar(out=ang2[:], in0=ang2[:], scalar1=PI * (1 - 1e-6),
                                                  scalar2=-PI * (1 - 1e-6), op0=ALU.min, op1=ALU.max),
                 reads=[b_ang2], writes=[b_ang2])
            S.op("act", lambda e: e.activation(out=dst, in_=ang2[:], func=AF.Sin), reads=[b_ang2], writes=[b_dst])

        rr(0.0, sin_t[:], b_sin)
        rr(0.5 * PI, cos_t[:], b_cos)
        S.barrier()

    def wload(stage_ring, w_ring, wsrc, kc, n, gain=None, cast_eng="pool"):
        stg, b_stg = stage_ring.next()
        wt, b_wt = w_ring.next()
        S.dma(lambda e: e.dma_start(out=stg[:, 0:kc, 0:n], in_=wsrc.rearrange("(kc p) n -> p kc n", p=128)),
              writes=[b_stg], sem_buf=b_stg)
        if gain is None:
            S.op(cast_eng, lambda e: e.tensor_copy(out=wt[:, 0:kc, 0:n], in_=stg[:, 0:kc, 0:n]),
                 reads=[b_stg], writes=[b_wt])
        else:
            g_ap, b_g = gain
            S.op(cast_eng, lambda e: e.tensor_tensor(out=wt[:, 0:kc, 0:n], in0=stg[:, 0:kc, 0:n],
                                                     in1=g_ap.unsqueeze(2).to_broadcast([128, kc, n]), op=ALU.mult),
                 reads=[b_stg, b_g], writes=[b_wt])
        return wt, b_wt

    def norm_block(src, b_src, tb, rings, hT, b_hT, ncols=D_):
        junk, b_junk = rings["junk"].next()
        stt, b_stt = rings["st"].next()
        hb, b_hb = rings["hb"].next()
        ptr, b_ptr = rings["ptr"].next()
        S.op("act", lambda e: e.activation(out=junk[:], in_=src, func=AF.Square, scale=1.0 / math.sqrt(ncols),
                                           accum_out=stt[:, 0:1]), reads=[b_src], writes=[b_junk, b_stt])
        S.op("act", lambda e: e.activation(out=stt[:, 1:2], in_=stt[:, 0:1], func=AF.Sqrt, bias=EPS_AP[:, 0:1]),
             reads=[b_stt], writes=[b_stt])
        S.op("dve", lambda e: e.reciprocal(out=stt[:, 2:3], in_=stt[:, 1:2]), reads=[b_stt], writes=[b_stt])
        S.op("dve", lambda e: e.tensor_scalar(out=hb[:], in0=src, scalar1=stt[:, 2:3], scalar2=None, op0=ALU.mult),
             reads=[b_src, b_stt], writes=[b_hb])
        pv = ptr[:, :].bitcast(BF16)
        for kc in range(8):
            S.op("pe", lambda e, kc=kc: e.transpose(out=pv[:, kc * 128:(kc + 1) * 128],
                                                     in_=hb[:, kc * 128:(kc + 1) * 128], identity=identb[:]),
                 reads=[b_hb, b_identb], writes=[b_ptr])
        S.op("act", lambda e: e.copy(out=hT[:, :, tb * 128:(tb + 1) * 128],
                                     in_=pv.rearrange("p (k t) -> p k t", k=8)),
             reads=[b_ptr], writes=[b_hT])

    eps_t = A("eps_t", [128, 1], F32); b_eps = Buf()
    S.op("pool", lambda e: e.memset(eps_t[:], EPS), writes=[b_eps])
    EPS_AP = eps_t
    S.barrier()

    def ffn_gateup(w_g, w_u, gain_col, hT, b_hT):
        with ExitStack() as st:
            stg = Ring(nc, st, un("gu_stg"), [128, 8, 128], F32, 6)
            wr = Ring(nc, st, un("gu_w"), [128, 8, 128], BF16, 6)
            psg = Ring(nc, st, un("gu_pg"), [128, 512], F32, 3, psum=True)
            psu = Ring(nc, st, un("gu_pu"), [128, 512], F32, 3, psum=True)
            sil = Ring(nc, st, un("gu_sil"), [128, 512], F32, 3)
            hm = Ring(nc, st, un("gu_hm"), [128, S_], BF16, 2)
            gain = (gfm_t[:, gain_col * 8:(gain_col + 1) * 8], b_gfm)

            def load(j):
                wg = wload(stg, wr, w_g[:, j * 128:(j + 1) * 128], 8, 128, gain)
                wu = wload(stg, wr, w_u[:, j * 128:(j + 1) * 128], 8, 128, gain)
                return wg, wu

            def comp(j, h):
                (wg, b_wg), (wu, b_wu) = h
                hmt, b_hm = hm.next()
                for tb in range(8):
                    pg, b_pg = psg.next()
                    pu, b_pu = psu.next()
                    sl, b_sl = sil.next()
                    ts = slice(tb * 512, (tb + 1) * 512)
                    for kc in range(8):
                        S.op("pe", lambda e, kc=kc, pg=pg, wg=wg, ts=ts: e.matmul(
                            out=pg[:], lhsT=wg[:, kc, :], rhs=hT[:, kc, ts], start=(kc == 0), stop=(kc == 7)),
                            reads=[b_wg, b_hT], writes=[b_pg])
                    for kc in range(8):
                        S.op("pe", lambda e, kc=kc, pu=pu, wu=wu, ts=ts: e.matmul(
                            out=pu[:], lhsT=wu[:, kc, :], rhs=hT[:, kc, ts], start=(kc == 0), stop=(kc == 7)),
                            reads=[b_wu, b_hT], writes=[b_pu])
                    S.op("act", lambda e, sl=sl, pg=pg: e.activation(out=sl[:], in_=pg[:], func=AF.Silu),
                         reads=[b_pg], writes=[b_sl])
                    S.op("dve", lambda e, sl=sl, pu=pu, hmt=hmt, ts=ts: e.tensor_tensor(
                        out=hmt[:, ts], in0=sl[:], in1=pu[:], op=ALU.mult), reads=[b_sl, b_pu], writes=[b_hm])
                S.dma(lambda e, hmt=hmt, j=j: e.dma_start(out=hmid[j * 128:(j + 1) * 128, :], in_=hmt[:]),
                      reads=[b_hm], writes=[b_hmid], sem_buf=b_hm, eng="pool")

            pipeline(NFF, load, comp, 2)
        S.barrier()

    def ffn_down(w_d, xsrc, b_xsrc, xdst, b_xdst, next_norm):
        with ExitStack() as st:
            wd = st.enter_context(nc.sbuf_tensor(un("wd"), [128, NFF, D_], BF16)); b_wd = Buf()
            stg = Ring(nc, st, un("dn_stg"), [128, 1, D_], F32, 3)
            for j in range(NFF):
                sg, b_sg = stg.next()
                S.dma(lambda e, sg=sg, j=j: e.dma_start(out=sg[:, 0, :], in_=w_d[j * 128:(j + 1) * 128, :]),
                      writes=[b_sg], sem_buf=b_sg)
                S.op("pool", lambda e, sg=sg, j=j: e.tensor_copy(out=wd[:, j, :], in_=sg[:, 0, :]),
                     reads=[b_sg], writes=[b_wd])
            hmr = Ring(nc, st, un("dn_hm"), [128, NFF, 512], BF16, 2)
            xr = Ring(nc, st, un("dn_x"), [128, D_], F32, 3)
            ps = Ring(nc, st, un("dn_ps"), [128, 512], F32, 4, psum=True)
            rings = None
            if next_norm:
                rings = dict(junk=Ring(nc, st, un("nj"), [128, D_], BF16, 2),
                             st=Ring(nc, st, un("nst"), [128, 4], F32, 3),
                             hb=Ring(nc, st, un("nhb"), [128, D_], BF16, 2),
                             ptr=Ring(nc, st, un("nptr"), [128, 512], F32, 2, psum=True))

            def load(t):
                hmt, b_hm = hmr.next()
                S.dma(lambda e: e.dma_start(out=hmt[:], in_=hmid[:, t * 512:(t + 1) * 512].rearrange(
                    "(j p) t -> p j t", p=128)), reads=[b_hmid], writes=[b_hm], sem_buf=b_hm)
                return hmt, b_hm

            def comp(t, h):
                hmt, b_hm = h
                for sb in range(4):
                    tb = t * 4 + sb
                    xt, b_xt = xr.next()
                    S.dma(lambda e, xt=xt, tb=tb: e.dma_start(out=xt[:], in_=xsrc[tb * 128:(tb + 1) * 128, :]),
                          reads=[b_xsrc], writes=[b_xt], sem_buf=b_xt)
                    for half in range(2):
                        p, b_p = ps.next()
                        for j in range(NFF):
                            S.op("pe", lambda e, j=j, p=p, sb=sb, half=half, hmt=hmt: e.matmul(
                                out=p[:], lhsT=hmt[:, j, sb * 128:(sb + 1) * 128],
                                rhs=wd[:, j, half * 512:(half + 1) * 512], start=(j == 0), stop=(j == NFF - 1)),
                                reads=[b_hm, b_wd], writes=[b_p])
                        S.op("dve", lambda e, p=p, xt=xt, half=half: e.scalar_tensor_tensor(
                            out=xt[:, half * 512:(half + 1) * 512], in0=p[:], scalar=0.5,
                            in1=xt[:, half * 512:(half + 1) * 512], op0=ALU.mult, op1=ALU.add),
                            reads=[b_p, b_xt], writes=[b_xt])
                    S.dma(lambda e, xt=xt, tb=tb: e.dma_start(out=xdst[tb * 128:(tb + 1) * 128, :], in_=xt[:]),
                          reads=[b_xt], writes=[b_xdst], sem_buf=b_xt, eng="pool")
                    if next_norm:
                        if pendn:
                            norm_block(*pendn.pop())
                        pendn.append((xt[:], b_xt, tb, rings, next_norm[0], next_norm[1]))

            pendn = []
            pipeline(8, load, comp, 1)
            if pendn:
                norm_block(*pendn.pop())
        S.barrier()

    def norm_phase(xsrc, b_xsrc, hT, b_hT):
        with ExitStack() as st:
            xr = Ring(nc, st, un("np_x"), [128, D_], F32, 3)
            rings = dict(junk=Ring(nc, st, un("nj"), [128, D_], BF16, 2),
                         st=Ring(nc, st, un("nst"), [128, 4], F32, 3),
                         hb=Ring(nc, st, un("nhb"), [128, D_], BF16, 2),
                         ptr=Ring(nc, st, un("nptr"), [128, 512], F32, 2, psum=True))

            def load(tb):
                xt, b_xt = xr.next()
                S.dma(lambda e: e.dma_start(out=xt[:], in_=xsrc[tb * 128:(tb + 1) * 128, :]),
                      reads=[b_xsrc], writes=[b_xt], sem_buf=b_xt)
                return xt, b_xt

            def comp(tb, h):
                norm_block(h[0][:], h[1], tb, rings, hT, b_hT)

            pipeline(32, load, comp, 2)
        S.barrier()

    b_x = Buf("x")
    hst = ExitStack()
    hT = hst.enter_context(nc.sbuf_tensor("hT", [128, 8, S_], BF16)); b_hT = Buf("hT")
    norm_phase(x, b_x, hT, b_hT)
    ffn_gateup(w_g1, w_u1, 0, hT, b_hT)
    ffn_down(w_d1, x, b_x, x1s, b_x1s, (hT, b_hT))
    if stop_after <= 1:
        S.emit(nc)
        return nc

    gmix = (gfm_t[:, 8:16], b_gfm)

    def rope(src3, H, tb, dst3, tmp_ring):
        ta, b_ta = tmp_ring.next()
        tb_, b_tb = tmp_ring.next()
        cb = cos_t[:, tb, :].unsqueeze(1).to_broadcast([128, H, 16])
        sb = sin_t[:, tb, :].unsqueeze(1).to_broadcast([128, H, 16])
        t1 = src3[:, :, 0:16]
        t2 = src3[:, :, 16:32]
        a_ = ta[:, 0:H, :]
        b_ = tb_[:, 0:H, :]
        return [
            (lambda e: e.tensor_tensor(out=a_, in0=t1, in1=cb, op=ALU.mult), [b_cos], [b_ta]),
            (lambda e: e.tensor_tensor(out=b_, in0=t2, in1=sb, op=ALU.mult), [b_sin], [b_tb]),
            (lambda e: e.tensor_tensor(out=dst3[:, :, 0:16], in0=a_, in1=b_, op=ALU.subtract), [b_ta, b_tb], []),
            (lambda e: e.tensor_tensor(out=a_, in0=t2, in1=cb, op=ALU.mult), [b_cos], [b_ta]),
            (lambda e: e.tensor_tensor(out=b_, in0=t1, in1=sb, op=ALU.mult), [b_sin], [b_tb]),
            (lambda e: e.tensor_tensor(out=dst3[:, :, 16:32], in0=a_, in1=b_, op=ALU.add), [b_ta, b_tb], []),
        ]

    with ExitStack() as st:
        wAr = Ring(nc, st, un("a_w"), [128, 8, 672], BF16, 1)
        wqr = Ring(nc, st, un("q_w"), [128, 3, 1536], BF16, 1)
        wkr = Ring(nc, st, un("kv_w"), [128, 2, 2048], BF16, 1)
        with ExitStack() as st2:
            stgA = Ring(nc, st2, un("a_stg"), [128, 8, 672], F32, 1)
            stgq = Ring(nc, st2, un("q_stg"), [128, 3, 1536], F32, 1)
            stgk = Ring(nc, st2, un("kv_stg"), [128, 2, 2048], F32, 1)
            wA, b_wA = wload(stgA, wAr, w_in[:, 0:672], 8, 672, gmix)
            wq, b_wq = wload(stgq, wqr, w_qb[:, :], 3, 1536, (gqa_t[:, :], b_gqa))
            wkv, b_wkv = wload(stgk, wkr, w_kvb[:, :], 2, 2048, (gkva_t[:, :], b_gkva))
            S.barrier()
        psA = Ring(nc, st, un("a_ps"), [128, 512], F32, 2, psum=True)
        psT = Ring(nc, st, un("a_pt"), [128, 512], F32, 2, psum=True)
        psQ = Ring(nc, st, un("a_pq"), [128, 512], F32, 3, psum=True)
        junk = Ring(nc, st, un("a_junk"), [128, 1536], F32, 1)
        stt = Ring(nc, st, un("a_st"), [128, 8], F32, 2)
        sst = Ring(nc, st, un("a_ss"), [128, 100], F32, 2)
        cnr = Ring(nc, st, un("a_cn"), [128, 640], BF16, 2)
        cTr = Ring(nc, st, un("a_cT"), [128, 5, 128], BF16, 2)
        kper = Ring(nc, st, un("a_kpe"), [128, 1, 32], F32, 2)
        kpgr = Ring(nc, st, un("a_kpg"), [128, 1, 32], F32, 2)
        krr = Ring(nc, st, un("a_kr"), [128, 1, 32], F32, 2)
        qsbr = Ring(nc, st, un("a_qsb"), [128, 1536], F32, 1)
        kvsbr = Ring(nc, st, un("a_kvsb"), [128, 2048], F32, 1)
        tmpkr = Ring(nc, st, un("a_tmpk"), [128, 16, 64], F32, 1)
        qbr = Ring(nc, st, un("a_qb"), [128, 16, 96], BF16, 2)
        kbr = Ring(nc, st, un("a_kb"), [128, 16, 96], BF16, 2)
        ropet = Ring(nc, st, un("a_rt"), [128, 16, 16], F32, 4)
        vbr = Ring(nc, st, un("a_vb"), [128, 16, 4, 65], BF16, 1)
        qTr = Ring(nc, st, un("a_qT"), [128, 16, 256], BF16, 2)
        kTr = Ring(nc, st, un("a_kT"), [128, 16, 256], BF16, 2)
        for (vt, b_v) in vbr.items:
            S.op("pool", lambda e, vt=vt: e.memset(vt[:], 1.0), writes=[b_v])
        gq = hn_t[:, 0:96]
        gk = hn_t[:, 96:192]
        sh = {}

        def block(tb):
            tsl = slice(tb * 128, (tb + 1) * 128)
            pA1, b_pA1 = psA.next()
            pA2, b_pA2 = psA.next()
            for kc in range(8):
                S.op("pe", lambda e, kc=kc, pA1=pA1, tsl=tsl: e.matmul(out=pA1[:, 0:384], lhsT=hT[:, kc, tsl], rhs=wA[:, kc, 0:384],
                                                     start=(kc == 0), stop=(kc == 7)), reads=[b_hT, b_wA], writes=[b_pA1])
            for kc in range(8):
                S.op("pe", lambda e, kc=kc, pA2=pA2, tsl=tsl: e.matmul(out=pA2[:, 0:288], lhsT=hT[:, kc, tsl], rhs=wA[:, kc, 384:672],
                                                     start=(kc == 0), stop=(kc == 7)), reads=[b_hT, b_wA], writes=[b_pA2])
            jk, b_jk = junk.next()
            s8, b_s8 = stt.next()
            S.op("act", lambda e, jk=jk, pA1=pA1, s8=s8: e.activation(out=jk[:, 0:384], in_=pA1[:, 0:384], func=AF.Square,
                                               scale=1.0 / math.sqrt(384.0), accum_out=s8[:, 0:1]),
                 reads=[b_pA1], writes=[b_jk, b_s8])
            S.op("act", lambda e, jk=jk, pA2=pA2, s8=s8: e.activation(out=jk[:, 0:256], in_=pA2[:, 0:256], func=AF.Square,
                                               scale=1.0 / 16.0, accum_out=s8[:, 1:2]),
                 reads=[b_pA2], writes=[b_jk, b_s8])
            S.op("act", lambda e, s8=s8: e.activation(out=s8[:, 2:4], in_=s8[:, 0:2], func=AF.Sqrt, bias=EPS_AP[:, 0:1]),
                 reads=[b_s8, b_eps], writes=[b_s8])
            S.op("dve", lambda e, s8=s8: e.reciprocal(out=s8[:, 4:6], in_=s8[:, 2:4]), reads=[b_s8], writes=[b_s8])
            cn, b_cn = cnr.next()
            S.op("dve", lambda e, cn=cn, pA1=pA1, s8=s8: e.tensor_scalar(out=cn[:, 0:384], in0=pA1[:, 0:384], scalar1=s8[:, 4:5],
                                                  scalar2=None, op0=ALU.mult), reads=[b_pA1, b_s8], writes=[b_cn])
            S.op("dve", lambda e, cn=cn, pA2=pA2, s8=s8: e.tensor_scalar(out=cn[:, 384:640], in0=pA2[:, 0:256], scalar1=s8[:, 5:6],
                                                  scalar2=None, op0=ALU.mult), reads=[b_pA2, b_s8], writes=[b_cn])
            kpe, b_kpe = kper.next()
            S.op("act", lambda e, kpe=kpe, pA2=pA2: e.copy(out=kpe[:, 0, :], in_=pA2[:, 256:288]), reads=[b_pA2], writes=[b_kpe])
            yield
            ptr, b_ptr = psT.next()
            pv = ptr[:, :].bitcast(BF16)
            for kc in range(5):
                S.op("pe", lambda e, kc=kc, pv=pv, cn=cn: e.transpose(out=pv[:, kc * 128:(kc + 1) * 128],
                                                         in_=cn[:, kc * 128:(kc + 1) * 128], identity=identb[:]),
                     reads=[b_cn, b_identb], writes=[b_ptr])
            cT, b_cT = cTr.next()
            S.op("dve", lambda e, cT=cT, pv=pv: e.tensor_copy(out=cT[:], in_=pv[:, 0:640].rearrange("p (k t) -> p k t", k=5)),
                 reads=[b_ptr], writes=[b_cT])
            yield
            qsb, b_qsb = qsbr.next()
            kvsb, b_kvsb = kvsbr.next()
            for nb in range(3):
                pq, b_pq = psQ.next()
                for kc in range(3):
                    S.op("pe", lambda e, kc=kc, nb=nb, pq=pq, cT=cT: e.matmul(out=pq[:], lhsT=cT[:, kc, :],
                                                                 rhs=wq[:, kc, nb * 512:(nb + 1) * 512],
                                                                 start=(kc == 0), stop=(kc == 2)),
                         reads=[b_cT, b_wq], writes=[b_pq])
                S.op("act", lambda e, nb=nb, pq=pq, qsb=qsb: e.copy(out=qsb[:, nb * 512:(nb + 1) * 512], in_=pq[:]),
                     reads=[b_pq], writes=[b_qsb])
            for nb in range(4):
                pq, b_pq = psQ.next()
                for kc in range(2):
                    S.op("pe", lambda e, kc=kc, nb=nb, pq=pq, cT=cT: e.matmul(out=pq[:], lhsT=cT[:, 3 + kc, :],
                                                                 rhs=wkv[:, kc, nb * 512:(nb + 1) * 512],
                                                                 start=(kc == 0), stop=(kc == 1)),
                         reads=[b_cT, b_wkv], writes=[b_pq])
                eng = "act" if nb % 2 == 0 else "dve"
                if eng == "act":
                    S.op("act", lambda e, nb=nb, pq=pq, kvsb=kvsb: e.copy(out=kvsb[:, nb * 512:(nb + 1) * 512], in_=pq[:]),
                         reads=[b_pq], writes=[b_kvsb])
                else:
                    S.op("dve", lambda e, nb=nb, pq=pq, kvsb=kvsb: e.tensor_copy(out=kvsb[:, nb * 512:(nb + 1) * 512], in_=pq[:]),
                         reads=[b_pq], writes=[b_kvsb])
            q3 = qsb[:, :].rearrange("p (h d) -> p h d", h=16)
            kv3 = kvsb[:, :].rearrange("p (h d) -> p h d", h=16)
            ss, b_ss = sst.next()
            S.op("act", lambda e, jk=jk, qsb=qsb: e.activation(out=jk[:, :], in_=qsb[:, :], func=AF.Square),
                 reads=[b_qsb], writes=[b_jk])
            S.op("dve", lambda e, jk=jk, ss=ss: e.tensor_reduce(out=ss[:, 0:16], in_=jk[:, :].rearrange("p (h d) -> p h d", h=16),
                                                  axis=AX.X, op=ALU.add), reads=[b_jk], writes=[b_ss])
            tk, b_tk = tmpkr.next()
            S.op("act", lambda e, tk=tk, kv3=kv3: e.activation(out=tk[:], in_=kv3[:, :, 0:64], func=AF.Square),
                 reads=[b_kvsb], writes=[b_tk])
            S.op("dve", lambda e, tk=tk, ss=ss: e.tensor_reduce(out=ss[:, 16:32], in_=tk[:], axis=AX.X, op=ALU.add),
                 reads=[b_tk], writes=[b_ss])
            kpg, b_kpg = kpgr.next()
            S.op("act", lambda e, kpg=kpg, kpe=kpe, ss=ss: e.activation(out=kpg[:, 0, :], in_=kpe[:, 0, :], func=AF.Square,
                                               accum_out=ss[:, 96:97]), reads=[b_kpe], writes=[b_kpg, b_ss])
            S.op("dve", lambda e, ss=ss: e.tensor_scalar(out=ss[:, 16:32], in0=ss[:, 16:32], scalar1=ss[:, 96:97], scalar2=None,
                                                  op0=ALU.add), reads=[b_ss], writes=[b_ss])
            S.op("act", lambda e, ss=ss: e.activation(out=ss[:, 32:64], in_=ss[:, 0:32], func=AF.Sqrt, bias=EPS_AP[:, 0:1],
                                               scale=1.0 / 96.0), reads=[b_ss, b_eps], writes=[b_ss])
            S.op("dve", lambda e, ss=ss: e.reciprocal(out=ss[:, 64:96], in_=ss[:, 32:64]), reads=[b_ss], writes=[b_ss])
            rsq = ss[:, 64:80]
            rsk = ss[:, 80:96]
            S.op("dve", lambda e, q3=q3, rsq=rsq: e.tensor_tensor(out=q3, in0=q3, in1=rsq.unsqueeze(2).to_broadcast([128, 16, 96]),
                                                  op=ALU.mult), reads=[b_qsb, b_ss], writes=[b_qsb])
            S.op("pool", lambda e, q3=q3: e.tensor_tensor(out=q3, in0=q3, in1=gq.unsqueeze(1).to_broadcast([128, 16, 96]),
                                                   op=ALU.mult), reads=[b_qsb, b_hn], writes=[b_qsb])
            qb, b_qb = qbr.next()
            S.op("act", lambda e, qb=qb, q3=q3: e.copy(out=qb[:, :, 0:64], in_=q3[:, :, 0:64]), reads=[b_qsb], writes=[b_qb])
            for fn, rd, wr in rope(q3[:, :, 64:96], 16, tb, qb[:, :, 64:96], ropet):
                S.op("dve", fn, reads=[b_qsb] + rd, writes=wr + ([b_qb] if not wr else []))
            S.op("dve", lambda e, tk=tk, kv3=kv3, rsk=rsk: e.tensor_tensor(out=tk[:], in0=kv3[:, :, 0:64],
                                                  in1=rsk.unsqueeze(2).to_broadcast([128, 16, 64]), op=ALU.mult),
                 reads=[b_kvsb, b_ss], writes=[b_tk])
            kb, b_kb = kbr.next()
            S.op("pool", lambda e, tk=tk, kb=kb: e.tensor_tensor(out=kb[:, :, 0:64], in0=tk[:],
                                                   in1=gk[:, 0:64].unsqueeze(1).to_broadcast([128, 16, 64]), op=ALU.mult),
                 reads=[b_tk, b_hn], writes=[b_kb])
            S.op("dve", lambda e, kpg=kpg, kpe=kpe: e.tensor_tensor(out=kpg[:, 0, :], in0=kpe[:, 0, :], in1=gk[:, 64:96], op=ALU.mult),
                 reads=[b_kpe, b_hn], writes=[b_kpg])
            kr, b_kr = krr.next()
            for fn, rd, wr in rope(kpg[:, :, :], 1, tb, kr[:, :, :], ropet):
                S.op("dve", fn, reads=[b_kpg] + rd, writes=wr + ([b_kr] if not wr else []))
            S.op("dve", lambda e, kb=kb, kr=kr, rsk=rsk: e.tensor_tensor(out=kb[:, :, 64:96],
                                                  in0=kr[:, 0, :].unsqueeze(1).to_broadcast([128, 16, 32]),
                                                  in1=rsk.unsqueeze(2).to_broadcast([128, 16, 32]), op=ALU.mult),
                 reads=[b_kr, b_ss], writes=[b_kb])
            if tb % 4 == 0:
                sh['vb'] = vbr.next()
            vb, b_vb = sh['vb']
            S.op("pool", lambda e, vb=vb, kv3=kv3, tb=tb: e.tensor_copy(out=vb[:, :, tb % 4, 0:64], in_=kv3[:, :, 64:128]),
                 reads=[b_kvsb], writes=[b_vb])
            yield
            if tb % 2 == 0:
                sh['qT'] = qTr.next()
                sh['kT'] = kTr.next()
            qTs, b_qTs = sh['qT']
            kTs, b_kTs = sh['kT']
            for (src, b_src, dstT, b_dstT) in ((qb, b_qb, qTs, b_qTs), (kb, b_kb, kTs, b_kTs)):
                for half in range(2):
                    ptr, b_ptr = psT.next()
                    pv = ptr[:, :].bitcast(BF16)
                    for hh in range(8):
                        S.op("pe", lambda e, hh=hh, pv=pv, src=src, half=half: e.transpose(
                            out=pv[0:96, hh * 128:(hh + 1) * 128], in_=src[:, half * 8 + hh, :], identity=identb[:]),
                            reads=[b_src, b_identb], writes=[b_ptr])
                    off = (tb % 2) * 128
                    eng = "act" if half == 0 else "dve"
                    if eng == "act":
                        S.op("act", lambda e, pv=pv, dstT=dstT, half=half, off=off: e.copy(
                            out=dstT[0:96, half * 8:(half + 1) * 8, off:off + 128],
                            in_=pv[0:96, :].rearrange("p (h t) -> p h t", h=8)), reads=[b_ptr], writes=[b_dstT])
                    else:
                        S.op("dve", lambda e, pv=pv, dstT=dstT, half=half, off=off: e.tensor_copy(
                            out=dstT[0:96, half * 8:(half + 1) * 8, off:off + 128],
                            in_=pv[0:96, :].rearrange("p (h t) -> p h t", h=8)), reads=[b_ptr], writes=[b_dstT])
            if tb % 2 == 1:
                t0 = (tb - 1) * 128
                S.dma(lambda e, qTs=qTs, t0=t0: e.dma_start(out=QT[:, :, t0:t0 + 256].rearrange("h p t -> p h t"),
                                                             in_=qTs[0:96, :, :]),
                      reads=[b_qTs], writes=[b_QT], sem_buf=b_qTs, eng="pool")
                S.dma(lambda e, kTs=kTs, t0=t0: e.dma_start(out=KT[:, :, t0:t0 + 256].rearrange("h p t -> p h t"),
                                                             in_=kTs[0:96, :, :]),
                      reads=[b_kTs], writes=[b_KT], sem_buf=b_kTs, eng="pool")
            if tb % 4 == 3:
                c0 = tb - 3
                S.dma(lambda e, vb=vb, c0=c0: e.dma_start(out=Vs[:, :, c0:c0 + 4, :].rearrange("h p t c -> p h (t c)"),
                                                           in_=vb[:, :, :, :].rearrange("p h t c -> p h (t c)")),
                      reads=[b_vb], writes=[b_Vs], sem_buf=b_vb, eng="pool")
        gens = {}
        for i in range(32 + 2):
            if i < 32:
                gens[i] = block(i)
                next(gens[i])
            if 0 <= i - 1 < 32:
                next(gens[i - 1])
            if 0 <= i - 2 < 32:
                next(gens[i - 2], None)
            if 0 <= i - 1 < 32:
                next(gens[i - 1])
        S.barrier()
    if stop_after <= 2:
        S.emit(nc)
        return nc

    with ExitStack() as st:
        wzr = Ring(nc, st, un("z_w"), [128, 8, 512], BF16, 4)
        wdr = Ring(nc, st, un("dt_w"), [128, 8, 64], BF16, 1)
        with ExitStack() as st2:
            stgz = Ring(nc, st2, un("z_stg"), [128, 8, 512], F32, 2)
            stgd = Ring(nc, st2, un("dt_stg"), [128, 8, 64], F32, 1)
            wz = [wload(stgz, wzr, w_in[:, C_Z + cb * 512:C_Z + (cb + 1) * 512], 8, 512, gmix) for cb in range(4)]
            wdt, b_wdt = wload(stgd, wdr, w_in[:, C_DTF:C_DTF + 64], 8, 64, gmix)
            S.barrier()
        psz = Ring(nc, st, un("z_ps"), [128, 512], F32, 4, psum=True)
        psd = Ring(nc, st, un("dt_ps"), [128, 512], F32, 2, psum=True)
        zst = Ring(nc, st, un("z_st"), [128, 2048], BF16, 3)
        for tb in range(32):
            tsl = slice(tb * 128, (tb + 1) * 128)
            zt, b_zt = zst.next()
            for cb in range(4):
                pz, b_pz = psz.next()
                w_, b_w = wz[cb]
                for kc in range(8):
                    S.op("pe", lambda e, kc=kc, pz=pz, w_=w_, tsl=tsl: e.matmul(out=pz[:], lhsT=hT[:, kc, tsl], rhs=w_[:, kc, :],
                                                                  start=(kc == 0), stop=(kc == 7)),
                         reads=[b_hT, b_w], writes=[b_pz])
                S.op("act", lambda e, pz=pz, zt=zt, cb=cb: e.activation(out=zt[:, cb * 512:(cb + 1) * 512], in_=pz[:], func=AF.Silu),
                     reads=[b_pz], writes=[b_zt])
            pd, b_pd = psd.next()
            for kc in range(8):
                S.op("pe", lambda e, kc=kc, pd=pd, tsl=tsl: e.matmul(out=pd[:, 0:64], lhsT=hT[:, kc, tsl], rhs=wdt[:, kc, :],
                                                       start=(kc == 0), stop=(kc == 7)), reads=[b_hT, b_wdt], writes=[b_pd])
            S.op("dve", lambda e, pd=pd, tb=tb: e.tensor_copy(out=dtraw[:, tb, :], in_=pd[:, 0:64]), reads=[b_pd], writes=[b_dtraw])
            S.dma(lambda e, zt=zt, tsl=tsl: e.dma_start(out=zs[tsl, :], in_=zt[:]), reads=[b_zt], writes=[b_zs],
                  sem_buf=b_zt, eng="pool")
        S.barrier()

    with ExitStack() as st:
        stg = Ring(nc, st, un("x_stg"), [128, 8, 128], F32, 3)
        wr = Ring(nc, st, un("x_w"), [128, 8, 128], BF16, 3)
        psx = Ring(nc, st, un("x_ps"), [128, 512], F32, 3, psum=True)
        psc = Ring(nc, st, un("x_pc"), [128, 512], F32, 3, psum=True)
        xpre = Ring(nc, st, un("x_pre"), [128, S_ + 4], BF16, 2)
        dgr = Ring(nc, st, un("x_dg"), [128, 5, 128], BF16, 2)
        xcst = Ring(nc, st, un("x_cst"), [128, S_], BF16, 2)
        for (t_, b_) in xpre.items:
            S.op("pool", lambda e, t_=t_: e.memset(t_[:], 0.0), writes=[b_])

        def loadx(j):
            return wload(stg, wr, w_in[:, C_XBC + j * 128:C_XBC + (j + 1) * 128], 8, 128, gmix)

        def compx(j, h):
            w_, b_w = h
            xp, b_xp = xpre.next()
            dg, b_dg = dgr.next()
            for k in range(5):
                S.op("dve", lambda e, k=k, dg=dg, j=j: e.tensor_scalar(out=dg[:, k, :], in0=identf, scalar1=convw_t[:, j, k:k + 1],
                                                           scalar2=None, op0=ALU.mult),
                     reads=[b_mats, b_convw], writes=[b_dg])
            for tb in range(8):
                px, b_px = psx.next()
                ts = slice(tb * 512, (tb + 1) * 512)
                for kc in range(8):
                    S.op("pe", lambda e, kc=kc, px=px, w_=w_, ts=ts: e.matmul(out=px[:], lhsT=w_[:, kc, :], rhs=hT[:, kc, ts],
                                                                start=(kc == 0), stop=(kc == 7)),
                         reads=[b_w, b_hT], writes=[b_px])
                if tb % 2 == 0:
                    S.op("act", lambda e, px=px, xp=xp, tb=tb: e.copy(out=xp[:, 2 + tb * 512:2 + (tb + 1) * 512], in_=px[:]),
                         reads=[b_px], writes=[b_xp])
                else:
                    S.op("dve", lambda e, px=px, xp=xp, tb=tb: e.tensor_copy(out=xp[:, 2 + tb * 512:2 + (tb + 1) * 512], in_=px[:]),
                         reads=[b_px], writes=[b_xp])
            xc_, b_xc = xcst.next()
            for tb in range(8):
                pc, b_pc = psc.next()
                for k in range(5):
                    S.op("pe", lambda e, k=k, pc=pc, dg=dg, xp=xp, tb=tb: e.matmul(
                        out=pc[:], lhsT=dg[:, k, :], rhs=xp[:, tb * 512 + k:tb * 512 + k + 512],
                        start=(k == 0), stop=(k == 4)), reads=[b_dg, b_xp], writes=[b_pc])
                S.op("act", lambda e, pc=pc, xc_=xc_, tb=tb, j=j: e.activation(out=xc_[:, tb * 512:(tb + 1) * 512], in_=pc[:],
                                                               func=AF.Silu, bias=convb_t[:, j:j + 1]),
                     reads=[b_pc, b_convb], writes=[b_xc])
            for q4 in range(4):
                S.dma(lambda e, xc_=xc_, j=j, q4=q4: e.dma_start(
                    out=xcs[q4 * 8:(q4 + 1) * 8, :, j, :].rearrange("c p t -> p c t"),
                    in_=xc_[:, q4 * 1024:(q4 + 1) * 1024].rearrange("p (c t) -> p c t", c=8)),
                    reads=[b_xc], writes=[b_xcs], sem_buf=b_xc, eng="pool")

        pipeline(24, loadx, compx, 2)

        def loadg(j):
            return wload(stg, wr, w_in[:, C_GA + j * 128:C_GA + (j + 1) * 128], 8, 128, gmix)

        def compg(j, h):
            w_, b_w = h
            gt_, b_gt = xcst.next()
            for tb in range(8):
                px, b_px = psx.next()
                ts = slice(tb * 512, (tb + 1) * 512)
                for kc in range(8):
                    S.op("pe", lambda e, kc=kc, px=px, w_=w_, ts=ts: e.matmul(out=px[:], lhsT=w_[:, kc, :], rhs=hT[:, kc, ts],
                                                                start=(kc == 0), stop=(kc == 7)),
                         reads=[b_w, b_hT], writes=[b_px])
                S.op("act", lambda e, px=px, gt_=gt_, ts=ts: e.activation(out=gt_[:, ts], in_=px[:], func=AF.Sigmoid),
                     reads=[b_px], writes=[b_gt])
            S.dma(lambda e, gt_=gt_, j=j: e.dma_start(out=gts[j * 128:(j + 1) * 128, :], in_=gt_[:]),
                  reads=[b_gt], writes=[b_gts], sem_buf=b_gt, eng="pool")

        pipeline(16, loadg, compg, 2)
        S.barrier()
    if stop_after <= 3:
        S.emit(nc)
        return nc
    hst.close()

    def ssd_pass(fwd):
        with ExitStack() as st:
            AT = lambda n, shp, dt=F32: st.enter_context(nc.sbuf_tensor(un(n), shp, dt))
            dt_all = AT("dt_all", [128, 32, 32]); b_dt = Buf()
            da_all = AT("da_all", [128, 32, 32]); b_da = Buf()
            P_all = AT("P_all", [128, 32, 32]); b_P = Buf()
            bias_all = AT("bias_all", [128, 32, 32]); b_bias = Buf()
            wgt = AT("wgt", [128, 32, 32]); b_wgt = Buf()
            scl = AT("scl", [128, 32, 32]); b_scl = Buf()
            cdc = AT("cdc", [128, 32, 32]); b_cdc = Buf()
            tot = AT("tot", [128, 32, 32]); b_tot = Buf()
            nega = AT("nega", [128, 32]); b_nega = Buf()
            tmpa = AT("tmpa", [128, 32, 32]); b_tmpa = Buf()
            Sf = AT("Sf", [128, 2048]); b_Sf = [Buf() for _ in range(4)]
            Sbf = AT("Sbf", [128, 2048], BF16); b_Sbf = [Buf() for _ in range(4)]
            off = 0 if fwd else 32
            alog = ssp_t[:, off:off + 32]
            dtb = ssp_t[:, 64 + off:96 + off]
            dsk = ssp_t[:, 128:160]
            Uc = Umat if fwd else Ustr
            midx = 0 if fwd else 1
            ptA = Ring(nc, st, un("s_ptA"), [128, 512], F32, 1, psum=True)
            segb = Ring(nc, st, un("s_seg"), [128, 512], F32, 3, psum=True)
            pyr = Ring(nc, st, un("s_py"), [128, 512], F32, 2, psum=True)
            por = Ring(nc, st, un("s_po"), [128, 512], F32, 1, psum=True)
            pstr = Ring(nc, st, un("s_pst"), [128, 512], F32, 1, psum=True)
            segs = []
            for (t_, _b) in segb.items:
                for q in range(4):
                    segs.append((t_[:, q * 128:(q + 1) * 128], Buf()))
            segi = [0]
            flat = lambda t_: t_[:, :, :].rearrange("p a b -> p (a b)")
            S.op("pool", lambda e: e.memset(Sf[:], 0.0), writes=b_Sf)
            S.op("pool", lambda e: e.memset(Sbf[:], 0.0), writes=b_Sbf)
            S.op("act", lambda e: e.activation(out=nega[:], in_=alog, func=AF.Exp), reads=[b_ssp], writes=[b_nega])
            S.op("dve", lambda e: e.tensor_scalar(out=nega[:], in0=nega[:], scalar1=-1.0, scalar2=None, op0=ALU.mult),
                 reads=[b_nega], writes=[b_nega])
            S.op("dve", lambda e: e.tensor_tensor(out=tmpa[:], in0=dtraw[:, :, off:off + 32],
                                                  in1=dtb.unsqueeze(1).to_broadcast([128, 32, 32]), op=ALU.add),
                 reads=[b_dtraw, b_ssp], writes=[b_tmpa])
            S.op("act", lambda e: e.activation(out=tmpa[:], in_=tmpa[:], func=AF.Exp), reads=[b_tmpa], writes=[b_tmpa])
            S.op("act", lambda e: e.activation(out=dt_all[:], in_=tmpa[:], func=AF.Ln, bias=1.0), reads=[b_tmpa], writes=[b_dt])
            S.op("dve", lambda e: e.tensor_tensor(out=da_all[:], in0=dt_all[:], in1=nega[:].unsqueeze(1).to_broadcast([128, 32, 32]),
                                                  op=ALU.mult), reads=[b_dt, b_nega], writes=[b_da])
            for half in range(2):
                pp, b_pp = ptA.next()
                S.op("pe", lambda e, pp=pp, half=half: e.matmul(out=pp[:], lhsT=Uc, rhs=flat(da_all)[:, half * 512:(half + 1) * 512],
                                                                start=True, stop=True), reads=[b_mats, b_da], writes=[b_pp])
                S.op("dve", lambda e, pp=pp, half=half: e.tensor_copy(out=flat(P_all)[:, half * 512:(half + 1) * 512], in_=pp[:]),
                     reads=[b_pp], writes=[b_P])
            for half in range(2):
                pp, b_pp = ptA.next()
                S.op("pe", lambda e, pp=pp, half=half: e.matmul(out=pp[:], lhsT=onesf, rhs=flat(da_all)[:, half * 512:(half + 1) * 512],
                                                                start=True, stop=True), reads=[b_mats, b_da], writes=[b_pp])
                S.op("dve", lambda e, pp=pp, half=half: e.tensor_copy(out=flat(tot)[:, half * 512:(half + 1) * 512], in_=pp[:]),
                     reads=[b_pp], writes=[b_tot])
            S.op("dve", lambda e: e.tensor_tensor(out=tmpa[:], in0=tot[:], in1=P_all[:], op=ALU.subtract),
                 reads=[b_tot, b_P, b_dt], writes=[b_tmpa])
            e1, b_e1 = (scl, b_scl) if fwd else (wgt, b_wgt)
            e2, b_e2 = (wgt, b_wgt) if fwd else (scl, b_scl)
            S.op("act", lambda e: e.activation(out=e1[:], in_=P_all[:], func=AF.Exp), reads=[b_P], writes=[b_e1])
            S.op("act", lambda e: e.activation(out=e2[:], in_=tmpa[:], func=AF.Exp), reads=[b_tmpa], writes=[b_e2])
            S.op("act", lambda e: e.activation(out=cdc[:], in_=tot[:], func=AF.Exp), reads=[b_tot], writes=[b_cdc])
            S.op("dve", lambda e: e.tensor_scalar(out=bias_all[:], in0=P_all[:], scalar1=(-1.0 if fwd else 1.0), scalar2=None,
                                                  op0=ALU.mult), reads=[b_P], writes=[b_bias])

            xcr = Ring(nc, st, un("s_xc"), [128, 24, 128], BF16, 3)
            xsr = Ring(nc, st, un("s_xs"), [128, 2048], BF16, 2)
            Btr = Ring(nc, st, un("s_Bt"), [128, 512], BF16, 2)
            cbr = Ring(nc, st, un("s_cb"), [128, 512], F32, 2)
            xdtr = Ring(nc, st, un("s_xdt"), [128, 2048], BF16, 2)
            xwr = Ring(nc, st, un("s_xw"), [128, 2048], BF16, 2)
            decr = Ring(nc, st, un("s_dec"), [128, 128], F32, 12)
            MTr = Ring(nc, st, un("s_MT"), [128, 128], BF16, 12)
            yaccr = Ring(nc, st, un("s_ya"), [128, 2048], F32, 2)
            tmpr = Ring(nc, st, un("s_tmp"), [128, 512], F32, 2)
            if fwd:
                dskr = Ring(nc, st, un("s_dsk"), [128, 2048], F32, 1)
            else:
                zr = Ring(nc, st, un("s_z"), [128, 2048], BF16, 4)
                yfr = Ring(nc, st, un("s_yf"), [128, 2048], F32, 4)
                jkr = Ring(nc, st, un("s_jk"), [128, 512], BF16, 1)
                st4r = Ring(nc, st, un("s_st4"), [128, 12], F32, 2)
                mbr = Ring(nc, st, un("s_mb"), [128, 2048], BF16, 2)
                mstr = Ring(nc, st, un("s_mst"), [128, 16, 128], BF16, 2)
            order = list(range(32)) if fwd else list(range(31, -1, -1))

            def load(ci):
                c = order[ci]
                xc_, b_xc = xcr.next()
                S.dma(lambda e: e.dma_start(out=xc_[:], in_=xcs[c, :, :, :]), reads=[b_xcs], writes=[b_xc], sem_buf=b_xc)
                if fwd:
                    return (xc_, b_xc)
                z_, b_z = zr.next()
                yf_, b_yf = yfr.next()
                S.dma(lambda e: e.dma_start(out=z_[:], in_=zs[c * 128:(c + 1) * 128, :]), reads=[b_zs], writes=[b_z], sem_buf=b_z)
                S.dma(lambda e: e.dma_start(out=yf_[:], in_=yfs[c * 128:(c + 1) * 128, :]), reads=[b_yfs], writes=[b_yf], sem_buf=b_yf)
                return (xc_, b_xc, z_, b_z, yf_, b_yf)

            def prologue(ci, h):
                c = order[ci]
                xc_, b_xc = h[0], h[1]
                xs, b_xs = xsr.next()
                for half in range(2):
                    pt, b_pt = ptA.next()
                    pv = pt[:, :].bitcast(BF16)
                    for jj in range(8):
                        S.op("pe", lambda e, pv=pv, jj=jj, half=half: e.transpose(out=pv[:, jj * 128:(jj + 1) * 128],
                                                                                 in_=xc_[:, half * 8 + jj, :], identity=identb[:]),
                             reads=[b_xc, b_identb], writes=[b_pt])
                    if half == 0:
                        S.op("act", lambda e, pv=pv: e.copy(out=xs[:, 0:1024], in_=pv[:, :]), reads=[b_pt], writes=[b_xs])
                    else:
                        S.op("dve", lambda e, pv=pv: e.tensor_copy(out=xs[:, 1024:2048], in_=pv[:, :]), reads=[b_pt], writes=[b_xs])
                pt, b_pt = ptA.next()
                pvb = pt[:, :].bitcast(BF16)
                for g in range(4):
                    S.op("pe", lambda e, g=g: e.transpose(out=pvb[:, g * 128:(g + 1) * 128], in_=xc_[:, 16 + g, :], identity=identb[:]),
                         reads=[b_xc, b_identb], writes=[b_pt])
                Bt, b_Bt = Btr.next()
                S.op("dve", lambda e: e.tensor_copy(out=Bt[:], in_=pvb[:, 0:512]), reads=[b_pt], writes=[b_Bt])
                pcb, b_pcb = ptA.next()
                for g in range(4):
                    S.op("pe", lambda e, g=g: e.matmul(out=pcb[:, g * 128:(g + 1) * 128], lhsT=xc_[:, 16 + g, :], rhs=xc_[:, 20 + g, :],
                                                       start=True, stop=True), reads=[b_xc], writes=[b_pcb])
                cbT, b_cbT = cbr.next()
                S.op("act", lambda e: e.copy(out=cbT[:], in_=pcb[:]), reads=[b_pcb], writes=[b_cbT])
                xdt, b_xdt = xdtr.next()
                xw, b_xw = xwr.next()
                v3 = lambda t_: t_[:, :].rearrange("p (h d) -> p h d", h=32)
                S.op("dve", lambda e: e.tensor_tensor(out=v3(xdt), in0=v3(xs), in1=dt_all[:, c, :].unsqueeze(2).to_broadcast([128, 32, 64]),
                                                      op=ALU.mult), reads=[b_xs, b_dt], writes=[b_xdt])
                S.op("pool", lambda e: e.tensor_tensor(out=v3(xw), in0=v3(xdt), in1=wgt[:, c, :].unsqueeze(2).to_broadcast([128, 32, 64]),
                                                       op=ALU.mult), reads=[b_xdt, b_wgt], writes=[b_xw])
                return (xs, b_xs, Bt, b_Bt, cbT, b_cbT, xdt, b_xdt, xw, b_xw)

            def comp(ci, h, pr):
                c = order[ci]
                xc_, b_xc = h[0], h[1]
                xs, b_xs, Bt, b_Bt, cbT, b_cbT, xdt, b_xdt, xw, b_xw = pr
                v3 = lambda t_: t_[:, :].rearrange("p (h d) -> p h d", h=32)
                g8 = lambda t_: t_.rearrange("p (h d) -> p h d", h=8)
                ya, b_ya = yaccr.next()
                LAGH = 4
                mts = {}
                cur = {}

                def stageA(bi):
                    sb_, b_sb = segb.next()
                    for q in range(4):
                        h_ = bi * 4 + q
                        seg = sb_[:, q * 128:(q + 1) * 128]
                        S.op("pe", lambda e, seg=seg, h_=h_: e.matmul(out=seg, lhsT=da_all[:, c, h_:h_ + 1].to_broadcast([128, 128]),
                                                                      rhs=Uc, start=True, stop=False),
                             reads=[b_da, b_mats], writes=[b_sb])
                        S.op("pe", lambda e, seg=seg: e.matmul(out=seg, lhsT=identb[:], rhs=negm[:, midx, :], start=False, stop=True),
                             reads=[b_identb, b_negm], writes=[b_sb])
                    for q in range(4):
                        h_ = bi * 4 + q
                        g = h_ // 8
                        seg = sb_[:, q * 128:(q + 1) * 128]
                        dec, b_dec = decr.next()
                        S.op("act", lambda e, seg=seg, dec=dec, h_=h_: e.activation(out=dec[:], in_=seg, func=AF.Exp,
                                                                                   bias=bias_all[:, c, h_:h_ + 1],
                                                                                   scale=(1.0 if fwd else -1.0)),
                             reads=[b_sb, b_bias], writes=[b_dec])
                        MT, b_MT = MTr.next()
                        S.op("dve" if h_ % 2 == 0 else "pool", lambda e, dec=dec, MT=MT, g=g: e.tensor_tensor(
                            out=MT[:], in0=dec[:], in1=cbT[:, g * 128:(g + 1) * 128], op=ALU.mult),
                            reads=[b_dec, b_cbT], writes=[b_MT])
                        mts[h_] = (MT, b_MT)

                def stageB(h_):
                    g = h_ // 8
                    hh = h_ % 8
                    if hh == 0:
                        cur[0] = pyr.next()
                    py, b_py = cur[0]
                    MT, b_MT = mts.pop(h_)
                    S.op("pe", lambda e, MT=MT, py=py, hh=hh, h_=h_: e.matmul(out=py[:, hh * 64:(hh + 1) * 64], lhsT=MT[:],
                                                                             rhs=xdt[:, h_ * 64:(h_ + 1) * 64], start=True, stop=True),
                         reads=[b_MT, b_xdt], writes=[b_py])
                    if hh != 7:
                        return
                    po, b_po = por.next()
                    S.op("pe", lambda e, po=po, g=g: e.matmul(out=po[:], lhsT=xc_[:, 20 + g, :], rhs=Sbf[:, g * 512:(g + 1) * 512],
                                                              start=True, stop=True), reads=[b_xc, b_Sbf[g]], writes=[b_po])
                    pst, b_pst = pstr.next()
                    S.op("pe", lambda e, pst=pst, g=g: e.matmul(out=pst[:], lhsT=Bt[:, g * 128:(g + 1) * 128], rhs=xw[:, g * 512:(g + 1) * 512],
                                                                start=True, stop=True), reads=[b_Bt, b_xw], writes=[b_pst])
                    tmp, b_tmp = tmpr.next()
                    S.op("dve", lambda e, po=po, tmp=tmp, g=g: e.tensor_tensor(
                        out=g8(tmp[:, :]), in0=g8(po[:, :]), in1=scl[:, c, g * 8:(g + 1) * 8].unsqueeze(2).to_broadcast([128, 8, 64]),
                        op=ALU.mult), reads=[b_po, b_scl], writes=[b_tmp])
                    S.op("dve", lambda e, py=py, tmp=tmp, g=g: e.tensor_tensor(out=ya[:, g * 512:(g + 1) * 512], in0=py[:], in1=tmp[:],
                                                                              op=ALU.add), reads=[b_py, b_tmp], writes=[b_ya])
                    S.op("pool", lambda e, g=g: e.tensor_tensor(
                        out=g8(Sf[:, g * 512:(g + 1) * 512]), in0=g8(Sf[:, g * 512:(g + 1) * 512]),
                        in1=cdc[:, c, g * 8:(g + 1) * 8].unsqueeze(2).to_broadcast([128, 8, 64]), op=ALU.mult),
                        reads=[b_Sf[g], b_cdc], writes=[b_Sf[g]])
                    S.op("dve", lambda e, pst=pst, g=g: e.tensor_tensor(out=Sf[:, g * 512:(g + 1) * 512], in0=pst[:],
                                                                       in1=Sf[:, g * 512:(g + 1) * 512], op=ALU.add),
                         reads=[b_pst, b_Sf[g]], writes=[b_Sf[g]])
                    S.op("act", lambda e, g=g: e.copy(out=Sbf[:, g * 512:(g + 1) * 512], in_=Sf[:, g * 512:(g + 1) * 512]),
                         reads=[b_Sf[g]], writes=[b_Sbf[g]])

                for k in range(8 + 2):
                    if k < 8:
                        stageA(k)
                    if k >= 2:
                        for q in range(4):
                            stageB((k - 2) * 4 + q)
                if fwd:
                    dk, b_dk = dskr.next()
                    S.op("pool", lambda e: e.tensor_tensor(out=v3(dk), in0=v3(xs), in1=dsk.unsqueeze(2).to_broadcast([128, 32, 64]),
                                                           op=ALU.mult), reads=[b_xs, b_ssp], writes=[b_dk])
                    S.op("pool", lambda e: e.tensor_tensor(out=ya[:], in0=ya[:], in1=dk[:], op=ALU.add),
                         reads=[b_ya, b_dk], writes=[b_ya])
                    S.dma(lambda e: e.dma_start(out=yfs[c * 128:(c + 1) * 128, :], in_=ya[:]), reads=[b_ya], writes=[b_yfs],
                          sem_buf=b_ya, eng="pool")
                    return
                def epi():
                    z_, b_z, yf_, b_yf = h[2], h[3], h[4], h[5]
                    if dbg:
                        S.dma(lambda e: e.dma_start(out=ybs[c * 128:(c + 1) * 128, :], in_=ya[:]), reads=[b_ya], writes=[b_ybs],
                              sem_buf=b_ya, eng="pool")
                    S.op("pool", lambda e: e.tensor_tensor(out=ya[:], in0=ya[:], in1=yf_[:], op=ALU.add), reads=[b_ya, b_yf], writes=[b_ya])
                    S.op("dve", lambda e: e.tensor_tensor(out=ya[:], in0=ya[:], in1=z_[:], op=ALU.mult), reads=[b_ya, b_z], writes=[b_ya])
                    jk, b_jk = jkr.next()
                    s4, b_s4 = st4r.next()
                    for g in range(4):
                        S.op("act", lambda e, g=g: e.activation(out=jk[:], in_=ya[:, g * 512:(g + 1) * 512], func=AF.Square,
                                                                scale=1.0 / math.sqrt(512.0), accum_out=s4[:, g:g + 1]),
                             reads=[b_ya], writes=[b_jk, b_s4])
                    S.op("act", lambda e: e.activation(out=s4[:, 4:8], in_=s4[:, 0:4], func=AF.Sqrt, bias=EPS_AP[:, 0:1]),
                         reads=[b_s4, b_eps], writes=[b_s4])
                    S.op("dve", lambda e: e.reciprocal(out=s4[:, 8:12], in_=s4[:, 4:8]), reads=[b_s4], writes=[b_s4])
                    mb, b_mb = mbr.next()
                    for g in range(4):
                        S.op("dve", lambda e, g=g: e.tensor_scalar(out=mb[:, g * 512:(g + 1) * 512], in0=ya[:, g * 512:(g + 1) * 512],
                                                                   scalar1=s4[:, 8 + g:9 + g], scalar2=None, op0=ALU.mult),
                             reads=[b_ya, b_s4], writes=[b_mb])
                    mst, b_mst = mstr.next()
                    for half in range(2):
                        pt, b_pt = ptA.next()
                        pv = pt[:, :].bitcast(BF16)
                        for jj in range(8):
                            j = half * 8 + jj
                            S.op("pe", lambda e, pv=pv, jj=jj, j=j: e.transpose(out=pv[:, jj * 128:(jj + 1) * 128],
                                                                               in_=mb[:, j * 128:(j + 1) * 128], identity=identb[:]),
                                 reads=[b_mb, b_identb], writes=[b_pt])
                        S.op("act", lambda e, pv=pv, half=half: e.copy(out=mst[:, half * 8:(half + 1) * 8, :],
                                                                       in_=pv[:, :].rearrange("p (j t) -> p j t", j=8)),
                             reads=[b_pt], writes=[b_mst])
                    for half in range(2):
                        S.dma(lambda e, half=half: e.dma_start(
                            out=mTs[half * 1024:(half + 1) * 1024, c * 128:(c + 1) * 128].rearrange("(j p) t -> p j t", p=128),
                            in_=mst[:, half * 8:(half + 1) * 8, :]), reads=[b_mst], writes=[b_mTs], sem_buf=b_mst, eng="pool")

                if pend_epi:
                    pend_epi.pop()()
                pend_epi.append(epi)

            pend_epi = []
            hs = {}
            prs = {}
            for i in range(32 + 2):
                if i < 32:
                    hs[i] = load(i)
                if 1 <= i <= 32:
                    prs[i - 1] = prologue(i - 1, hs[i - 1])
                if i >= 2:
                    comp(i - 2, hs.pop(i - 2), prs.pop(i - 2))
            if pend_epi:
                pend_epi.pop()()
        S.barrier()

    ssd_pass(True)
    if stop_after <= 4 and stop_after == 4:
        pass
    ssd_pass(False)
    if stop_after <= 4:
        S.emit(nc)
        return nc

    with ExitStack() as st:
        ktr = Ring(nc, st, un("t_k"), [128, S_], BF16, 2)
        qtr = Ring(nc, st, un("t_q"), [128, S_], BF16, 2)
        vtr = Ring(nc, st, un("t_v"), [128, 32, 65], BF16, 2)
        psS = Ring(nc, st, un("t_ps"), [128, 1536], F32, 2, psum=True)
        psO = Ring(nc, st, un("t_po"), [128, 512], F32, 2, psum=True)
        pTr = Ring(nc, st, un("t_pT"), [128, 1536], BF16, 3)
        rdr = Ring(nc, st, un("t_rd"), [128, 512], F32, 2)
        osr = Ring(nc, st, un("t_os"), [128, 512], F32, 2)
        aor = Ring(nc, st, un("t_ao"), [128, S_], BF16, 2)
        sc = 1.0 / math.sqrt(96.0)
        LAG = 1
        tiles = {}

        def ensure(h_):
            if h_ >= NH or h_ in tiles:
                return
            kt, b_kt = ktr.next()
            qt, b_qt = qtr.next()
            vt, b_vt = vtr.next()
            S.dma(lambda e: e.dma_start(out=kt[0:96, :], in_=KT[h_, :, :]), reads=[b_KT], writes=[b_kt], sem_buf=b_kt)
            S.dma(lambda e: e.dma_start(out=qt[0:96, :], in_=QT[h_, :, :]), reads=[b_QT], writes=[b_qt], sem_buf=b_qt)
            S.dma(lambda e: e.dma_start(out=vt[:], in_=Vs[h_, :, :, :]), reads=[b_Vs], writes=[b_vt], sem_buf=b_vt)
            tiles[h_] = (kt, b_kt, qt, b_qt, vt, b_vt)

        steps = [(h_, qb, gi) for h_ in range(NH) for qb in range(8) for gi in range(11)]
        pend = {}
        cur_po = {}
        cur_ao = {}
        ensure(0)
        for i in range(len(steps) + LAG):
            if i < len(steps):
                h_, qb, gi = steps[i]
                kcs = list(range(3 * gi, min(3 * gi + 3, 32)))
                kt, b_kt, qt, b_qt, vt, b_vt = tiles[h_]
                ps, b_ps = psS.next()
                for j, kc in enumerate(kcs):
                    S.op("pe", lambda e, ps=ps, kc=kc, qb=qb, j=j, kt=kt, qt=qt: e.matmul(
                        out=ps[:, j * 512:(j + 1) * 512], lhsT=kt[0:96, kc * 128:(kc + 1) * 128],
                        rhs=qt[0:96, qb * 512:(qb + 1) * 512], start=True, stop=True),
                        reads=[b_kt, b_qt], writes=[b_ps])
                pT, b_pT = pTr.next()
                n = len(kcs) * 512
                S.op("act", lambda e, ps=ps, pT=pT, n=n: e.activation(out=pT[:, 0:n], in_=ps[:, 0:n], func=AF.Exp, scale=sc),
                     reads=[b_ps], writes=[b_pT])
                pend[i] = (pT, b_pT)
            if i >= LAG:
                h_, qb, gi = steps[i - LAG]
                kcs = list(range(3 * gi, min(3 * gi + 3, 32)))
                kt, b_kt, qt, b_qt, vt, b_vt = tiles[h_]
                pT, b_pT = pend.pop(i - LAG)
                if gi == 0:
                    cur_po[0] = psO.next()
                    if qb == 0:
                        cur_ao[0] = aor.next()
                        ensure(h_ + 1)
                po, b_po = cur_po[0]
                ao, b_ao = cur_ao[0]
                for j, kc in enumerate(kcs):
                    S.op("pe", lambda e, po=po, pT=pT, kc=kc, j=j, vt=vt: e.matmul(
                        out=po[0:65, :], lhsT=vt[:, kc, :], rhs=pT[:, j * 512:(j + 1) * 512],
                        start=(kc == 0), stop=(kc == 31)), reads=[b_vt, b_pT], writes=[b_po])
                if gi == 10:
                    osb, b_osb = osr.next()
                    S.op("dve", lambda e, po=po, osb=osb: e.tensor_copy(out=osb[0:65, :], in_=po[0:65, :]), reads=[b_po], writes=[b_osb])
                    rd, b_rd = rdr.next()
                    S.op("dve", lambda e, osb=osb, rd=rd: e.reciprocal(out=rd[64:65, :], in_=osb[64:65, :]), reads=[b_osb], writes=[b_rd])
                    pb, b_pb = psS.next()
                    S.op("pe", lambda e, pb=pb, rd=rd: e.matmul(out=pb[0:64, 0:512], lhsT=mats[64:65, 2, 0:64],
                                                                rhs=rd[64:65, :], start=True, stop=True),
                         reads=[b_mats, b_rd], writes=[b_pb], multi=True)
                    qs = slice(qb * 512, (qb + 1) * 512)
                    S.op("dve", lambda e, pb=pb, osb=osb, qs=qs, ao=ao: e.tensor_tensor(
                        out=ao[0:64, qs], in0=pb[0:64, 0:512], in1=osb[0:64, :], op=ALU.mult),
                        reads=[b_pb, b_osb], writes=[b_ao])
                    if qb == 7:
                        S.dma(lambda e, h_=h_, ao=ao: e.dma_start(out=aTs[h_ * 64:(h_ + 1) * 64, :], in_=ao[0:64, :]),
                              reads=[b_ao], writes=[b_aTs], sem_buf=b_ao, eng="pool")
        S.barrier()
    if stop_after <= 5:
        S.emit(nc)
        return nc

    with ExitStack() as st:
        wpar = Ring(nc, st, un("m_wpa"), [128, 8, D_], BF16, 1)
        wor = Ring(nc, st, un("m_wo"), [128, 8, D_], BF16, 1)
        wpb_t = st.enter_context(nc.sbuf_tensor(un("m_wpbt"), [128, 16, D_], BF16)); b_wpb = Buf()
        wpb = wpb_t
        with ExitStack() as st2:
            s8 = Ring(nc, st2, un("m_s8"), [128, 8, D_], F32, 1)
            wpa, b_wpa = wload(s8, wpar, w_pa[:, :], 8, D_, None)
            for half in range(2):
                sg, b_sg = s8.next()
                S.dma(lambda e, sg=sg, half=half: e.dma_start(
                    out=sg[:], in_=w_pb[half * 1024:(half + 1) * 1024, :].rearrange("(kc p) n -> p kc n", p=128)),
                    writes=[b_sg], sem_buf=b_sg)
                S.op("pool", lambda e, sg=sg, half=half: e.tensor_tensor(
                    out=wpb_t[:, half * 8:(half + 1) * 8, :], in0=sg[:],
                    in1=gssm_t[:, half * 8:(half + 1) * 8].unsqueeze(2).to_broadcast([128, 8, D_]), op=ALU.mult),
                    reads=[b_sg, b_gssm], writes=[b_wpb])
            wo, b_wo = wload(s8, wor, w_o[:, :], 8, D_, None)
            S.barrier()
        atr = Ring(nc, st, un("m_at"), [128, 8, 512], BF16, 2)
        mtr = Ring(nc, st, un("m_mt"), [128, 16, 512], BF16, 2)
        gtr = Ring(nc, st, un("m_gt"), [128, 16, 512], BF16, 2)
        mgr = Ring(nc, st, un("m_mg"), [128, 8, 512], BF16, 2)
        t1r = Ring(nc, st, un("m_t1"), [128, 512], F32, 2)
        t2r = Ring(nc, st, un("m_t2"), [128, 512], F32, 2)
        xr = Ring(nc, st, un("m_x"), [128, D_], F32, 3)
        ps = Ring(nc, st, un("m_ps"), [128, 512], F32, 6, psum=True)
        def loadm(t):
            at, b_at = atr.next()
            mt, b_mt = mtr.next()
            gt_, b_gt = gtr.next()
            ts = slice(t * 512, (t + 1) * 512)
            S.dma(lambda e: e.dma_start(out=at[:], in_=aTs[:, ts].rearrange("(k p) t -> p k t", p=128)), reads=[b_aTs], writes=[b_at], sem_buf=b_at)
            S.dma(lambda e: e.dma_start(out=mt[:], in_=mTs[:, ts].rearrange("(k p) t -> p k t", p=128)), reads=[b_mTs], writes=[b_mt], sem_buf=b_mt)
            S.dma(lambda e: e.dma_start(out=gt_[:], in_=gts[:, ts].rearrange("(k p) t -> p k t", p=128)), reads=[b_gts], writes=[b_gt], sem_buf=b_gt)
            return (at, b_at, mt, b_mt, gt_, b_gt)

        def compm(t, hd):
            at, b_at, mt, b_mt, gt_, b_gt = hd
            mg, b_mg = mgr.next()
            for dc in range(8):
                pa, b_pa = ps.next()
                pb, b_pb = ps.next()
                for kc in range(8):
                    S.op("pe", lambda e, pa=pa, kc=kc, dc=dc: e.matmul(out=pa[:], lhsT=wpa[:, kc, dc * 128:(dc + 1) * 128], rhs=at[:, kc, :],
                                                                       start=(kc == 0), stop=(kc == 7)), reads=[b_wpa, b_at], writes=[b_pa])
                for kc in range(16):
                    S.op("pe", lambda e, pb=pb, kc=kc, dc=dc: e.matmul(out=pb[:], lhsT=wpb[:, kc, dc * 128:(dc + 1) * 128], rhs=mt[:, kc, :],
                                                                       start=(kc == 0), stop=(kc == 15)), reads=[b_wpb, b_mt], writes=[b_pb])
                t1, b_t1 = t1r.next()
                t2, b_t2 = t2r.next()
                S.op("dve", lambda e, pa=pa, t1=t1, dc=dc: e.tensor_tensor(out=t1[:], in0=pa[:], in1=gt_[:, dc, :], op=ALU.mult),
                     reads=[b_pa, b_gt], writes=[b_t1])
                S.op("dve", lambda e, pb=pb, t2=t2, dc=dc: e.tensor_tensor(out=t2[:], in0=pb[:], in1=gt_[:, 8 + dc, :], op=ALU.mult),
                     reads=[b_pb, b_gt], writes=[b_t2])
                S.op("pool", lambda e, t1=t1, t2=t2, dc=dc: e.tensor_tensor(out=mg[:, dc, :], in0=t1[:], in1=t2[:], op=ALU.add),
                     reads=[b_t1, b_t2], writes=[b_mg])
            for sb in range(4):
                tb = t * 4 + sb
                xt, b_xt = xr.next()
                S.dma(lambda e, xt=xt, tb=tb: e.dma_start(out=xt[:], in_=x1s[tb * 128:(tb + 1) * 128, :]),
                      reads=[b_x1s], writes=[b_xt], sem_buf=b_xt)
                for half in range(2):
                    p, b_p = ps.next()
                    for kc in range(8):
                        S.op("pe", lambda e, p=p, kc=kc, sb=sb, half=half: e.matmul(
                            out=p[:], lhsT=mg[:, kc, sb * 128:(sb + 1) * 128], rhs=wo[:, kc, half * 512:(half + 1) * 512],
                            start=(kc == 0), stop=(kc == 7)), reads=[b_mg, b_wo], writes=[b_p])
                    S.op("dve", lambda e, p=p, xt=xt, half=half: e.tensor_tensor(out=xt[:, half * 512:(half + 1) * 512], in0=p[:],
                                                                                in1=xt[:, half * 512:(half + 1) * 512], op=ALU.add),
                         reads=[b_p, b_xt], writes=[b_xt])
                S.dma(lambda e, xt=xt, tb=tb: e.dma_start(out=x2s[tb * 128:(tb + 1) * 128, :], in_=xt[:]),
                      reads=[b_xt], writes=[b_x2s], sem_buf=b_xt, eng="pool")

        pipeline(8, loadm, compm, 1)
        S.barrier()
    hst2 = ExitStack()
    hT2 = hst2.enter_context(nc.sbuf_tensor("hT2", [128, 8, S_], BF16)); b_hT2 = Buf("hT2")
    norm_phase(x2s, b_x2s, hT2, b_hT2)

    ffn_gateup(w_g2, w_u2, 2, hT2, b_hT2)
    ffn_down(w_d2, x2s, b_x2s, y_out, b_yout, None)
    S.emit(nc)
    return nc


def _fm(v, kc):
    return np.ascontiguousarray(np.asarray(v, np.float32).reshape(kc, 128).T)


_CACHE = {}


def consts():
    ii = np.arange(128)
    U = (ii[:, None] <= ii[None, :]).astype(np.float32)
    Us = (ii[:, None] < ii[None, :]).astype(np.float32)
    ones = np.ones((128, 128), np.float32)
    I = np.eye(128, dtype=np.float32)
    mats = np.ascontiguousarray(np.stack([U, Us, ones, I], axis=1))
    negf = np.where(ii[:, None] > ii[None, :], -30000.0, 0.0).astype(np.float32)
    posb = np.where(ii[:, None] < ii[None, :], 30000.0, 0.0).astype(np.float32)
    neg = np.ascontiguousarray(np.stack([negf, posb], axis=1)).astype(ml_dtypes.bfloat16)
    invf = (1.0 / (10000.0 ** (np.arange(0, 32, 2, dtype=np.float32) / 32.0))).astype(np.float32)[None, :]
    return dict(c_identb=I.astype(ml_dtypes.bfloat16), c_mats=mats, c_neg=neg, c_invf=invf)


def make_shared(inp):
    f = lambda k: np.asarray(inp[k], np.float32)[0]
    d = {}
    d["gfm"] = np.ascontiguousarray(np.concatenate([_fm(f("ffn1_norm"), 8), _fm(f("mix_norm"), 8), _fm(f("ffn2_norm"), 8)], axis=1))
    d["gqa"] = _fm(f("q_a_norm"), 3)
    d["gkva"] = _fm(f("kv_a_norm"), 2)
    d["gssm"] = _fm(f("ssm_norm"), 16)
    d["w_g1"] = f("ffn1_w_gate"); d["w_u1"] = f("ffn1_w_up"); d["w_d1"] = f("ffn1_w_down")
    d["w_g2"] = f("ffn2_w_gate"); d["w_u2"] = f("ffn2_w_up"); d["w_d2"] = f("ffn2_w_down")
    d["w_in"] = f("w_in"); d["w_qb"] = f("w_q_b"); d["w_kvb"] = f("w_kv_b")
    d["hn"] = np.concatenate([f("q_head_norm"), f("k_head_norm")])[None, :].astype(np.float32)
    cw = f("conv_w")[:, 0, :]
    d["convw"] = np.ascontiguousarray(cw.T.reshape(24, 128, 5).transpose(1, 0, 2))
    d["convb"] = _fm(f("conv_b"), 24)
    d["ssp"] = np.concatenate([f("a_log_fwd"), f("a_log_bwd"), f("dt_bias_fwd"), f("dt_bias_bwd"), f("d_skip")])[None, :].astype(np.float32)
    d["w_pa"] = f("w_attn_branch"); d["w_pb"] = f("w_ssm_branch"); d["w_o"] = f("w_out")
    d.update(consts())
    return d


def make_inmap(inp, shared, b):
    d = dict(shared)
    d["x"] = np.ascontiguousarray(np.asarray(inp["x"], np.float32)[b])
    p = np.asarray(inp["positions"], np.int32)[b]
    d["pos"] = np.ascontiguousarray(p.reshape(32, 128).T)
    return d


def kernel(**inputs):
    nb = int(np.asarray(inputs["x"]).shape[0])
    nc = build(dbg=False)
    shared = make_shared(inputs)
    in_maps = [make_inmap(inputs, shared, b) for b in range(nb)]
    res = run_bass_kernel_spmd(nc, in_maps, core_ids=list(range(nb)))
    out = np.stack([np.asarray(res.results[b]["y"], dtype=np.float32) for b in range(nb)], axis=0)
    return out
```

```python
import math
from contextlib import ExitStack
import numpy as np
import ml_dtypes
import concourse.bass as bass
import concourse.mybir as mybir
from concourse.bass_utils import run_bass_kernel_spmd

F32 = mybir.dt.float32
BF16 = mybir.dt.bfloat16
I32 = mybir.dt.int32
AF = mybir.ActivationFunctionType
ALU = mybir.AluOpType
AX = mybir.AxisListType

S_ = 4096
D_ = 1024
FF = 2816
NFF = 22
NH = 16
EPS = 1e-6
C_Q, C_KV, C_PE, C_Z, C_XBC, C_DTF, C_DTB, C_GA, C_GB = 0, 384, 640, 672, 2720, 5792, 5824, 5856, 6880
IN_DIM = 7904
ENGS = ("pe", "act", "dve", "pool", "sp")
FUSE_WAITS = True


class DSem:
    def __init__(self):
        self.count = 0
        self.handle = None


class Buf:
    __slots__ = ("name", "lw", "rd", "dsem", "ep")

    def __init__(self, name=""):
        self.name = name
        self.lw = None
        self.rd = []
        self.dsem = None
        self.ep = -1


class Op:
    __slots__ = ("eng", "fn", "idx", "waits", "dwaits", "inc", "dsem", "know", "seq", "multi")


class Sched:
    def __init__(self):
        self.ops = {e: [] for e in ENGS}
        self.know = {e: {} for e in ENGS}
        self.dsems = []
        self.free = []
        self.epoch = 0

    def _add(self, eng, fn, reads, writes, dsem=None, extra=(), extra_ds=()):
        op = Op()
        op.eng = eng
        op.fn = fn
        op.idx = len(self.ops[eng])
        op.waits = {}
        op.dwaits = {}
        op.inc = False
        op.dsem = dsem
        op.seq = None
        op.multi = False
        know = self.know[eng]
        deps = list(extra)
        for b in reads:
            if b.lw is not None:
                deps.append(b.lw)
        for b in writes:
            if b.lw is not None:
                deps.append(b.lw)
            deps.extend(b.rd)
        for a in deps:
            if a is op:
                continue
            if a.dsem is None:
                if a.eng == "pe" and eng == "pe":
                    continue
                if know.get(a.eng, -1) >= a.idx:
                    continue
                a.inc = True
                cur = op.waits.get(a.eng)
                if cur is None or cur.idx < a.idx:
                    op.waits[a.eng] = a
                for k, v in a.know.items():
                    if know.get(k, -1) < v:
                        know[k] = v
                know[a.eng] = max(know.get(a.eng, -1), a.idx)
            else:
                ds = a.dsem
                v = ds.count
                if know.get(ds, -1) >= v:
                    continue
                op.dwaits[ds] = v
                for k, vv in a.know.items():
                    if know.get(k, -1) < vv:
                        know[k] = vv
                know[ds] = v
        for ds in extra_ds:
            v = ds.count
            if know.get(ds, -1) < v:
                op.dwaits[ds] = v
                know[ds] = v
        if dsem is not None:
            dsem.count += 16
        op.know = dict(know)
        for b in reads:
            b.rd.append(op)
        for b in writes:
            b.lw = op
            b.rd = []
        self.ops[eng].append(op)
        return op

    def op(self, eng, fn, reads=(), writes=(), multi=False):
        o = self._add(eng, fn, reads, writes)
        o.multi = multi
        return o

    def dma(self, fn, reads=(), writes=(), sem_buf=None, eng="sp"):
        if sem_buf.dsem is None or sem_buf.ep != self.epoch:
            if self.free:
                sem_buf.dsem = self.free.pop()
            else:
                sem_buf.dsem = DSem()
                self.dsems.append(sem_buf.dsem)
            sem_buf.ep = self.epoch
        return self._add(eng, fn, reads, writes, dsem=sem_buf.dsem)

    def barrier(self):
        lasts = []
        for e in ENGS:
            if e == "sp":
                continue
            for o in reversed(self.ops[e]):
                if o.dsem is None:
                    lasts.append(o)
                    break
        spop = self._add("sp", lambda e: e.nop(), (), (), extra=lasts, extra_ds=list(self.dsems))
        self.epoch += 1
        self.free = list(self.dsems)
        for e in ENGS:
            if e == "sp":
                continue
            self._add(e, lambda eh: eh.nop(), (), (), extra=[spop])

    def emit(self, nc):
        with ExitStack() as st:
            esem = {e: st.enter_context(nc.semaphore("es_" + e)) for e in ENGS}
            for i, d in enumerate(self.dsems):
                d.handle = st.enter_context(nc.semaphore("ds%d" % i))
            for e in ENGS:
                c = 0
                for o in self.ops[e]:
                    if o.dsem is None and o.inc:
                        c += 1
                        o.seq = c
            block = st.enter_context(nc.Block())

            def run(e, eh):
                for o in self.ops[e]:
                    wl = [(esem[se], a.seq) for se, a in o.waits.items()] + [(ds.handle, v) for ds, v in o.dwaits.items()]
                    attach = None
                    if wl and o.dsem is None and not o.multi and e != "sp" and FUSE_WAITS:
                        attach = wl.pop()
                    for hh_, vv_ in wl:
                        eh.wait_ge(hh_, vv_)
                    n0 = nc.n_instructions()
                    ins = o.fn(eh)
                    if attach is not None:
                        if nc.n_instructions() - n0 != 1:
                            raise RuntimeError("multi-instruction op with fused wait on %s (%d)" % (e, nc.n_instructions() - n0))
                        ins._wait_ge(attach[0], attach[1])
                    if o.dsem is not None:
                        ins.then_inc(o.dsem.handle, 16)
                    elif o.inc:
                        ins.then_inc(esem[e], 1)
                if e == "sp":
                    for ds in self.dsems:
                        eh.wait_ge(ds.handle, ds.count)

            @block.tensor
            def _(eh):
                run("pe", eh)

            @block.scalar
            def _(eh):
                run("act", eh)

            @block.vector
            def _(eh):
                run("dve", eh)

            @block.gpsimd
            def _(eh):
                run("pool", eh)

            @block.sync
            def _(eh):
                run("sp", eh)


class Ring:
    def __init__(self, nc, st, name, shape, dtype, n, psum=False):
        self.items = []
        for i in range(n):
            if psum:
                t = st.enter_context(nc.psum_tensor("%s%d" % (name, i), shape, dtype))
            else:
                t = st.enter_context(nc.sbuf_tensor("%s%d" % (name, i), shape, dtype))
            self.items.append((t, Buf("%s%d" % (name, i))))
        self.i = 0

    def next(self):
        r = self.items[self.i % len(self.items)]
        self.i += 1
        return r


def pipeline(n, load_fn, compute_fn, depth):
    hs = {}
    for i in range(n + depth):
        if i < n:
            hs[i] = load_fn(i)
        if i >= depth:
            compute_fn(i - depth, hs.pop(i - depth))


class K:
    pass


def build(dbg=False, stop_after=99):
    nc = bass.Bass("TRN2", target_bir_lowering=False)
    S = Sched()
    uid = [0]

    def un(p):
        uid[0] += 1
        return "%s_%d" % (p, uid[0])

    def inp(name, shape, dt=F32):
        return nc.dram_tensor(name, shape, dt, kind="ExternalInput").ap()

    def scratch(name, shape, dt, out=False):
        kind = "ExternalOutput" if (out or dbg) else "Internal"
        return nc.dram_tensor(name, shape, dt, kind=kind).ap(), Buf(name)

    x = inp("x", [S_, D_])
    pos = inp("pos", [128, 32], I32)
    gfm = inp("gfm", [128, 24])
    gqa = inp("gqa", [128, 3])
    gkva = inp("gkva", [128, 2])
    gssm = inp("gssm", [128, 16])
    w_g1 = inp("w_g1", [D_, FF]); w_u1 = inp("w_u1", [D_, FF]); w_d1 = inp("w_d1", [FF, D_])
    w_g2 = inp("w_g2", [D_, FF]); w_u2 = inp("w_u2", [D_, FF]); w_d2 = inp("w_d2", [FF, D_])
    w_in = inp("w_in", [D_, IN_DIM])
    w_qb = inp("w_qb", [384, 1536]); w_kvb = inp("w_kvb", [256, 2048])
    hn = inp("hn", [1, 192])
    convw = inp("convw", [128, 24, 5]); convb = inp("convb", [128, 24])
    ssp = inp("ssp", [1, 160])
    w_pa = inp("w_pa", [D_, D_]); w_pb = inp("w_pb", [2048, D_]); w_o = inp("w_o", [D_, D_])
    c_identb = inp("c_identb", [128, 128], BF16)
    c_mats = inp("c_mats", [128, 4, 128])
    c_neg = inp("c_neg", [128, 2, 128], BF16)
    c_invf = inp("c_invf", [1, 16])

    y_out, b_yout = scratch("y", [S_, D_], F32, out=True)
    x1s, b_x1s = scratch("x1s", [S_, D_], F32)
    x2s, b_x2s = scratch("x2s", [S_, D_], F32)
    hmid, b_hmid = scratch("hmid", [FF, S_], BF16)
    QT, b_QT = scratch("QT", [NH, 96, S_], BF16)
    KT, b_KT = scratch("KT", [NH, 96, S_], BF16)
    Vs, b_Vs = scratch("Vs", [NH, 128, 32, 65], BF16)
    zs, b_zs = scratch("zs", [S_, 2048], BF16)
    xcs, b_xcs = scratch("xcs", [32, 128, 24, 128], BF16)
    gts, b_gts = scratch("gts", [2048, S_], BF16)
    yfs, b_yfs = scratch("yfs", [S_, 2048], F32)
    mTs, b_mTs = scratch("mTs", [2048, S_], BF16)
    aTs, b_aTs = scratch("aTs", [D_, S_], BF16)
    if dbg:
        ybs, b_ybs = scratch("ybs", [S_, 2048], F32)

    top = ExitStack()
    A = lambda name, shape, dt: top.enter_context(nc.sbuf_tensor(name, shape, dt))
    identb = A("identb", [128, 128], BF16); b_identb = Buf()
    mats = A("mats", [128, 4, 128], F32); b_mats = Buf()
    negm = A("negm", [128, 2, 128], BF16); b_negm = Buf()
    gfm_t = A("gfm_t", [128, 24], F32); b_gfm = Buf()
    gqa_t = A("gqa_t", [128, 3], F32); b_gqa = Buf()
    gkva_t = A("gkva_t", [128, 2], F32); b_gkva = Buf()
    gssm_t = A("gssm_t", [128, 16], F32); b_gssm = Buf()
    hn_t = A("hn_t", [128, 192], F32); b_hn = Buf()
    ssp_t = A("ssp_t", [128, 160], F32); b_ssp = Buf()
    convw_t = A("convw_t", [128, 24, 5], F32); b_convw = Buf()
    convb_t = A("convb_t", [128, 24], F32); b_convb = Buf()
    cos_t = A("cos_t", [128, 32, 16], F32); b_cos = Buf()
    sin_t = A("sin_t", [128, 32, 16], F32); b_sin = Buf()
    dtraw = A("dtraw", [128, 32, 64], F32); b_dtraw = Buf()

    def ld(dst, src, b):
        S.dma(lambda e: e.dma_start(out=dst, in_=src), writes=[b], sem_buf=b)

    ld(identb[:], c_identb[:, :], b_identb)
    ld(mats[:], c_mats[:, :, :], b_mats)
    ld(negm[:], c_neg[:, :, :], b_negm)
    ld(gfm_t[:], gfm[:, :], b_gfm)
    ld(gqa_t[:], gqa[:, :], b_gqa)
    ld(gkva_t[:], gkva[:, :], b_gkva)
    ld(gssm_t[:], gssm[:, :], b_gssm)
    ld(hn_t[:], hn.partition_broadcast(128), b_hn)
    ld(ssp_t[:], ssp.partition_broadcast(128), b_ssp)
    ld(convw_t[:], convw[:, :, :], b_convw)
    ld(convb_t[:], convb[:, :], b_convb)
    Umat = mats[:, 0, :]
    Ustr = mats[:, 1, :]
    onesf = mats[:, 2, :]
    identf = mats[:, 3, :]

    with ExitStack() as st:
        post = st.enter_context(nc.sbuf_tensor("post", [128, 32], I32)); b_post = Buf()
        posf = st.enter_context(nc.sbuf_tensor("posf", [128, 32], F32)); b_posf = Buf()
        invf = st.enter_context(nc.sbuf_tensor("invf", [128, 16], F32)); b_invf = Buf()
        ang = st.enter_context(nc.sbuf_tensor("ang", [128, 32, 16], F32)); b_ang = Buf()
        ang2 = st.enter_context(nc.sbuf_tensor("ang2", [128, 32, 16], F32)); b_ang2 = Buf()
        ld(post[:], pos[:, :], b_post)
        ld(invf[:], c_invf.partition_broadcast(128), b_invf)
        S.op("dve", lambda e: e.tensor_copy(out=posf[:], in_=post[:]), reads=[b_post], writes=[b_posf])
        S.op("dve", lambda e: e.tensor_tensor(out=ang[:], in0=posf[:].unsqueeze(2).to_broadcast([128, 32, 16]),
                                              in1=invf[:].unsqueeze(1).to_broadcast([128, 32, 16]), op=ALU.mult),
             reads=[b_posf, b_invf], writes=[b_ang])
        PI = math.pi
        angi = st.enter_context(nc.sbuf_tensor("angi", [128, 32, 16], I32)); b_angi = Buf()
        ang3 = st.enter_context(nc.sbuf_tensor("ang3", [128, 32, 16], F32)); b_ang3 = Buf()

        def rr(shift, dst, b_dst):
            S.op("dve", lambda e: e.tensor_scalar(out=ang2[:], in0=ang[:], scalar1=shift, scalar2=None, op0=ALU.add),
                 reads=[b_ang], writes=[b_ang2])
            S.op("dve", lambda e: e.tensor_scalar(out=ang3[:], in0=ang2[:], scalar1=1.0 / (2 * PI), scalar2=None,
                                                  op0=ALU.mult), reads=[b_ang2], writes=[b_ang3])
            S.op("dve", lambda e: e.tensor_copy(out=angi[:], in_=ang3[:]), reads=[b_ang3], writes=[b_angi])
            S.op("dve", lambda e: e.tensor_copy(out=ang3[:], in_=angi[:]), reads=[b_angi], writes=[b_ang3])
            S.op("dve", lambda e: e.scalar_tensor_tensor(out=ang2[:], in0=ang3[:], scalar=-2 * PI, in1=ang2[:],
                                                         op0=ALU.mult, op1=ALU.add), reads=[b_ang3, b_ang2], writes=[b_ang2])
            S.op("dve", lambda e: e.tensor_scalar(out=ang3[:], in0=ang2[:], scalar1=-PI, scalar2=1e9,
                                                  op0=ALU.add, op1=ALU.mult), reads=[b_ang2], writes=[b_ang3])
            S.op("dve", lambda e: e.tensor_scalar(out=ang3[:], in0=ang3[:], scalar1=0.0, scalar2=1.0,
                                                  op0=ALU.max, op1=ALU.min), reads=[b_ang3], writes=[b_ang3])
            S.op("dve", lambda e: e.scalar_tensor_tensor(out=ang2[:], in0=ang3[:], scalar=-2 * PI, in1=ang2[:],
                                                         op0=ALU.mult, op1=ALU.add), reads=[b_ang3, b_ang2], writes=[b_ang2])
            S.op("dve", lambda e: e.tensor_scalar(out=ang3[:], in0=ang2[:], scalar1=PI, scalar2=-1e9,
                                                  op0=ALU.add, op1=ALU.mult), reads=[b_ang2], writes=[b_ang3])
            S.op("dve", lambda e: e.tensor_scalar(out=ang3[:], in0=ang3[:], scalar1=0.0, scalar2=1.0,
                                                  op0=ALU.max, op1=ALU.min), reads=[b_ang3], writes=[b_ang3])
            S.op("dve", lambda e: e.scalar_tensor_tensor(out=ang2[:], in0=ang3[:], scalar=2 * PI, in1=ang2[:],
                                                         op0=ALU.mult, op1=ALU.add), reads=[b_ang3, b_ang2], writes=[b_ang2])
            S.op("dve", lambda e: e.tensor_scalar(out=ang2[:], in0=ang2[:], scalar1=PI * (1 - 1e-6),
                                                  scalar2=-PI * (1 - 1e-6), op0=ALU.min, op1=ALU.max),
                 reads=[b_ang2], writes=[b_ang2])
            S.op("act", lambda e: e.activation(out=dst, in_=ang2[:], func=AF.Sin), reads=[b_ang2], writes=[b_dst])

        rr(0.0, sin_t[:], b_sin)
        rr(0.5 * PI, cos_t[:], b_cos)
        S.barrier()

    def wload(stage_ring, w_ring, wsrc, kc, n, gain=None, cast_eng="pool"):
        stg, b_stg = stage_ring.next()
        wt, b_wt = w_ring.next()
        S.dma(lambda e: e.dma_start(out=stg[:, 0:kc, 0:n], in_=wsrc.rearrange("(kc p) n -> p kc n", p=128)),
              writes=[b_stg], sem_buf=b_stg)
        if gain is None:
            S.op(cast_eng, lambda e: e.tensor_copy(out=wt[:, 0:kc, 0:n], in_=stg[:, 0:kc, 0:n]),
                 reads=[b_stg], writes=[b_wt])
        else:
            g_ap, b_g = gain
            S.op(cast_eng, lambda e: e.tensor_tensor(out=wt[:, 0:kc, 0:n], in0=stg[:, 0:kc, 0:n],
                                                     in1=g_ap.unsqueeze(2).to_broadcast([128, kc, n]), op=ALU.mult),
                 reads=[b_stg, b_g], writes=[b_wt])
        return wt, b_wt

    def norm_block(src, b_src, tb, rings, hT, b_hT, ncols=D_):
        junk, b_junk = rings["junk"].next()
        stt, b_stt = rings["st"].next()
        hb, b_hb = rings["hb"].next()
        ptr, b_ptr = rings["ptr"].next()
        S.op("act", lambda e: e.activation(out=junk[:], in_=src, func=AF.Square, scale=1.0 / math.sqrt(ncols),
                                           accum_out=stt[:, 0:1]), reads=[b_src], writes=[b_junk, b_stt])
        S.op("act", lambda e: e.activation(out=stt[:, 1:2], in_=stt[:, 0:1], func=AF.Sqrt, bias=EPS_AP[:, 0:1]),
             reads=[b_stt], writes=[b_stt])
        S.op("dve", lambda e: e.reciprocal(out=stt[:, 2:3], in_=stt[:, 1:2]), reads=[b_stt], writes=[b_stt])
        S.op("dve", lambda e: e.tensor_scalar(out=hb[:], in0=src, scalar1=stt[:, 2:3], scalar2=None, op0=ALU.mult),
             reads=[b_src, b_stt], writes=[b_hb])
        pv = ptr[:, :].bitcast(BF16)
        for kc in range(8):
            S.op("pe", lambda e, kc=kc: e.transpose(out=pv[:, kc * 128:(kc + 1) * 128],
                                                     in_=hb[:, kc * 128:(kc + 1) * 128], identity=identb[:]),
                 reads=[b_hb, b_identb], writes=[b_ptr])
        S.op("act", lambda e: e.copy(out=hT[:, :, tb * 128:(tb + 1) * 128],
                                     in_=pv.rearrange("p (k t) -> p k t", k=8)),
             reads=[b_ptr], writes=[b_hT])

    eps_t = A("eps_t", [128, 1], F32); b_eps = Buf()
    S.op("pool", lambda e: e.memset(eps_t[:], EPS), writes=[b_eps])
    EPS_AP = eps_t
    S.barrier()

    def ffn_gateup(w_g, w_u, gain_col, hT, b_hT, w_d=None, wd=None, b_wd=None):
        with ExitStack() as st:
            stg = Ring(nc, st, un("gu_stg"), [128, 8, 128], F32, 6)
            wr = Ring(nc, st, un("gu_w"), [128, 8, 128], BF16, 6)
            psg = Ring(nc, st, un("gu_pg"), [128, 512], F32, 3, psum=True)
            psu = Ring(nc, st, un("gu_pu"), [128, 512], F32, 3, psum=True)
            sil = Ring(nc, st, un("gu_sil"), [128, 512], F32, 3)
            hm = Ring(nc, st, un("gu_hm"), [128, S_], BF16, 2)
            gain = (gfm_t[:, gain_col * 8:(gain_col + 1) * 8], b_gfm)

            dstg = Ring(nc, st, un("gu_dstg"), [128, 1, D_], F32, 3)

            def load(j):
                wg = wload(stg, wr, w_g[:, j * 128:(j + 1) * 128], 8, 128, gain)
                wu = wload(stg, wr, w_u[:, j * 128:(j + 1) * 128], 8, 128, gain)
                if w_d is not None:
                    sg, b_sg = dstg.next()
                    S.dma(lambda e, sg=sg, j=j: e.dma_start(out=sg[:, 0, :], in_=w_d[j * 128:(j + 1) * 128, :]),
                          writes=[b_sg], sem_buf=b_sg)
                    S.op("pool", lambda e, sg=sg, j=j: e.tensor_copy(out=wd[:, j, :], in_=sg[:, 0, :]),
                         reads=[b_sg], writes=[b_wd])
                return wg, wu

            def comp(j, h):
                (wg, b_wg), (wu, b_wu) = h
                hmt, b_hm = hm.next()
                for tb in range(8):
                    pg, b_pg = psg.next()
                    pu, b_pu = psu.next()
                    sl, b_sl = sil.next()
                    ts = slice(tb * 512, (tb + 1) * 512)
                    for kc in range(8):
                        S.op("pe", lambda e, kc=kc, pg=pg, wg=wg, ts=ts: e.matmul(
                            out=pg[:], lhsT=wg[:, kc, :], rhs=hT[:, kc, ts], start=(kc == 0), stop=(kc == 7)),
                            reads=[b_wg, b_hT], writes=[b_pg])
                    for kc in range(8):
                        S.op("pe", lambda e, kc=kc, pu=pu, wu=wu, ts=ts: e.matmul(
                            out=pu[:], lhsT=wu[:, kc, :], rhs=hT[:, kc, ts], start=(kc == 0), stop=(kc == 7)),
                            reads=[b_wu, b_hT], writes=[b_pu])
                    S.op("act", lambda e, sl=sl, pg=pg: e.activation(out=sl[:], in_=pg[:], func=AF.Silu),
                         reads=[b_pg], writes=[b_sl])
                    S.op("dve", lambda e, sl=sl, pu=pu, hmt=hmt, ts=ts: e.tensor_tensor(
                        out=hmt[:, ts], in0=sl[:], in1=pu[:], op=ALU.mult), reads=[b_sl, b_pu], writes=[b_hm])
                S.dma(lambda e, hmt=hmt, j=j: e.dma_start(out=hmid[j * 128:(j + 1) * 128, :], in_=hmt[:]),
                      reads=[b_hm], writes=[b_hmid], sem_buf=b_hm, eng="pool")

            pipeline(NFF, load, comp, 2)
        S.barrier()

    def ffn_down(w_d, xsrc, b_xsrc, xdst, b_xdst, next_norm, wd=None, b_wd=None):
        with ExitStack() as st:
            pre = wd is not None
            if not pre:
                wd = st.enter_context(nc.sbuf_tensor(un("wd"), [128, NFF, D_], BF16)); b_wd = Buf()
            stg = Ring(nc, st, un("dn_stg"), [128, 1, D_], F32, 3)
            for j in range(NFF if not pre else 0):
                sg, b_sg = stg.next()
                S.dma(lambda e, sg=sg, j=j: e.dma_start(out=sg[:, 0, :], in_=w_d[j * 128:(j + 1) * 128, :]),
                      writes=[b_sg], sem_buf=b_sg)
                S.op("pool", lambda e, sg=sg, j=j: e.tensor_copy(out=wd[:, j, :], in_=sg[:, 0, :]),
                     reads=[b_sg], writes=[b_wd])
            hmr = Ring(nc, st, un("dn_hm"), [128, NFF, 512], BF16, 2)
            xr = Ring(nc, st, un("dn_x"), [128, D_], F32, 3)
            ps = Ring(nc, st, un("dn_ps"), [128, 512], F32, 4, psum=True)
            rings = None
            if next_norm:
                rings = dict(junk=Ring(nc, st, un("nj"), [128, D_], BF16, 2),
                             st=Ring(nc, st, un("nst"), [128, 4], F32, 3),
                             hb=Ring(nc, st, un("nhb"), [128, D_], BF16, 2),
                             ptr=Ring(nc, st, un("nptr"), [128, 512], F32, 2, psum=True))

            def load(t):
                hmt, b_hm = hmr.next()
                S.dma(lambda e: e.dma_start(out=hmt[:], in_=hmid[:, t * 512:(t + 1) * 512].rearrange(
                    "(j p) t -> p j t", p=128)), reads=[b_hmid], writes=[b_hm], sem_buf=b_hm)
                return hmt, b_hm

            def comp(t, h):
                hmt, b_hm = h
                for sb in range(4):
                    tb = t * 4 + sb
                    xt, b_xt = xr.next()
                    S.dma(lambda e, xt=xt, tb=tb: e.dma_start(out=xt[:], in_=xsrc[tb * 128:(tb + 1) * 128, :]),
                          reads=[b_xsrc], writes=[b_xt], sem_buf=b_xt)
                    for half in range(2):
                        p, b_p = ps.next()
                        for j in range(NFF):
                            S.op("pe", lambda e, j=j, p=p, sb=sb, half=half, hmt=hmt: e.matmul(
                                out=p[:], lhsT=hmt[:, j, sb * 128:(sb + 1) * 128],
                                rhs=wd[:, j, half * 512:(half + 1) * 512], start=(j == 0), stop=(j == NFF - 1)),
                                reads=[b_hm, b_wd], writes=[b_p])
                        S.op("dve", lambda e, p=p, xt=xt, half=half: e.scalar_tensor_tensor(
                            out=xt[:, half * 512:(half + 1) * 512], in0=p[:], scalar=0.5,
                            in1=xt[:, half * 512:(half + 1) * 512], op0=ALU.mult, op1=ALU.add),
                            reads=[b_p, b_xt], writes=[b_xt])
                    S.dma(lambda e, xt=xt, tb=tb: e.dma_start(out=xdst[tb * 128:(tb + 1) * 128, :], in_=xt[:]),
                          reads=[b_xt], writes=[b_xdst], sem_buf=b_xt, eng="pool")
                    if next_norm:
                        if pendn:
                            norm_block(*pendn.pop())
                        pendn.append((xt[:], b_xt, tb, rings, next_norm[0], next_norm[1]))

            pendn = []
            pipeline(8, load, comp, 1)
            if pendn:
                norm_block(*pendn.pop())
        S.barrier()

    def norm_phase(xsrc, b_xsrc, hT, b_hT):
        with ExitStack() as st:
            xr = Ring(nc, st, un("np_x"), [128, D_], F32, 3)
            rings = dict(junk=Ring(nc, st, un("nj"), [128, D_], BF16, 2),
                         st=Ring(nc, st, un("nst"), [128, 4], F32, 3),
                         hb=Ring(nc, st, un("nhb"), [128, D_], BF16, 2),
                         ptr=Ring(nc, st, un("nptr"), [128, 512], F32, 2, psum=True))

            def load(tb):
                xt, b_xt = xr.next()
                S.dma(lambda e: e.dma_start(out=xt[:], in_=xsrc[tb * 128:(tb + 1) * 128, :]),
                      reads=[b_xsrc], writes=[b_xt], sem_buf=b_xt)
                return xt, b_xt

            def comp(tb, h):
                norm_block(h[0][:], h[1], tb, rings, hT, b_hT)

            pipeline(32, load, comp, 2)
        S.barrier()

    b_x = Buf("x")
    hst = ExitStack()
    hT = hst.enter_context(nc.sbuf_tensor("hT", [128, 8, S_], BF16)); b_hT = Buf("hT")
    norm_phase(x, b_x, hT, b_hT)
    with ExitStack() as fst:
        wd1 = fst.enter_context(nc.sbuf_tensor("wd1", [128, NFF, D_], BF16)); b_wd1 = Buf()
        ffn_gateup(w_g1, w_u1, 0, hT, b_hT, w_d1, wd1, b_wd1)
        ffn_down(w_d1, x, b_x, x1s, b_x1s, (hT, b_hT), wd1, b_wd1)
    if stop_after <= 1:
        S.emit(nc)
        return nc

    gmix = (gfm_t[:, 8:16], b_gfm)

    def rope(src3, H, tb, dst3, tmp_ring):
        ta, b_ta = tmp_ring.next()
        tb_, b_tb = tmp_ring.next()
        cb = cos_t[:, tb, :].unsqueeze(1).to_broadcast([128, H, 16])
        sb = sin_t[:, tb, :].unsqueeze(1).to_broadcast([128, H, 16])
        t1 = src3[:, :, 0:16]
        t2 = src3[:, :, 16:32]
        a_ = ta[:, 0:H, :]
        b_ = tb_[:, 0:H, :]
        return [
            (lambda e: e.tensor_tensor(out=a_, in0=t1, in1=cb, op=ALU.mult), [b_cos], [b_ta]),
            (lambda e: e.tensor_tensor(out=b_, in0=t2, in1=sb, op=ALU.mult), [b_sin], [b_tb]),
            (lambda e: e.tensor_tensor(out=dst3[:, :, 0:16], in0=a_, in1=b_, op=ALU.subtract), [b_ta, b_tb], []),
            (lambda e: e.tensor_tensor(out=a_, in0=t2, in1=cb, op=ALU.mult), [b_cos], [b_ta]),
            (lambda e: e.tensor_tensor(out=b_, in0=t1, in1=sb, op=ALU.mult), [b_sin], [b_tb]),
            (lambda e: e.tensor_tensor(out=dst3[:, :, 16:32], in0=a_, in1=b_, op=ALU.add), [b_ta, b_tb], []),
        ]

    with ExitStack() as st:
        wAr = Ring(nc, st, un("a_w"), [128, 8, 672], BF16, 1)
        wqr = Ring(nc, st, un("q_w"), [128, 3, 1536], BF16, 1)
        wkr = Ring(nc, st, un("kv_w"), [128, 2, 2048], BF16, 1)
        with ExitStack() as st2:
            stgA = Ring(nc, st2, un("a_stg"), [128, 8, 672], F32, 1)
            stgq = Ring(nc, st2, un("q_stg"), [128, 3, 1536], F32, 1)
            stgk = Ring(nc, st2, un("kv_stg"), [128, 2, 2048], F32, 1)
            wA, b_wA = wload(stgA, wAr, w_in[:, 0:672], 8, 672, gmix)
            wq, b_wq = wload(stgq, wqr, w_qb[:, :], 3, 1536, (gqa_t[:, :], b_gqa))
            wkv, b_wkv = wload(stgk, wkr, w_kvb[:, :], 2, 2048, (gkva_t[:, :], b_gkva))
            S.barrier()
        psA = Ring(nc, st, un("a_ps"), [128, 512], F32, 2, psum=True)
        psT = Ring(nc, st, un("a_pt"), [128, 512], F32, 2, psum=True)
        psQ = Ring(nc, st, un("a_pq"), [128, 512], F32, 3, psum=True)
        junk = Ring(nc, st, un("a_junk"), [128, 1536], F32, 1)
        stt = Ring(nc, st, un("a_st"), [128, 8], F32, 2)
        sst = Ring(nc, st, un("a_ss"), [128, 100], F32, 2)
        cnr = Ring(nc, st, un("a_cn"), [128, 640], BF16, 2)
        cTr = Ring(nc, st, un("a_cT"), [128, 5, 128], BF16, 2)
        kper = Ring(nc, st, un("a_kpe"), [128, 1, 32], F32, 2)
        kpgr = Ring(nc, st, un("a_kpg"), [128, 1, 32], F32, 2)
        krr = Ring(nc, st, un("a_kr"), [128, 1, 32], F32, 2)
        qsbr = Ring(nc, st, un("a_qsb"), [128, 1536], F32, 1)
        kvsbr = Ring(nc, st, un("a_kvsb"), [128, 2048], F32, 1)
        tmpkr = Ring(nc, st, un("a_tmpk"), [128, 16, 64], F32, 1)
        qbr = Ring(nc, st, un("a_qb"), [128, 16, 96], BF16, 2)
        kbr = Ring(nc, st, un("a_kb"), [128, 16, 96], BF16, 2)
        ropet = Ring(nc, st, un("a_rt"), [128, 16, 16], F32, 4)
        vbr = Ring(nc, st, un("a_vb"), [128, 16, 4, 65], BF16, 1)
        qTr = Ring(nc, st, un("a_qT"), [128, 16, 256], BF16, 2)
        kTr = Ring(nc, st, un("a_kT"), [128, 16, 256], BF16, 2)
        for (vt, b_v) in vbr.items:
            S.op("pool", lambda e, vt=vt: e.memset(vt[:], 1.0), writes=[b_v])
        gq = hn_t[:, 0:96]
        gk = hn_t[:, 96:192]
        sh = {}

        def block(tb):
            tsl = slice(tb * 128, (tb + 1) * 128)
            pA1, b_pA1 = psA.next()
            pA2, b_pA2 = psA.next()
            for kc in range(8):
                S.op("pe", lambda e, kc=kc, pA1=pA1, tsl=tsl: e.matmul(out=pA1[:, 0:384], lhsT=hT[:, kc, tsl], rhs=wA[:, kc, 0:384],
                                                     start=(kc == 0), stop=(kc == 7)), reads=[b_hT, b_wA], writes=[b_pA1])
            for kc in range(8):
                S.op("pe", lambda e, kc=kc, pA2=pA2, tsl=tsl: e.matmul(out=pA2[:, 0:288], lhsT=hT[:, kc, tsl], rhs=wA[:, kc, 384:672],
                                                     start=(kc == 0), stop=(kc == 7)), reads=[b_hT, b_wA], writes=[b_pA2])
            jk, b_jk = junk.next()
            s8, b_s8 = stt.next()
            S.op("act", lambda e, jk=jk, pA1=pA1, s8=s8: e.activation(out=jk[:, 0:384], in_=pA1[:, 0:384], func=AF.Square,
                                               scale=1.0 / math.sqrt(384.0), accum_out=s8[:, 0:1]),
                 reads=[b_pA1], writes=[b_jk, b_s8])
            S.op("act", lambda e, jk=jk, pA2=pA2, s8=s8: e.activation(out=jk[:, 0:256], in_=pA2[:, 0:256], func=AF.Square,
                                               scale=1.0 / 16.0, accum_out=s8[:, 1:2]),
                 reads=[b_pA2], writes=[b_jk, b_s8])
            S.op("act", lambda e, s8=s8: e.activation(out=s8[:, 2:4], in_=s8[:, 0:2], func=AF.Sqrt, bias=EPS_AP[:, 0:1]),
                 reads=[b_s8, b_eps], writes=[b_s8])
            S.op("dve", lambda e, s8=s8: e.reciprocal(out=s8[:, 4:6], in_=s8[:, 2:4]), reads=[b_s8], writes=[b_s8])
            cn, b_cn = cnr.next()
            S.op("dve", lambda e, cn=cn, pA1=pA1, s8=s8: e.tensor_scalar(out=cn[:, 0:384], in0=pA1[:, 0:384], scalar1=s8[:, 4:5],
                                                  scalar2=None, op0=ALU.mult), reads=[b_pA1, b_s8], writes=[b_cn])
            S.op("dve", lambda e, cn=cn, pA2=pA2, s8=s8: e.tensor_scalar(out=cn[:, 384:640], in0=pA2[:, 0:256], scalar1=s8[:, 5:6],
                                                  scalar2=None, op0=ALU.mult), reads=[b_pA2, b_s8], writes=[b_cn])
            kpe, b_kpe = kper.next()
            S.op("act", lambda e, kpe=kpe, pA2=pA2: e.copy(out=kpe[:, 0, :], in_=pA2[:, 256:288]), reads=[b_pA2], writes=[b_kpe])
            yield
            ptr, b_ptr = psT.next()
            pv = ptr[:, :].bitcast(BF16)
            for kc in range(5):
                S.op("pe", lambda e, kc=kc, pv=pv, cn=cn: e.transpose(out=pv[:, kc * 128:(kc + 1) * 128],
                                                         in_=cn[:, kc * 128:(kc + 1) * 128], identity=identb[:]),
                     reads=[b_cn, b_identb], writes=[b_ptr])
            cT, b_cT = cTr.next()
            S.op("dve", lambda e, cT=cT, pv=pv: e.tensor_copy(out=cT[:], in_=pv[:, 0:640].rearrange("p (k t) -> p k t", k=5)),
                 reads=[b_ptr], writes=[b_cT])
            yield
            qsb, b_qsb = qsbr.next()
            kvsb, b_kvsb = kvsbr.next()
            for nb in range(3):
                pq, b_pq = psQ.next()
                for kc in range(3):
                    S.op("pe", lambda e, kc=kc, nb=nb, pq=pq, cT=cT: e.matmul(out=pq[:], lhsT=cT[:, kc, :],
                                                                 rhs=wq[:, kc, nb * 512:(nb + 1) * 512],
                                                                 start=(kc == 0), stop=(kc == 2)),
                         reads=[b_cT, b_wq], writes=[b_pq])
                S.op("act", lambda e, nb=nb, pq=pq, qsb=qsb: e.copy(out=qsb[:, nb * 512:(nb + 1) * 512], in_=pq[:]),
                     reads=[b_pq], writes=[b_qsb])
            for nb in range(4):
                pq, b_pq = psQ.next()
                for kc in range(2):
                    S.op("pe", lambda e, kc=kc, nb=nb, pq=pq, cT=cT: e.matmul(out=pq[:], lhsT=cT[:, 3 + kc, :],
                                                                 rhs=wkv[:, kc, nb * 512:(nb + 1) * 512],
                                                                 start=(kc == 0), stop=(kc == 1)),
                         reads=[b_cT, b_wkv], writes=[b_pq])
                eng = "act" if nb % 2 == 0 else "dve"
                if eng == "act":
                    S.op("act", lambda e, nb=nb, pq=pq, kvsb=kvsb: e.copy(out=kvsb[:, nb * 512:(nb + 1) * 512], in_=pq[:]),
                         reads=[b_pq], writes=[b_kvsb])
                else:
                    S.op("dve", lambda e, nb=nb, pq=pq, kvsb=kvsb: e.tensor_copy(out=kvsb[:, nb * 512:(nb + 1) * 512], in_=pq[:]),
                         reads=[b_pq], writes=[b_kvsb])
            q3 = qsb[:, :].rearrange("p (h d) -> p h d", h=16)
            kv3 = kvsb[:, :].rearrange("p (h d) -> p h d", h=16)
            ss, b_ss = sst.next()
            S.op("act", lambda e, jk=jk, qsb=qsb: e.activation(out=jk[:, :], in_=qsb[:, :], func=AF.Square),
                 reads=[b_qsb], writes=[b_jk])
            S.op("dve", lambda e, jk=jk, ss=ss: e.tensor_reduce(out=ss[:, 0:16], in_=jk[:, :].rearrange("p (h d) -> p h d", h=16),
                                                  axis=AX.X, op=ALU.add), reads=[b_jk], writes=[b_ss])
            tk, b_tk = tmpkr.next()
            S.op("act", lambda e, tk=tk, kv3=kv3: e.activation(out=tk[:], in_=kv3[:, :, 0:64], func=AF.Square),
                 reads=[b_kvsb], writes=[b_tk])
            S.op("dve", lambda e, tk=tk, ss=ss: e.tensor_reduce(out=ss[:, 16:32], in_=tk[:], axis=AX.X, op=ALU.add),
                 reads=[b_tk], writes=[b_ss])
            kpg, b_kpg = kpgr.next()
            S.op("act", lambda e, kpg=kpg, kpe=kpe, ss=ss: e.activation(out=kpg[:, 0, :], in_=kpe[:, 0, :], func=AF.Square,
                                               accum_out=ss[:, 96:97]), reads=[b_kpe], writes=[b_kpg, b_ss])
            S.op("dve", lambda e, ss=ss: e.tensor_scalar(out=ss[:, 16:32], in0=ss[:, 16:32], scalar1=ss[:, 96:97], scalar2=None,
                                                  op0=ALU.add), reads=[b_ss], writes=[b_ss])
            S.op("act", lambda e, ss=ss: e.activation(out=ss[:, 32:64], in_=ss[:, 0:32], func=AF.Sqrt, bias=EPS_AP[:, 0:1],
                                               scale=1.0 / 96.0), reads=[b_ss, b_eps], writes=[b_ss])
            S.op("dve", lambda e, ss=ss: e.reciprocal(out=ss[:, 64:96], in_=ss[:, 32:64]), reads=[b_ss], writes=[b_ss])
            rsq = ss[:, 64:80]
            rsk = ss[:, 80:96]
            S.op("dve", lambda e, q3=q3, rsq=rsq: e.tensor_tensor(out=q3, in0=q3, in1=rsq.unsqueeze(2).to_broadcast([128, 16, 96]),
                                                  op=ALU.mult), reads=[b_qsb, b_ss], writes=[b_qsb])
            S.op("pool", lambda e, q3=q3: e.tensor_tensor(out=q3, in0=q3, in1=gq.unsqueeze(1).to_broadcast([128, 16, 96]),
                                                   op=ALU.mult), reads=[b_qsb, b_hn], writes=[b_qsb])
            qb, b_qb = qbr.next()
            S.op("act", lambda e, qb=qb, q3=q3: e.copy(out=qb[:, :, 0:64], in_=q3[:, :, 0:64]), reads=[b_qsb], writes=[b_qb])
            for fn, rd, wr in rope(q3[:, :, 64:96], 16, tb, qb[:, :, 64:96], ropet):
                S.op("dve", fn, reads=[b_qsb] + rd, writes=wr + ([b_qb] if not wr else []))
            S.op("dve", lambda e, tk=tk, kv3=kv3, rsk=rsk: e.tensor_tensor(out=tk[:], in0=kv3[:, :, 0:64],
                                                  in1=rsk.unsqueeze(2).to_broadcast([128, 16, 64]), op=ALU.mult),
                 reads=[b_kvsb, b_ss], writes=[b_tk])
            kb, b_kb = kbr.next()
            S.op("pool", lambda e, tk=tk, kb=kb: e.tensor_tensor(out=kb[:, :, 0:64], in0=tk[:],
                                                   in1=gk[:, 0:64].unsqueeze(1).to_broadcast([128, 16, 64]), op=ALU.mult),
                 reads=[b_tk, b_hn], writes=[b_kb])
            S.op("dve", lambda e, kpg=kpg, kpe=kpe: e.tensor_tensor(out=kpg[:, 0, :], in0=kpe[:, 0, :], in1=gk[:, 64:96], op=ALU.mult),
                 reads=[b_kpe, b_hn], writes=[b_kpg])
            kr, b_kr = krr.next()
            for fn, rd, wr in rope(kpg[:, :, :], 1, tb, kr[:, :, :], ropet):
                S.op("dve", fn, reads=[b_kpg] + rd, writes=wr + ([b_kr] if not wr else []))
            S.op("dve", lambda e, kb=kb, kr=kr, rsk=rsk: e.tensor_tensor(out=kb[:, :, 64:96],
                                                  in0=kr[:, 0, :].unsqueeze(1).to_broadcast([128, 16, 32]),
                                                  in1=rsk.unsqueeze(2).to_broadcast([128, 16, 32]), op=ALU.mult),
                 reads=[b_kr, b_ss], writes=[b_kb])
            if tb % 4 == 0:
                sh['vb'] = vbr.next()
            vb, b_vb = sh['vb']
            S.op("pool", lambda e, vb=vb, kv3=kv3, tb=tb: e.tensor_copy(out=vb[:, :, tb % 4, 0:64], in_=kv3[:, :, 64:128]),
                 reads=[b_kvsb], writes=[b_vb])
            yield
            if tb % 2 == 0:
                sh['qT'] = qTr.next()
                sh['kT'] = kTr.next()
            qTs, b_qTs = sh['qT']
            kTs, b_kTs = sh['kT']
            for (src, b_src, dstT, b_dstT) in ((qb, b_qb, qTs, b_qTs), (kb, b_kb, kTs, b_kTs)):
                for half in range(2):
                    ptr, b_ptr = psT.next()
                    pv = ptr[:, :].bitcast(BF16)
                    for hh in range(8):
                        S.op("pe", lambda e, hh=hh, pv=pv, src=src, half=half: e.transpose(
                            out=pv[0:96, hh * 128:(hh + 1) * 128], in_=src[:, half * 8 + hh, :], identity=identb[:]),
                            reads=[b_src, b_identb], writes=[b_ptr])
                    off = (tb % 2) * 128
                    eng = "act" if half == 0 else "dve"
                    if eng == "act":
                        S.op("act", lambda e, pv=pv, dstT=dstT, half=half, off=off: e.copy(
                            out=dstT[0:96, half * 8:(half + 1) * 8, off:off + 128],
                            in_=pv[0:96, :].rearrange("p (h t) -> p h t", h=8)), reads=[b_ptr], writes=[b_dstT])
                    else:
                        S.op("dve", lambda e, pv=pv, dstT=dstT, half=half, off=off: e.tensor_copy(
                            out=dstT[0:96, half * 8:(half + 1) * 8, off:off + 128],
                            in_=pv[0:96, :].rearrange("p (h t) -> p h t", h=8)), reads=[b_ptr], writes=[b_dstT])
            if tb % 2 == 1:
                t0 = (tb - 1) * 128
                S.dma(lambda e, qTs=qTs, t0=t0: e.dma_start(out=QT[:, :, t0:t0 + 256].rearrange("h p t -> p h t"),
                                                             in_=qTs[0:96, :, :]),
                      reads=[b_qTs], writes=[b_QT], sem_buf=b_qTs, eng="pool")
                S.dma(lambda e, kTs=kTs, t0=t0: e.dma_start(out=KT[:, :, t0:t0 + 256].rearrange("h p t -> p h t"),
                                                             in_=kTs[0:96, :, :]),
                      reads=[b_kTs], writes=[b_KT], sem_buf=b_kTs, eng="pool")
            if tb % 4 == 3:
                c0 = tb - 3
                S.dma(lambda e, vb=vb, c0=c0: e.dma_start(out=Vs[:, :, c0:c0 + 4, :].rearrange("h p t c -> p h (t c)"),
                                                           in_=vb[:, :, :, :].rearrange("p h t c -> p h (t c)")),
                      reads=[b_vb], writes=[b_Vs], sem_buf=b_vb, eng="pool")
        gens = {}
        for i in range(32 + 2):
            if i < 32:
                gens[i] = block(i)
                next(gens[i])
            if 0 <= i - 1 < 32:
                next(gens[i - 1])
            if 0 <= i - 2 < 32:
                next(gens[i - 2], None)
            if 0 <= i - 1 < 32:
                next(gens[i - 1])
        S.barrier()
    if stop_after <= 2:
        S.emit(nc)
        return nc

    with ExitStack() as st:
        wzr = Ring(nc, st, un("z_w"), [128, 8, 512], BF16, 4)
        wdr = Ring(nc, st, un("dt_w"), [128, 8, 64], BF16, 1)
        with ExitStack() as st2:
            stgz = Ring(nc, st2, un("z_stg"), [128, 8, 512], F32, 2)
            stgd = Ring(nc, st2, un("dt_stg"), [128, 8, 64], F32, 1)
            wz = [wload(stgz, wzr, w_in[:, C_Z + cb * 512:C_Z + (cb + 1) * 512], 8, 512, gmix) for cb in range(4)]
            wdt, b_wdt = wload(stgd, wdr, w_in[:, C_DTF:C_DTF + 64], 8, 64, gmix)
            S.barrier()
        psz = Ring(nc, st, un("z_ps"), [128, 512], F32, 4, psum=True)
        psd = Ring(nc, st, un("dt_ps"), [128, 512], F32, 2, psum=True)
        zst = Ring(nc, st, un("z_st"), [128, 2048], BF16, 3)
        for tb in range(32):
            tsl = slice(tb * 128, (tb + 1) * 128)
            zt, b_zt = zst.next()
            for cb in range(4):
                pz, b_pz = psz.next()
                w_, b_w = wz[cb]
                for kc in range(8):
                    S.op("pe", lambda e, kc=kc, pz=pz, w_=w_, tsl=tsl: e.matmul(out=pz[:], lhsT=hT[:, kc, tsl], rhs=w_[:, kc, :],
                                                                  start=(kc == 0), stop=(kc == 7)),
                         reads=[b_hT, b_w], writes=[b_pz])
                S.op("act", lambda e, pz=pz, zt=zt, cb=cb: e.activation(out=zt[:, cb * 512:(cb + 1) * 512], in_=pz[:], func=AF.Silu),
                     reads=[b_pz], writes=[b_zt])
            pd, b_pd = psd.next()
            for kc in range(8):
                S.op("pe", lambda e, kc=kc, pd=pd, tsl=tsl: e.matmul(out=pd[:, 0:64], lhsT=hT[:, kc, tsl], rhs=wdt[:, kc, :],
                                                       start=(kc == 0), stop=(kc == 7)), reads=[b_hT, b_wdt], writes=[b_pd])
            S.op("dve", lambda e, pd=pd, tb=tb: e.tensor_copy(out=dtraw[:, tb, :], in_=pd[:, 0:64]), reads=[b_pd], writes=[b_dtraw])
            S.dma(lambda e, zt=zt, tsl=tsl: e.dma_start(out=zs[tsl, :], in_=zt[:]), reads=[b_zt], writes=[b_zs],
                  sem_buf=b_zt, eng="pool")
        S.barrier()

    with ExitStack() as st:
        stg = Ring(nc, st, un("x_stg"), [128, 8, 128], F32, 3)
        wr = Ring(nc, st, un("x_w"), [128, 8, 128], BF16, 3)
        psx = Ring(nc, st, un("x_ps"), [128, 512], F32, 3, psum=True)
        psc = Ring(nc, st, un("x_pc"), [128, 512], F32, 3, psum=True)
        xpre = Ring(nc, st, un("x_pre"), [128, S_ + 4], BF16, 2)
        dgr = Ring(nc, st, un("x_dg"), [128, 5, 128], BF16, 2)
        xcst = Ring(nc, st, un("x_cst"), [128, S_], BF16, 2)
        for (t_, b_) in xpre.items:
            S.op("pool", lambda e, t_=t_: e.memset(t_[:], 0.0), writes=[b_])

        def loadx(j):
            return wload(stg, wr, w_in[:, C_XBC + j * 128:C_XBC + (j + 1) * 128], 8, 128, gmix)

        def compx(j, h):
            w_, b_w = h
            xp, b_xp = xpre.next()
            dg, b_dg = dgr.next()
            for k in range(5):
                S.op("dve", lambda e, k=k, dg=dg, j=j: e.tensor_scalar(out=dg[:, k, :], in0=identf, scalar1=convw_t[:, j, k:k + 1],
                                                           scalar2=None, op0=ALU.mult),
                     reads=[b_mats, b_convw], writes=[b_dg])
            for tb in range(8):
                px, b_px = psx.next()
                ts = slice(tb * 512, (tb + 1) * 512)
                for kc in range(8):
                    S.op("pe", lambda e, kc=kc, px=px, w_=w_, ts=ts: e.matmul(out=px[:], lhsT=w_[:, kc, :], rhs=hT[:, kc, ts],
                                                                start=(kc == 0), stop=(kc == 7)),
                         reads=[b_w, b_hT], writes=[b_px])
                if tb % 2 == 0:
                    S.op("act", lambda e, px=px, xp=xp, tb=tb: e.copy(out=xp[:, 2 + tb * 512:2 + (tb + 1) * 512], in_=px[:]),
                         reads=[b_px], writes=[b_xp])
                else:
                    S.op("dve", lambda e, px=px, xp=xp, tb=tb: e.tensor_copy(out=xp[:, 2 + tb * 512:2 + (tb + 1) * 512], in_=px[:]),
                         reads=[b_px], writes=[b_xp])
            xc_, b_xc = xcst.next()
            for tb in range(8):
                pc, b_pc = psc.next()
                for k in range(5):
                    S.op("pe", lambda e, k=k, pc=pc, dg=dg, xp=xp, tb=tb: e.matmul(
                        out=pc[:], lhsT=dg[:, k, :], rhs=xp[:, tb * 512 + k:tb * 512 + k + 512],
                        start=(k == 0), stop=(k == 4)), reads=[b_dg, b_xp], writes=[b_pc])
                S.op("act", lambda e, pc=pc, xc_=xc_, tb=tb, j=j: e.activation(out=xc_[:, tb * 512:(tb + 1) * 512], in_=pc[:],
                                                               func=AF.Silu, bias=convb_t[:, j:j + 1]),
                     reads=[b_pc, b_convb], writes=[b_xc])
            for q4 in range(4):
                S.dma(lambda e, xc_=xc_, j=j, q4=q4: e.dma_start(
                    out=xcs[q4 * 8:(q4 + 1) * 8, :, j, :].rearrange("c p t -> p c t"),
                    in_=xc_[:, q4 * 1024:(q4 + 1) * 1024].rearrange("p (c t) -> p c t", c=8)),
                    reads=[b_xc], writes=[b_xcs], sem_buf=b_xc, eng="pool")

        pipeline(24, loadx, compx, 2)

        def loadg(j):
            return wload(stg, wr, w_in[:, C_GA + j * 128:C_GA + (j + 1) * 128], 8, 128, gmix)

        def compg(j, h):
            w_, b_w = h
            gt_, b_gt = xcst.next()
            for tb in range(8):
                px, b_px = psx.next()
                ts = slice(tb * 512, (tb + 1) * 512)
                for kc in range(8):
                    S.op("pe", lambda e, kc=kc, px=px, w_=w_, ts=ts: e.matmul(out=px[:], lhsT=w_[:, kc, :], rhs=hT[:, kc, ts],
                                                                start=(kc == 0), stop=(kc == 7)),
                         reads=[b_w, b_hT], writes=[b_px])
                S.op("act", lambda e, px=px, gt_=gt_, ts=ts: e.activation(out=gt_[:, ts], in_=px[:], func=AF.Sigmoid),
                     reads=[b_px], writes=[b_gt])
            S.dma(lambda e, gt_=gt_, j=j: e.dma_start(out=gts[j * 128:(j + 1) * 128, :], in_=gt_[:]),
                  reads=[b_gt], writes=[b_gts], sem_buf=b_gt, eng="pool")

        pipeline(16, loadg, compg, 2)
        S.barrier()
    if stop_after <= 3:
        S.emit(nc)
        return nc
    hst.close()

    def ssd_pass(fwd):
        with ExitStack() as st:
            AT = lambda n, shp, dt=F32: st.enter_context(nc.sbuf_tensor(un(n), shp, dt))
            dt_all = AT("dt_all", [128, 32, 32]); b_dt = Buf()
            da_all = AT("da_all", [128, 32, 32]); b_da = Buf()
            P_all = AT("P_all", [128, 32, 32]); b_P = Buf()
            bias_all = AT("bias_all", [128, 32, 32]); b_bias = Buf()
            wgt = AT("wgt", [128, 32, 32]); b_wgt = Buf()
            scl = AT("scl", [128, 32, 32]); b_scl = Buf()
            cdc = AT("cdc", [128, 32, 32]); b_cdc = Buf()
            tot = AT("tot", [128, 32, 32]); b_tot = Buf()
            nega = AT("nega", [128, 32]); b_nega = Buf()
            tmpa = AT("tmpa", [128, 32, 32]); b_tmpa = Buf()
            Sf = AT("Sf", [128, 2048]); b_Sf = [Buf() for _ in range(4)]
            Sbf = AT("Sbf", [128, 2048], BF16); b_Sbf = [Buf() for _ in range(4)]
            off = 0 if fwd else 32
            alog = ssp_t[:, off:off + 32]
            dtb = ssp_t[:, 64 + off:96 + off]
            dsk = ssp_t[:, 128:160]
            Uc = Umat if fwd else Ustr
            midx = 0 if fwd else 1
            ptA = Ring(nc, st, un("s_ptA"), [128, 512], F32, 1, psum=True)
            segb = Ring(nc, st, un("s_seg"), [128, 512], F32, 3, psum=True)
            pyr = Ring(nc, st, un("s_py"), [128, 512], F32, 2, psum=True)
            por = Ring(nc, st, un("s_po"), [128, 512], F32, 1, psum=True)
            pstr = Ring(nc, st, un("s_pst"), [128, 512], F32, 1, psum=True)
            segs = []
            for (t_, _b) in segb.items:
                for q in range(4):
                    segs.append((t_[:, q * 128:(q + 1) * 128], Buf()))
            segi = [0]
            flat = lambda t_: t_[:, :, :].rearrange("p a b -> p (a b)")
            S.op("pool", lambda e: e.memset(Sf[:], 0.0), writes=b_Sf)
            S.op("pool", lambda e: e.memset(Sbf[:], 0.0), writes=b_Sbf)
            S.op("act", lambda e: e.activation(out=nega[:], in_=alog, func=AF.Exp), reads=[b_ssp], writes=[b_nega])
            S.op("dve", lambda e: e.tensor_scalar(out=nega[:], in0=nega[:], scalar1=-1.0, scalar2=None, op0=ALU.mult),
                 reads=[b_nega], writes=[b_nega])
            S.op("dve", lambda e: e.tensor_tensor(out=tmpa[:], in0=dtraw[:, :, off:off + 32],
                                                  in1=dtb.unsqueeze(1).to_broadcast([128, 32, 32]), op=ALU.add),
                 reads=[b_dtraw, b_ssp], writes=[b_tmpa])
            S.op("act", lambda e: e.activation(out=tmpa[:], in_=tmpa[:], func=AF.Exp), reads=[b_tmpa], writes=[b_tmpa])
            S.op("act", lambda e: e.activation(out=dt_all[:], in_=tmpa[:], func=AF.Ln, bias=1.0), reads=[b_tmpa], writes=[b_dt])
            S.op("dve", lambda e: e.tensor_tensor(out=da_all[:], in0=dt_all[:], in1=nega[:].unsqueeze(1).to_broadcast([128, 32, 32]),
                                                  op=ALU.mult), reads=[b_dt, b_nega], writes=[b_da])
            for half in range(2):
                pp, b_pp = ptA.next()
                S.op("pe", lambda e, pp=pp, half=half: e.matmul(out=pp[:], lhsT=Uc, rhs=flat(da_all)[:, half * 512:(half + 1) * 512],
                                                                start=True, stop=True), reads=[b_mats, b_da], writes=[b_pp])
                S.op("dve", lambda e, pp=pp, half=half: e.tensor_copy(out=flat(P_all)[:, half * 512:(half + 1) * 512], in_=pp[:]),
                     reads=[b_pp], writes=[b_P])
            for half in range(2):
                pp, b_pp = ptA.next()
                S.op("pe", lambda e, pp=pp, half=half: e.matmul(out=pp[:], lhsT=onesf, rhs=flat(da_all)[:, half * 512:(half + 1) * 512],
                                                                start=True, stop=True), reads=[b_mats, b_da], writes=[b_pp])
                S.op("dve", lambda e, pp=pp, half=half: e.tensor_copy(out=flat(tot)[:, half * 512:(half + 1) * 512], in_=pp[:]),
                     reads=[b_pp], writes=[b_tot])
            S.op("dve", lambda e: e.tensor_tensor(out=tmpa[:], in0=tot[:], in1=P_all[:], op=ALU.subtract),
                 reads=[b_tot, b_P, b_dt], writes=[b_tmpa])
            e1, b_e1 = (scl, b_scl) if fwd else (wgt, b_wgt)
            e2, b_e2 = (wgt, b_wgt) if fwd else (scl, b_scl)
            S.op("act", lambda e: e.activation(out=e1[:], in_=P_all[:], func=AF.Exp), reads=[b_P], writes=[b_e1])
            S.op("act", lambda e: e.activation(out=e2[:], in_=tmpa[:], func=AF.Exp), reads=[b_tmpa], writes=[b_e2])
            S.op("act", lambda e: e.activation(out=cdc[:], in_=tot[:], func=AF.Exp), reads=[b_tot], writes=[b_cdc])
            S.op("dve", lambda e: e.tensor_scalar(out=bias_all[:], in0=P_all[:], scalar1=(-1.0 if fwd else 1.0), scalar2=None,
                                                  op0=ALU.mult), reads=[b_P], writes=[b_bias])

            xcr = Ring(nc, st, un("s_xc"), [128, 24, 128], BF16, 3)
            xsr = Ring(nc, st, un("s_xs"), [128, 2048], BF16, 2)
            Btr = Ring(nc, st, un("s_Bt"), [128, 512], BF16, 2)
            cbr = Ring(nc, st, un("s_cb"), [128, 512], F32, 2)
            xdtr = Ring(nc, st, un("s_xdt"), [128, 2048], BF16, 2)
            xwr = Ring(nc, st, un("s_xw"), [128, 2048], BF16, 2)
            decr = Ring(nc, st, un("s_dec"), [128, 128], F32, 12)
            MTr = Ring(nc, st, un("s_MT"), [128, 128], BF16, 12)
            yaccr = Ring(nc, st, un("s_ya"), [128, 2048], F32, 2)
            tmpr = Ring(nc, st, un("s_tmp"), [128, 512], F32, 2)
            if fwd:
                dskr = Ring(nc, st, un("s_dsk"), [128, 2048], F32, 1)
            else:
                zr = Ring(nc, st, un("s_z"), [128, 2048], BF16, 4)
                yfr = Ring(nc, st, un("s_yf"), [128, 2048], F32, 4)
                jkr = Ring(nc, st, un("s_jk"), [128, 512], BF16, 1)
                st4r = Ring(nc, st, un("s_st4"), [128, 12], F32, 2)
                mbr = Ring(nc, st, un("s_mb"), [128, 2048], BF16, 2)
                mstr = Ring(nc, st, un("s_mst"), [128, 16, 128], BF16, 2)
            order = list(range(32)) if fwd else list(range(31, -1, -1))

            def load(ci):
                c = order[ci]
                xc_, b_xc = xcr.next()
                S.dma(lambda e: e.dma_start(out=xc_[:], in_=xcs[c, :, :, :]), reads=[b_xcs], writes=[b_xc], sem_buf=b_xc)
                if fwd:
                    return (xc_, b_xc)
                z_, b_z = zr.next()
                yf_, b_yf = yfr.next()
                S.dma(lambda e: e.dma_start(out=z_[:], in_=zs[c * 128:(c + 1) * 128, :]), reads=[b_zs], writes=[b_z], sem_buf=b_z)
                S.dma(lambda e: e.dma_start(out=yf_[:], in_=yfs[c * 128:(c + 1) * 128, :]), reads=[b_yfs], writes=[b_yf], sem_buf=b_yf)
                return (xc_, b_xc, z_, b_z, yf_, b_yf)

            def prologue(ci, h):
                c = order[ci]
                xc_, b_xc = h[0], h[1]
                xs, b_xs = xsr.next()
                for half in range(2):
                    pt, b_pt = ptA.next()
                    pv = pt[:, :].bitcast(BF16)
                    for jj in range(8):
                        S.op("pe", lambda e, pv=pv, jj=jj, half=half: e.transpose(out=pv[:, jj * 128:(jj + 1) * 128],
                                                                                 in_=xc_[:, half * 8 + jj, :], identity=identb[:]),
                             reads=[b_xc, b_identb], writes=[b_pt])
                    if half == 0:
                        S.op("act", lambda e, pv=pv: e.copy(out=xs[:, 0:1024], in_=pv[:, :]), reads=[b_pt], writes=[b_xs])
                    else:
                        S.op("dve", lambda e, pv=pv: e.tensor_copy(out=xs[:, 1024:2048], in_=pv[:, :]), reads=[b_pt], writes=[b_xs])
                pt, b_pt = ptA.next()
                pvb = pt[:, :].bitcast(BF16)
                for g in range(4):
                    S.op("pe", lambda e, g=g: e.transpose(out=pvb[:, g * 128:(g + 1) * 128], in_=xc_[:, 16 + g, :], identity=identb[:]),
                         reads=[b_xc, b_identb], writes=[b_pt])
                Bt, b_Bt = Btr.next()
                S.op("dve", lambda e: e.tensor_copy(out=Bt[:], in_=pvb[:, 0:512]), reads=[b_pt], writes=[b_Bt])
                pcb, b_pcb = ptA.next()
                for g in range(4):
                    S.op("pe", lambda e, g=g: e.matmul(out=pcb[:, g * 128:(g + 1) * 128], lhsT=xc_[:, 16 + g, :], rhs=xc_[:, 20 + g, :],
                                                       start=True, stop=True), reads=[b_xc], writes=[b_pcb])
                cbT, b_cbT = cbr.next()
                S.op("act", lambda e: e.copy(out=cbT[:], in_=pcb[:]), reads=[b_pcb], writes=[b_cbT])
                xdt, b_xdt = xdtr.next()
                xw, b_xw = xwr.next()
                v3 = lambda t_: t_[:, :].rearrange("p (h d) -> p h d", h=32)
                S.op("dve", lambda e: e.tensor_tensor(out=v3(xdt), in0=v3(xs), in1=dt_all[:, c, :].unsqueeze(2).to_broadcast([128, 32, 64]),
                                                      op=ALU.mult), reads=[b_xs, b_dt], writes=[b_xdt])
                S.op("pool", lambda e: e.tensor_tensor(out=v3(xw), in0=v3(xdt), in1=wgt[:, c, :].unsqueeze(2).to_broadcast([128, 32, 64]),
                                                       op=ALU.mult), reads=[b_xdt, b_wgt], writes=[b_xw])
                return (xs, b_xs, Bt, b_Bt, cbT, b_cbT, xdt, b_xdt, xw, b_xw)

            def comp(ci, h, pr):
                c = order[ci]
                xc_, b_xc = h[0], h[1]
                xs, b_xs, Bt, b_Bt, cbT, b_cbT, xdt, b_xdt, xw, b_xw = pr
                v3 = lambda t_: t_[:, :].rearrange("p (h d) -> p h d", h=32)
                g8 = lambda t_: t_.rearrange("p (h d) -> p h d", h=8)
                ya, b_ya = yaccr.next()
                LAGH = 4
                mts = {}
                cur = {}

                def stageA(bi):
                    sb_, b_sb = segb.next()
                    for q in range(4):
                        h_ = bi * 4 + q
                        seg = sb_[:, q * 128:(q + 1) * 128]
                        S.op("pe", lambda e, seg=seg, h_=h_: e.matmul(out=seg, lhsT=da_all[:, c, h_:h_ + 1].to_broadcast([128, 128]),
                                                                      rhs=Uc, start=True, stop=False),
                             reads=[b_da, b_mats], writes=[b_sb])
                        S.op("pe", lambda e, seg=seg: e.matmul(out=seg, lhsT=identb[:], rhs=negm[:, midx, :], start=False, stop=True),
                             reads=[b_identb, b_negm], writes=[b_sb])
                    for q in range(4):
                        h_ = bi * 4 + q
                        g = h_ // 8
                        seg = sb_[:, q * 128:(q + 1) * 128]
                        dec, b_dec = decr.next()
                        S.op("act", lambda e, seg=seg, dec=dec, h_=h_: e.activation(out=dec[:], in_=seg, func=AF.Exp,
                                                                                   bias=bias_all[:, c, h_:h_ + 1],
                                                                                   scale=(1.0 if fwd else -1.0)),
                             reads=[b_sb, b_bias], writes=[b_dec])
                        MT, b_MT = MTr.next()
                        S.op("dve" if h_ % 2 == 0 else "pool", lambda e, dec=dec, MT=MT, g=g: e.tensor_tensor(
                            out=MT[:], in0=dec[:], in1=cbT[:, g * 128:(g + 1) * 128], op=ALU.mult),
                            reads=[b_dec, b_cbT], writes=[b_MT])
                        mts[h_] = (MT, b_MT)

                def stageB(h_):
                    g = h_ // 8
                    hh = h_ % 8
                    if hh == 0:
                        cur[0] = pyr.next()
                    py, b_py = cur[0]
                    MT, b_MT = mts.pop(h_)
                    S.op("pe", lambda e, MT=MT, py=py, hh=hh, h_=h_: e.matmul(out=py[:, hh * 64:(hh + 1) * 64], lhsT=MT[:],
                                                                             rhs=xdt[:, h_ * 64:(h_ + 1) * 64], start=True, stop=True),
                         reads=[b_MT, b_xdt], writes=[b_py])
                    if hh != 7:
                        return
                    po, b_po = por.next()
                    S.op("pe", lambda e, po=po, g=g: e.matmul(out=po[:], lhsT=xc_[:, 20 + g, :], rhs=Sbf[:, g * 512:(g + 1) * 512],
                                                              start=True, stop=True), reads=[b_xc, b_Sbf[g]], writes=[b_po])
                    pst, b_pst = pstr.next()
                    S.op("pe", lambda e, pst=pst, g=g: e.matmul(out=pst[:], lhsT=Bt[:, g * 128:(g + 1) * 128], rhs=xw[:, g * 512:(g + 1) * 512],
                                                                start=True, stop=True), reads=[b_Bt, b_xw], writes=[b_pst])
                    tmp, b_tmp = tmpr.next()
                    S.op("dve", lambda e, po=po, tmp=tmp, g=g: e.tensor_tensor(
                        out=g8(tmp[:, :]), in0=g8(po[:, :]), in1=scl[:, c, g * 8:(g + 1) * 8].unsqueeze(2).to_broadcast([128, 8, 64]),
                        op=ALU.mult), reads=[b_po, b_scl], writes=[b_tmp])
                    S.op("dve", lambda e, py=py, tmp=tmp, g=g: e.tensor_tensor(out=ya[:, g * 512:(g + 1) * 512], in0=py[:], in1=tmp[:],
                                                                              op=ALU.add), reads=[b_py, b_tmp], writes=[b_ya])
                    S.op("pool", lambda e, g=g: e.tensor_tensor(
                        out=g8(Sf[:, g * 512:(g + 1) * 512]), in0=g8(Sf[:, g * 512:(g + 1) * 512]),
                        in1=cdc[:, c, g * 8:(g + 1) * 8].unsqueeze(2).to_broadcast([128, 8, 64]), op=ALU.mult),
                        reads=[b_Sf[g], b_cdc], writes=[b_Sf[g]])
                    S.op("dve", lambda e, pst=pst, g=g: e.tensor_tensor(out=Sf[:, g * 512:(g + 1) * 512], in0=pst[:],
                                                                       in1=Sf[:, g * 512:(g + 1) * 512], op=ALU.add),
                         reads=[b_pst, b_Sf[g]], writes=[b_Sf[g]])
                    S.op("act", lambda e, g=g: e.copy(out=Sbf[:, g * 512:(g + 1) * 512], in_=Sf[:, g * 512:(g + 1) * 512]),
                         reads=[b_Sf[g]], writes=[b_Sbf[g]])

                for k in range(8 + 2):
                    if k < 8:
                        stageA(k)
                    if k >= 2:
                        for q in range(4):
                            stageB((k - 2) * 4 + q)
                if fwd:
                    dk, b_dk = dskr.next()
                    S.op("pool", lambda e: e.tensor_tensor(out=v3(dk), in0=v3(xs), in1=dsk.unsqueeze(2).to_broadcast([128, 32, 64]),
                                                           op=ALU.mult), reads=[b_xs, b_ssp], writes=[b_dk])
                    S.op("pool", lambda e: e.tensor_tensor(out=ya[:], in0=ya[:], in1=dk[:], op=ALU.add),
                         reads=[b_ya, b_dk], writes=[b_ya])
                    S.dma(lambda e: e.dma_start(out=yfs[c * 128:(c + 1) * 128, :], in_=ya[:]), reads=[b_ya], writes=[b_yfs],
                          sem_buf=b_ya, eng="pool")
                    return
                def epi():
                    z_, b_z, yf_, b_yf = h[2], h[3], h[4], h[5]
                    if dbg:
                        S.dma(lambda e: e.dma_start(out=ybs[c * 128:(c + 1) * 128, :], in_=ya[:]), reads=[b_ya], writes=[b_ybs],
                              sem_buf=b_ya, eng="pool")
                    S.op("pool", lambda e: e.tensor_tensor(out=ya[:], in0=ya[:], in1=yf_[:], op=ALU.add), reads=[b_ya, b_yf], writes=[b_ya])
                    S.op("dve", lambda e: e.tensor_tensor(out=ya[:], in0=ya[:], in1=z_[:], op=ALU.mult), reads=[b_ya, b_z], writes=[b_ya])
                    jk, b_jk = jkr.next()
                    s4, b_s4 = st4r.next()
                    for g in range(4):
                        S.op("act", lambda e, g=g: e.activation(out=jk[:], in_=ya[:, g * 512:(g + 1) * 512], func=AF.Square,
                                                                scale=1.0 / math.sqrt(512.0), accum_out=s4[:, g:g + 1]),
                             reads=[b_ya], writes=[b_jk, b_s4])
                    S.op("act", lambda e: e.activation(out=s4[:, 4:8], in_=s4[:, 0:4], func=AF.Sqrt, bias=EPS_AP[:, 0:1]),
                         reads=[b_s4, b_eps], writes=[b_s4])
                    S.op("dve", lambda e: e.reciprocal(out=s4[:, 8:12], in_=s4[:, 4:8]), reads=[b_s4], writes=[b_s4])
                    mb, b_mb = mbr.next()
                    for g in range(4):
                        S.op("dve", lambda e, g=g: e.tensor_scalar(out=mb[:, g * 512:(g + 1) * 512], in0=ya[:, g * 512:(g + 1) * 512],
                                                                   scalar1=s4[:, 8 + g:9 + g], scalar2=None, op0=ALU.mult),
                             reads=[b_ya, b_s4], writes=[b_mb])
                    mst, b_mst = mstr.next()
                    for half in range(2):
                        pt, b_pt = ptA.next()
                        pv = pt[:, :].bitcast(BF16)
                        for jj in range(8):
                            j = half * 8 + jj
                            S.op("pe", lambda e, pv=pv, jj=jj, j=j: e.transpose(out=pv[:, jj * 128:(jj + 1) * 128],
                                                                               in_=mb[:, j * 128:(j + 1) * 128], identity=identb[:]),
                                 reads=[b_mb, b_identb], writes=[b_pt])
                        S.op("act", lambda e, pv=pv, half=half: e.copy(out=mst[:, half * 8:(half + 1) * 8, :],
                                                                       in_=pv[:, :].rearrange("p (j t) -> p j t", j=8)),
                             reads=[b_pt], writes=[b_mst])
                    for half in range(2):
                        S.dma(lambda e, half=half: e.dma_start(
                            out=mTs[half * 1024:(half + 1) * 1024, c * 128:(c + 1) * 128].rearrange("(j p) t -> p j t", p=128),
                            in_=mst[:, half * 8:(half + 1) * 8, :]), reads=[b_mst], writes=[b_mTs], sem_buf=b_mst, eng="pool")

                if pend_epi:
                    pend_epi.pop()()
                pend_epi.append(epi)

            pend_epi = []
            hs = {}
            prs = {}
            for i in range(32 + 2):
                if i < 32:
                    hs[i] = load(i)
                if 1 <= i <= 32:
                    prs[i - 1] = prologue(i - 1, hs[i - 1])
                if i >= 2:
                    comp(i - 2, hs.pop(i - 2), prs.pop(i - 2))
            if pend_epi:
                pend_epi.pop()()
        S.barrier()

    ssd_pass(True)
    if stop_after <= 4 and stop_after == 4:
        pass
    ssd_pass(False)
    if stop_after <= 4:
        S.emit(nc)
        return nc

    mw = ExitStack()
    wpa = mw.enter_context(nc.sbuf_tensor("m_wpa", [128, 8, D_], BF16)); b_wpa = Buf()
    wpb = mw.enter_context(nc.sbuf_tensor("m_wpb", [128, 16, D_], BF16)); b_wpb = Buf()
    wo = mw.enter_context(nc.sbuf_tensor("m_wo", [128, 8, D_], BF16)); b_wo = Buf()
    mws = ExitStack()
    ms8 = mws.enter_context(nc.sbuf_tensor("m_s8", [128, 8, D_], F32)); b_ms8 = Buf()

    def preload_merge_weights():
        jobs = [(w_pa[:, :], wpa[:, :, :], b_wpa, None), (w_pb[0:1024, :], wpb[:, 0:8, :], b_wpb, gssm_t[:, 0:8]),
                (w_pb[1024:2048, :], wpb[:, 8:16, :], b_wpb, gssm_t[:, 8:16]), (w_o[:, :], wo[:, :, :], b_wo, None)]
        for src, dst, b_dst, g_ in jobs:
            S.dma(lambda e, src=src: e.dma_start(out=ms8[:], in_=src.rearrange("(kc p) n -> p kc n", p=128)),
                  writes=[b_ms8], sem_buf=b_ms8)
            if g_ is None:
                S.op("pool", lambda e, dst=dst: e.tensor_copy(out=dst, in_=ms8[:]), reads=[b_ms8], writes=[b_dst])
            else:
                S.op("pool", lambda e, dst=dst, g_=g_: e.tensor_tensor(out=dst, in0=ms8[:], in1=g_.unsqueeze(2).to_broadcast([128, 8, D_]),
                                                                      op=ALU.mult), reads=[b_ms8, b_gssm], writes=[b_dst])

    with ExitStack() as st:
        ktr = Ring(nc, st, un("t_k"), [128, S_], BF16, 2)
        qtr = Ring(nc, st, un("t_q"), [128, S_], BF16, 2)
        vtr = Ring(nc, st, un("t_v"), [128, 32, 65], BF16, 2)
        psS = Ring(nc, st, un("t_ps"), [128, 1024], F32, 3, psum=True)
        psO = Ring(nc, st, un("t_po"), [128, 1024], F32, 1, psum=True)
        pTr = Ring(nc, st, un("t_pT"), [128, 1024], BF16, 4)
        rdr = Ring(nc, st, un("t_rd"), [128, 1024], F32, 2)
        osr = Ring(nc, st, un("t_os"), [128, 1024], F32, 2)
        aor = Ring(nc, st, un("t_ao"), [128, S_], BF16, 2)
        sc = 1.0 / math.sqrt(96.0)
        LAG = 2
        tiles = {}

        def ensure(h_):
            if h_ >= NH or h_ in tiles:
                return
            kt, b_kt = ktr.next()
            qt, b_qt = qtr.next()
            vt, b_vt = vtr.next()
            S.dma(lambda e: e.dma_start(out=kt[0:96, :], in_=KT[h_, :, :]), reads=[b_KT], writes=[b_kt], sem_buf=b_kt)
            S.dma(lambda e: e.dma_start(out=qt[0:96, :], in_=QT[h_, :, :]), reads=[b_QT], writes=[b_qt], sem_buf=b_qt)
            S.dma(lambda e: e.dma_start(out=vt[:], in_=Vs[h_, :, :, :]), reads=[b_Vs], writes=[b_vt], sem_buf=b_vt)
            tiles[h_] = (kt, b_kt, qt, b_qt, vt, b_vt)

        steps = [(h_, sb, kc) for h_ in range(NH) for sb in range(4) for kc in range(32)]
        pend = {}
        cur_po = {}
        cur_ao = {}
        ensure(0)
        preload_merge_weights()
        for i in range(len(steps) + LAG):
            if i < len(steps):
                h_, sb, kc = steps[i]
                kt, b_kt, qt, b_qt, vt, b_vt = tiles[h_]
                ps, b_ps = psS.next()
                for u in range(2):
                    S.op("pe", lambda e, ps=ps, kc=kc, sb=sb, u=u, kt=kt, qt=qt: e.matmul(
                        out=ps[:, u * 512:(u + 1) * 512], lhsT=kt[0:96, kc * 128:(kc + 1) * 128],
                        rhs=qt[0:96, sb * 1024 + u * 512:sb * 1024 + (u + 1) * 512], start=True, stop=True),
                        reads=[b_kt, b_qt], writes=[b_ps])
                pT, b_pT = pTr.next()
                S.op("act", lambda e, ps=ps, pT=pT: e.activation(out=pT[:], in_=ps[:], func=AF.Exp, scale=sc),
                     reads=[b_ps], writes=[b_pT])
                pend[i] = (pT, b_pT)
            if i >= LAG:
                h_, sb, kc = steps[i - LAG]
                kt, b_kt, qt, b_qt, vt, b_vt = tiles[h_]
                pT, b_pT = pend.pop(i - LAG)
                if kc == 0:
                    cur_po[0] = psO.next()
                    if sb == 0:
                        cur_ao[0] = aor.next()
                        ensure(h_ + 1)
                po, b_po = cur_po[0]
                ao, b_ao = cur_ao[0]
                for u in range(2):
                    S.op("pe", lambda e, po=po, pT=pT, kc=kc, u=u, vt=vt: e.matmul(
                        out=po[0:65, u * 512:(u + 1) * 512], lhsT=vt[:, kc, :], rhs=pT[:, u * 512:(u + 1) * 512],
                        start=(kc == 0), stop=(kc == 31)), reads=[b_vt, b_pT], writes=[b_po])
                if kc == 31:
                    osb, b_osb = osr.next()
                    S.op("dve", lambda e, po=po, osb=osb: e.tensor_copy(out=osb[0:65, :], in_=po[0:65, :]), reads=[b_po], writes=[b_osb])
                    rd, b_rd = rdr.next()
                    S.op("dve", lambda e, osb=osb, rd=rd: e.reciprocal(out=rd[64:65, :], in_=osb[64:65, :]), reads=[b_osb], writes=[b_rd])
                    pb, b_pb = psS.next()
                    for u in range(2):
                        S.op("pe", lambda e, pb=pb, rd=rd, u=u: e.matmul(out=pb[0:64, u * 512:(u + 1) * 512], lhsT=mats[64:65, 2, 0:64],
                                                                         rhs=rd[64:65, u * 512:(u + 1) * 512], start=True, stop=True),
                             reads=[b_mats, b_rd], writes=[b_pb])
                    qs = slice(sb * 1024, (sb + 1) * 1024)
                    S.op("dve", lambda e, pb=pb, osb=osb, qs=qs, ao=ao: e.tensor_tensor(
                        out=ao[0:64, qs], in0=pb[0:64, :], in1=osb[0:64, :], op=ALU.mult),
                        reads=[b_pb, b_osb], writes=[b_ao])
                    if sb == 3:
                        S.dma(lambda e, h_=h_, ao=ao: e.dma_start(out=aTs[h_ * 64:(h_ + 1) * 64, :], in_=ao[0:64, :]),
                              reads=[b_ao], writes=[b_aTs], sem_buf=b_ao, eng="pool")
        S.barrier()
    if stop_after <= 5:
        S.emit(nc)
        return nc

    mws.close()
    with ExitStack() as st:
        atr = Ring(nc, st, un("m_at"), [128, 8, 512], BF16, 2)
        mtr = Ring(nc, st, un("m_mt"), [128, 16, 512], BF16, 2)
        gtr = Ring(nc, st, un("m_gt"), [128, 16, 512], BF16, 2)
        mgr = Ring(nc, st, un("m_mg"), [128, 8, 512], BF16, 2)
        t1r = Ring(nc, st, un("m_t1"), [128, 512], F32, 2)
        t2r = Ring(nc, st, un("m_t2"), [128, 512], F32, 2)
        xr = Ring(nc, st, un("m_x"), [128, D_], F32, 3)
        ps = Ring(nc, st, un("m_ps"), [128, 512], F32, 6, psum=True)
        def loadm(t):
            at, b_at = atr.next()
            mt, b_mt = mtr.next()
            gt_, b_gt = gtr.next()
            ts = slice(t * 512, (t + 1) * 512)
            S.dma(lambda e: e.dma_start(out=at[:], in_=aTs[:, ts].rearrange("(k p) t -> p k t", p=128)), reads=[b_aTs], writes=[b_at], sem_buf=b_at)
            S.dma(lambda e: e.dma_start(out=mt[:], in_=mTs[:, ts].rearrange("(k p) t -> p k t", p=128)), reads=[b_mTs], writes=[b_mt], sem_buf=b_mt)
            S.dma(lambda e: e.dma_start(out=gt_[:], in_=gts[:, ts].rearrange("(k p) t -> p k t", p=128)), reads=[b_gts], writes=[b_gt], sem_buf=b_gt)
            return (at, b_at, mt, b_mt, gt_, b_gt)

        def compm(t, hd):
            at, b_at, mt, b_mt, gt_, b_gt = hd
            mg, b_mg = mgr.next()
            for dc in range(8):
                pa, b_pa = ps.next()
                pb, b_pb = ps.next()
                for kc in range(8):
                    S.op("pe", lambda e, pa=pa, kc=kc, dc=dc: e.matmul(out=pa[:], lhsT=wpa[:, kc, dc * 128:(dc + 1) * 128], rhs=at[:, kc, :],
                                                                       start=(kc == 0), stop=(kc == 7)), reads=[b_wpa, b_at], writes=[b_pa])
                for kc in range(16):
                    S.op("pe", lambda e, pb=pb, kc=kc, dc=dc: e.matmul(out=pb[:], lhsT=wpb[:, kc, dc * 128:(dc + 1) * 128], rhs=mt[:, kc, :],
                                                                       start=(kc == 0), stop=(kc == 15)), reads=[b_wpb, b_mt], writes=[b_pb])
                t1, b_t1 = t1r.next()
                t2, b_t2 = t2r.next()
                S.op("dve", lambda e, pa=pa, t1=t1, dc=dc: e.tensor_tensor(out=t1[:], in0=pa[:], in1=gt_[:, dc, :], op=ALU.mult),
                     reads=[b_pa, b_gt], writes=[b_t1])
                S.op("dve", lambda e, pb=pb, t2=t2, dc=dc: e.tensor_tensor(out=t2[:], in0=pb[:], in1=gt_[:, 8 + dc, :], op=ALU.mult),
                     reads=[b_pb, b_gt], writes=[b_t2])
                S.op("pool", lambda e, t1=t1, t2=t2, dc=dc: e.tensor_tensor(out=mg[:, dc, :], in0=t1[:], in1=t2[:], op=ALU.add),
                     reads=[b_t1, b_t2], writes=[b_mg])
            for sb in range(4):
                tb = t * 4 + sb
                xt, b_xt = xr.next()
                S.dma(lambda e, xt=xt, tb=tb: e.dma_start(out=xt[:], in_=x1s[tb * 128:(tb + 1) * 128, :]),
                      reads=[b_x1s], writes=[b_xt], sem_buf=b_xt)
                for half in range(2):
                    p, b_p = ps.next()
                    for kc in range(8):
                        S.op("pe", lambda e, p=p, kc=kc, sb=sb, half=half: e.matmul(
                            out=p[:], lhsT=mg[:, kc, sb * 128:(sb + 1) * 128], rhs=wo[:, kc, half * 512:(half + 1) * 512],
                            start=(kc == 0), stop=(kc == 7)), reads=[b_mg, b_wo], writes=[b_p])
                    S.op("dve", lambda e, p=p, xt=xt, half=half: e.tensor_tensor(out=xt[:, half * 512:(half + 1) * 512], in0=p[:],
                                                                                in1=xt[:, half * 512:(half + 1) * 512], op=ALU.add),
                         reads=[b_p, b_xt], writes=[b_xt])
                S.dma(lambda e, xt=xt, tb=tb: e.dma_start(out=x2s[tb * 128:(tb + 1) * 128, :], in_=xt[:]),
                      reads=[b_xt], writes=[b_x2s], sem_buf=b_xt, eng="pool")

        pipeline(8, loadm, compm, 1)
        S.barrier()
    mw.close()
    hst2 = ExitStack()
    hT2 = hst2.enter_context(nc.sbuf_tensor("hT2", [128, 8, S_], BF16)); b_hT2 = Buf("hT2")
    norm_phase(x2s, b_x2s, hT2, b_hT2)

    with ExitStack() as fst:
        wd2 = fst.enter_context(nc.sbuf_tensor("wd2", [128, NFF, D_], BF16)); b_wd2 = Buf()
        ffn_gateup(w_g2, w_u2, 2, hT2, b_hT2, w_d2, wd2, b_wd2)
        ffn_down(w_d2, x2s, b_x2s, y_out, b_yout, None, wd2, b_wd2)
    S.emit(nc)
    return nc


def _fm(v, kc):
    return np.ascontiguousarray(np.asarray(v, np.float32).reshape(kc, 128).T)


_CACHE = {}


def consts():
    ii = np.arange(128)
    U = (ii[:, None] <= ii[None, :]).astype(np.float32)
    Us = (ii[:, None] < ii[None, :]).astype(np.float32)
    ones = np.ones((128, 128), np.float32)
    I = np.eye(128, dtype=np.float32)
    mats = np.ascontiguousarray(np.stack([U, Us, ones, I], axis=1))
    negf = np.where(ii[:, None] > ii[None, :], -30000.0, 0.0).astype(np.float32)
    posb = np.where(ii[:, None] < ii[None, :], 30000.0, 0.0).astype(np.float32)
    neg = np.ascontiguousarray(np.stack([negf, posb], axis=1)).astype(ml_dtypes.bfloat16)
    invf = (1.0 / (10000.0 ** (np.arange(0, 32, 2, dtype=np.float32) / 32.0))).astype(np.float32)[None, :]
    return dict(c_identb=I.astype(ml_dtypes.bfloat16), c_mats=mats, c_neg=neg, c_invf=invf)


def make_shared(inp):
    f = lambda k: np.asarray(inp[k], np.float32)[0]
    d = {}
    d["gfm"] = np.ascontiguousarray(np.concatenate([_fm(f("ffn1_norm"), 8), _fm(f("mix_norm"), 8), _fm(f("ffn2_norm"), 8)], axis=1))
    d["gqa"] = _fm(f("q_a_norm"), 3)
    d["gkva"] = _fm(f("kv_a_norm"), 2)
    d["gssm"] = _fm(f("ssm_norm"), 16)
    d["w_g1"] = f("ffn1_w_gate"); d["w_u1"] = f("ffn1_w_up"); d["w_d1"] = f("ffn1_w_down")
    d["w_g2"] = f("ffn2_w_gate"); d["w_u2"] = f("ffn2_w_up"); d["w_d2"] = f("ffn2_w_down")
    d["w_in"] = f("w_in"); d["w_qb"] = f("w_q_b"); d["w_kvb"] = f("w_kv_b")
    d["hn"] = np.concatenate([f("q_head_norm"), f("k_head_norm")])[None, :].astype(np.float32)
    cw = f("conv_w")[:, 0, :]
    d["convw"] = np.ascontiguousarray(cw.T.reshape(24, 128, 5).transpose(1, 0, 2))
    d["convb"] = _fm(f("conv_b"), 24)
    d["ssp"] = np.concatenate([f("a_log_fwd"), f("a_log_bwd"), f("dt_bias_fwd"), f("dt_bias_bwd"), f("d_skip")])[None, :].astype(np.float32)
    d["w_pa"] = f("w_attn_branch"); d["w_pb"] = f("w_ssm_branch"); d["w_o"] = f("w_out")
    d.update(consts())
    return d


def make_inmap(inp, shared, b):
    d = dict(shared)
    d["x"] = np.ascontiguousarray(np.asarray(inp["x"], np.float32)[b])
    p = np.asarray(inp["positions"], np.int32)[b]
    d["pos"] = np.ascontiguousarray(p.reshape(32, 128).T)
    return d


def kernel(**inputs):
    nb = int(np.asarray(inputs["x"]).shape[0])
    nc = build(dbg=False)
    shared = make_shared(inputs)
    in_maps = [make_inmap(inputs, shared, b) for b in range(nb)]
    res = run_bass_kernel_spmd(nc, in_maps, core_ids=list(range(nb)))
    out = np.stack([np.asarray(res.results[b]["y"], dtype=np.float32) for b in range(nb)], axis=0)
    return out
```

```python
import math
from contextlib import ExitStack
import numpy as np
import ml_dtypes
import concourse.bass as bass
import concourse.mybir as mybir
from concourse.bass_utils import run_bass_kernel_spmd

F32 = mybir.dt.float32
BF16 = mybir.dt.bfloat16
I32 = mybir.dt.int32
AF = mybir.ActivationFunctionType
ALU = mybir.AluOpType
AX = mybir.AxisListType

S_ = 4096
D_ = 1024
FF = 2816
NFF = 22
NH = 16
EPS = 1e-6
C_Q, C_KV, C_PE, C_Z, C_XBC, C_DTF, C_DTB, C_GA, C_GB = 0, 384, 640, 672, 2720, 5792, 5824, 5856, 6880
IN_DIM = 7904
ENGS = ("pe", "act", "dve", "pool", "sp")
FUSE_WAITS = True


class DSem:
    def __init__(self):
        self.count = 0
        self.handle = None


class Buf:
    __slots__ = ("name", "lw", "rd", "dsem", "ep")

    def __init__(self, name=""):
        self.name = name
        self.lw = None
        self.rd = []
        self.dsem = None
        self.ep = -1


class Op:
    __slots__ = ("eng", "fn", "idx", "waits", "dwaits", "inc", "dsem", "know", "seq", "multi")


class Sched:
    def __init__(self):
        self.ops = {e: [] for e in ENGS}
        self.know = {e: {} for e in ENGS}
        self.dsems = []
        self.free = []
        self.epoch = 0

    def _add(self, eng, fn, reads, writes, dsem=None, extra=(), extra_ds=()):
        op = Op()
        op.eng = eng
        op.fn = fn
        op.idx = len(self.ops[eng])
        op.waits = {}
        op.dwaits = {}
        op.inc = False
        op.dsem = dsem
        op.seq = None
        op.multi = False
        know = self.know[eng]
        deps = list(extra)
        for b in reads:
            if b.lw is not None:
                deps.append(b.lw)
        for b in writes:
            if b.lw is not None:
                deps.append(b.lw)
            deps.extend(b.rd)
        for a in deps:
            if a is op:
                continue
            if a.dsem is None:
                if a.eng == "pe" and eng == "pe":
                    continue
                if know.get(a.eng, -1) >= a.idx:
                    continue
                a.inc = True
                cur = op.waits.get(a.eng)
                if cur is None or cur.idx < a.idx:
                    op.waits[a.eng] = a
                for k, v in a.know.items():
                    if know.get(k, -1) < v:
                        know[k] = v
                know[a.eng] = max(know.get(a.eng, -1), a.idx)
            else:
                ds = a.dsem
                v = ds.count
                if know.get(ds, -1) >= v:
                    continue
                op.dwaits[ds] = v
                for k, vv in a.know.items():
                    if know.get(k, -1) < vv:
                        know[k] = vv
                know[ds] = v
        for ds in extra_ds:
            v = ds.count
            if know.get(ds, -1) < v:
                op.dwaits[ds] = v
                know[ds] = v
        if dsem is not None:
            dsem.count += 16
        op.know = dict(know)
        for b in reads:
            b.rd.append(op)
        for b in writes:
            b.lw = op
            b.rd = []
        self.ops[eng].append(op)
        return op

    def op(self, eng, fn, reads=(), writes=(), multi=False):
        o = self._add(eng, fn, reads, writes)
        o.multi = multi
        return o

    def dma(self, fn, reads=(), writes=(), sem_buf=None, eng="sp"):
        if sem_buf.dsem is None or sem_buf.ep != self.epoch:
            if self.free:
                sem_buf.dsem = self.free.pop()
            else:
                sem_buf.dsem = DSem()
                self.dsems.append(sem_buf.dsem)
            sem_buf.ep = self.epoch
        return self._add(eng, fn, reads, writes, dsem=sem_buf.dsem)

    def barrier(self):
        lasts = []
        for e in ENGS:
            if e == "sp":
                continue
            for o in reversed(self.ops[e]):
                if o.dsem is None:
                    lasts.append(o)
                    break
        spop = self._add("sp", lambda e: e.nop(), (), (), extra=lasts, extra_ds=list(self.dsems))
        self.epoch += 1
        self.free = list(self.dsems)
        for e in ENGS:
            if e == "sp":
                continue
            self._add(e, lambda eh: eh.nop(), (), (), extra=[spop])

    def emit(self, nc):
        with ExitStack() as st:
            esem = {e: st.enter_context(nc.semaphore("es_" + e)) for e in ENGS}
            for i, d in enumerate(self.dsems):
                d.handle = st.enter_context(nc.semaphore("ds%d" % i))
            for e in ENGS:
                c = 0
                for o in self.ops[e]:
                    if o.dsem is None and o.inc:
                        c += 1
                        o.seq = c
            block = st.enter_context(nc.Block())

            def run(e, eh):
                for o in self.ops[e]:
                    wl = [(esem[se], a.seq) for se, a in o.waits.items()] + [(ds.handle, v) for ds, v in o.dwaits.items()]
                    attach = None
                    if wl and o.dsem is None and not o.multi and e != "sp" and FUSE_WAITS:
                        attach = wl.pop()
                    for hh_, vv_ in wl:
                        eh.wait_ge(hh_, vv_)
                    n0 = nc.n_instructions()
                    ins = o.fn(eh)
                    if attach is not None:
                        if nc.n_instructions() - n0 != 1:
                            raise RuntimeError("multi-instruction op with fused wait on %s (%d)" % (e, nc.n_instructions() - n0))
                        ins._wait_ge(attach[0], attach[1])
                    if o.dsem is not None:
                        ins.then_inc(o.dsem.handle, 16)
                    elif o.inc:
                        ins.then_inc(esem[e], 1)
                if e == "sp":
                    for ds in self.dsems:
                        eh.wait_ge(ds.handle, ds.count)

            @block.tensor
            def _(eh):
                run("pe", eh)

            @block.scalar
            def _(eh):
                run("act", eh)

            @block.vector
            def _(eh):
                run("dve", eh)

            @block.gpsimd
            def _(eh):
                run("pool", eh)

            @block.sync
            def _(eh):
                run("sp", eh)


class Ring:
    def __init__(self, nc, st, name, shape, dtype, n, psum=False):
        self.items = []
        for i in range(n):
            if psum:
                t = st.enter_context(nc.psum_tensor("%s%d" % (name, i), shape, dtype))
            else:
                t = st.enter_context(nc.sbuf_tensor("%s%d" % (name, i), shape, dtype))
            self.items.append((t, Buf("%s%d" % (name, i))))
        self.i = 0

    def next(self):
        r = self.items[self.i % len(self.items)]
        self.i += 1
        return r


def pipeline(n, load_fn, compute_fn, depth):
    hs = {}
    for i in range(n + depth):
        if i < n:
            hs[i] = load_fn(i)
        if i >= depth:
            compute_fn(i - depth, hs.pop(i - depth))


class K:
    pass


def build(dbg=False, stop_after=99):
    nc = bass.Bass("TRN2", target_bir_lowering=False)
    S = Sched()
    uid = [0]

    def un(p):
        uid[0] += 1
        return "%s_%d" % (p, uid[0])

    def inp(name, shape, dt=F32):
        return nc.dram_tensor(name, shape, dt, kind="ExternalInput").ap()

    def scratch(name, shape, dt, out=False):
        kind = "ExternalOutput" if (out or dbg) else "Internal"
        return nc.dram_tensor(name, shape, dt, kind=kind).ap(), Buf(name)

    x = inp("x", [S_, D_])
    pos = inp("pos", [128, 32], I32)
    gfm = inp("gfm", [128, 24])
    gqa = inp("gqa", [128, 3])
    gkva = inp("gkva", [128, 2])
    gssm = inp("gssm", [128, 16])
    w_g1 = inp("w_g1", [D_, FF]); w_u1 = inp("w_u1", [D_, FF]); w_d1 = inp("w_d1", [FF, D_])
    w_g2 = inp("w_g2", [D_, FF]); w_u2 = inp("w_u2", [D_, FF]); w_d2 = inp("w_d2", [FF, D_])
    w_in = inp("w_in", [D_, IN_DIM])
    w_qb = inp("w_qb", [384, 1536]); w_kvb = inp("w_kvb", [256, 2048])
    hn = inp("hn", [1, 192])
    convw = inp("convw", [128, 24, 5]); convb = inp("convb", [128, 24])
    ssp = inp("ssp", [1, 160])
    w_pa = inp("w_pa", [D_, D_]); w_pb = inp("w_pb", [2048, D_]); w_o = inp("w_o", [D_, D_])
    c_identb = inp("c_identb", [128, 128], BF16)
    c_mats = inp("c_mats", [128, 4, 128])
    c_neg = inp("c_neg", [128, 2, 128], BF16)
    c_invf = inp("c_invf", [1, 16])

    y_out, b_yout = scratch("y", [S_, D_], F32, out=True)
    x1s, b_x1s = scratch("x1s", [S_, D_], F32)
    x2s, b_x2s = scratch("x2s", [S_, D_], F32)
    hmid, b_hmid = scratch("hmid", [FF, S_], BF16)
    QT, b_QT = scratch("QT", [NH, 96, S_], BF16)
    KT, b_KT = scratch("KT", [NH, 96, S_], BF16)
    Vs, b_Vs = scratch("Vs", [NH, 128, 32, 65], BF16)
    zs, b_zs = scratch("zs", [S_, 2048], BF16)
    xcs, b_xcs = scratch("xcs", [32, 128, 24, 128], BF16)
    gts, b_gts = scratch("gts", [2048, S_], BF16)
    yfs, b_yfs = scratch("yfs", [S_, 2048], F32)
    mTs, b_mTs = scratch("mTs", [2048, S_], BF16)
    aTs, b_aTs = scratch("aTs", [D_, S_], BF16)
    if dbg:
        ybs, b_ybs = scratch("ybs", [S_, 2048], F32)

    top = ExitStack()
    A = lambda name, shape, dt: top.enter_context(nc.sbuf_tensor(name, shape, dt))
    identb = A("identb", [128, 128], BF16); b_identb = Buf()
    mats = A("mats", [128, 4, 128], F32); b_mats = Buf()
    negm = A("negm", [128, 2, 128], BF16); b_negm = Buf()
    gfm_t = A("gfm_t", [128, 24], F32); b_gfm = Buf()
    gqa_t = A("gqa_t", [128, 3], F32); b_gqa = Buf()
    gkva_t = A("gkva_t", [128, 2], F32); b_gkva = Buf()
    gssm_t = A("gssm_t", [128, 16], F32); b_gssm = Buf()
    hn_t = A("hn_t", [128, 192], F32); b_hn = Buf()
    ssp_t = A("ssp_t", [128, 160], F32); b_ssp = Buf()
    convw_t = A("convw_t", [128, 24, 5], F32); b_convw = Buf()
    convb_t = A("convb_t", [128, 24], F32); b_convb = Buf()
    cos_t = A("cos_t", [128, 32, 16], F32); b_cos = Buf()
    sin_t = A("sin_t", [128, 32, 16], F32); b_sin = Buf()
    dtraw = A("dtraw", [128, 32, 64], F32); b_dtraw = Buf()

    def ld(dst, src, b):
        S.dma(lambda e: e.dma_start(out=dst, in_=src), writes=[b], sem_buf=b)

    ld(identb[:], c_identb[:, :], b_identb)
    ld(mats[:], c_mats[:, :, :], b_mats)
    ld(negm[:], c_neg[:, :, :], b_negm)
    ld(gfm_t[:], gfm[:, :], b_gfm)
    ld(gqa_t[:], gqa[:, :], b_gqa)
    ld(gkva_t[:], gkva[:, :], b_gkva)
    ld(gssm_t[:], gssm[:, :], b_gssm)
    ld(hn_t[:], hn.partition_broadcast(128), b_hn)
    ld(ssp_t[:], ssp.partition_broadcast(128), b_ssp)
    ld(convw_t[:], convw[:, :, :], b_convw)
    ld(convb_t[:], convb[:, :], b_convb)
    Umat = mats[:, 0, :]
    Ustr = mats[:, 1, :]
    onesf = mats[:, 2, :]
    identf = mats[:, 3, :]

    with ExitStack() as st:
        post = st.enter_context(nc.sbuf_tensor("post", [128, 32], I32)); b_post = Buf()
        posf = st.enter_context(nc.sbuf_tensor("posf", [128, 32], F32)); b_posf = Buf()
        invf = st.enter_context(nc.sbuf_tensor("invf", [128, 16], F32)); b_invf = Buf()
        ang = st.enter_context(nc.sbuf_tensor("ang", [128, 32, 16], F32)); b_ang = Buf()
        ang2 = st.enter_context(nc.sbuf_tensor("ang2", [128, 32, 16], F32)); b_ang2 = Buf()
        ld(post[:], pos[:, :], b_post)
        ld(invf[:], c_invf.partition_broadcast(128), b_invf)
        S.op("dve", lambda e: e.tensor_copy(out=posf[:], in_=post[:]), reads=[b_post], writes=[b_posf])
        S.op("dve", lambda e: e.tensor_tensor(out=ang[:], in0=posf[:].unsqueeze(2).to_broadcast([128, 32, 16]),
                                              in1=invf[:].unsqueeze(1).to_broadcast([128, 32, 16]), op=ALU.mult),
             reads=[b_posf, b_invf], writes=[b_ang])
        PI = math.pi
        angi = st.enter_context(nc.sbuf_tensor("angi", [128, 32, 16], I32)); b_angi = Buf()
        ang3 = st.enter_context(nc.sbuf_tensor("ang3", [128, 32, 16], F32)); b_ang3 = Buf()

        def rr(shift, dst, b_dst):
            S.op("dve", lambda e: e.tensor_scalar(out=ang2[:], in0=ang[:], scalar1=shift, scalar2=None, op0=ALU.add),
                 reads=[b_ang], writes=[b_ang2])
            S.op("dve", lambda e: e.tensor_scalar(out=ang3[:], in0=ang2[:], scalar1=1.0 / (2 * PI), scalar2=None,
                                                  op0=ALU.mult), reads=[b_ang2], writes=[b_ang3])
            S.op("dve", lambda e: e.tensor_copy(out=angi[:], in_=ang3[:]), reads=[b_ang3], writes=[b_angi])
            S.op("dve", lambda e: e.tensor_copy(out=ang3[:], in_=angi[:]), reads=[b_angi], writes=[b_ang3])
            S.op("dve", lambda e: e.scalar_tensor_tensor(out=ang2[:], in0=ang3[:], scalar=-2 * PI, in1=ang2[:],
                                                         op0=ALU.mult, op1=ALU.add), reads=[b_ang3, b_ang2], writes=[b_ang2])
            S.op("dve", lambda e: e.tensor_scalar(out=ang3[:], in0=ang2[:], scalar1=-PI, scalar2=1e9,
                                                  op0=ALU.add, op1=ALU.mult), reads=[b_ang2], writes=[b_ang3])
            S.op("dve", lambda e: e.tensor_scalar(out=ang3[:], in0=ang3[:], scalar1=0.0, scalar2=1.0,
                                                  op0=ALU.max, op1=ALU.min), reads=[b_ang3], writes=[b_ang3])
            S.op("dve", lambda e: e.scalar_tensor_tensor(out=ang2[:], in0=ang3[:], scalar=-2 * PI, in1=ang2[:],
                                                         op0=ALU.mult, op1=ALU.add), reads=[b_ang3, b_ang2], writes=[b_ang2])
            S.op("dve", lambda e: e.tensor_scalar(out=ang3[:], in0=ang2[:], scalar1=PI, scalar2=-1e9,
                                                  op0=ALU.add, op1=ALU.mult), reads=[b_ang2], writes=[b_ang3])
            S.op("dve", lambda e: e.tensor_scalar(out=ang3[:], in0=ang3[:], scalar1=0.0, scalar2=1.0,
                                                  op0=ALU.max, op1=ALU.min), reads=[b_ang3], writes=[b_ang3])
            S.op("dve", lambda e: e.scalar_tensor_tensor(out=ang2[:], in0=ang3[:], scalar=2 * PI, in1=ang2[:],
                                                         op0=ALU.mult, op1=ALU.add), reads=[b_ang3, b_ang2], writes=[b_ang2])
            S.op("dve", lambda e: e.tensor_scalar(out=ang2[:], in0=ang2[:], scalar1=PI * (1 - 1e-6),
                                                  scalar2=-PI * (1 - 1e-6), op0=ALU.min, op1=ALU.max),
                 reads=[b_ang2], writes=[b_ang2])
            S.op("act", lambda e: e.activation(out=dst, in_=ang2[:], func=AF.Sin), reads=[b_ang2], writes=[b_dst])

        rr(0.0, sin_t[:], b_sin)
        rr(0.5 * PI, cos_t[:], b_cos)
        S.barrier()

    def wload(stage_ring, w_ring, wsrc, kc, n, gain=None, cast_eng="pool"):
        stg, b_stg = stage_ring.next()
        wt, b_wt = w_ring.next()
        S.dma(lambda e: e.dma_start(out=stg[:, 0:kc, 0:n], in_=wsrc.rearrange("(kc p) n -> p kc n", p=128)),
              writes=[b_stg], sem_buf=b_stg)
        if gain is None:
            S.op(cast_eng, lambda e: e.tensor_copy(out=wt[:, 0:kc, 0:n], in_=stg[:, 0:kc, 0:n]),
                 reads=[b_stg], writes=[b_wt])
        else:
            g_ap, b_g = gain
            S.op(cast_eng, lambda e: e.tensor_tensor(out=wt[:, 0:kc, 0:n], in0=stg[:, 0:kc, 0:n],
                                                     in1=g_ap.unsqueeze(2).to_broadcast([128, kc, n]), op=ALU.mult),
                 reads=[b_stg, b_g], writes=[b_wt])
        return wt, b_wt

    def norm_block(src, b_src, tb, rings, hT, b_hT, ncols=D_):
        junk, b_junk = rings["junk"].next()
        stt, b_stt = rings["st"].next()
        hb, b_hb = rings["hb"].next()
        ptr, b_ptr = rings["ptr"].next()
        S.op("act", lambda e: e.activation(out=junk[:], in_=src, func=AF.Square, scale=1.0 / math.sqrt(ncols),
                                           accum_out=stt[:, 0:1]), reads=[b_src], writes=[b_junk, b_stt])
        S.op("act", lambda e: e.activation(out=stt[:, 1:2], in_=stt[:, 0:1], func=AF.Sqrt, bias=EPS_AP[:, 0:1]),
             reads=[b_stt], writes=[b_stt])
        S.op("dve", lambda e: e.reciprocal(out=stt[:, 2:3], in_=stt[:, 1:2]), reads=[b_stt], writes=[b_stt])
        S.op("dve", lambda e: e.tensor_scalar(out=hb[:], in0=src, scalar1=stt[:, 2:3], scalar2=None, op0=ALU.mult),
             reads=[b_src, b_stt], writes=[b_hb])
        pv = ptr[:, :].bitcast(BF16)
        for kc in range(8):
            S.op("pe", lambda e, kc=kc: e.transpose(out=pv[:, kc * 128:(kc + 1) * 128],
                                                     in_=hb[:, kc * 128:(kc + 1) * 128], identity=identb[:]),
                 reads=[b_hb, b_identb], writes=[b_ptr])
        S.op("act", lambda e: e.copy(out=hT[:, :, tb * 128:(tb + 1) * 128],
                                     in_=pv.rearrange("p (k t) -> p k t", k=8)),
             reads=[b_ptr], writes=[b_hT])

    eps_t = A("eps_t", [128, 1], F32); b_eps = Buf()
    S.op("pool", lambda e: e.memset(eps_t[:], EPS), writes=[b_eps])
    EPS_AP = eps_t
    S.barrier()

    def ffn_gateup(w_g, w_u, gain_col, hT, b_hT, w_d=None, wd=None, b_wd=None):
        with ExitStack() as st:
            stg = Ring(nc, st, un("gu_stg"), [128, 8, 128], F32, 6)
            wr = Ring(nc, st, un("gu_w"), [128, 8, 128], BF16, 6)
            psg = Ring(nc, st, un("gu_pg"), [128, 512], F32, 3, psum=True)
            psu = Ring(nc, st, un("gu_pu"), [128, 512], F32, 3, psum=True)
            sil = Ring(nc, st, un("gu_sil"), [128, 512], F32, 3)
            hm = Ring(nc, st, un("gu_hm"), [128, S_], BF16, 2)
            gain = (gfm_t[:, gain_col * 8:(gain_col + 1) * 8], b_gfm)

            dstg = Ring(nc, st, un("gu_dstg"), [128, 1, D_], F32, 3)

            def load(j):
                wg = wload(stg, wr, w_g[:, j * 128:(j + 1) * 128], 8, 128, gain)
                wu = wload(stg, wr, w_u[:, j * 128:(j + 1) * 128], 8, 128, gain)
                if w_d is not None:
                    sg, b_sg = dstg.next()
                    S.dma(lambda e, sg=sg, j=j: e.dma_start(out=sg[:, 0, :], in_=w_d[j * 128:(j + 1) * 128, :]),
                          writes=[b_sg], sem_buf=b_sg)
                    S.op("pool", lambda e, sg=sg, j=j: e.tensor_copy(out=wd[:, j, :], in_=sg[:, 0, :]),
                         reads=[b_sg], writes=[b_wd])
                return wg, wu

            def comp(j, h):
                (wg, b_wg), (wu, b_wu) = h
                hmt, b_hm = hm.next()
                for tb in range(8):
                    pg, b_pg = psg.next()
                    pu, b_pu = psu.next()
                    sl, b_sl = sil.next()
                    ts = slice(tb * 512, (tb + 1) * 512)
                    for kc in range(8):
                        S.op("pe", lambda e, kc=kc, pg=pg, wg=wg, ts=ts: e.matmul(
                            out=pg[:], lhsT=wg[:, kc, :], rhs=hT[:, kc, ts], start=(kc == 0), stop=(kc == 7)),
                            reads=[b_wg, b_hT], writes=[b_pg])
                    for kc in range(8):
                        S.op("pe", lambda e, kc=kc, pu=pu, wu=wu, ts=ts: e.matmul(
                            out=pu[:], lhsT=wu[:, kc, :], rhs=hT[:, kc, ts], start=(kc == 0), stop=(kc == 7)),
                            reads=[b_wu, b_hT], writes=[b_pu])
                    S.op("act", lambda e, sl=sl, pg=pg: e.activation(out=sl[:], in_=pg[:], func=AF.Silu),
                         reads=[b_pg], writes=[b_sl])
                    S.op("dve", lambda e, sl=sl, pu=pu, hmt=hmt, ts=ts: e.tensor_tensor(
                        out=hmt[:, ts], in0=sl[:], in1=pu[:], op=ALU.mult), reads=[b_sl, b_pu], writes=[b_hm])
                S.dma(lambda e, hmt=hmt, j=j: e.dma_start(out=hmid[j * 128:(j + 1) * 128, :], in_=hmt[:]),
                      reads=[b_hm], writes=[b_hmid], sem_buf=b_hm, eng="pool")

            pipeline(NFF, load, comp, 2)
        S.barrier()

    def ffn_down(w_d, xsrc, b_xsrc, xdst, b_xdst, next_norm, wd=None, b_wd=None):
        with ExitStack() as st:
            pre = wd is not None
            if not pre:
                wd = st.enter_context(nc.sbuf_tensor(un("wd"), [128, NFF, D_], BF16)); b_wd = Buf()
            stg = Ring(nc, st, un("dn_stg"), [128, 1, D_], F32, 3)
            for j in range(NFF if not pre else 0):
                sg, b_sg = stg.next()
                S.dma(lambda e, sg=sg, j=j: e.dma_start(out=sg[:, 0, :], in_=w_d[j * 128:(j + 1) * 128, :]),
                      writes=[b_sg], sem_buf=b_sg)
                S.op("pool", lambda e, sg=sg, j=j: e.tensor_copy(out=wd[:, j, :], in_=sg[:, 0, :]),
                     reads=[b_sg], writes=[b_wd])
            hmr = Ring(nc, st, un("dn_hm"), [128, NFF, 512], BF16, 2)
            xr = Ring(nc, st, un("dn_x"), [128, D_], F32, 3)
            ps = Ring(nc, st, un("dn_ps"), [128, 512], F32, 4, psum=True)
            rings = None
            if next_norm:
                rings = dict(junk=Ring(nc, st, un("nj"), [128, D_], BF16, 2),
                             st=Ring(nc, st, un("nst"), [128, 4], F32, 3),
                             hb=Ring(nc, st, un("nhb"), [128, D_], BF16, 2),
                             ptr=Ring(nc, st, un("nptr"), [128, 512], F32, 2, psum=True))

            def load(t):
                hmt, b_hm = hmr.next()
                S.dma(lambda e: e.dma_start(out=hmt[:], in_=hmid[:, t * 512:(t + 1) * 512].rearrange(
                    "(j p) t -> p j t", p=128)), reads=[b_hmid], writes=[b_hm], sem_buf=b_hm)
                return hmt, b_hm

            def comp(t, h):
                hmt, b_hm = h
                for sb in range(4):
                    tb = t * 4 + sb
                    xt, b_xt = xr.next()
                    S.dma(lambda e, xt=xt, tb=tb: e.dma_start(out=xt[:], in_=xsrc[tb * 128:(tb + 1) * 128, :]),
                          reads=[b_xsrc], writes=[b_xt], sem_buf=b_xt)
                    for half in range(2):
                        p, b_p = ps.next()
                        for j in range(NFF):
                            S.op("pe", lambda e, j=j, p=p, sb=sb, half=half, hmt=hmt: e.matmul(
                                out=p[:], lhsT=hmt[:, j, sb * 128:(sb + 1) * 128],
                                rhs=wd[:, j, half * 512:(half + 1) * 512], start=(j == 0), stop=(j == NFF - 1)),
                                reads=[b_hm, b_wd], writes=[b_p])
                        S.op("dve", lambda e, p=p, xt=xt, half=half: e.scalar_tensor_tensor(
                            out=xt[:, half * 512:(half + 1) * 512], in0=p[:], scalar=0.5,
                            in1=xt[:, half * 512:(half + 1) * 512], op0=ALU.mult, op1=ALU.add),
                            reads=[b_p, b_xt], writes=[b_xt])
                    S.dma(lambda e, xt=xt, tb=tb: e.dma_start(out=xdst[tb * 128:(tb + 1) * 128, :], in_=xt[:]),
                          reads=[b_xt], writes=[b_xdst], sem_buf=b_xt, eng="pool")
                    if next_norm:
                        if pendn:
                            norm_block(*pendn.pop())
                        pendn.append((xt[:], b_xt, tb, rings, next_norm[0], next_norm[1]))

            pendn = []
            pipeline(8, load, comp, 1)
            if pendn:
                norm_block(*pendn.pop())
        S.barrier()

    def norm_phase(xsrc, b_xsrc, hT, b_hT):
        with ExitStack() as st:
            xr = Ring(nc, st, un("np_x"), [128, D_], F32, 3)
            rings = dict(junk=Ring(nc, st, un("nj"), [128, D_], BF16, 2),
                         st=Ring(nc, st, un("nst"), [128, 4], F32, 3),
                         hb=Ring(nc, st, un("nhb"), [128, D_], BF16, 2),
                         ptr=Ring(nc, st, un("nptr"), [128, 512], F32, 2, psum=True))

            def load(tb):
                xt, b_xt = xr.next()
                S.dma(lambda e: e.dma_start(out=xt[:], in_=xsrc[tb * 128:(tb + 1) * 128, :]),
                      reads=[b_xsrc], writes=[b_xt], sem_buf=b_xt)
                return xt, b_xt

            def comp(tb, h):
                norm_block(h[0][:], h[1], tb, rings, hT, b_hT)

            pipeline(32, load, comp, 2)
        S.barrier()

    b_x = Buf("x")
    hst = ExitStack()
    hT = hst.enter_context(nc.sbuf_tensor("hT", [128, 8, S_], BF16)); b_hT = Buf("hT")
    norm_phase(x, b_x, hT, b_hT)
    with ExitStack() as fst:
        wd1 = fst.enter_context(nc.sbuf_tensor("wd1", [128, NFF, D_], BF16)); b_wd1 = Buf()
        ffn_gateup(w_g1, w_u1, 0, hT, b_hT, w_d1, wd1, b_wd1)
        ffn_down(w_d1, x, b_x, x1s, b_x1s, (hT, b_hT), wd1, b_wd1)
    if stop_after <= 1:
        S.emit(nc)
        return nc

    gmix = (gfm_t[:, 8:16], b_gfm)

    def rope(src3, H, tb, dst3, tmp_ring):
        ta, b_ta = tmp_ring.next()
        tb_, b_tb = tmp_ring.next()
        cb = cos_t[:, tb, :].unsqueeze(1).to_broadcast([128, H, 16])
        sb = sin_t[:, tb, :].unsqueeze(1).to_broadcast([128, H, 16])
        t1 = src3[:, :, 0:16]
        t2 = src3[:, :, 16:32]
        a_ = ta[:, 0:H, :]
        b_ = tb_[:, 0:H, :]
        return [
            (lambda e: e.tensor_tensor(out=a_, in0=t1, in1=cb, op=ALU.mult), [b_cos], [b_ta]),
            (lambda e: e.tensor_tensor(out=b_, in0=t2, in1=sb, op=ALU.mult), [b_sin], [b_tb]),
            (lambda e: e.tensor_tensor(out=dst3[:, :, 0:16], in0=a_, in1=b_, op=ALU.subtract), [b_ta, b_tb], []),
            (lambda e: e.tensor_tensor(out=a_, in0=t2, in1=cb, op=ALU.mult), [b_cos], [b_ta]),
            (lambda e: e.tensor_tensor(out=b_, in0=t1, in1=sb, op=ALU.mult), [b_sin], [b_tb]),
            (lambda e: e.tensor_tensor(out=dst3[:, :, 16:32], in0=a_, in1=b_, op=ALU.add), [b_ta, b_tb], []),
        ]

    with ExitStack() as st:
        wAr = Ring(nc, st, un("a_w"), [128, 8, 672], BF16, 1)
        wqr = Ring(nc, st, un("q_w"), [128, 3, 1536], BF16, 1)
        wkr = Ring(nc, st, un("kv_w"), [128, 2, 2048], BF16, 1)
        with ExitStack() as st2:
            stgA = Ring(nc, st2, un("a_stg"), [128, 8, 672], F32, 1)
            stgq = Ring(nc, st2, un("q_stg"), [128, 3, 1536], F32, 1)
            stgk = Ring(nc, st2, un("kv_stg"), [128, 2, 2048], F32, 1)
            wA, b_wA = wload(stgA, wAr, w_in[:, 0:672], 8, 672, gmix)
            wq, b_wq = wload(stgq, wqr, w_qb[:, :], 3, 1536, (gqa_t[:, :], b_gqa))
            wkv, b_wkv = wload(stgk, wkr, w_kvb[:, :], 2, 2048, (gkva_t[:, :], b_gkva))
            S.barrier()
        psA = Ring(nc, st, un("a_ps"), [128, 512], F32, 2, psum=True)
        psT = Ring(nc, st, un("a_pt"), [128, 512], F32, 2, psum=True)
        psQ = Ring(nc, st, un("a_pq"), [128, 512], F32, 3, psum=True)
        junk = Ring(nc, st, un("a_junk"), [128, 1536], F32, 1)
        stt = Ring(nc, st, un("a_st"), [128, 8], F32, 2)
        sst = Ring(nc, st, un("a_ss"), [128, 100], F32, 2)
        cnr = Ring(nc, st, un("a_cn"), [128, 640], BF16, 2)
        cTr = Ring(nc, st, un("a_cT"), [128, 5, 128], BF16, 2)
        kper = Ring(nc, st, un("a_kpe"), [128, 1, 32], F32, 2)
        kpgr = Ring(nc, st, un("a_kpg"), [128, 1, 32], F32, 2)
        krr = Ring(nc, st, un("a_kr"), [128, 1, 32], F32, 2)
        qsbr = Ring(nc, st, un("a_qsb"), [128, 1536], F32, 1)
        kvsbr = Ring(nc, st, un("a_kvsb"), [128, 2048], F32, 1)
        tmpkr = Ring(nc, st, un("a_tmpk"), [128, 16, 64], F32, 1)
        qbr = Ring(nc, st, un("a_qb"), [128, 16, 96], BF16, 2)
        kbr = Ring(nc, st, un("a_kb"), [128, 16, 96], BF16, 2)
        ropet = Ring(nc, st, un("a_rt"), [128, 16, 16], F32, 4)
        vbr = Ring(nc, st, un("a_vb"), [128, 16, 4, 65], BF16, 1)
        qTr = Ring(nc, st, un("a_qT"), [128, 16, 256], BF16, 2)
        kTr = Ring(nc, st, un("a_kT"), [128, 16, 256], BF16, 2)
        for (vt, b_v) in vbr.items:
            S.op("pool", lambda e, vt=vt: e.memset(vt[:], 1.0), writes=[b_v])
        gq = hn_t[:, 0:96]
        gk = hn_t[:, 96:192]
        sh = {}

        def block(tb):
            tsl = slice(tb * 128, (tb + 1) * 128)
            pA1, b_pA1 = psA.next()
            pA2, b_pA2 = psA.next()
            for kc in range(8):
                S.op("pe", lambda e, kc=kc, pA1=pA1, tsl=tsl: e.matmul(out=pA1[:, 0:384], lhsT=hT[:, kc, tsl], rhs=wA[:, kc, 0:384],
                                                     start=(kc == 0), stop=(kc == 7)), reads=[b_hT, b_wA], writes=[b_pA1])
            for kc in range(8):
                S.op("pe", lambda e, kc=kc, pA2=pA2, tsl=tsl: e.matmul(out=pA2[:, 0:288], lhsT=hT[:, kc, tsl], rhs=wA[:, kc, 384:672],
                                                     start=(kc == 0), stop=(kc == 7)), reads=[b_hT, b_wA], writes=[b_pA2])
            jk, b_jk = junk.next()
            s8, b_s8 = stt.next()
            S.op("act", lambda e, jk=jk, pA1=pA1, s8=s8: e.activation(out=jk[:, 0:384], in_=pA1[:, 0:384], func=AF.Square,
                                               scale=1.0 / math.sqrt(384.0), accum_out=s8[:, 0:1]),
                 reads=[b_pA1], writes=[b_jk, b_s8])
            S.op("act", lambda e, jk=jk, pA2=pA2, s8=s8: e.activation(out=jk[:, 0:256], in_=pA2[:, 0:256], func=AF.Square,
                                               scale=1.0 / 16.0, accum_out=s8[:, 1:2]),
                 reads=[b_pA2], writes=[b_jk, b_s8])
            S.op("act", lambda e, s8=s8: e.activation(out=s8[:, 2:4], in_=s8[:, 0:2], func=AF.Sqrt, bias=EPS_AP[:, 0:1]),
                 reads=[b_s8, b_eps], writes=[b_s8])
            S.op("dve", lambda e, s8=s8: e.reciprocal(out=s8[:, 4:6], in_=s8[:, 2:4]), reads=[b_s8], writes=[b_s8])
            cn, b_cn = cnr.next()
            S.op("dve", lambda e, cn=cn, pA1=pA1, s8=s8: e.tensor_scalar(out=cn[:, 0:384], in0=pA1[:, 0:384], scalar1=s8[:, 4:5],
                                                  scalar2=None, op0=ALU.mult), reads=[b_pA1, b_s8], writes=[b_cn])
            S.op("dve", lambda e, cn=cn, pA2=pA2, s8=s8: e.tensor_scalar(out=cn[:, 384:640], in0=pA2[:, 0:256], scalar1=s8[:, 5:6],
                                                  scalar2=None, op0=ALU.mult), reads=[b_pA2, b_s8], writes=[b_cn])
            kpe, b_kpe = kper.next()
            S.op("act", lambda e, kpe=kpe, pA2=pA2: e.copy(out=kpe[:, 0, :], in_=pA2[:, 256:288]), reads=[b_pA2], writes=[b_kpe])
            yield
            ptr, b_ptr = psT.next()
            pv = ptr[:, :].bitcast(BF16)
            for kc in range(5):
                S.op("pe", lambda e, kc=kc, pv=pv, cn=cn: e.transpose(out=pv[:, kc * 128:(kc + 1) * 128],
                                                         in_=cn[:, kc * 128:(kc + 1) * 128], identity=identb[:]),
                     reads=[b_cn, b_identb], writes=[b_ptr])
            cT, b_cT = cTr.next()
            S.op("dve", lambda e, cT=cT, pv=pv: e.tensor_copy(out=cT[:], in_=pv[:, 0:640].rearrange("p (k t) -> p k t", k=5)),
                 reads=[b_ptr], writes=[b_cT])
            yield
            qsb, b_qsb = qsbr.next()
            kvsb, b_kvsb = kvsbr.next()
            for nb in range(3):
                pq, b_pq = psQ.next()
                for kc in range(3):
                    S.op("pe", lambda e, kc=kc, nb=nb, pq=pq, cT=cT: e.matmul(out=pq[:], lhsT=cT[:, kc, :],
                                                                 rhs=wq[:, kc, nb * 512:(nb + 1) * 512],
                                                                 start=(kc == 0), stop=(kc == 2)),
                         reads=[b_cT, b_wq], writes=[b_pq])
                S.op("act", lambda e, nb=nb, pq=pq, qsb=qsb: e.copy(out=qsb[:, nb * 512:(nb + 1) * 512], in_=pq[:]),
                     reads=[b_pq], writes=[b_qsb])
            for nb in range(4):
                pq, b_pq = psQ.next()
                for kc in range(2):
                    S.op("pe", lambda e, kc=kc, nb=nb, pq=pq, cT=cT: e.matmul(out=pq[:], lhsT=cT[:, 3 + kc, :],
                                                                 rhs=wkv[:, kc, nb * 512:(nb + 1) * 512],
                                                                 start=(kc == 0), stop=(kc == 1)),
                         reads=[b_cT, b_wkv], writes=[b_pq])
                eng = "act" if nb % 2 == 0 else "dve"
                if eng == "act":
                    S.op("act", lambda e, nb=nb, pq=pq, kvsb=kvsb: e.copy(out=kvsb[:, nb * 512:(nb + 1) * 512], in_=pq[:]),
                         reads=[b_pq], writes=[b_kvsb])
                else:
                    S.op("dve", lambda e, nb=nb, pq=pq, kvsb=kvsb: e.tensor_copy(out=kvsb[:, nb * 512:(nb + 1) * 512], in_=pq[:]),
                         reads=[b_pq], writes=[b_kvsb])
            q3 = qsb[:, :].rearrange("p (h d) -> p h d", h=16)
            kv3 = kvsb[:, :].rearrange("p (h d) -> p h d", h=16)
            ss, b_ss = sst.next()
            S.op("act", lambda e, jk=jk, qsb=qsb: e.activation(out=jk[:, :], in_=qsb[:, :], func=AF.Square),
                 reads=[b_qsb], writes=[b_jk])
            S.op("dve", lambda e, jk=jk, ss=ss: e.tensor_reduce(out=ss[:, 0:16], in_=jk[:, :].rearrange("p (h d) -> p h d", h=16),
                                                  axis=AX.X, op=ALU.add), reads=[b_jk], writes=[b_ss])
            tk, b_tk = tmpkr.next()
            S.op("act", lambda e, tk=tk, kv3=kv3: e.activation(out=tk[:], in_=kv3[:, :, 0:64], func=AF.Square),
                 reads=[b_kvsb], writes=[b_tk])
            S.op("dve", lambda e, tk=tk, ss=ss: e.tensor_reduce(out=ss[:, 16:32], in_=tk[:], axis=AX.X, op=ALU.add),
                 reads=[b_tk], writes=[b_ss])
            kpg, b_kpg = kpgr.next()
            S.op("act", lambda e, kpg=kpg, kpe=kpe, ss=ss: e.activation(out=kpg[:, 0, :], in_=kpe[:, 0, :], func=AF.Square,
                                               accum_out=ss[:, 96:97]), reads=[b_kpe], writes=[b_kpg, b_ss])
            S.op("dve", lambda e, ss=ss: e.tensor_scalar(out=ss[:, 16:32], in0=ss[:, 16:32], scalar1=ss[:, 96:97], scalar2=None,
                                                  op0=ALU.add), reads=[b_ss], writes=[b_ss])
            S.op("act", lambda e, ss=ss: e.activation(out=ss[:, 32:64], in_=ss[:, 0:32], func=AF.Sqrt, bias=EPS_AP[:, 0:1],
                                               scale=1.0 / 96.0), reads=[b_ss, b_eps], writes=[b_ss])
            S.op("dve", lambda e, ss=ss: e.reciprocal(out=ss[:, 64:96], in_=ss[:, 32:64]), reads=[b_ss], writes=[b_ss])
            rsq = ss[:, 64:80]
            rsk = ss[:, 80:96]
            S.op("dve", lambda e, q3=q3, rsq=rsq: e.tensor_tensor(out=q3, in0=q3, in1=rsq.unsqueeze(2).to_broadcast([128, 16, 96]),
                                                  op=ALU.mult), reads=[b_qsb, b_ss], writes=[b_qsb])
            S.op("pool", lambda e, q3=q3: e.tensor_tensor(out=q3, in0=q3, in1=gq.unsqueeze(1).to_broadcast([128, 16, 96]),
                                                   op=ALU.mult), reads=[b_qsb, b_hn], writes=[b_qsb])
            qb, b_qb = qbr.next()
            S.op("act", lambda e, qb=qb, q3=q3: e.copy(out=qb[:, :, 0:64], in_=q3[:, :, 0:64]), reads=[b_qsb], writes=[b_qb])
            for fn, rd, wr in rope(q3[:, :, 64:96], 16, tb, qb[:, :, 64:96], ropet):
                S.op("dve", fn, reads=[b_qsb] + rd, writes=wr + ([b_qb] if not wr else []))
            S.op("dve", lambda e, tk=tk, kv3=kv3, rsk=rsk: e.tensor_tensor(out=tk[:], in0=kv3[:, :, 0:64],
                                                  in1=rsk.unsqueeze(2).to_broadcast([128, 16, 64]), op=ALU.mult),
                 reads=[b_kvsb, b_ss], writes=[b_tk])
            kb, b_kb = kbr.next()
            S.op("pool", lambda e, tk=tk, kb=kb: e.tensor_tensor(out=kb[:, :, 0:64], in0=tk[:],
                                                   in1=gk[:, 0:64].unsqueeze(1).to_broadcast([128, 16, 64]), op=ALU.mult),
                 reads=[b_tk, b_hn], writes=[b_kb])
            S.op("dve", lambda e, kpg=kpg, kpe=kpe: e.tensor_tensor(out=kpg[:, 0, :], in0=kpe[:, 0, :], in1=gk[:, 64:96], op=ALU.mult),
                 reads=[b_kpe, b_hn], writes=[b_kpg])
            kr, b_kr = krr.next()
            for fn, rd, wr in rope(kpg[:, :, :], 1, tb, kr[:, :, :], ropet):
                S.op("dve", fn, reads=[b_kpg] + rd, writes=wr + ([b_kr] if not wr else []))
            S.op("dve", lambda e, kb=kb, kr=kr, rsk=rsk: e.tensor_tensor(out=kb[:, :, 64:96],
                                                  in0=kr[:, 0, :].unsqueeze(1).to_broadcast([128, 16, 32]),
                                                  in1=rsk.unsqueeze(2).to_broadcast([128, 16, 32]), op=ALU.mult),
                 reads=[b_kr, b_ss], writes=[b_kb])
            if tb % 4 == 0:
                sh['vb'] = vbr.next()
            vb, b_vb = sh['vb']
            S.op("pool", lambda e, vb=vb, kv3=kv3, tb=tb: e.tensor_copy(out=vb[:, :, tb % 4, 0:64], in_=kv3[:, :, 64:128]),
                 reads=[b_kvsb], writes=[b_vb])
            yield
            if tb % 2 == 0:
                sh['qT'] = qTr.next()
                sh['kT'] = kTr.next()
            qTs, b_qTs = sh['qT']
            kTs, b_kTs = sh['kT']
            for (src, b_src, dstT, b_dstT) in ((qb, b_qb, qTs, b_qTs), (kb, b_kb, kTs, b_kTs)):
                for half in range(2):
                    ptr, b_ptr = psT.next()
                    pv = ptr[:, :].bitcast(BF16)
                    for hh in range(8):
                        S.op("pe", lambda e, hh=hh, pv=pv, src=src, half=half: e.transpose(
                            out=pv[0:96, hh * 128:(hh + 1) * 128], in_=src[:, half * 8 + hh, :], identity=identb[:]),
                            reads=[b_src, b_identb], writes=[b_ptr])
                    off = (tb % 2) * 128
                    eng = "act" if half == 0 else "dve"
                    if eng == "act":
                        S.op("act", lambda e, pv=pv, dstT=dstT, half=half, off=off: e.copy(
                            out=dstT[0:96, half * 8:(half + 1) * 8, off:off + 128],
                            in_=pv[0:96, :].rearrange("p (h t) -> p h t", h=8)), reads=[b_ptr], writes=[b_dstT])
                    else:
                        S.op("dve", lambda e, pv=pv, dstT=dstT, half=half, off=off: e.tensor_copy(
                            out=dstT[0:96, half * 8:(half + 1) * 8, off:off + 128],
                            in_=pv[0:96, :].rearrange("p (h t) -> p h t", h=8)), reads=[b_ptr], writes=[b_dstT])
            if tb % 2 == 1:
                t0 = (tb - 1) * 128
                S.dma(lambda e, qTs=qTs, t0=t0: e.dma_start(out=QT[:, :, t0:t0 + 256].rearrange("h p t -> p h t"),
                                                             in_=qTs[0:96, :, :]),
                      reads=[b_qTs], writes=[b_QT], sem_buf=b_qTs, eng="pool")
                S.dma(lambda e, kTs=kTs, t0=t0: e.dma_start(out=KT[:, :, t0:t0 + 256].rearrange("h p t -> p h t"),
                                                             in_=kTs[0:96, :, :]),
                      reads=[b_kTs], writes=[b_KT], sem_buf=b_kTs, eng="pool")
            if tb % 4 == 3:
                c0 = tb - 3
                S.dma(lambda e, vb=vb, c0=c0: e.dma_start(out=Vs[:, :, c0:c0 + 4, :].rearrange("h p t c -> p h (t c)"),
                                                           in_=vb[:, :, :, :].rearrange("p h t c -> p h (t c)")),
                      reads=[b_vb], writes=[b_Vs], sem_buf=b_vb, eng="pool")
        gens = {}
        for i in range(32 + 2):
            if i < 32:
                gens[i] = block(i)
                next(gens[i])
            if 0 <= i - 1 < 32:
                next(gens[i - 1])
            if 0 <= i - 2 < 32:
                next(gens[i - 2], None)
            if 0 <= i - 1 < 32:
                next(gens[i - 1])
        S.barrier()
    if stop_after <= 2:
        S.emit(nc)
        return nc

    with ExitStack() as st:
        wzr = Ring(nc, st, un("z_w"), [128, 8, 512], BF16, 4)
        wdr = Ring(nc, st, un("dt_w"), [128, 8, 64], BF16, 1)
        with ExitStack() as st2:
            stgz = Ring(nc, st2, un("z_stg"), [128, 8, 512], F32, 2)
            stgd = Ring(nc, st2, un("dt_stg"), [128, 8, 64], F32, 1)
            wz = [wload(stgz, wzr, w_in[:, C_Z + cb * 512:C_Z + (cb + 1) * 512], 8, 512, gmix) for cb in range(4)]
            wdt, b_wdt = wload(stgd, wdr, w_in[:, C_DTF:C_DTF + 64], 8, 64, gmix)
            S.barrier()
        psz = Ring(nc, st, un("z_ps"), [128, 512], F32, 4, psum=True)
        psd = Ring(nc, st, un("dt_ps"), [128, 512], F32, 2, psum=True)
        zst = Ring(nc, st, un("z_st"), [128, 2048], BF16, 3)
        for tb in range(32):
            tsl = slice(tb * 128, (tb + 1) * 128)
            zt, b_zt = zst.next()
            for cb in range(4):
                pz, b_pz = psz.next()
                w_, b_w = wz[cb]
                for kc in range(8):
                    S.op("pe", lambda e, kc=kc, pz=pz, w_=w_, tsl=tsl: e.matmul(out=pz[:], lhsT=hT[:, kc, tsl], rhs=w_[:, kc, :],
                                                                  start=(kc == 0), stop=(kc == 7)),
                         reads=[b_hT, b_w], writes=[b_pz])
                S.op("act", lambda e, pz=pz, zt=zt, cb=cb: e.activation(out=zt[:, cb * 512:(cb + 1) * 512], in_=pz[:], func=AF.Silu),
                     reads=[b_pz], writes=[b_zt])
            pd, b_pd = psd.next()
            for kc in range(8):
                S.op("pe", lambda e, kc=kc, pd=pd, tsl=tsl: e.matmul(out=pd[:, 0:64], lhsT=hT[:, kc, tsl], rhs=wdt[:, kc, :],
                                                       start=(kc == 0), stop=(kc == 7)), reads=[b_hT, b_wdt], writes=[b_pd])
            S.op("dve", lambda e, pd=pd, tb=tb: e.tensor_copy(out=dtraw[:, tb, :], in_=pd[:, 0:64]), reads=[b_pd], writes=[b_dtraw])
            S.dma(lambda e, zt=zt, tsl=tsl: e.dma_start(out=zs[tsl, :], in_=zt[:]), reads=[b_zt], writes=[b_zs],
                  sem_buf=b_zt, eng="pool")
        S.barrier()

    with ExitStack() as st:
        stg = Ring(nc, st, un("x_stg"), [128, 8, 128], F32, 3)
        wr = Ring(nc, st, un("x_w"), [128, 8, 128], BF16, 3)
        psx = Ring(nc, st, un("x_ps"), [128, 512], F32, 3, psum=True)
        psc = Ring(nc, st, un("x_pc"), [128, 512], F32, 3, psum=True)
        xpre = Ring(nc, st, un("x_pre"), [128, S_ + 4], BF16, 2)
        dgr = Ring(nc, st, un("x_dg"), [128, 5, 128], BF16, 2)
        xcst = Ring(nc, st, un("x_cst"), [128, S_], BF16, 2)
        for (t_, b_) in xpre.items:
            S.op("pool", lambda e, t_=t_: e.memset(t_[:], 0.0), writes=[b_])

        def loadx(j):
            return wload(stg, wr, w_in[:, C_XBC + j * 128:C_XBC + (j + 1) * 128], 8, 128, gmix)

        def compx(j, h):
            w_, b_w = h
            xp, b_xp = xpre.next()
            dg, b_dg = dgr.next()
            for k in range(5):
                S.op("dve", lambda e, k=k, dg=dg, j=j: e.tensor_scalar(out=dg[:, k, :], in0=identf, scalar1=convw_t[:, j, k:k + 1],
                                                           scalar2=None, op0=ALU.mult),
                     reads=[b_mats, b_convw], writes=[b_dg])
            for tb in range(8):
                px, b_px = psx.next()
                ts = slice(tb * 512, (tb + 1) * 512)
                for kc in range(8):
                    S.op("pe", lambda e, kc=kc, px=px, w_=w_, ts=ts: e.matmul(out=px[:], lhsT=w_[:, kc, :], rhs=hT[:, kc, ts],
                                                                start=(kc == 0), stop=(kc == 7)),
                         reads=[b_w, b_hT], writes=[b_px])
                if tb % 2 == 0:
                    S.op("act", lambda e, px=px, xp=xp, tb=tb: e.copy(out=xp[:, 2 + tb * 512:2 + (tb + 1) * 512], in_=px[:]),
                         reads=[b_px], writes=[b_xp])
                else:
                    S.op("dve", lambda e, px=px, xp=xp, tb=tb: e.tensor_copy(out=xp[:, 2 + tb * 512:2 + (tb + 1) * 512], in_=px[:]),
                         reads=[b_px], writes=[b_xp])
            xc_, b_xc = xcst.next()
            for tb in range(8):
                pc, b_pc = psc.next()
                for k in range(5):
                    S.op("pe", lambda e, k=k, pc=pc, dg=dg, xp=xp, tb=tb: e.matmul(
                        out=pc[:], lhsT=dg[:, k, :], rhs=xp[:, tb * 512 + k:tb * 512 + k + 512],
                        start=(k == 0), stop=(k == 4)), reads=[b_dg, b_xp], writes=[b_pc])
                S.op("act", lambda e, pc=pc, xc_=xc_, tb=tb, j=j: e.activation(out=xc_[:, tb * 512:(tb + 1) * 512], in_=pc[:],
                                                               func=AF.Silu, bias=convb_t[:, j:j + 1]),
                     reads=[b_pc, b_convb], writes=[b_xc])
            for q4 in range(4):
                S.dma(lambda e, xc_=xc_, j=j, q4=q4: e.dma_start(
                    out=xcs[q4 * 8:(q4 + 1) * 8, :, j, :].rearrange("c p t -> p c t"),
                    in_=xc_[:, q4 * 1024:(q4 + 1) * 1024].rearrange("p (c t) -> p c t", c=8)),
                    reads=[b_xc], writes=[b_xcs], sem_buf=b_xc, eng="pool")

        pipeline(24, loadx, compx, 2)

        def loadg(j):
            return wload(stg, wr, w_in[:, C_GA + j * 128:C_GA + (j + 1) * 128], 8, 128, gmix)

        def compg(j, h):
            w_, b_w = h
            gt_, b_gt = xcst.next()
            for tb in range(8):
                px, b_px = psx.next()
                ts = slice(tb * 512, (tb + 1) * 512)
                for kc in range(8):
                    S.op("pe", lambda e, kc=kc, px=px, w_=w_, ts=ts: e.matmul(out=px[:], lhsT=w_[:, kc, :], rhs=hT[:, kc, ts],
                                                                start=(kc == 0), stop=(kc == 7)),
                         reads=[b_w, b_hT], writes=[b_px])
                S.op("act", lambda e, px=px, gt_=gt_, ts=ts: e.activation(out=gt_[:, ts], in_=px[:], func=AF.Sigmoid),
                     reads=[b_px], writes=[b_gt])
            S.dma(lambda e, gt_=gt_, j=j: e.dma_start(out=gts[j * 128:(j + 1) * 128, :], in_=gt_[:]),
                  reads=[b_gt], writes=[b_gts], sem_buf=b_gt, eng="pool")

        pipeline(16, loadg, compg, 2)
        S.barrier()
    if stop_after <= 3:
        S.emit(nc)
        return nc
    hst.close()

    def ssd_pass(fwd):
        with ExitStack() as st:
            AT = lambda n, shp, dt=F32: st.enter_context(nc.sbuf_tensor(un(n), shp, dt))
            dt_all = AT("dt_all", [128, 32, 32]); b_dt = Buf()
            da_all = AT("da_all", [128, 32, 32]); b_da = Buf()
            P_all = AT("P_all", [128, 32, 32]); b_P = Buf()
            bias_all = AT("bias_all", [128, 32, 32]); b_bias = Buf()
            wgt = AT("wgt", [128, 32, 32]); b_wgt = Buf()
            scl = AT("scl", [128, 32, 32]); b_scl = Buf()
            cdc = AT("cdc", [128, 32, 32]); b_cdc = Buf()
            tot = AT("tot", [128, 32, 32]); b_tot = Buf()
            nega = AT("nega", [128, 32]); b_nega = Buf()
            tmpa = AT("tmpa", [128, 32, 32]); b_tmpa = Buf()
            Sf = AT("Sf", [128, 2048]); b_Sf = [Buf() for _ in range(4)]
            Sbf = AT("Sbf", [128, 2048], BF16); b_Sbf = [Buf() for _ in range(4)]
            off = 0 if fwd else 32
            alog = ssp_t[:, off:off + 32]
            dtb = ssp_t[:, 64 + off:96 + off]
            dsk = ssp_t[:, 128:160]
            Uc = Umat if fwd else Ustr
            midx = 0 if fwd else 1
            ptA = Ring(nc, st, un("s_ptA"), [128, 512], F32, 1, psum=True)
            segb = Ring(nc, st, un("s_seg"), [128, 512], F32, 3, psum=True)
            pyr = Ring(nc, st, un("s_py"), [128, 512], F32, 2, psum=True)
            por = Ring(nc, st, un("s_po"), [128, 512], F32, 1, psum=True)
            pstr = Ring(nc, st, un("s_pst"), [128, 512], F32, 1, psum=True)
            segs = []
            for (t_, _b) in segb.items:
                for q in range(4):
                    segs.append((t_[:, q * 128:(q + 1) * 128], Buf()))
            segi = [0]
            flat = lambda t_: t_[:, :, :].rearrange("p a b -> p (a b)")
            S.op("pool", lambda e: e.memset(Sf[:], 0.0), writes=b_Sf)
            S.op("pool", lambda e: e.memset(Sbf[:], 0.0), writes=b_Sbf)
            S.op("act", lambda e: e.activation(out=nega[:], in_=alog, func=AF.Exp), reads=[b_ssp], writes=[b_nega])
            S.op("dve", lambda e: e.tensor_scalar(out=nega[:], in0=nega[:], scalar1=-1.0, scalar2=None, op0=ALU.mult),
                 reads=[b_nega], writes=[b_nega])
            S.op("dve", lambda e: e.tensor_tensor(out=tmpa[:], in0=dtraw[:, :, off:off + 32],
                                                  in1=dtb.unsqueeze(1).to_broadcast([128, 32, 32]), op=ALU.add),
                 reads=[b_dtraw, b_ssp], writes=[b_tmpa])
            S.op("act", lambda e: e.activation(out=tmpa[:], in_=tmpa[:], func=AF.Exp), reads=[b_tmpa], writes=[b_tmpa])
            S.op("act", lambda e: e.activation(out=dt_all[:], in_=tmpa[:], func=AF.Ln, bias=1.0), reads=[b_tmpa], writes=[b_dt])
            S.op("dve", lambda e: e.tensor_tensor(out=da_all[:], in0=dt_all[:], in1=nega[:].unsqueeze(1).to_broadcast([128, 32, 32]),
                                                  op=ALU.mult), reads=[b_dt, b_nega], writes=[b_da])
            for half in range(2):
                pp, b_pp = ptA.next()
                S.op("pe", lambda e, pp=pp, half=half: e.matmul(out=pp[:], lhsT=Uc, rhs=flat(da_all)[:, half * 512:(half + 1) * 512],
                                                                start=True, stop=True), reads=[b_mats, b_da], writes=[b_pp])
                S.op("dve", lambda e, pp=pp, half=half: e.tensor_copy(out=flat(P_all)[:, half * 512:(half + 1) * 512], in_=pp[:]),
                     reads=[b_pp], writes=[b_P])
            for half in range(2):
                pp, b_pp = ptA.next()
                S.op("pe", lambda e, pp=pp, half=half: e.matmul(out=pp[:], lhsT=onesf, rhs=flat(da_all)[:, half * 512:(half + 1) * 512],
                                                                start=True, stop=True), reads=[b_mats, b_da], writes=[b_pp])
                S.op("dve", lambda e, pp=pp, half=half: e.tensor_copy(out=flat(tot)[:, half * 512:(half + 1) * 512], in_=pp[:]),
                     reads=[b_pp], writes=[b_tot])
            S.op("dve", lambda e: e.tensor_tensor(out=tmpa[:], in0=tot[:], in1=P_all[:], op=ALU.subtract),
                 reads=[b_tot, b_P, b_dt], writes=[b_tmpa])
            e1, b_e1 = (scl, b_scl) if fwd else (wgt, b_wgt)
            e2, b_e2 = (wgt, b_wgt) if fwd else (scl, b_scl)
            S.op("act", lambda e: e.activation(out=e1[:], in_=P_all[:], func=AF.Exp), reads=[b_P], writes=[b_e1])
            S.op("act", lambda e: e.activation(out=e2[:], in_=tmpa[:], func=AF.Exp), reads=[b_tmpa], writes=[b_e2])
            S.op("act", lambda e: e.activation(out=cdc[:], in_=tot[:], func=AF.Exp), reads=[b_tot], writes=[b_cdc])
            S.op("dve", lambda e: e.tensor_scalar(out=bias_all[:], in0=P_all[:], scalar1=(-1.0 if fwd else 1.0), scalar2=None,
                                                  op0=ALU.mult), reads=[b_P], writes=[b_bias])

            xcr = Ring(nc, st, un("s_xc"), [128, 24, 128], BF16, 3)
            xsr = Ring(nc, st, un("s_xs"), [128, 2048], BF16, 2)
            Btr = Ring(nc, st, un("s_Bt"), [128, 512], BF16, 2)
            cbr = Ring(nc, st, un("s_cb"), [128, 512], F32, 2)
            xdtr = Ring(nc, st, un("s_xdt"), [128, 2048], BF16, 2)
            xwr = Ring(nc, st, un("s_xw"), [128, 2048], BF16, 2)
            decr = Ring(nc, st, un("s_dec"), [128, 128], F32, 12)
            MTr = Ring(nc, st, un("s_MT"), [128, 128], BF16, 12)
            yaccr = Ring(nc, st, un("s_ya"), [128, 2048], F32, 2)
            tmpr = Ring(nc, st, un("s_tmp"), [128, 512], F32, 2)
            if fwd:
                dskr = Ring(nc, st, un("s_dsk"), [128, 2048], F32, 1)
            else:
                zr = Ring(nc, st, un("s_z"), [128, 2048], BF16, 4)
                yfr = Ring(nc, st, un("s_yf"), [128, 2048], F32, 4)
                jkr = Ring(nc, st, un("s_jk"), [128, 512], BF16, 1)
                st4r = Ring(nc, st, un("s_st4"), [128, 12], F32, 2)
                mbr = Ring(nc, st, un("s_mb"), [128, 2048], BF16, 2)
                mstr = Ring(nc, st, un("s_mst"), [128, 16, 128], BF16, 2)
            order = list(range(32)) if fwd else list(range(31, -1, -1))

            def load(ci):
                c = order[ci]
                xc_, b_xc = xcr.next()
                S.dma(lambda e: e.dma_start(out=xc_[:], in_=xcs[c, :, :, :]), reads=[b_xcs], writes=[b_xc], sem_buf=b_xc)
                if fwd:
                    return (xc_, b_xc)
                z_, b_z = zr.next()
                yf_, b_yf = yfr.next()
                S.dma(lambda e: e.dma_start(out=z_[:], in_=zs[c * 128:(c + 1) * 128, :]), reads=[b_zs], writes=[b_z], sem_buf=b_z)
                S.dma(lambda e: e.dma_start(out=yf_[:], in_=yfs[c * 128:(c + 1) * 128, :]), reads=[b_yfs], writes=[b_yf], sem_buf=b_yf)
                return (xc_, b_xc, z_, b_z, yf_, b_yf)

            def prologue(ci, h, out):
                c = order[ci]
                xc_, b_xc = h[0], h[1]
                xs, b_xs = xsr.next()
                for half in range(2):
                    pt, b_pt = ptA.next()
                    pv = pt[:, :].bitcast(BF16)
                    for jj in range(8):
                        S.op("pe", lambda e, pv=pv, jj=jj, half=half: e.transpose(out=pv[:, jj * 128:(jj + 1) * 128],
                                                                                 in_=xc_[:, half * 8 + jj, :], identity=identb[:]),
                             reads=[b_xc, b_identb], writes=[b_pt])
                    if half == 0:
                        S.op("act", lambda e, pv=pv: e.copy(out=xs[:, 0:1024], in_=pv[:, :]), reads=[b_pt], writes=[b_xs])
                    else:
                        S.op("dve", lambda e, pv=pv: e.tensor_copy(out=xs[:, 1024:2048], in_=pv[:, :]), reads=[b_pt], writes=[b_xs])
                    yield
                pt, b_pt = ptA.next()
                pvb = pt[:, :].bitcast(BF16)
                for g in range(4):
                    S.op("pe", lambda e, g=g: e.transpose(out=pvb[:, g * 128:(g + 1) * 128], in_=xc_[:, 16 + g, :], identity=identb[:]),
                         reads=[b_xc, b_identb], writes=[b_pt])
                Bt, b_Bt = Btr.next()
                S.op("dve", lambda e: e.tensor_copy(out=Bt[:], in_=pvb[:, 0:512]), reads=[b_pt], writes=[b_Bt])
                yield
                pcb, b_pcb = ptA.next()
                for g in range(4):
                    S.op("pe", lambda e, g=g: e.matmul(out=pcb[:, g * 128:(g + 1) * 128], lhsT=xc_[:, 16 + g, :], rhs=xc_[:, 20 + g, :],
                                                       start=True, stop=True), reads=[b_xc], writes=[b_pcb])
                cbT, b_cbT = cbr.next()
                S.op("act", lambda e: e.copy(out=cbT[:], in_=pcb[:]), reads=[b_pcb], writes=[b_cbT])
                yield
                xdt, b_xdt = xdtr.next()
                xw, b_xw = xwr.next()
                v3 = lambda t_: t_[:, :].rearrange("p (h d) -> p h d", h=32)
                S.op("dve", lambda e: e.tensor_tensor(out=v3(xdt), in0=v3(xs), in1=dt_all[:, c, :].unsqueeze(2).to_broadcast([128, 32, 64]),
                                                      op=ALU.mult), reads=[b_xs, b_dt], writes=[b_xdt])
                S.op("pool", lambda e: e.tensor_tensor(out=v3(xw), in0=v3(xdt), in1=wgt[:, c, :].unsqueeze(2).to_broadcast([128, 32, 64]),
                                                       op=ALU.mult), reads=[b_xdt, b_wgt], writes=[b_xw])
                out['v'] = (xs, b_xs, Bt, b_Bt, cbT, b_cbT, xdt, b_xdt, xw, b_xw)

            def comp(ci, h, pr, side):
                c = order[ci]
                xc_, b_xc = h[0], h[1]
                xs, b_xs, Bt, b_Bt, cbT, b_cbT, xdt, b_xdt, xw, b_xw = pr
                v3 = lambda t_: t_[:, :].rearrange("p (h d) -> p h d", h=32)
                g8 = lambda t_: t_.rearrange("p (h d) -> p h d", h=8)
                ya, b_ya = yaccr.next()
                LAGH = 4
                mts = {}
                cur = {}

                def stageA(bi):
                    sb_, b_sb = segb.next()
                    for q in range(4):
                        h_ = bi * 4 + q
                        seg = sb_[:, q * 128:(q + 1) * 128]
                        S.op("pe", lambda e, seg=seg, h_=h_: e.matmul(out=seg, lhsT=da_all[:, c, h_:h_ + 1].to_broadcast([128, 128]),
                                                                      rhs=Uc, start=True, stop=False),
                             reads=[b_da, b_mats], writes=[b_sb])
                        S.op("pe", lambda e, seg=seg: e.matmul(out=seg, lhsT=identb[:], rhs=negm[:, midx, :], start=False, stop=True),
                             reads=[b_identb, b_negm], writes=[b_sb])
                    for q in range(4):
                        h_ = bi * 4 + q
                        g = h_ // 8
                        seg = sb_[:, q * 128:(q + 1) * 128]
                        dec, b_dec = decr.next()
                        S.op("act", lambda e, seg=seg, dec=dec, h_=h_: e.activation(out=dec[:], in_=seg, func=AF.Exp,
                                                                                   bias=bias_all[:, c, h_:h_ + 1],
                                                                                   scale=(1.0 if fwd else -1.0)),
                             reads=[b_sb, b_bias], writes=[b_dec])
                        MT, b_MT = MTr.next()
                        S.op("dve" if h_ % 2 == 0 else "pool", lambda e, dec=dec, MT=MT, g=g: e.tensor_tensor(
                            out=MT[:], in0=dec[:], in1=cbT[:, g * 128:(g + 1) * 128], op=ALU.mult),
                            reads=[b_dec, b_cbT], writes=[b_MT])
                        mts[h_] = (MT, b_MT)

                def stageB(h_):
                    g = h_ // 8
                    hh = h_ % 8
                    if hh == 0:
                        cur[0] = pyr.next()
                    py, b_py = cur[0]
                    MT, b_MT = mts.pop(h_)
                    S.op("pe", lambda e, MT=MT, py=py, hh=hh, h_=h_: e.matmul(out=py[:, hh * 64:(hh + 1) * 64], lhsT=MT[:],
                                                                             rhs=xdt[:, h_ * 64:(h_ + 1) * 64], start=True, stop=True),
                         reads=[b_MT, b_xdt], writes=[b_py])
                    if hh != 7:
                        return
                    po, b_po = por.next()
                    S.op("pe", lambda e, po=po, g=g: e.matmul(out=po[:], lhsT=xc_[:, 20 + g, :], rhs=Sbf[:, g * 512:(g + 1) * 512],
                                                              start=True, stop=True), reads=[b_xc, b_Sbf[g]], writes=[b_po])
                    pst, b_pst = pstr.next()
                    S.op("pe", lambda e, pst=pst, g=g: e.matmul(out=pst[:], lhsT=Bt[:, g * 128:(g + 1) * 128], rhs=xw[:, g * 512:(g + 1) * 512],
                                                                start=True, stop=True), reads=[b_Bt, b_xw], writes=[b_pst])
                    tmp, b_tmp = tmpr.next()
                    S.op("dve", lambda e, po=po, tmp=tmp, g=g: e.tensor_tensor(
                        out=g8(tmp[:, :]), in0=g8(po[:, :]), in1=scl[:, c, g * 8:(g + 1) * 8].unsqueeze(2).to_broadcast([128, 8, 64]),
                        op=ALU.mult), reads=[b_po, b_scl], writes=[b_tmp])
                    S.op("dve", lambda e, py=py, tmp=tmp, g=g: e.tensor_tensor(out=ya[:, g * 512:(g + 1) * 512], in0=py[:], in1=tmp[:],
                                                                              op=ALU.add), reads=[b_py, b_tmp], writes=[b_ya])
                    S.op("pool", lambda e, g=g: e.tensor_tensor(
                        out=g8(Sf[:, g * 512:(g + 1) * 512]), in0=g8(Sf[:, g * 512:(g + 1) * 512]),
                        in1=cdc[:, c, g * 8:(g + 1) * 8].unsqueeze(2).to_broadcast([128, 8, 64]), op=ALU.mult),
                        reads=[b_Sf[g], b_cdc], writes=[b_Sf[g]])
                    S.op("dve", lambda e, pst=pst, g=g: e.tensor_tensor(out=Sf[:, g * 512:(g + 1) * 512], in0=pst[:],
                                                                       in1=Sf[:, g * 512:(g + 1) * 512], op=ALU.add),
                         reads=[b_pst, b_Sf[g]], writes=[b_Sf[g]])
                    S.op("act", lambda e, g=g: e.copy(out=Sbf[:, g * 512:(g + 1) * 512], in_=Sf[:, g * 512:(g + 1) * 512]),
                         reads=[b_Sf[g]], writes=[b_Sbf[g]])

                for k in range(8 + 2):
                    if k < 8:
                        stageA(k)
                    if k >= 2:
                        for q in range(4):
                            stageB((k - 2) * 4 + q)
                    for g_ in side:
                        next(g_, None)
                if fwd:
                    dk, b_dk = dskr.next()
                    S.op("pool", lambda e: e.tensor_tensor(out=v3(dk), in0=v3(xs), in1=dsk.unsqueeze(2).to_broadcast([128, 32, 64]),
                                                           op=ALU.mult), reads=[b_xs, b_ssp], writes=[b_dk])
                    S.op("pool", lambda e: e.tensor_tensor(out=ya[:], in0=ya[:], in1=dk[:], op=ALU.add),
                         reads=[b_ya, b_dk], writes=[b_ya])
                    S.dma(lambda e: e.dma_start(out=yfs[c * 128:(c + 1) * 128, :], in_=ya[:]), reads=[b_ya], writes=[b_yfs],
                          sem_buf=b_ya, eng="pool")
                    return
                def epi():
                    z_, b_z, yf_, b_yf = h[2], h[3], h[4], h[5]
                    if dbg:
                        S.dma(lambda e: e.dma_start(out=ybs[c * 128:(c + 1) * 128, :], in_=ya[:]), reads=[b_ya], writes=[b_ybs],
                              sem_buf=b_ya, eng="pool")
                    S.op("pool", lambda e: e.tensor_tensor(out=ya[:], in0=ya[:], in1=yf_[:], op=ALU.add), reads=[b_ya, b_yf], writes=[b_ya])
                    S.op("dve", lambda e: e.tensor_tensor(out=ya[:], in0=ya[:], in1=z_[:], op=ALU.mult), reads=[b_ya, b_z], writes=[b_ya])
                    jk, b_jk = jkr.next()
                    s4, b_s4 = st4r.next()
                    for g in range(4):
                        S.op("act", lambda e, g=g: e.activation(out=jk[:], in_=ya[:, g * 512:(g + 1) * 512], func=AF.Square,
                                                                scale=1.0 / math.sqrt(512.0), accum_out=s4[:, g:g + 1]),
                             reads=[b_ya], writes=[b_jk, b_s4])
                    S.op("act", lambda e: e.activation(out=s4[:, 4:8], in_=s4[:, 0:4], func=AF.Sqrt, bias=EPS_AP[:, 0:1]),
                         reads=[b_s4, b_eps], writes=[b_s4])
                    S.op("dve", lambda e: e.reciprocal(out=s4[:, 8:12], in_=s4[:, 4:8]), reads=[b_s4], writes=[b_s4])
                    yield
                    mb, b_mb = mbr.next()
                    for g in range(4):
                        S.op("dve", lambda e, g=g: e.tensor_scalar(out=mb[:, g * 512:(g + 1) * 512], in0=ya[:, g * 512:(g + 1) * 512],
                                                                   scalar1=s4[:, 8 + g:9 + g], scalar2=None, op0=ALU.mult),
                             reads=[b_ya, b_s4], writes=[b_mb])
                    mst, b_mst = mstr.next()
                    yield
                    yield
                    for half in range(2):
                        yield
                        pt, b_pt = ptA.next()
                        pv = pt[:, :].bitcast(BF16)
                        for jj in range(8):
                            j = half * 8 + jj
                            S.op("pe", lambda e, pv=pv, jj=jj, j=j: e.transpose(out=pv[:, jj * 128:(jj + 1) * 128],
                                                                               in_=mb[:, j * 128:(j + 1) * 128], identity=identb[:]),
                                 reads=[b_mb, b_identb], writes=[b_pt])
                        S.op("act", lambda e, pv=pv, half=half: e.copy(out=mst[:, half * 8:(half + 1) * 8, :],
                                                                       in_=pv[:, :].rearrange("p (j t) -> p j t", j=8)),
                             reads=[b_pt], writes=[b_mst])
                    for half in range(2):
                        S.dma(lambda e, half=half: e.dma_start(
                            out=mTs[half * 1024:(half + 1) * 1024, c * 128:(c + 1) * 128].rearrange("(j p) t -> p j t", p=128),
                            in_=mst[:, half * 8:(half + 1) * 8, :]), reads=[b_mst], writes=[b_mTs], sem_buf=b_mst, eng="pool")

                pend_epi.append(epi())

            pend_epi = []
            hs = {}
            outs = {}
            pg = {}
            for i in range(32 + 2):
                if i < 32:
                    hs[i] = load(i)
                if 1 <= i <= 32:
                    outs[i - 1] = {}
                    pg[i - 1] = prologue(i - 1, hs[i - 1], outs[i - 1])
                    if i - 1 == 0:
                        for _ in pg[0]:
                            pass
                if i >= 2:
                    side = []
                    if (i - 1) in pg and i - 1 <= 31:
                        side.append(pg[i - 1])
                    old_epi = list(pend_epi)
                    del pend_epi[:]
                    side.extend(old_epi)
                    comp(i - 2, hs.pop(i - 2), outs.pop(i - 2)['v'], side)
                    for g_ in side:
                        for _ in g_:
                            pass
            for g_ in pend_epi:
                for _ in g_:
                    pass
        S.barrier()

    ssd_pass(True)
    if stop_after <= 4 and stop_after == 4:
        pass
    ssd_pass(False)
    if stop_after <= 4:
        S.emit(nc)
        return nc

    mw = ExitStack()
    wpa = mw.enter_context(nc.sbuf_tensor("m_wpa", [128, 8, D_], BF16)); b_wpa = Buf()
    wpb = mw.enter_context(nc.sbuf_tensor("m_wpb", [128, 16, D_], BF16)); b_wpb = Buf()
    wo = mw.enter_context(nc.sbuf_tensor("m_wo", [128, 8, D_], BF16)); b_wo = Buf()
    mws = ExitStack()
    ms8 = mws.enter_context(nc.sbuf_tensor("m_s8", [128, 8, D_], F32)); b_ms8 = Buf()

    def preload_merge_weights():
        jobs = [(w_pa[:, :], wpa[:, :, :], b_wpa, None), (w_pb[0:1024, :], wpb[:, 0:8, :], b_wpb, gssm_t[:, 0:8]),
                (w_pb[1024:2048, :], wpb[:, 8:16, :], b_wpb, gssm_t[:, 8:16]), (w_o[:, :], wo[:, :, :], b_wo, None)]
        for src, dst, b_dst, g_ in jobs:
            S.dma(lambda e, src=src: e.dma_start(out=ms8[:], in_=src.rearrange("(kc p) n -> p kc n", p=128)),
                  writes=[b_ms8], sem_buf=b_ms8)
            if g_ is None:
                S.op("pool", lambda e, dst=dst: e.tensor_copy(out=dst, in_=ms8[:]), reads=[b_ms8], writes=[b_dst])
            else:
                S.op("pool", lambda e, dst=dst, g_=g_: e.tensor_tensor(out=dst, in0=ms8[:], in1=g_.unsqueeze(2).to_broadcast([128, 8, D_]),
                                                                      op=ALU.mult), reads=[b_ms8, b_gssm], writes=[b_dst])

    with ExitStack() as st:
        ktr = Ring(nc, st, un("t_k"), [128, S_], BF16, 2)
        qtr = Ring(nc, st, un("t_q"), [128, S_], BF16, 2)
        vtr = Ring(nc, st, un("t_v"), [128, 32, 65], BF16, 2)
        psS = Ring(nc, st, un("t_ps"), [128, 1024], F32, 3, psum=True)
        psO = Ring(nc, st, un("t_po"), [128, 1024], F32, 1, psum=True)
        pTr = Ring(nc, st, un("t_pT"), [128, 1024], BF16, 4)
        rdr = Ring(nc, st, un("t_rd"), [128, 1024], F32, 2)
        osr = Ring(nc, st, un("t_os"), [128, 1024], F32, 2)
        aor = Ring(nc, st, un("t_ao"), [128, S_], BF16, 2)
        sc = 1.0 / math.sqrt(96.0)
        LAG = 2
        tiles = {}

        def ensure(h_):
            if h_ >= NH or h_ in tiles:
                return
            kt, b_kt = ktr.next()
            qt, b_qt = qtr.next()
            vt, b_vt = vtr.next()
            S.dma(lambda e: e.dma_start(out=kt[0:96, :], in_=KT[h_, :, :]), reads=[b_KT], writes=[b_kt], sem_buf=b_kt)
            S.dma(lambda e: e.dma_start(out=qt[0:96, :], in_=QT[h_, :, :]), reads=[b_QT], writes=[b_qt], sem_buf=b_qt)
            S.dma(lambda e: e.dma_start(out=vt[:], in_=Vs[h_, :, :, :]), reads=[b_Vs], writes=[b_vt], sem_buf=b_vt)
            tiles[h_] = (kt, b_kt, qt, b_qt, vt, b_vt)

        steps = [(h_, sb, kc) for h_ in range(NH) for sb in range(4) for kc in range(32)]
        pend = {}
        cur_po = {}
        cur_ao = {}
        ensure(0)
        preload_merge_weights()
        for i in range(len(steps) + LAG):
            if i < len(steps):
                h_, sb, kc = steps[i]
                kt, b_kt, qt, b_qt, vt, b_vt = tiles[h_]
                ps, b_ps = psS.next()
                for u in range(2):
                    S.op("pe", lambda e, ps=ps, kc=kc, sb=sb, u=u, kt=kt, qt=qt: e.matmul(
                        out=ps[:, u * 512:(u + 1) * 512], lhsT=kt[0:96, kc * 128:(kc + 1) * 128],
                        rhs=qt[0:96, sb * 1024 + u * 512:sb * 1024 + (u + 1) * 512], start=True, stop=True),
                        reads=[b_kt, b_qt], writes=[b_ps])
                pT, b_pT = pTr.next()
                S.op("act", lambda e, ps=ps, pT=pT: e.activation(out=pT[:], in_=ps[:], func=AF.Exp, scale=sc),
                     reads=[b_ps], writes=[b_pT])
                pend[i] = (pT, b_pT)
            if i >= LAG:
                h_, sb, kc = steps[i - LAG]
                kt, b_kt, qt, b_qt, vt, b_vt = tiles[h_]
                pT, b_pT = pend.pop(i - LAG)
                if kc == 0:
                    cur_po[0] = psO.next()
                    if sb == 0:
                        cur_ao[0] = aor.next()
                        ensure(h_ + 1)
                po, b_po = cur_po[0]
                ao, b_ao = cur_ao[0]
                for u in range(2):
                    S.op("pe", lambda e, po=po, pT=pT, kc=kc, u=u, vt=vt: e.matmul(
                        out=po[0:65, u * 512:(u + 1) * 512], lhsT=vt[:, kc, :], rhs=pT[:, u * 512:(u + 1) * 512],
                        start=(kc == 0), stop=(kc == 31)), reads=[b_vt, b_pT], writes=[b_po])
                if kc == 31:
                    osb, b_osb = osr.next()
                    S.op("dve", lambda e, po=po, osb=osb: e.tensor_copy(out=osb[0:65, :], in_=po[0:65, :]), reads=[b_po], writes=[b_osb])
                    rd, b_rd = rdr.next()
                    S.op("dve", lambda e, osb=osb, rd=rd: e.reciprocal(out=rd[64:65, :], in_=osb[64:65, :]), reads=[b_osb], writes=[b_rd])
                    pb, b_pb = psS.next()
                    for u in range(2):
                        S.op("pe", lambda e, pb=pb, rd=rd, u=u: e.matmul(out=pb[0:64, u * 512:(u + 1) * 512], lhsT=mats[64:65, 2, 0:64],
                                                                         rhs=rd[64:65, u * 512:(u + 1) * 512], start=True, stop=True),
                             reads=[b_mats, b_rd], writes=[b_pb])
                    qs = slice(sb * 1024, (sb + 1) * 1024)
                    S.op("dve", lambda e, pb=pb, osb=osb, qs=qs, ao=ao: e.tensor_tensor(
                        out=ao[0:64, qs], in0=pb[0:64, :], in1=osb[0:64, :], op=ALU.mult),
                        reads=[b_pb, b_osb], writes=[b_ao])
                    if sb == 3:
                        S.dma(lambda e, h_=h_, ao=ao: e.dma_start(out=aTs[h_ * 64:(h_ + 1) * 64, :], in_=ao[0:64, :]),
                              reads=[b_ao], writes=[b_aTs], sem_buf=b_ao, eng="pool")
        S.barrier()
    if stop_after <= 5:
        S.emit(nc)
        return nc

    mws.close()
    with ExitStack() as st:
        atr = Ring(nc, st, un("m_at"), [128, 8, 512], BF16, 2)
        mtr = Ring(nc, st, un("m_mt"), [128, 16, 512], BF16, 2)
        gtr = Ring(nc, st, un("m_gt"), [128, 16, 512], BF16, 2)
        mgr = Ring(nc, st, un("m_mg"), [128, 8, 512], BF16, 2)
        t1r = Ring(nc, st, un("m_t1"), [128, 512], F32, 2)
        t2r = Ring(nc, st, un("m_t2"), [128, 512], F32, 2)
        xr = Ring(nc, st, un("m_x"), [128, D_], F32, 3)
        ps = Ring(nc, st, un("m_ps"), [128, 512], F32, 6, psum=True)
        def loadm(t):
            at, b_at = atr.next()
            mt, b_mt = mtr.next()
            gt_, b_gt = gtr.next()
            ts = slice(t * 512, (t + 1) * 512)
            S.dma(lambda e: e.dma_start(out=at[:], in_=aTs[:, ts].rearrange("(k p) t -> p k t", p=128)), reads=[b_aTs], writes=[b_at], sem_buf=b_at)
            S.dma(lambda e: e.dma_start(out=mt[:], in_=mTs[:, ts].rearrange("(k p) t -> p k t", p=128)), reads=[b_mTs], writes=[b_mt], sem_buf=b_mt)
            S.dma(lambda e: e.dma_start(out=gt_[:], in_=gts[:, ts].rearrange("(k p) t -> p k t", p=128)), reads=[b_gts], writes=[b_gt], sem_buf=b_gt)
            return (at, b_at, mt, b_mt, gt_, b_gt)

        def compm(t, hd):
            at, b_at, mt, b_mt, gt_, b_gt = hd
            mg, b_mg = mgr.next()
            for dc in range(8):
                pa, b_pa = ps.next()
                pb, b_pb = ps.next()
                for kc in range(8):
                    S.op("pe", lambda e, pa=pa, kc=kc, dc=dc: e.matmul(out=pa[:], lhsT=wpa[:, kc, dc * 128:(dc + 1) * 128], rhs=at[:, kc, :],
                                                                       start=(kc == 0), stop=(kc == 7)), reads=[b_wpa, b_at], writes=[b_pa])
                for kc in range(16):
                    S.op("pe", lambda e, pb=pb, kc=kc, dc=dc: e.matmul(out=pb[:], lhsT=wpb[:, kc, dc * 128:(dc + 1) * 128], rhs=mt[:, kc, :],
                                                                       start=(kc == 0), stop=(kc == 15)), reads=[b_wpb, b_mt], writes=[b_pb])
                t1, b_t1 = t1r.next()
                t2, b_t2 = t2r.next()
                S.op("dve", lambda e, pa=pa, t1=t1, dc=dc: e.tensor_tensor(out=t1[:], in0=pa[:], in1=gt_[:, dc, :], op=ALU.mult),
                     reads=[b_pa, b_gt], writes=[b_t1])
                S.op("dve", lambda e, pb=pb, t2=t2, dc=dc: e.tensor_tensor(out=t2[:], in0=pb[:], in1=gt_[:, 8 + dc, :], op=ALU.mult),
                     reads=[b_pb, b_gt], writes=[b_t2])
                S.op("pool", lambda e, t1=t1, t2=t2, dc=dc: e.tensor_tensor(out=mg[:, dc, :], in0=t1[:], in1=t2[:], op=ALU.add),
                     reads=[b_t1, b_t2], writes=[b_mg])
            for sb in range(4):
                tb = t * 4 + sb
                xt, b_xt = xr.next()
                S.dma(lambda e, xt=xt, tb=tb: e.dma_start(out=xt[:], in_=x1s[tb * 128:(tb + 1) * 128, :]),
                      reads=[b_x1s], writes=[b_xt], sem_buf=b_xt)
                for half in range(2):
                    p, b_p = ps.next()
                    for kc in range(8):
                        S.op("pe", lambda e, p=p, kc=kc, sb=sb, half=half: e.matmul(
                            out=p[:], lhsT=mg[:, kc, sb * 128:(sb + 1) * 128], rhs=wo[:, kc, half * 512:(half + 1) * 512],
                            start=(kc == 0), stop=(kc == 7)), reads=[b_mg, b_wo], writes=[b_p])
                    S.op("dve", lambda e, p=p, xt=xt, half=half: e.tensor_tensor(out=xt[:, half * 512:(half + 1) * 512], in0=p[:],
                                                                                in1=xt[:, half * 512:(half + 1) * 512], op=ALU.add),
                         reads=[b_p, b_xt], writes=[b_xt])
                S.dma(lambda e, xt=xt, tb=tb: e.dma_start(out=x2s[tb * 128:(tb + 1) * 128, :], in_=xt[:]),
                      reads=[b_xt], writes=[b_x2s], sem_buf=b_xt, eng="pool")

        pipeline(8, loadm, compm, 1)
        S.barrier()
    mw.close()
    hst2 = ExitStack()
    hT2 = hst2.enter_context(nc.sbuf_tensor("hT2", [128, 8, S_], BF16)); b_hT2 = Buf("hT2")
    norm_phase(x2s, b_x2s, hT2, b_hT2)

    with ExitStack() as fst:
        wd2 = fst.enter_context(nc.sbuf_tensor("wd2", [128, NFF, D_], BF16)); b_wd2 = Buf()
        ffn_gateup(w_g2, w_u2, 2, hT2, b_hT2, w_d2, wd2, b_wd2)
        ffn_down(w_d2, x2s, b_x2s, y_out, b_yout, None, wd2, b_wd2)
    S.emit(nc)
    return nc


def _fm(v, kc):
    return np.ascontiguousarray(np.asarray(v, np.float32).reshape(kc, 128).T)


_CACHE = {}


def consts():
    ii = np.arange(128)
    U = (ii[:, None] <= ii[None, :]).astype(np.float32)
    Us = (ii[:, None] < ii[None, :]).astype(np.float32)
    ones = np.ones((128, 128), np.float32)
    I = np.eye(128, dtype=np.float32)
    mats = np.ascontiguousarray(np.stack([U, Us, ones, I], axis=1))
    negf = np.where(ii[:, None] > ii[None, :], -30000.0, 0.0).astype(np.float32)
    posb = np.where(ii[:, None] < ii[None, :], 30000.0, 0.0).astype(np.float32)
    neg = np.ascontiguousarray(np.stack([negf, posb], axis=1)).astype(ml_dtypes.bfloat16)
    invf = (1.0 / (10000.0 ** (np.arange(0, 32, 2, dtype=np.float32) / 32.0))).astype(np.float32)[None, :]
    return dict(c_identb=I.astype(ml_dtypes.bfloat16), c_mats=mats, c_neg=neg, c_invf=invf)


def make_shared(inp):
    f = lambda k: np.asarray(inp[k], np.float32)[0]
    d = {}
    d["gfm"] = np.ascontiguousarray(np.concatenate([_fm(f("ffn1_norm"), 8), _fm(f("mix_norm"), 8), _fm(f("ffn2_norm"), 8)], axis=1))
    d["gqa"] = _fm(f("q_a_norm"), 3)
    d["gkva"] = _fm(f("kv_a_norm"), 2)
    d["gssm"] = _fm(f("ssm_norm"), 16)
    d["w_g1"] = f("ffn1_w_gate"); d["w_u1"] = f("ffn1_w_up"); d["w_d1"] = f("ffn1_w_down")
    d["w_g2"] = f("ffn2_w_gate"); d["w_u2"] = f("ffn2_w_up"); d["w_d2"] = f("ffn2_w_down")
    d["w_in"] = f("w_in"); d["w_qb"] = f("w_q_b"); d["w_kvb"] = f("w_kv_b")
    d["hn"] = np.concatenate([f("q_head_norm"), f("k_head_norm")])[None, :].astype(np.float32)
    cw = f("conv_w")[:, 0, :]
    d["convw"] = np.ascontiguousarray(cw.T.reshape(24, 128, 5).transpose(1, 0, 2))
    d["convb"] = _fm(f("conv_b"), 24)
    d["ssp"] = np.concatenate([f("a_log_fwd"), f("a_log_bwd"), f("dt_bias_fwd"), f("dt_bias_bwd"), f("d_skip")])[None, :].astype(np.float32)
    d["w_pa"] = f("w_attn_branch"); d["w_pb"] = f("w_ssm_branch"); d["w_o"] = f("w_out")
    d.update(consts())
    return d


def make_inmap(inp, shared, b):
    d = dict(shared)
    d["x"] = np.ascontiguousarray(np.asarray(inp["x"], np.float32)[b])
    p = np.asarray(inp["positions"], np.int32)[b]
    d["pos"] = np.ascontiguousarray(p.reshape(32, 128).T)
    return d


def kernel(**inputs):
    nb = int(np.asarray(inputs["x"]).shape[0])
    nc = build(dbg=False)
    shared = make_shared(inputs)
    in_maps = [make_inmap(inputs, shared, b) for b in range(nb)]
    res = run_bass_kernel_spmd(nc, in_maps, core_ids=list(range(nb)))
    out = np.stack([np.asarray(res.results[b]["y"], dtype=np.float32) for b in range(nb)], axis=0)
    return out
```

```python
import math
from contextlib import ExitStack
import numpy as np
import ml_dtypes
import concourse.bass as bass
import concourse.mybir as mybir
from concourse.bass_utils import run_bass_kernel_spmd

F32 = mybir.dt.float32
BF16 = mybir.dt.bfloat16
I32 = mybir.dt.int32
AF = mybir.ActivationFunctionType
ALU = mybir.AluOpType
AX = mybir.AxisListType

S_ = 4096
D_ = 1024
FF = 2816
NFF = 22
NH = 16
EPS = 1e-6
C_Q, C_KV, C_PE, C_Z, C_XBC, C_DTF, C_DTB, C_GA, C_GB = 0, 384, 640, 672, 2720, 5792, 5824, 5856, 6880
IN_DIM = 7904
ENGS = ("pe", "act", "dve", "pool", "sp")
FUSE_WAITS = True


class DSem:
    def __init__(self):
        self.count = 0
        self.handle = None


class Buf:
    __slots__ = ("name", "lw", "rd", "dsem", "ep")

    def __init__(self, name=""):
        self.name = name
        self.lw = None
        self.rd = []
        self.dsem = None
        self.ep = -1


class Op:
    __slots__ = ("eng", "fn", "idx", "waits", "dwaits", "inc", "dsem", "know", "seq", "multi")


class Sched:
    def __init__(self):
        self.ops = {e: [] for e in ENGS}
        self.know = {e: {} for e in ENGS}
        self.dsems = []
        self.free = []
        self.epoch = 0

    def _add(self, eng, fn, reads, writes, dsem=None, extra=(), extra_ds=()):
        op = Op()
        op.eng = eng
        op.fn = fn
        op.idx = len(self.ops[eng])
        op.waits = {}
        op.dwaits = {}
        op.inc = False
        op.dsem = dsem
        op.seq = None
        op.multi = False
        know = self.know[eng]
        deps = list(extra)
        for b in reads:
            if b.lw is not None:
                deps.append(b.lw)
        for b in writes:
            if b.lw is not None:
                deps.append(b.lw)
            deps.extend(b.rd)
        for a in deps:
            if a is op:
                continue
            if a.dsem is None:
                if a.eng == "pe" and eng == "pe":
                    continue
                if know.get(a.eng, -1) >= a.idx:
                    continue
                a.inc = True
                cur = op.waits.get(a.eng)
                if cur is None or cur.idx < a.idx:
                    op.waits[a.eng] = a
                for k, v in a.know.items():
                    if know.get(k, -1) < v:
                        know[k] = v
                know[a.eng] = max(know.get(a.eng, -1), a.idx)
            else:
                ds = a.dsem
                v = ds.count
                if know.get(ds, -1) >= v:
                    continue
                op.dwaits[ds] = v
                for k, vv in a.know.items():
                    if know.get(k, -1) < vv:
                        know[k] = vv
                know[ds] = v
        for ds in extra_ds:
            v = ds.count
            if know.get(ds, -1) < v:
                op.dwaits[ds] = v
                know[ds] = v
        if dsem is not None:
            dsem.count += 16
        op.know = dict(know)
        for b in reads:
            b.rd.append(op)
        for b in writes:
            b.lw = op
            b.rd = []
        self.ops[eng].append(op)
        return op

    def op(self, eng, fn, reads=(), writes=(), multi=False):
        o = self._add(eng, fn, reads, writes)
        o.multi = multi
        return o

    def dma(self, fn, reads=(), writes=(), sem_buf=None, eng="sp"):
        if sem_buf.dsem is None or sem_buf.ep != self.epoch:
            if self.free:
                sem_buf.dsem = self.free.pop()
            else:
                sem_buf.dsem = DSem()
                self.dsems.append(sem_buf.dsem)
            sem_buf.ep = self.epoch
        return self._add(eng, fn, reads, writes, dsem=sem_buf.dsem)

    def barrier(self):
        lasts = []
        for e in ENGS:
            if e == "sp":
                continue
            for o in reversed(self.ops[e]):
                if o.dsem is None:
                    lasts.append(o)
                    break
        spop = self._add("sp", lambda e: e.nop(), (), (), extra=lasts, extra_ds=list(self.dsems))
        self.epoch += 1
        self.free = list(self.dsems)
        for e in ENGS:
            if e == "sp":
                continue
            self._add(e, lambda eh: eh.nop(), (), (), extra=[spop])

    def emit(self, nc):
        with ExitStack() as st:
            esem = {e: st.enter_context(nc.semaphore("es_" + e)) for e in ENGS}
            for i, d in enumerate(self.dsems):
                d.handle = st.enter_context(nc.semaphore("ds%d" % i))
            for e in ENGS:
                c = 0
                for o in self.ops[e]:
                    if o.dsem is None and o.inc:
                        c += 1
                        o.seq = c
            block = st.enter_context(nc.Block())

            def run(e, eh):
                for o in self.ops[e]:
                    wl = [(esem[se], a.seq) for se, a in o.waits.items()] + [(ds.handle, v) for ds, v in o.dwaits.items()]
                    attach = None
                    if wl and o.dsem is None and not o.multi and e != "sp" and FUSE_WAITS:
                        attach = wl.pop()
                    for hh_, vv_ in wl:
                        eh.wait_ge(hh_, vv_)
                    n0 = nc.n_instructions()
                    ins = o.fn(eh)
                    if attach is not None:
                        if nc.n_instructions() - n0 != 1:
                            raise RuntimeError("multi-instruction op with fused wait on %s (%d)" % (e, nc.n_instructions() - n0))
                        ins._wait_ge(attach[0], attach[1])
                    if o.dsem is not None:
                        ins.then_inc(o.dsem.handle, 16)
                    elif o.inc:
                        ins.then_inc(esem[e], 1)
                if e == "sp":
                    for ds in self.dsems:
                        eh.wait_ge(ds.handle, ds.count)

            @block.tensor
            def _(eh):
                run("pe", eh)

            @block.scalar
            def _(eh):
                run("act", eh)

            @block.vector
            def _(eh):
                run("dve", eh)

            @block.gpsimd
            def _(eh):
                run("pool", eh)

            @block.sync
            def _(eh):
                run("sp", eh)


class Ring:
    def __init__(self, nc, st, name, shape, dtype, n, psum=False):
        self.items = []
        for i in range(n):
            if psum:
                t = st.enter_context(nc.psum_tensor("%s%d" % (name, i), shape, dtype))
            else:
                t = st.enter_context(nc.sbuf_tensor("%s%d" % (name, i), shape, dtype))
            self.items.append((t, Buf("%s%d" % (name, i))))
        self.i = 0

    def next(self):
        r = self.items[self.i % len(self.items)]
        self.i += 1
        return r


def pipeline(n, load_fn, compute_fn, depth):
    hs = {}
    for i in range(n + depth):
        if i < n:
            hs[i] = load_fn(i)
        if i >= depth:
            compute_fn(i - depth, hs.pop(i - depth))


class K:
    pass


def build(dbg=False, stop_after=99):
    nc = bass.Bass("TRN2", target_bir_lowering=False)
    S = Sched()
    uid = [0]

    def un(p):
        uid[0] += 1
        return "%s_%d" % (p, uid[0])

    def inp(name, shape, dt=F32):
        return nc.dram_tensor(name, shape, dt, kind="ExternalInput").ap()

    def scratch(name, shape, dt, out=False):
        kind = "ExternalOutput" if (out or dbg) else "Internal"
        return nc.dram_tensor(name, shape, dt, kind=kind).ap(), Buf(name)

    x = inp("x", [S_, D_])
    pos = inp("pos", [128, 32], I32)
    gfm = inp("gfm", [128, 24])
    gqa = inp("gqa", [128, 3])
    gkva = inp("gkva", [128, 2])
    gssm = inp("gssm", [128, 16])
    w_g1 = inp("w_g1", [D_, FF]); w_u1 = inp("w_u1", [D_, FF]); w_d1 = inp("w_d1", [FF, D_])
    w_g2 = inp("w_g2", [D_, FF]); w_u2 = inp("w_u2", [D_, FF]); w_d2 = inp("w_d2", [FF, D_])
    w_in = inp("w_in", [D_, IN_DIM])
    w_qb = inp("w_qb", [384, 1536]); w_kvb = inp("w_kvb", [256, 2048])
    hn = inp("hn", [1, 192])
    convw = inp("convw", [128, 24, 5]); convb = inp("convb", [128, 24])
    ssp = inp("ssp", [1, 160])
    w_pa = inp("w_pa", [D_, D_]); w_pb = inp("w_pb", [2048, D_]); w_o = inp("w_o", [D_, D_])
    c_identb = inp("c_identb", [128, 128], BF16)
    c_mats = inp("c_mats", [128, 4, 128])
    c_neg = inp("c_neg", [128, 2, 128], BF16)
    c_invf = inp("c_invf", [1, 16])

    y_out, b_yout = scratch("y", [S_, D_], F32, out=True)
    x1s, b_x1s = scratch("x1s", [S_, D_], F32)
    x2s, b_x2s = scratch("x2s", [S_, D_], F32)
    hmid, b_hmid = scratch("hmid", [FF, S_], BF16)
    QT, b_QT = scratch("QT", [NH, 96, S_], BF16)
    KT, b_KT = scratch("KT", [NH, 96, S_], BF16)
    Vs, b_Vs = scratch("Vs", [NH, 128, 32, 65], BF16)
    zs, b_zs = scratch("zs", [S_, 2048], BF16)
    xcs, b_xcs = scratch("xcs", [32, 128, 24, 128], BF16)
    gts, b_gts = scratch("gts", [2048, S_], BF16)
    yfs, b_yfs = scratch("yfs", [S_, 2048], F32)
    mTs, b_mTs = scratch("mTs", [2048, S_], BF16)
    aTs, b_aTs = scratch("aTs", [D_, S_], BF16)
    if dbg:
        ybs, b_ybs = scratch("ybs", [S_, 2048], F32)

    top = ExitStack()
    A = lambda name, shape, dt: top.enter_context(nc.sbuf_tensor(name, shape, dt))
    identb = A("identb", [128, 128], BF16); b_identb = Buf()
    mats = A("mats", [128, 4, 128], F32); b_mats = Buf()
    negm = A("negm", [128, 2, 128], BF16); b_negm = Buf()
    gfm_t = A("gfm_t", [128, 24], F32); b_gfm = Buf()
    gqa_t = A("gqa_t", [128, 3], F32); b_gqa = Buf()
    gkva_t = A("gkva_t", [128, 2], F32); b_gkva = Buf()
    gssm_t = A("gssm_t", [128, 16], F32); b_gssm = Buf()
    hn_t = A("hn_t", [128, 192], F32); b_hn = Buf()
    ssp_t = A("ssp_t", [128, 160], F32); b_ssp = Buf()
    convw_t = A("convw_t", [128, 24, 5], F32); b_convw = Buf()
    convb_t = A("convb_t", [128, 24], F32); b_convb = Buf()
    cos_t = A("cos_t", [128, 32, 16], F32); b_cos = Buf()
    sin_t = A("sin_t", [128, 32, 16], F32); b_sin = Buf()
    dtraw = A("dtraw", [128, 32, 64], F32); b_dtraw = Buf()

    def ld(dst, src, b):
        S.dma(lambda e: e.dma_start(out=dst, in_=src), writes=[b], sem_buf=b)

    ld(identb[:], c_identb[:, :], b_identb)
    ld(mats[:], c_mats[:, :, :], b_mats)
    ld(negm[:], c_neg[:, :, :], b_negm)
    ld(gfm_t[:], gfm[:, :], b_gfm)
    ld(gqa_t[:], gqa[:, :], b_gqa)
    ld(gkva_t[:], gkva[:, :], b_gkva)
    ld(gssm_t[:], gssm[:, :], b_gssm)
    ld(hn_t[:], hn.partition_broadcast(128), b_hn)
    ld(ssp_t[:], ssp.partition_broadcast(128), b_ssp)
    ld(convw_t[:], convw[:, :, :], b_convw)
    ld(convb_t[:], convb[:, :], b_convb)
    Umat = mats[:, 0, :]
    Ustr = mats[:, 1, :]
    onesf = mats[:, 2, :]
    identf = mats[:, 3, :]

    with ExitStack() as st:
        post = st.enter_context(nc.sbuf_tensor("post", [128, 32], I32)); b_post = Buf()
        posf = st.enter_context(nc.sbuf_tensor("posf", [128, 32], F32)); b_posf = Buf()
        invf = st.enter_context(nc.sbuf_tensor("invf", [128, 16], F32)); b_invf = Buf()
        ang = st.enter_context(nc.sbuf_tensor("ang", [128, 32, 16], F32)); b_ang = Buf()
        ang2 = st.enter_context(nc.sbuf_tensor("ang2", [128, 32, 16], F32)); b_ang2 = Buf()
        ld(post[:], pos[:, :], b_post)
        ld(invf[:], c_invf.partition_broadcast(128), b_invf)
        S.op("dve", lambda e: e.tensor_copy(out=posf[:], in_=post[:]), reads=[b_post], writes=[b_posf])
        S.op("dve", lambda e: e.tensor_tensor(out=ang[:], in0=posf[:].unsqueeze(2).to_broadcast([128, 32, 16]),
                                              in1=invf[:].unsqueeze(1).to_broadcast([128, 32, 16]), op=ALU.mult),
             reads=[b_posf, b_invf], writes=[b_ang])
        PI = math.pi
        angi = st.enter_context(nc.sbuf_tensor("angi", [128, 32, 16], I32)); b_angi = Buf()
        ang3 = st.enter_context(nc.sbuf_tensor("ang3", [128, 32, 16], F32)); b_ang3 = Buf()

        def rr(shift, dst, b_dst):
            S.op("dve", lambda e: e.tensor_scalar(out=ang2[:], in0=ang[:], scalar1=shift, scalar2=None, op0=ALU.add),
                 reads=[b_ang], writes=[b_ang2])
            S.op("dve", lambda e: e.tensor_scalar(out=ang3[:], in0=ang2[:], scalar1=1.0 / (2 * PI), scalar2=None,
                                                  op0=ALU.mult), reads=[b_ang2], writes=[b_ang3])
            S.op("dve", lambda e: e.tensor_copy(out=angi[:], in_=ang3[:]), reads=[b_ang3], writes=[b_angi])
            S.op("dve", lambda e: e.tensor_copy(out=ang3[:], in_=angi[:]), reads=[b_angi], writes=[b_ang3])
            S.op("dve", lambda e: e.scalar_tensor_tensor(out=ang2[:], in0=ang3[:], scalar=-2 * PI, in1=ang2[:],
                                                         op0=ALU.mult, op1=ALU.add), reads=[b_ang3, b_ang2], writes=[b_ang2])
            S.op("dve", lambda e: e.tensor_scalar(out=ang3[:], in0=ang2[:], scalar1=-PI, scalar2=1e9,
                                                  op0=ALU.add, op1=ALU.mult), reads=[b_ang2], writes=[b_ang3])
            S.op("dve", lambda e: e.tensor_scalar(out=ang3[:], in0=ang3[:], scalar1=0.0, scalar2=1.0,
                                                  op0=ALU.max, op1=ALU.min), reads=[b_ang3], writes=[b_ang3])
            S.op("dve", lambda e: e.scalar_tensor_tensor(out=ang2[:], in0=ang3[:], scalar=-2 * PI, in1=ang2[:],
                                                         op0=ALU.mult, op1=ALU.add), reads=[b_ang3, b_ang2], writes=[b_ang2])
            S.op("dve", lambda e: e.tensor_scalar(out=ang3[:], in0=ang2[:], scalar1=PI, scalar2=-1e9,
                                                  op0=ALU.add, op1=ALU.mult), reads=[b_ang2], writes=[b_ang3])
            S.op("dve", lambda e: e.tensor_scalar(out=ang3[:], in0=ang3[:], scalar1=0.0, scalar2=1.0,
                                                  op0=ALU.max, op1=ALU.min), reads=[b_ang3], writes=[b_ang3])
            S.op("dve", lambda e: e.scalar_tensor_tensor(out=ang2[:], in0=ang3[:], scalar=2 * PI, in1=ang2[:],
                                                         op0=ALU.mult, op1=ALU.add), reads=[b_ang3, b_ang2], writes=[b_ang2])
            S.op("dve", lambda e: e.tensor_scalar(out=ang2[:], in0=ang2[:], scalar1=PI * (1 - 1e-6),
                                                  scalar2=-PI * (1 - 1e-6), op0=ALU.min, op1=ALU.max),
                 reads=[b_ang2], writes=[b_ang2])
            S.op("act", lambda e: e.activation(out=dst, in_=ang2[:], func=AF.Sin), reads=[b_ang2], writes=[b_dst])

        rr(0.0, sin_t[:], b_sin)
        rr(0.5 * PI, cos_t[:], b_cos)
        S.barrier()

    def wload(stage_ring, w_ring, wsrc, kc, n, gain=None, cast_eng="pool"):
        stg, b_stg = stage_ring.next()
        wt, b_wt = w_ring.next()
        S.dma(lambda e: e.dma_start(out=stg[:, 0:kc, 0:n], in_=wsrc.rearrange("(kc p) n -> p kc n", p=128)),
              writes=[b_stg], sem_buf=b_stg)
        if gain is None:
            S.op(cast_eng, lambda e: e.tensor_copy(out=wt[:, 0:kc, 0:n], in_=stg[:, 0:kc, 0:n]),
                 reads=[b_stg], writes=[b_wt])
        else:
            g_ap, b_g = gain
            S.op(cast_eng, lambda e: e.tensor_tensor(out=wt[:, 0:kc, 0:n], in0=stg[:, 0:kc, 0:n],
                                                     in1=g_ap.unsqueeze(2).to_broadcast([128, kc, n]), op=ALU.mult),
                 reads=[b_stg, b_g], writes=[b_wt])
        return wt, b_wt

    def norm_block(src, b_src, tb, rings, hT, b_hT, ncols=D_):
        junk, b_junk = rings["junk"].next()
        stt, b_stt = rings["st"].next()
        hb, b_hb = rings["hb"].next()
        ptr, b_ptr = rings["ptr"].next()
        S.op("act", lambda e: e.activation(out=junk[:], in_=src, func=AF.Square, scale=1.0 / math.sqrt(ncols),
                                           accum_out=stt[:, 0:1]), reads=[b_src], writes=[b_junk, b_stt])
        S.op("act", lambda e: e.activation(out=stt[:, 1:2], in_=stt[:, 0:1], func=AF.Sqrt, bias=EPS_AP[:, 0:1]),
             reads=[b_stt], writes=[b_stt])
        S.op("dve", lambda e: e.reciprocal(out=stt[:, 2:3], in_=stt[:, 1:2]), reads=[b_stt], writes=[b_stt])
        S.op("dve", lambda e: e.tensor_scalar(out=hb[:], in0=src, scalar1=stt[:, 2:3], scalar2=None, op0=ALU.mult),
             reads=[b_src, b_stt], writes=[b_hb])
        pv = ptr[:, :].bitcast(BF16)
        for kc in range(8):
            S.op("pe", lambda e, kc=kc: e.transpose(out=pv[:, kc * 128:(kc + 1) * 128],
                                                     in_=hb[:, kc * 128:(kc + 1) * 128], identity=identb[:]),
                 reads=[b_hb, b_identb], writes=[b_ptr])
        S.op("act", lambda e: e.copy(out=hT[:, :, tb * 128:(tb + 1) * 128],
                                     in_=pv.rearrange("p (k t) -> p k t", k=8)),
             reads=[b_ptr], writes=[b_hT])

    eps_t = A("eps_t", [128, 1], F32); b_eps = Buf()
    S.op("pool", lambda e: e.memset(eps_t[:], EPS), writes=[b_eps])
    EPS_AP = eps_t
    S.barrier()

    def ffn_gateup(w_g, w_u, gain_col, hT, b_hT, w_d=None, wd=None, b_wd=None):
        with ExitStack() as st:
            stg = Ring(nc, st, un("gu_stg"), [128, 8, 128], F32, 6)
            wr = Ring(nc, st, un("gu_w"), [128, 8, 128], BF16, 6)
            psg = Ring(nc, st, un("gu_pg"), [128, 512], F32, 3, psum=True)
            psu = Ring(nc, st, un("gu_pu"), [128, 512], F32, 3, psum=True)
            sil = Ring(nc, st, un("gu_sil"), [128, 512], F32, 3)
            hm = Ring(nc, st, un("gu_hm"), [128, S_], BF16, 2)
            gain = (gfm_t[:, gain_col * 8:(gain_col + 1) * 8], b_gfm)

            dstg = Ring(nc, st, un("gu_dstg"), [128, 1, D_], F32, 3)

            def load(j):
                wg = wload(stg, wr, w_g[:, j * 128:(j + 1) * 128], 8, 128, gain)
                wu = wload(stg, wr, w_u[:, j * 128:(j + 1) * 128], 8, 128, gain)
                if w_d is not None:
                    sg, b_sg = dstg.next()
                    S.dma(lambda e, sg=sg, j=j: e.dma_start(out=sg[:, 0, :], in_=w_d[j * 128:(j + 1) * 128, :]),
                          writes=[b_sg], sem_buf=b_sg)
                    S.op("pool", lambda e, sg=sg, j=j: e.tensor_copy(out=wd[:, j, :], in_=sg[:, 0, :]),
                         reads=[b_sg], writes=[b_wd])
                return wg, wu

            def comp(j, h):
                (wg, b_wg), (wu, b_wu) = h
                hmt, b_hm = hm.next()
                for tb in range(8):
                    pg, b_pg = psg.next()
                    pu, b_pu = psu.next()
                    sl, b_sl = sil.next()
                    ts = slice(tb * 512, (tb + 1) * 512)
                    for kc in range(8):
                        S.op("pe", lambda e, kc=kc, pg=pg, wg=wg, ts=ts: e.matmul(
                            out=pg[:], lhsT=wg[:, kc, :], rhs=hT[:, kc, ts], start=(kc == 0), stop=(kc == 7)),
                            reads=[b_wg, b_hT], writes=[b_pg])
                    for kc in range(8):
                        S.op("pe", lambda e, kc=kc, pu=pu, wu=wu, ts=ts: e.matmul(
                            out=pu[:], lhsT=wu[:, kc, :], rhs=hT[:, kc, ts], start=(kc == 0), stop=(kc == 7)),
                            reads=[b_wu, b_hT], writes=[b_pu])
                    S.op("act", lambda e, sl=sl, pg=pg: e.activation(out=sl[:], in_=pg[:], func=AF.Silu),
                         reads=[b_pg], writes=[b_sl])
                    S.op("dve", lambda e, sl=sl, pu=pu, hmt=hmt, ts=ts: e.tensor_tensor(
                        out=hmt[:, ts], in0=sl[:], in1=pu[:], op=ALU.mult), reads=[b_sl, b_pu], writes=[b_hm])
                S.dma(lambda e, hmt=hmt, j=j: e.dma_start(out=hmid[j * 128:(j + 1) * 128, :], in_=hmt[:]),
                      reads=[b_hm], writes=[b_hmid], sem_buf=b_hm, eng="pool")

            pipeline(NFF, load, comp, 2)
        S.barrier()

    def ffn_down(w_d, xsrc, b_xsrc, xdst, b_xdst, next_norm, wd=None, b_wd=None):
        with ExitStack() as st:
            pre = wd is not None
            if not pre:
                wd = st.enter_context(nc.sbuf_tensor(un("wd"), [128, NFF, D_], BF16)); b_wd = Buf()
            stg = Ring(nc, st, un("dn_stg"), [128, 1, D_], F32, 3)
            for j in range(NFF if not pre else 0):
                sg, b_sg = stg.next()
                S.dma(lambda e, sg=sg, j=j: e.dma_start(out=sg[:, 0, :], in_=w_d[j * 128:(j + 1) * 128, :]),
                      writes=[b_sg], sem_buf=b_sg)
                S.op("pool", lambda e, sg=sg, j=j: e.tensor_copy(out=wd[:, j, :], in_=sg[:, 0, :]),
                     reads=[b_sg], writes=[b_wd])
            hmr = Ring(nc, st, un("dn_hm"), [128, NFF, 512], BF16, 2)
            xr = Ring(nc, st, un("dn_x"), [128, D_], F32, 3)
            ps = Ring(nc, st, un("dn_ps"), [128, 512], F32, 4, psum=True)
            rings = None
            if next_norm:
                rings = dict(junk=Ring(nc, st, un("nj"), [128, D_], BF16, 2),
                             st=Ring(nc, st, un("nst"), [128, 4], F32, 3),
                             hb=Ring(nc, st, un("nhb"), [128, D_], BF16, 2),
                             ptr=Ring(nc, st, un("nptr"), [128, 512], F32, 2, psum=True))

            def load(t):
                hmt, b_hm = hmr.next()
                S.dma(lambda e: e.dma_start(out=hmt[:], in_=hmid[:, t * 512:(t + 1) * 512].rearrange(
                    "(j p) t -> p j t", p=128)), reads=[b_hmid], writes=[b_hm], sem_buf=b_hm)
                return hmt, b_hm

            def comp(t, h):
                hmt, b_hm = h
                for sb in range(4):
                    tb = t * 4 + sb
                    xt, b_xt = xr.next()
                    S.dma(lambda e, xt=xt, tb=tb: e.dma_start(out=xt[:], in_=xsrc[tb * 128:(tb + 1) * 128, :]),
                          reads=[b_xsrc], writes=[b_xt], sem_buf=b_xt)
                    for half in range(2):
                        p, b_p = ps.next()
                        for j in range(NFF):
                            S.op("pe", lambda e, j=j, p=p, sb=sb, half=half, hmt=hmt: e.matmul(
                                out=p[:], lhsT=hmt[:, j, sb * 128:(sb + 1) * 128],
                                rhs=wd[:, j, half * 512:(half + 1) * 512], start=(j == 0), stop=(j == NFF - 1)),
                                reads=[b_hm, b_wd], writes=[b_p])
                        S.op("dve", lambda e, p=p, xt=xt, half=half: e.scalar_tensor_tensor(
                            out=xt[:, half * 512:(half + 1) * 512], in0=p[:], scalar=0.5,
                            in1=xt[:, half * 512:(half + 1) * 512], op0=ALU.mult, op1=ALU.add),
                            reads=[b_p, b_xt], writes=[b_xt])
                    S.dma(lambda e, xt=xt, tb=tb: e.dma_start(out=xdst[tb * 128:(tb + 1) * 128, :], in_=xt[:]),
                          reads=[b_xt], writes=[b_xdst], sem_buf=b_xt, eng="pool")
                    if next_norm:
                        if pendn:
                            norm_block(*pendn.pop())
                        pendn.append((xt[:], b_xt, tb, rings, next_norm[0], next_norm[1]))

            pendn = []
            pipeline(8, load, comp, 1)
            if pendn:
                norm_block(*pendn.pop())
        S.barrier()

    def norm_phase(xsrc, b_xsrc, hT, b_hT):
        with ExitStack() as st:
            xr = Ring(nc, st, un("np_x"), [128, D_], F32, 3)
            rings = dict(junk=Ring(nc, st, un("nj"), [128, D_], BF16, 2),
                         st=Ring(nc, st, un("nst"), [128, 4], F32, 3),
                         hb=Ring(nc, st, un("nhb"), [128, D_], BF16, 2),
                         ptr=Ring(nc, st, un("nptr"), [128, 512], F32, 2, psum=True))

            def load(tb):
                xt, b_xt = xr.next()
                S.dma(lambda e: e.dma_start(out=xt[:], in_=xsrc[tb * 128:(tb + 1) * 128, :]),
                      reads=[b_xsrc], writes=[b_xt], sem_buf=b_xt)
                return xt, b_xt

            def comp(tb, h):
                norm_block(h[0][:], h[1], tb, rings, hT, b_hT)

            pipeline(32, load, comp, 2)
        S.barrier()

    b_x = Buf("x")
    hst = ExitStack()
    hT = hst.enter_context(nc.sbuf_tensor("hT", [128, 8, S_], BF16)); b_hT = Buf("hT")
    norm_phase(x, b_x, hT, b_hT)
    with ExitStack() as fst:
        wd1 = fst.enter_context(nc.sbuf_tensor("wd1", [128, NFF, D_], BF16)); b_wd1 = Buf()
        ffn_gateup(w_g1, w_u1, 0, hT, b_hT, w_d1, wd1, b_wd1)
        ffn_down(w_d1, x, b_x, x1s, b_x1s, (hT, b_hT), wd1, b_wd1)
    if stop_after <= 1:
        S.emit(nc)
        return nc

    gmix = (gfm_t[:, 8:16], b_gfm)

    def rope(src3, H, tb, dst3, tmp_ring):
        ta, b_ta = tmp_ring.next()
        tb_, b_tb = tmp_ring.next()
        cb = cos_t[:, tb, :].unsqueeze(1).to_broadcast([128, H, 16])
        sb = sin_t[:, tb, :].unsqueeze(1).to_broadcast([128, H, 16])
        t1 = src3[:, :, 0:16]
        t2 = src3[:, :, 16:32]
        a_ = ta[:, 0:H, :]
        b_ = tb_[:, 0:H, :]
        return [
            (lambda e: e.tensor_tensor(out=a_, in0=t1, in1=cb, op=ALU.mult), [b_cos], [b_ta]),
            (lambda e: e.tensor_tensor(out=b_, in0=t2, in1=sb, op=ALU.mult), [b_sin], [b_tb]),
            (lambda e: e.tensor_tensor(out=dst3[:, :, 0:16], in0=a_, in1=b_, op=ALU.subtract), [b_ta, b_tb], []),
            (lambda e: e.tensor_tensor(out=a_, in0=t2, in1=cb, op=ALU.mult), [b_cos], [b_ta]),
            (lambda e: e.tensor_tensor(out=b_, in0=t1, in1=sb, op=ALU.mult), [b_sin], [b_tb]),
            (lambda e: e.tensor_tensor(out=dst3[:, :, 16:32], in0=a_, in1=b_, op=ALU.add), [b_ta, b_tb], []),
        ]

    with ExitStack() as st:
        wAr = Ring(nc, st, un("a_w"), [128, 8, 672], BF16, 1)
        wqr = Ring(nc, st, un("q_w"), [128, 3, 1536], BF16, 1)
        wkr = Ring(nc, st, un("kv_w"), [128, 2, 2048], BF16, 1)
        with ExitStack() as st2:
            stgA = Ring(nc, st2, un("a_stg"), [128, 8, 672], F32, 1)
            stgq = Ring(nc, st2, un("q_stg"), [128, 3, 1536], F32, 1)
            stgk = Ring(nc, st2, un("kv_stg"), [128, 2, 2048], F32, 1)
            wA, b_wA = wload(stgA, wAr, w_in[:, 0:672], 8, 672, gmix)
            wq, b_wq = wload(stgq, wqr, w_qb[:, :], 3, 1536, (gqa_t[:, :], b_gqa))
            wkv, b_wkv = wload(stgk, wkr, w_kvb[:, :], 2, 2048, (gkva_t[:, :], b_gkva))
            S.barrier()
        psA = Ring(nc, st, un("a_ps"), [128, 512], F32, 2, psum=True)
        psT = Ring(nc, st, un("a_pt"), [128, 512], F32, 2, psum=True)
        psQ = Ring(nc, st, un("a_pq"), [128, 512], F32, 3, psum=True)
        junk = Ring(nc, st, un("a_junk"), [128, 1536], F32, 1)
        stt = Ring(nc, st, un("a_st"), [128, 8], F32, 2)
        sst = Ring(nc, st, un("a_ss"), [128, 100], F32, 2)
        cnr = Ring(nc, st, un("a_cn"), [128, 640], BF16, 2)
        cTr = Ring(nc, st, un("a_cT"), [128, 5, 128], BF16, 2)
        kper = Ring(nc, st, un("a_kpe"), [128, 1, 32], F32, 2)
        kpgr = Ring(nc, st, un("a_kpg"), [128, 1, 32], F32, 2)
        krr = Ring(nc, st, un("a_kr"), [128, 1, 32], F32, 2)
        qsbr = Ring(nc, st, un("a_qsb"), [128, 1536], F32, 1)
        kvsbr = Ring(nc, st, un("a_kvsb"), [128, 2048], F32, 1)
        tmpkr = Ring(nc, st, un("a_tmpk"), [128, 16, 64], F32, 1)
        qbr = Ring(nc, st, un("a_qb"), [128, 16, 96], BF16, 2)
        kbr = Ring(nc, st, un("a_kb"), [128, 16, 96], BF16, 2)
        ropet = Ring(nc, st, un("a_rt"), [128, 16, 16], F32, 4)
        vbr = Ring(nc, st, un("a_vb"), [128, 16, 4, 65], BF16, 1)
        qTr = Ring(nc, st, un("a_qT"), [128, 16, 256], BF16, 2)
        kTr = Ring(nc, st, un("a_kT"), [128, 16, 256], BF16, 2)
        for (vt, b_v) in vbr.items:
            S.op("pool", lambda e, vt=vt: e.memset(vt[:], 1.0), writes=[b_v])
        gq = hn_t[:, 0:96]
        gk = hn_t[:, 96:192]
        sh = {}

        def block(tb):
            tsl = slice(tb * 128, (tb + 1) * 128)
            pA1, b_pA1 = psA.next()
            pA2, b_pA2 = psA.next()
            for kc in range(8):
                S.op("pe", lambda e, kc=kc, pA1=pA1, tsl=tsl: e.matmul(out=pA1[:, 0:384], lhsT=hT[:, kc, tsl], rhs=wA[:, kc, 0:384],
                                                     start=(kc == 0), stop=(kc == 7)), reads=[b_hT, b_wA], writes=[b_pA1])
            for kc in range(8):
                S.op("pe", lambda e, kc=kc, pA2=pA2, tsl=tsl: e.matmul(out=pA2[:, 0:288], lhsT=hT[:, kc, tsl], rhs=wA[:, kc, 384:672],
                                                     start=(kc == 0), stop=(kc == 7)), reads=[b_hT, b_wA], writes=[b_pA2])
            jk, b_jk = junk.next()
            s8, b_s8 = stt.next()
            S.op("act", lambda e, jk=jk, pA1=pA1, s8=s8: e.activation(out=jk[:, 0:384], in_=pA1[:, 0:384], func=AF.Square,
                                               scale=1.0 / math.sqrt(384.0), accum_out=s8[:, 0:1]),
                 reads=[b_pA1], writes=[b_jk, b_s8])
            S.op("act", lambda e, jk=jk, pA2=pA2, s8=s8: e.activation(out=jk[:, 0:256], in_=pA2[:, 0:256], func=AF.Square,
                                               scale=1.0 / 16.0, accum_out=s8[:, 1:2]),
                 reads=[b_pA2], writes=[b_jk, b_s8])
            S.op("act", lambda e, s8=s8: e.activation(out=s8[:, 2:4], in_=s8[:, 0:2], func=AF.Sqrt, bias=EPS_AP[:, 0:1]),
                 reads=[b_s8, b_eps], writes=[b_s8])
            S.op("dve", lambda e, s8=s8: e.reciprocal(out=s8[:, 4:6], in_=s8[:, 2:4]), reads=[b_s8], writes=[b_s8])
            cn, b_cn = cnr.next()
            S.op("dve", lambda e, cn=cn, pA1=pA1, s8=s8: e.tensor_scalar(out=cn[:, 0:384], in0=pA1[:, 0:384], scalar1=s8[:, 4:5],
                                                  scalar2=None, op0=ALU.mult), reads=[b_pA1, b_s8], writes=[b_cn])
            S.op("dve", lambda e, cn=cn, pA2=pA2, s8=s8: e.tensor_scalar(out=cn[:, 384:640], in0=pA2[:, 0:256], scalar1=s8[:, 5:6],
                                                  scalar2=None, op0=ALU.mult), reads=[b_pA2, b_s8], writes=[b_cn])
            kpe, b_kpe = kper.next()
            S.op("act", lambda e, kpe=kpe, pA2=pA2: e.copy(out=kpe[:, 0, :], in_=pA2[:, 256:288]), reads=[b_pA2], writes=[b_kpe])
            yield
            ptr, b_ptr = psT.next()
            pv = ptr[:, :].bitcast(BF16)
            for kc in range(5):
                S.op("pe", lambda e, kc=kc, pv=pv, cn=cn: e.transpose(out=pv[:, kc * 128:(kc + 1) * 128],
                                                         in_=cn[:, kc * 128:(kc + 1) * 128], identity=identb[:]),
                     reads=[b_cn, b_identb], writes=[b_ptr])
            cT, b_cT = cTr.next()
            S.op("dve", lambda e, cT=cT, pv=pv: e.tensor_copy(out=cT[:], in_=pv[:, 0:640].rearrange("p (k t) -> p k t", k=5)),
                 reads=[b_ptr], writes=[b_cT])
            yield
            qsb, b_qsb = qsbr.next()
            kvsb, b_kvsb = kvsbr.next()
            for nb in range(3):
                pq, b_pq = psQ.next()
                for kc in range(3):
                    S.op("pe", lambda e, kc=kc, nb=nb, pq=pq, cT=cT: e.matmul(out=pq[:], lhsT=cT[:, kc, :],
                                                                 rhs=wq[:, kc, nb * 512:(nb + 1) * 512],
                                                                 start=(kc == 0), stop=(kc == 2)),
                         reads=[b_cT, b_wq], writes=[b_pq])
                S.op("act", lambda e, nb=nb, pq=pq, qsb=qsb: e.copy(out=qsb[:, nb * 512:(nb + 1) * 512], in_=pq[:]),
                     reads=[b_pq], writes=[b_qsb])
            for nb in range(4):
                pq, b_pq = psQ.next()
                for kc in range(2):
                    S.op("pe", lambda e, kc=kc, nb=nb, pq=pq, cT=cT: e.matmul(out=pq[:], lhsT=cT[:, 3 + kc, :],
                                                                 rhs=wkv[:, kc, nb * 512:(nb + 1) * 512],
                                                                 start=(kc == 0), stop=(kc == 1)),
                         reads=[b_cT, b_wkv], writes=[b_pq])
                eng = "act" if nb % 2 == 0 else "dve"
                if eng == "act":
                    S.op("act", lambda e, nb=nb, pq=pq, kvsb=kvsb: e.copy(out=kvsb[:, nb * 512:(nb + 1) * 512], in_=pq[:]),
                         reads=[b_pq], writes=[b_kvsb])
                else:
                    S.op("dve", lambda e, nb=nb, pq=pq, kvsb=kvsb: e.tensor_copy(out=kvsb[:, nb * 512:(nb + 1) * 512], in_=pq[:]),
                         reads=[b_pq], writes=[b_kvsb])
            q3 = qsb[:, :].rearrange("p (h d) -> p h d", h=16)
            kv3 = kvsb[:, :].rearrange("p (h d) -> p h d", h=16)
            ss, b_ss = sst.next()
            S.op("act", lambda e, jk=jk, qsb=qsb: e.activation(out=jk[:, :], in_=qsb[:, :], func=AF.Square),
                 reads=[b_qsb], writes=[b_jk])
            S.op("dve", lambda e, jk=jk, ss=ss: e.tensor_reduce(out=ss[:, 0:16], in_=jk[:, :].rearrange("p (h d) -> p h d", h=16),
                                                  axis=AX.X, op=ALU.add), reads=[b_jk], writes=[b_ss])
            tk, b_tk = tmpkr.next()
            S.op("act", lambda e, tk=tk, kv3=kv3: e.activation(out=tk[:], in_=kv3[:, :, 0:64], func=AF.Square),
                 reads=[b_kvsb], writes=[b_tk])
            S.op("dve", lambda e, tk=tk, ss=ss: e.tensor_reduce(out=ss[:, 16:32], in_=tk[:], axis=AX.X, op=ALU.add),
                 reads=[b_tk], writes=[b_ss])
            kpg, b_kpg = kpgr.next()
            S.op("act", lambda e, kpg=kpg, kpe=kpe, ss=ss: e.activation(out=kpg[:, 0, :], in_=kpe[:, 0, :], func=AF.Square,
                                               accum_out=ss[:, 96:97]), reads=[b_kpe], writes=[b_kpg, b_ss])
            S.op("dve", lambda e, ss=ss: e.tensor_scalar(out=ss[:, 16:32], in0=ss[:, 16:32], scalar1=ss[:, 96:97], scalar2=None,
                                                  op0=ALU.add), reads=[b_ss], writes=[b_ss])
            S.op("act", lambda e, ss=ss: e.activation(out=ss[:, 32:64], in_=ss[:, 0:32], func=AF.Sqrt, bias=EPS_AP[:, 0:1],
                                               scale=1.0 / 96.0), reads=[b_ss, b_eps], writes=[b_ss])
            S.op("dve", lambda e, ss=ss: e.reciprocal(out=ss[:, 64:96], in_=ss[:, 32:64]), reads=[b_ss], writes=[b_ss])
            rsq = ss[:, 64:80]
            rsk = ss[:, 80:96]
            S.op("dve", lambda e, q3=q3, rsq=rsq: e.tensor_tensor(out=q3, in0=q3, in1=rsq.unsqueeze(2).to_broadcast([128, 16, 96]),
                                                  op=ALU.mult), reads=[b_qsb, b_ss], writes=[b_qsb])
            S.op("dve", lambda e, q3=q3: e.tensor_tensor(out=q3, in0=q3, in1=gq.unsqueeze(1).to_broadcast([128, 16, 96]),
                                                   op=ALU.mult), reads=[b_qsb, b_hn], writes=[b_qsb])
            qb, b_qb = qbr.next()
            S.op("act", lambda e, qb=qb, q3=q3: e.copy(out=qb[:, :, 0:64], in_=q3[:, :, 0:64]), reads=[b_qsb], writes=[b_qb])
            for fn, rd, wr in rope(q3[:, :, 64:96], 16, tb, qb[:, :, 64:96], ropet):
                S.op("dve", fn, reads=[b_qsb] + rd, writes=wr + ([b_qb] if not wr else []))
            S.op("dve", lambda e, tk=tk, kv3=kv3, rsk=rsk: e.tensor_tensor(out=tk[:], in0=kv3[:, :, 0:64],
                                                  in1=rsk.unsqueeze(2).to_broadcast([128, 16, 64]), op=ALU.mult),
                 reads=[b_kvsb, b_ss], writes=[b_tk])
            kb, b_kb = kbr.next()
            S.op("dve", lambda e, tk=tk, kb=kb: e.tensor_tensor(out=kb[:, :, 0:64], in0=tk[:],
                                                   in1=gk[:, 0:64].unsqueeze(1).to_broadcast([128, 16, 64]), op=ALU.mult),
                 reads=[b_tk, b_hn], writes=[b_kb])
            S.op("dve", lambda e, kpg=kpg, kpe=kpe: e.tensor_tensor(out=kpg[:, 0, :], in0=kpe[:, 0, :], in1=gk[:, 64:96], op=ALU.mult),
                 reads=[b_kpe, b_hn], writes=[b_kpg])
            kr, b_kr = krr.next()
            for fn, rd, wr in rope(kpg[:, :, :], 1, tb, kr[:, :, :], ropet):
                S.op("dve", fn, reads=[b_kpg] + rd, writes=wr + ([b_kr] if not wr else []))
            S.op("dve", lambda e, kb=kb, kr=kr, rsk=rsk: e.tensor_tensor(out=kb[:, :, 64:96],
                                                  in0=kr[:, 0, :].unsqueeze(1).to_broadcast([128, 16, 32]),
                                                  in1=rsk.unsqueeze(2).to_broadcast([128, 16, 32]), op=ALU.mult),
                 reads=[b_kr, b_ss], writes=[b_kb])
            if tb % 4 == 0:
                sh['vb'] = vbr.next()
            vb, b_vb = sh['vb']
            S.op("pool", lambda e, vb=vb, kv3=kv3, tb=tb: e.tensor_copy(out=vb[:, :, tb % 4, 0:64], in_=kv3[:, :, 64:128]),
                 reads=[b_kvsb], writes=[b_vb])
            yield
            if tb % 2 == 0:
                sh['qT'] = qTr.next()
                sh['kT'] = kTr.next()
            qTs, b_qTs = sh['qT']
            kTs, b_kTs = sh['kT']
            for (src, b_src, dstT, b_dstT) in ((qb, b_qb, qTs, b_qTs), (kb, b_kb, kTs, b_kTs)):
                for half in range(2):
                    ptr, b_ptr = psT.next()
                    pv = ptr[:, :].bitcast(BF16)
                    for hh in range(8):
                        S.op("pe", lambda e, hh=hh, pv=pv, src=src, half=half: e.transpose(
                            out=pv[0:96, hh * 128:(hh + 1) * 128], in_=src[:, half * 8 + hh, :], identity=identb[:]),
                            reads=[b_src, b_identb], writes=[b_ptr])
                    off = (tb % 2) * 128
                    eng = "act" if half == 0 else "dve"
                    if eng == "act":
                        S.op("act", lambda e, pv=pv, dstT=dstT, half=half, off=off: e.copy(
                            out=dstT[0:96, half * 8:(half + 1) * 8, off:off + 128],
                            in_=pv[0:96, :].rearrange("p (h t) -> p h t", h=8)), reads=[b_ptr], writes=[b_dstT])
                    else:
                        S.op("dve", lambda e, pv=pv, dstT=dstT, half=half, off=off: e.tensor_copy(
                            out=dstT[0:96, half * 8:(half + 1) * 8, off:off + 128],
                            in_=pv[0:96, :].rearrange("p (h t) -> p h t", h=8)), reads=[b_ptr], writes=[b_dstT])
            if tb % 2 == 1:
                t0 = (tb - 1) * 128
                S.dma(lambda e, qTs=qTs, t0=t0: e.dma_start(out=QT[:, :, t0:t0 + 256].rearrange("h p t -> p h t"),
                                                             in_=qTs[0:96, :, :]),
                      reads=[b_qTs], writes=[b_QT], sem_buf=b_qTs, eng="pool")
                S.dma(lambda e, kTs=kTs, t0=t0: e.dma_start(out=KT[:, :, t0:t0 + 256].rearrange("h p t -> p h t"),
                                                             in_=kTs[0:96, :, :]),
                      reads=[b_kTs], writes=[b_KT], sem_buf=b_kTs, eng="pool")
            if tb % 4 == 3:
                c0 = tb - 3
                S.dma(lambda e, vb=vb, c0=c0: e.dma_start(out=Vs[:, :, c0:c0 + 4, :].rearrange("h p t c -> p h (t c)"),
                                                           in_=vb[:, :, :, :].rearrange("p h t c -> p h (t c)")),
                      reads=[b_vb], writes=[b_Vs], sem_buf=b_vb, eng="pool")
        gens = {}
        for i in range(32 + 2):
            if i < 32:
                gens[i] = block(i)
                next(gens[i])
            if 0 <= i - 1 < 32:
                next(gens[i - 1])
            if 0 <= i - 2 < 32:
                next(gens[i - 2], None)
            if 0 <= i - 1 < 32:
                next(gens[i - 1])
        S.barrier()
    if stop_after <= 2:
        S.emit(nc)
        return nc

    with ExitStack() as st:
        wzr = Ring(nc, st, un("z_w"), [128, 8, 512], BF16, 4)
        wdr = Ring(nc, st, un("dt_w"), [128, 8, 64], BF16, 1)
        with ExitStack() as st2:
            stgz = Ring(nc, st2, un("z_stg"), [128, 8, 512], F32, 2)
            stgd = Ring(nc, st2, un("dt_stg"), [128, 8, 64], F32, 1)
            wz = [wload(stgz, wzr, w_in[:, C_Z + cb * 512:C_Z + (cb + 1) * 512], 8, 512, gmix) for cb in range(4)]
            wdt, b_wdt = wload(stgd, wdr, w_in[:, C_DTF:C_DTF + 64], 8, 64, gmix)
            S.barrier()
        psz = Ring(nc, st, un("z_ps"), [128, 512], F32, 4, psum=True)
        psd = Ring(nc, st, un("dt_ps"), [128, 512], F32, 2, psum=True)
        zst = Ring(nc, st, un("z_st"), [128, 2048], BF16, 3)
        for tb in range(32):
            tsl = slice(tb * 128, (tb + 1) * 128)
            zt, b_zt = zst.next()
            for cb in range(4):
                pz, b_pz = psz.next()
                w_, b_w = wz[cb]
                for kc in range(8):
                    S.op("pe", lambda e, kc=kc, pz=pz, w_=w_, tsl=tsl: e.matmul(out=pz[:], lhsT=hT[:, kc, tsl], rhs=w_[:, kc, :],
                                                                  start=(kc == 0), stop=(kc == 7)),
                         reads=[b_hT, b_w], writes=[b_pz])
                S.op("act", lambda e, pz=pz, zt=zt, cb=cb: e.activation(out=zt[:, cb * 512:(cb + 1) * 512], in_=pz[:], func=AF.Silu),
                     reads=[b_pz], writes=[b_zt])
            pd, b_pd = psd.next()
            for kc in range(8):
                S.op("pe", lambda e, kc=kc, pd=pd, tsl=tsl: e.matmul(out=pd[:, 0:64], lhsT=hT[:, kc, tsl], rhs=wdt[:, kc, :],
                                                       start=(kc == 0), stop=(kc == 7)), reads=[b_hT, b_wdt], writes=[b_pd])
            S.op("dve", lambda e, pd=pd, tb=tb: e.tensor_copy(out=dtraw[:, tb, :], in_=pd[:, 0:64]), reads=[b_pd], writes=[b_dtraw])
            S.dma(lambda e, zt=zt, tsl=tsl: e.dma_start(out=zs[tsl, :], in_=zt[:]), reads=[b_zt], writes=[b_zs],
                  sem_buf=b_zt, eng="pool")
        S.barrier()

    with ExitStack() as st:
        stg = Ring(nc, st, un("x_stg"), [128, 8, 128], F32, 3)
        wr = Ring(nc, st, un("x_w"), [128, 8, 128], BF16, 3)
        psx = Ring(nc, st, un("x_ps"), [128, 512], F32, 3, psum=True)
        psc = Ring(nc, st, un("x_pc"), [128, 512], F32, 3, psum=True)
        xpre = Ring(nc, st, un("x_pre"), [128, S_ + 4], BF16, 2)
        dgr = Ring(nc, st, un("x_dg"), [128, 5, 128], BF16, 2)
        xcst = Ring(nc, st, un("x_cst"), [128, S_], BF16, 2)
        for (t_, b_) in xpre.items:
            S.op("pool", lambda e, t_=t_: e.memset(t_[:], 0.0), writes=[b_])

        def loadx(j):
            return wload(stg, wr, w_in[:, C_XBC + j * 128:C_XBC + (j + 1) * 128], 8, 128, gmix)

        def compx(j, h):
            w_, b_w = h
            xp, b_xp = xpre.next()
            dg, b_dg = dgr.next()
            for k in range(5):
                S.op("dve", lambda e, k=k, dg=dg, j=j: e.tensor_scalar(out=dg[:, k, :], in0=identf, scalar1=convw_t[:, j, k:k + 1],
                                                           scalar2=None, op0=ALU.mult),
                     reads=[b_mats, b_convw], writes=[b_dg])
            for tb in range(8):
                px, b_px = psx.next()
                ts = slice(tb * 512, (tb + 1) * 512)
                for kc in range(8):
                    S.op("pe", lambda e, kc=kc, px=px, w_=w_, ts=ts: e.matmul(out=px[:], lhsT=w_[:, kc, :], rhs=hT[:, kc, ts],
                                                                start=(kc == 0), stop=(kc == 7)),
                         reads=[b_w, b_hT], writes=[b_px])
                if tb % 2 == 0:
                    S.op("act", lambda e, px=px, xp=xp, tb=tb: e.copy(out=xp[:, 2 + tb * 512:2 + (tb + 1) * 512], in_=px[:]),
                         reads=[b_px], writes=[b_xp])
                else:
                    S.op("dve", lambda e, px=px, xp=xp, tb=tb: e.tensor_copy(out=xp[:, 2 + tb * 512:2 + (tb + 1) * 512], in_=px[:]),
                         reads=[b_px], writes=[b_xp])
            xc_, b_xc = xcst.next()
            for tb in range(8):
                pc, b_pc = psc.next()
                for k in range(5):
                    S.op("pe", lambda e, k=k, pc=pc, dg=dg, xp=xp, tb=tb: e.matmul(
                        out=pc[:], lhsT=dg[:, k, :], rhs=xp[:, tb * 512 + k:tb * 512 + k + 512],
                        start=(k == 0), stop=(k == 4)), reads=[b_dg, b_xp], writes=[b_pc])
                S.op("act", lambda e, pc=pc, xc_=xc_, tb=tb, j=j: e.activation(out=xc_[:, tb * 512:(tb + 1) * 512], in_=pc[:],
                                                               func=AF.Silu, bias=convb_t[:, j:j + 1]),
                     reads=[b_pc, b_convb], writes=[b_xc])
            for q4 in range(4):
                S.dma(lambda e, xc_=xc_, j=j, q4=q4: e.dma_start(
                    out=xcs[q4 * 8:(q4 + 1) * 8, :, j, :].rearrange("c p t -> p c t"),
                    in_=xc_[:, q4 * 1024:(q4 + 1) * 1024].rearrange("p (c t) -> p c t", c=8)),
                    reads=[b_xc], writes=[b_xcs], sem_buf=b_xc, eng="pool")

        pipeline(24, loadx, compx, 2)

        def loadg(j):
            return wload(stg, wr, w_in[:, C_GA + j * 128:C_GA + (j + 1) * 128], 8, 128, gmix)

        def compg(j, h):
            w_, b_w = h
            gt_, b_gt = xcst.next()
            for tb in range(8):
                px, b_px = psx.next()
                ts = slice(tb * 512, (tb + 1) * 512)
                for kc in range(8):
                    S.op("pe", lambda e, kc=kc, px=px, w_=w_, ts=ts: e.matmul(out=px[:], lhsT=w_[:, kc, :], rhs=hT[:, kc, ts],
                                                                start=(kc == 0), stop=(kc == 7)),
                         reads=[b_w, b_hT], writes=[b_px])
                S.op("act", lambda e, px=px, gt_=gt_, ts=ts: e.activation(out=gt_[:, ts], in_=px[:], func=AF.Sigmoid),
                     reads=[b_px], writes=[b_gt])
            S.dma(lambda e, gt_=gt_, j=j: e.dma_start(out=gts[j * 128:(j + 1) * 128, :], in_=gt_[:]),
                  reads=[b_gt], writes=[b_gts], sem_buf=b_gt, eng="pool")

        pipeline(16, loadg, compg, 2)
        S.barrier()
    if stop_after <= 3:
        S.emit(nc)
        return nc
    hst.close()

    def ssd_pass(fwd):
        with ExitStack() as st:
            AT = lambda n, shp, dt=F32: st.enter_context(nc.sbuf_tensor(un(n), shp, dt))
            dt_all = AT("dt_all", [128, 32, 32]); b_dt = Buf()
            da_all = AT("da_all", [128, 32, 32]); b_da = Buf()
            P_all = AT("P_all", [128, 32, 32]); b_P = Buf()
            bias_all = AT("bias_all", [128, 32, 32]); b_bias = Buf()
            wgt = AT("wgt", [128, 32, 32]); b_wgt = Buf()
            scl = AT("scl", [128, 32, 32]); b_scl = Buf()
            cdc = AT("cdc", [128, 32, 32]); b_cdc = Buf()
            tot = AT("tot", [128, 32, 32]); b_tot = Buf()
            nega = AT("nega", [128, 32]); b_nega = Buf()
            tmpa = AT("tmpa", [128, 32, 32]); b_tmpa = Buf()
            Sf = AT("Sf", [128, 2048]); b_Sf = [Buf() for _ in range(4)]
            Sbf = AT("Sbf", [128, 2048], BF16); b_Sbf = [Buf() for _ in range(4)]
            off = 0 if fwd else 32
            alog = ssp_t[:, off:off + 32]
            dtb = ssp_t[:, 64 + off:96 + off]
            dsk = ssp_t[:, 128:160]
            Uc = Umat if fwd else Ustr
            midx = 0 if fwd else 1
            ptA = Ring(nc, st, un("s_ptA"), [128, 512], F32, 1, psum=True)
            segb = Ring(nc, st, un("s_seg"), [128, 512], F32, 3, psum=True)
            pyr = Ring(nc, st, un("s_py"), [128, 512], F32, 2, psum=True)
            por = Ring(nc, st, un("s_po"), [128, 512], F32, 1, psum=True)
            pstr = Ring(nc, st, un("s_pst"), [128, 512], F32, 1, psum=True)
            segs = []
            for (t_, _b) in segb.items:
                for q in range(4):
                    segs.append((t_[:, q * 128:(q + 1) * 128], Buf()))
            segi = [0]
            flat = lambda t_: t_[:, :, :].rearrange("p a b -> p (a b)")
            S.op("pool", lambda e: e.memset(Sf[:], 0.0), writes=b_Sf)
            S.op("pool", lambda e: e.memset(Sbf[:], 0.0), writes=b_Sbf)
            S.op("act", lambda e: e.activation(out=nega[:], in_=alog, func=AF.Exp), reads=[b_ssp], writes=[b_nega])
            S.op("dve", lambda e: e.tensor_scalar(out=nega[:], in0=nega[:], scalar1=-1.0, scalar2=None, op0=ALU.mult),
                 reads=[b_nega], writes=[b_nega])
            S.op("dve", lambda e: e.tensor_tensor(out=tmpa[:], in0=dtraw[:, :, off:off + 32],
                                                  in1=dtb.unsqueeze(1).to_broadcast([128, 32, 32]), op=ALU.add),
                 reads=[b_dtraw, b_ssp], writes=[b_tmpa])
            S.op("act", lambda e: e.activation(out=tmpa[:], in_=tmpa[:], func=AF.Exp), reads=[b_tmpa], writes=[b_tmpa])
            S.op("act", lambda e: e.activation(out=dt_all[:], in_=tmpa[:], func=AF.Ln, bias=1.0), reads=[b_tmpa], writes=[b_dt])
            S.op("dve", lambda e: e.tensor_tensor(out=da_all[:], in0=dt_all[:], in1=nega[:].unsqueeze(1).to_broadcast([128, 32, 32]),
                                                  op=ALU.mult), reads=[b_dt, b_nega], writes=[b_da])
            for half in range(2):
                pp, b_pp = ptA.next()
                S.op("pe", lambda e, pp=pp, half=half: e.matmul(out=pp[:], lhsT=Uc, rhs=flat(da_all)[:, half * 512:(half + 1) * 512],
                                                                start=True, stop=True), reads=[b_mats, b_da], writes=[b_pp])
                S.op("dve", lambda e, pp=pp, half=half: e.tensor_copy(out=flat(P_all)[:, half * 512:(half + 1) * 512], in_=pp[:]),
                     reads=[b_pp], writes=[b_P])
            for half in range(2):
                pp, b_pp = ptA.next()
                S.op("pe", lambda e, pp=pp, half=half: e.matmul(out=pp[:], lhsT=onesf, rhs=flat(da_all)[:, half * 512:(half + 1) * 512],
                                                                start=True, stop=True), reads=[b_mats, b_da], writes=[b_pp])
                S.op("dve", lambda e, pp=pp, half=half: e.tensor_copy(out=flat(tot)[:, half * 512:(half + 1) * 512], in_=pp[:]),
                     reads=[b_pp], writes=[b_tot])
            S.op("dve", lambda e: e.tensor_tensor(out=tmpa[:], in0=tot[:], in1=P_all[:], op=ALU.subtract),
                 reads=[b_tot, b_P, b_dt], writes=[b_tmpa])
            e1, b_e1 = (scl, b_scl) if fwd else (wgt, b_wgt)
            e2, b_e2 = (wgt, b_wgt) if fwd else (scl, b_scl)
            S.op("act", lambda e: e.activation(out=e1[:], in_=P_all[:], func=AF.Exp), reads=[b_P], writes=[b_e1])
            S.op("act", lambda e: e.activation(out=e2[:], in_=tmpa[:], func=AF.Exp), reads=[b_tmpa], writes=[b_e2])
            S.op("act", lambda e: e.activation(out=cdc[:], in_=tot[:], func=AF.Exp), reads=[b_tot], writes=[b_cdc])
            S.op("dve", lambda e: e.tensor_scalar(out=bias_all[:], in0=P_all[:], scalar1=(-1.0 if fwd else 1.0), scalar2=None,
                                                  op0=ALU.mult), reads=[b_P], writes=[b_bias])

            xcr = Ring(nc, st, un("s_xc"), [128, 24, 128], BF16, 3)
            xsr = Ring(nc, st, un("s_xs"), [128, 2048], BF16, 2)
            Btr = Ring(nc, st, un("s_Bt"), [128, 512], BF16, 2)
            cbr = Ring(nc, st, un("s_cb"), [128, 512], F32, 2)
            xdtr = Ring(nc, st, un("s_xdt"), [128, 2048], BF16, 2)
            xwr = Ring(nc, st, un("s_xw"), [128, 2048], BF16, 2)
            decr = Ring(nc, st, un("s_dec"), [128, 128], F32, 12)
            MTr = Ring(nc, st, un("s_MT"), [128, 128], BF16, 12)
            yaccr = Ring(nc, st, un("s_ya"), [128, 2048], F32, 2)
            tmpr = Ring(nc, st, un("s_tmp"), [128, 512], F32, 2)
            if fwd:
                dskr = Ring(nc, st, un("s_dsk"), [128, 2048], F32, 1)
            else:
                zr = Ring(nc, st, un("s_z"), [128, 2048], BF16, 4)
                yfr = Ring(nc, st, un("s_yf"), [128, 2048], F32, 4)
                jkr = Ring(nc, st, un("s_jk"), [128, 512], BF16, 1)
                st4r = Ring(nc, st, un("s_st4"), [128, 12], F32, 2)
                mbr = Ring(nc, st, un("s_mb"), [128, 2048], BF16, 2)
                mstr = Ring(nc, st, un("s_mst"), [128, 16, 128], BF16, 2)
            order = list(range(32)) if fwd else list(range(31, -1, -1))

            def load(ci):
                c = order[ci]
                xc_, b_xc = xcr.next()
                S.dma(lambda e: e.dma_start(out=xc_[:], in_=xcs[c, :, :, :]), reads=[b_xcs], writes=[b_xc], sem_buf=b_xc)
                if fwd:
                    return (xc_, b_xc)
                z_, b_z = zr.next()
                yf_, b_yf = yfr.next()
                S.dma(lambda e: e.dma_start(out=z_[:], in_=zs[c * 128:(c + 1) * 128, :]), reads=[b_zs], writes=[b_z], sem_buf=b_z)
                S.dma(lambda e: e.dma_start(out=yf_[:], in_=yfs[c * 128:(c + 1) * 128, :]), reads=[b_yfs], writes=[b_yf], sem_buf=b_yf)
                return (xc_, b_xc, z_, b_z, yf_, b_yf)

            def prologue(ci, h):
                c = order[ci]
                xc_, b_xc = h[0], h[1]
                xs, b_xs = xsr.next()
                for half in range(2):
                    pt, b_pt = ptA.next()
                    pv = pt[:, :].bitcast(BF16)
                    for jj in range(8):
                        S.op("pe", lambda e, pv=pv, jj=jj, half=half: e.transpose(out=pv[:, jj * 128:(jj + 1) * 128],
                                                                                 in_=xc_[:, half * 8 + jj, :], identity=identb[:]),
                             reads=[b_xc, b_identb], writes=[b_pt])
                    if half == 0:
                        S.op("act", lambda e, pv=pv: e.copy(out=xs[:, 0:1024], in_=pv[:, :]), reads=[b_pt], writes=[b_xs])
                    else:
                        S.op("dve", lambda e, pv=pv: e.tensor_copy(out=xs[:, 1024:2048], in_=pv[:, :]), reads=[b_pt], writes=[b_xs])
                pt, b_pt = ptA.next()
                pvb = pt[:, :].bitcast(BF16)
                for g in range(4):
                    S.op("pe", lambda e, g=g: e.transpose(out=pvb[:, g * 128:(g + 1) * 128], in_=xc_[:, 16 + g, :], identity=identb[:]),
                         reads=[b_xc, b_identb], writes=[b_pt])
                Bt, b_Bt = Btr.next()
                S.op("dve", lambda e: e.tensor_copy(out=Bt[:], in_=pvb[:, 0:512]), reads=[b_pt], writes=[b_Bt])
                pcb, b_pcb = ptA.next()
                for g in range(4):
                    S.op("pe", lambda e, g=g: e.matmul(out=pcb[:, g * 128:(g + 1) * 128], lhsT=xc_[:, 16 + g, :], rhs=xc_[:, 20 + g, :],
                                                       start=True, stop=True), reads=[b_xc], writes=[b_pcb])
                cbT, b_cbT = cbr.next()
                S.op("act", lambda e: e.copy(out=cbT[:], in_=pcb[:]), reads=[b_pcb], writes=[b_cbT])
                xdt, b_xdt = xdtr.next()
                xw, b_xw = xwr.next()
                v3 = lambda t_: t_[:, :].rearrange("p (h d) -> p h d", h=32)
                S.op("dve", lambda e: e.tensor_tensor(out=v3(xdt), in0=v3(xs), in1=dt_all[:, c, :].unsqueeze(2).to_broadcast([128, 32, 64]),
                                                      op=ALU.mult), reads=[b_xs, b_dt], writes=[b_xdt])
                S.op("pool", lambda e: e.tensor_tensor(out=v3(xw), in0=v3(xdt), in1=wgt[:, c, :].unsqueeze(2).to_broadcast([128, 32, 64]),
                                                       op=ALU.mult), reads=[b_xdt, b_wgt], writes=[b_xw])
                return (xs, b_xs, Bt, b_Bt, cbT, b_cbT, xdt, b_xdt, xw, b_xw)

            def comp(ci, h, pr):
                c = order[ci]
                xc_, b_xc = h[0], h[1]
                xs, b_xs, Bt, b_Bt, cbT, b_cbT, xdt, b_xdt, xw, b_xw = pr
                v3 = lambda t_: t_[:, :].rearrange("p (h d) -> p h d", h=32)
                g8 = lambda t_: t_.rearrange("p (h d) -> p h d", h=8)
                ya, b_ya = yaccr.next()
                LAGH = 4
                mts = {}
                cur = {}

                def stageA(bi):
                    sb_, b_sb = segb.next()
                    for q in range(4):
                        h_ = bi * 4 + q
                        seg = sb_[:, q * 128:(q + 1) * 128]
                        S.op("pe", lambda e, seg=seg, h_=h_: e.matmul(out=seg, lhsT=da_all[:, c, h_:h_ + 1].to_broadcast([128, 128]),
                                                                      rhs=Uc, start=True, stop=False),
                             reads=[b_da, b_mats], writes=[b_sb])
                        S.op("pe", lambda e, seg=seg: e.matmul(out=seg, lhsT=identb[:], rhs=negm[:, midx, :], start=False, stop=True),
                             reads=[b_identb, b_negm], writes=[b_sb])
                    for q in range(4):
                        h_ = bi * 4 + q
                        g = h_ // 8
                        seg = sb_[:, q * 128:(q + 1) * 128]
                        dec, b_dec = decr.next()
                        S.op("act", lambda e, seg=seg, dec=dec, h_=h_: e.activation(out=dec[:], in_=seg, func=AF.Exp,
                                                                                   bias=bias_all[:, c, h_:h_ + 1],
                                                                                   scale=(1.0 if fwd else -1.0)),
                             reads=[b_sb, b_bias], writes=[b_dec])
                        MT, b_MT = MTr.next()
                        S.op("dve", lambda e, dec=dec, MT=MT, g=g: e.tensor_tensor(
                            out=MT[:], in0=dec[:], in1=cbT[:, g * 128:(g + 1) * 128], op=ALU.mult),
                            reads=[b_dec, b_cbT], writes=[b_MT])
                        mts[h_] = (MT, b_MT)

                def stageB(h_):
                    g = h_ // 8
                    hh = h_ % 8
                    if hh == 0:
                        cur[0] = pyr.next()
                    py, b_py = cur[0]
                    MT, b_MT = mts.pop(h_)
                    S.op("pe", lambda e, MT=MT, py=py, hh=hh, h_=h_: e.matmul(out=py[:, hh * 64:(hh + 1) * 64], lhsT=MT[:],
                                                                             rhs=xdt[:, h_ * 64:(h_ + 1) * 64], start=True, stop=True),
                         reads=[b_MT, b_xdt], writes=[b_py])
                    if hh != 7:
                        return
                    po, b_po = por.next()
                    S.op("pe", lambda e, po=po, g=g: e.matmul(out=po[:], lhsT=xc_[:, 20 + g, :], rhs=Sbf[:, g * 512:(g + 1) * 512],
                                                              start=True, stop=True), reads=[b_xc, b_Sbf[g]], writes=[b_po])
                    pst, b_pst = pstr.next()
                    S.op("pe", lambda e, pst=pst, g=g: e.matmul(out=pst[:], lhsT=Bt[:, g * 128:(g + 1) * 128], rhs=xw[:, g * 512:(g + 1) * 512],
                                                                start=True, stop=True), reads=[b_Bt, b_xw], writes=[b_pst])
                    tmp, b_tmp = tmpr.next()
                    S.op("dve", lambda e, po=po, tmp=tmp, g=g: e.tensor_tensor(
                        out=g8(tmp[:, :]), in0=g8(po[:, :]), in1=scl[:, c, g * 8:(g + 1) * 8].unsqueeze(2).to_broadcast([128, 8, 64]),
                        op=ALU.mult), reads=[b_po, b_scl], writes=[b_tmp])
                    S.op("dve", lambda e, py=py, tmp=tmp, g=g: e.tensor_tensor(out=ya[:, g * 512:(g + 1) * 512], in0=py[:], in1=tmp[:],
                                                                              op=ALU.add), reads=[b_py, b_tmp], writes=[b_ya])
                    S.op("pool", lambda e, g=g: e.tensor_tensor(
                        out=g8(Sf[:, g * 512:(g + 1) * 512]), in0=g8(Sf[:, g * 512:(g + 1) * 512]),
                        in1=cdc[:, c, g * 8:(g + 1) * 8].unsqueeze(2).to_broadcast([128, 8, 64]), op=ALU.mult),
                        reads=[b_Sf[g], b_cdc], writes=[b_Sf[g]])
                    S.op("dve", lambda e, pst=pst, g=g: e.tensor_tensor(out=Sf[:, g * 512:(g + 1) * 512], in0=pst[:],
                                                                       in1=Sf[:, g * 512:(g + 1) * 512], op=ALU.add),
                         reads=[b_pst, b_Sf[g]], writes=[b_Sf[g]])
                    S.op("act", lambda e, g=g: e.copy(out=Sbf[:, g * 512:(g + 1) * 512], in_=Sf[:, g * 512:(g + 1) * 512]),
                         reads=[b_Sf[g]], writes=[b_Sbf[g]])

                for k in range(8 + 2):
                    if k < 8:
                        stageA(k)
                    if k >= 2:
                        for q in range(4):
                            stageB((k - 2) * 4 + q)
                if fwd:
                    dk, b_dk = dskr.next()
                    S.op("pool", lambda e: e.tensor_tensor(out=v3(dk), in0=v3(xs), in1=dsk.unsqueeze(2).to_broadcast([128, 32, 64]),
                                                           op=ALU.mult), reads=[b_xs, b_ssp], writes=[b_dk])
                    S.op("pool", lambda e: e.tensor_tensor(out=ya[:], in0=ya[:], in1=dk[:], op=ALU.add),
                         reads=[b_ya, b_dk], writes=[b_ya])
                    S.dma(lambda e: e.dma_start(out=yfs[c * 128:(c + 1) * 128, :], in_=ya[:]), reads=[b_ya], writes=[b_yfs],
                          sem_buf=b_ya, eng="pool")
                    return
                def epi():
                    z_, b_z, yf_, b_yf = h[2], h[3], h[4], h[5]
                    if dbg:
                        S.dma(lambda e: e.dma_start(out=ybs[c * 128:(c + 1) * 128, :], in_=ya[:]), reads=[b_ya], writes=[b_ybs],
                              sem_buf=b_ya, eng="pool")
                    S.op("pool", lambda e: e.tensor_tensor(out=ya[:], in0=ya[:], in1=yf_[:], op=ALU.add), reads=[b_ya, b_yf], writes=[b_ya])
                    S.op("dve", lambda e: e.tensor_tensor(out=ya[:], in0=ya[:], in1=z_[:], op=ALU.mult), reads=[b_ya, b_z], writes=[b_ya])
                    jk, b_jk = jkr.next()
                    s4, b_s4 = st4r.next()
                    for g in range(4):
                        S.op("act", lambda e, g=g: e.activation(out=jk[:], in_=ya[:, g * 512:(g + 1) * 512], func=AF.Square,
                                                                scale=1.0 / math.sqrt(512.0), accum_out=s4[:, g:g + 1]),
                             reads=[b_ya], writes=[b_jk, b_s4])
                    S.op("act", lambda e: e.activation(out=s4[:, 4:8], in_=s4[:, 0:4], func=AF.Sqrt, bias=EPS_AP[:, 0:1]),
                         reads=[b_s4, b_eps], writes=[b_s4])
                    S.op("dve", lambda e: e.reciprocal(out=s4[:, 8:12], in_=s4[:, 4:8]), reads=[b_s4], writes=[b_s4])
                    mb, b_mb = mbr.next()
                    for g in range(4):
                        S.op("dve", lambda e, g=g: e.tensor_scalar(out=mb[:, g * 512:(g + 1) * 512], in0=ya[:, g * 512:(g + 1) * 512],
                                                                   scalar1=s4[:, 8 + g:9 + g], scalar2=None, op0=ALU.mult),
                             reads=[b_ya, b_s4], writes=[b_mb])
                    mst, b_mst = mstr.next()
                    for half in range(2):
                        pt, b_pt = ptA.next()
                        pv = pt[:, :].bitcast(BF16)
                        for jj in range(8):
                            j = half * 8 + jj
                            S.op("pe", lambda e, pv=pv, jj=jj, j=j: e.transpose(out=pv[:, jj * 128:(jj + 1) * 128],
                                                                               in_=mb[:, j * 128:(j + 1) * 128], identity=identb[:]),
                                 reads=[b_mb, b_identb], writes=[b_pt])
                        S.op("act", lambda e, pv=pv, half=half: e.copy(out=mst[:, half * 8:(half + 1) * 8, :],
                                                                       in_=pv[:, :].rearrange("p (j t) -> p j t", j=8)),
                             reads=[b_pt], writes=[b_mst])
                    for half in range(2):
                        S.dma(lambda e, half=half: e.dma_start(
                            out=mTs[half * 1024:(half + 1) * 1024, c * 128:(c + 1) * 128].rearrange("(j p) t -> p j t", p=128),
                            in_=mst[:, half * 8:(half + 1) * 8, :]), reads=[b_mst], writes=[b_mTs], sem_buf=b_mst, eng="pool")

                if pend_epi:
                    pend_epi.pop()()
                pend_epi.append(epi)

            pend_epi = []
            hs = {}
            prs = {}
            for i in range(32 + 2):
                if i < 32:
                    hs[i] = load(i)
                if 1 <= i <= 32:
                    prs[i - 1] = prologue(i - 1, hs[i - 1])
                if i >= 2:
                    comp(i - 2, hs.pop(i - 2), prs.pop(i - 2))
            if pend_epi:
                pend_epi.pop()()
        S.barrier()

    ssd_pass(True)
    if stop_after <= 4 and stop_after == 4:
        pass
    ssd_pass(False)
    if stop_after <= 4:
        S.emit(nc)
        return nc

    mw = ExitStack()
    wpa = mw.enter_context(nc.sbuf_tensor("m_wpa", [128, 8, D_], BF16)); b_wpa = Buf()
    wpb = mw.enter_context(nc.sbuf_tensor("m_wpb", [128, 16, D_], BF16)); b_wpb = Buf()
    wo = mw.enter_context(nc.sbuf_tensor("m_wo", [128, 8, D_], BF16)); b_wo = Buf()
    mws = ExitStack()
    ms8 = mws.enter_context(nc.sbuf_tensor("m_s8", [128, 8, D_], F32)); b_ms8 = Buf()

    def preload_merge_weights():
        jobs = [(w_pa[:, :], wpa[:, :, :], b_wpa, None), (w_pb[0:1024, :], wpb[:, 0:8, :], b_wpb, gssm_t[:, 0:8]),
                (w_pb[1024:2048, :], wpb[:, 8:16, :], b_wpb, gssm_t[:, 8:16]), (w_o[:, :], wo[:, :, :], b_wo, None)]
        for src, dst, b_dst, g_ in jobs:
            S.dma(lambda e, src=src: e.dma_start(out=ms8[:], in_=src.rearrange("(kc p) n -> p kc n", p=128)),
                  writes=[b_ms8], sem_buf=b_ms8)
            if g_ is None:
                S.op("pool", lambda e, dst=dst: e.tensor_copy(out=dst, in_=ms8[:]), reads=[b_ms8], writes=[b_dst])
            else:
                S.op("pool", lambda e, dst=dst, g_=g_: e.tensor_tensor(out=dst, in0=ms8[:], in1=g_.unsqueeze(2).to_broadcast([128, 8, D_]),
                                                                      op=ALU.mult), reads=[b_ms8, b_gssm], writes=[b_dst])

    with ExitStack() as st:
        ktr = Ring(nc, st, un("t_k"), [128, S_], BF16, 2)
        qtr = Ring(nc, st, un("t_q"), [128, S_], BF16, 2)
        vtr = Ring(nc, st, un("t_v"), [128, 32, 65], BF16, 2)
        psS = Ring(nc, st, un("t_ps"), [128, 1024], F32, 3, psum=True)
        psO = Ring(nc, st, un("t_po"), [128, 1024], F32, 1, psum=True)
        pTr = Ring(nc, st, un("t_pT"), [128, 1024], BF16, 4)
        rdr = Ring(nc, st, un("t_rd"), [128, 1024], F32, 2)
        osr = Ring(nc, st, un("t_os"), [128, 1024], F32, 2)
        aor = Ring(nc, st, un("t_ao"), [128, S_], BF16, 2)
        sc = 1.0 / math.sqrt(96.0)
        LAG = 2
        tiles = {}

        def ensure(h_):
            if h_ >= NH or h_ in tiles:
                return
            kt, b_kt = ktr.next()
            qt, b_qt = qtr.next()
            vt, b_vt = vtr.next()
            S.dma(lambda e: e.dma_start(out=kt[0:96, :], in_=KT[h_, :, :]), reads=[b_KT], writes=[b_kt], sem_buf=b_kt)
            S.dma(lambda e: e.dma_start(out=qt[0:96, :], in_=QT[h_, :, :]), reads=[b_QT], writes=[b_qt], sem_buf=b_qt)
            S.dma(lambda e: e.dma_start(out=vt[:], in_=Vs[h_, :, :, :]), reads=[b_Vs], writes=[b_vt], sem_buf=b_vt)
            tiles[h_] = (kt, b_kt, qt, b_qt, vt, b_vt)

        steps = [(h_, sb, kc) for h_ in range(NH) for sb in range(4) for kc in range(32)]
        pend = {}
        cur_po = {}
        cur_ao = {}
        ensure(0)
        preload_merge_weights()
        for i in range(len(steps) + LAG):
            if i < len(steps):
                h_, sb, kc = steps[i]
                kt, b_kt, qt, b_qt, vt, b_vt = tiles[h_]
                ps, b_ps = psS.next()
                for u in range(2):
                    S.op("pe", lambda e, ps=ps, kc=kc, sb=sb, u=u, kt=kt, qt=qt: e.matmul(
                        out=ps[:, u * 512:(u + 1) * 512], lhsT=kt[0:96, kc * 128:(kc + 1) * 128],
                        rhs=qt[0:96, sb * 1024 + u * 512:sb * 1024 + (u + 1) * 512], start=True, stop=True),
                        reads=[b_kt, b_qt], writes=[b_ps])
                pT, b_pT = pTr.next()
                S.op("act", lambda e, ps=ps, pT=pT: e.activation(out=pT[:], in_=ps[:], func=AF.Exp, scale=sc),
                     reads=[b_ps], writes=[b_pT])
                pend[i] = (pT, b_pT)
            if i >= LAG:
                h_, sb, kc = steps[i - LAG]
                kt, b_kt, qt, b_qt, vt, b_vt = tiles[h_]
                pT, b_pT = pend.pop(i - LAG)
                if kc == 0:
                    cur_po[0] = psO.next()
                    if sb == 0:
                        cur_ao[0] = aor.next()
                        ensure(h_ + 1)
                po, b_po = cur_po[0]
                ao, b_ao = cur_ao[0]
                for u in range(2):
                    S.op("pe", lambda e, po=po, pT=pT, kc=kc, u=u, vt=vt: e.matmul(
                        out=po[0:65, u * 512:(u + 1) * 512], lhsT=vt[:, kc, :], rhs=pT[:, u * 512:(u + 1) * 512],
                        start=(kc == 0), stop=(kc == 31)), reads=[b_vt, b_pT], writes=[b_po])
                if kc == 31:
                    osb, b_osb = osr.next()
                    S.op("dve", lambda e, po=po, osb=osb: e.tensor_copy(out=osb[0:65, :], in_=po[0:65, :]), reads=[b_po], writes=[b_osb])
                    rd, b_rd = rdr.next()
                    S.op("dve", lambda e, osb=osb, rd=rd: e.reciprocal(out=rd[64:65, :], in_=osb[64:65, :]), reads=[b_osb], writes=[b_rd])
                    pb, b_pb = psS.next()
                    for u in range(2):
                        S.op("pe", lambda e, pb=pb, rd=rd, u=u: e.matmul(out=pb[0:64, u * 512:(u + 1) * 512], lhsT=mats[64:65, 2, 0:64],
                                                                         rhs=rd[64:65, u * 512:(u + 1) * 512], start=True, stop=True),
                             reads=[b_mats, b_rd], writes=[b_pb])
                    qs = slice(sb * 1024, (sb + 1) * 1024)
                    S.op("dve", lambda e, pb=pb, osb=osb, qs=qs, ao=ao: e.tensor_tensor(
                        out=ao[0:64, qs], in0=pb[0:64, :], in1=osb[0:64, :], op=ALU.mult),
                        reads=[b_pb, b_osb], writes=[b_ao])
                    if sb == 3:
                        S.dma(lambda e, h_=h_, ao=ao: e.dma_start(out=aTs[h_ * 64:(h_ + 1) * 64, :], in_=ao[0:64, :]),
                              reads=[b_ao], writes=[b_aTs], sem_buf=b_ao, eng="pool")
        S.barrier()
    if stop_after <= 5:
        S.emit(nc)
        return nc

    mws.close()
    with ExitStack() as st:
        atr = Ring(nc, st, un("m_at"), [128, 8, 512], BF16, 2)
        mtr = Ring(nc, st, un("m_mt"), [128, 16, 512], BF16, 2)
        gtr = Ring(nc, st, un("m_gt"), [128, 16, 512], BF16, 2)
        mgr = Ring(nc, st, un("m_mg"), [128, 8, 512], BF16, 2)
        t1r = Ring(nc, st, un("m_t1"), [128, 512], F32, 2)
        t2r = Ring(nc, st, un("m_t2"), [128, 512], F32, 2)
        xr = Ring(nc, st, un("m_x"), [128, D_], F32, 3)
        ps = Ring(nc, st, un("m_ps"), [128, 512], F32, 6, psum=True)
        def loadm(t):
            at, b_at = atr.next()
            mt, b_mt = mtr.next()
            gt_, b_gt = gtr.next()
            ts = slice(t * 512, (t + 1) * 512)
            S.dma(lambda e: e.dma_start(out=at[:], in_=aTs[:, ts].rearrange("(k p) t -> p k t", p=128)), reads=[b_aTs], writes=[b_at], sem_buf=b_at)
            S.dma(lambda e: e.dma_start(out=mt[:], in_=mTs[:, ts].rearrange("(k p) t -> p k t", p=128)), reads=[b_mTs], writes=[b_mt], sem_buf=b_mt)
            S.dma(lambda e: e.dma_start(out=gt_[:], in_=gts[:, ts].rearrange("(k p) t -> p k t", p=128)), reads=[b_gts], writes=[b_gt], sem_buf=b_gt)
            return (at, b_at, mt, b_mt, gt_, b_gt)

        def compm(t, hd):
            at, b_at, mt, b_mt, gt_, b_gt = hd
            mg, b_mg = mgr.next()
            for dc in range(8):
                pa, b_pa = ps.next()
                pb, b_pb = ps.next()
                for kc in range(8):
                    S.op("pe", lambda e, pa=pa, kc=kc, dc=dc: e.matmul(out=pa[:], lhsT=wpa[:, kc, dc * 128:(dc + 1) * 128], rhs=at[:, kc, :],
                                                                       start=(kc == 0), stop=(kc == 7)), reads=[b_wpa, b_at], writes=[b_pa])
                for kc in range(16):
                    S.op("pe", lambda e, pb=pb, kc=kc, dc=dc: e.matmul(out=pb[:], lhsT=wpb[:, kc, dc * 128:(dc + 1) * 128], rhs=mt[:, kc, :],
                                                                       start=(kc == 0), stop=(kc == 15)), reads=[b_wpb, b_mt], writes=[b_pb])
                t1, b_t1 = t1r.next()
                t2, b_t2 = t2r.next()
                S.op("dve", lambda e, pa=pa, t1=t1, dc=dc: e.tensor_tensor(out=t1[:], in0=pa[:], in1=gt_[:, dc, :], op=ALU.mult),
                     reads=[b_pa, b_gt], writes=[b_t1])
                S.op("dve", lambda e, pb=pb, t2=t2, dc=dc: e.tensor_tensor(out=t2[:], in0=pb[:], in1=gt_[:, 8 + dc, :], op=ALU.mult),
                     reads=[b_pb, b_gt], writes=[b_t2])
                S.op("pool", lambda e, t1=t1, t2=t2, dc=dc: e.tensor_tensor(out=mg[:, dc, :], in0=t1[:], in1=t2[:], op=ALU.add),
                     reads=[b_t1, b_t2], writes=[b_mg])
            for sb in range(4):
                tb = t * 4 + sb
                xt, b_xt = xr.next()
                S.dma(lambda e, xt=xt, tb=tb: e.dma_start(out=xt[:], in_=x1s[tb * 128:(tb + 1) * 128, :]),
                      reads=[b_x1s], writes=[b_xt], sem_buf=b_xt)
                for half in range(2):
                    p, b_p = ps.next()
                    for kc in range(8):
                        S.op("pe", lambda e, p=p, kc=kc, sb=sb, half=half: e.matmul(
                            out=p[:], lhsT=mg[:, kc, sb * 128:(sb + 1) * 128], rhs=wo[:, kc, half * 512:(half + 1) * 512],
                            start=(kc == 0), stop=(kc == 7)), reads=[b_mg, b_wo], writes=[b_p])
                    S.op("dve", lambda e, p=p, xt=xt, half=half: e.tensor_tensor(out=xt[:, half * 512:(half + 1) * 512], in0=p[:],
                                                                                in1=xt[:, half * 512:(half + 1) * 512], op=ALU.add),
                         reads=[b_p, b_xt], writes=[b_xt])
                S.dma(lambda e, xt=xt, tb=tb: e.dma_start(out=x2s[tb * 128:(tb + 1) * 128, :], in_=xt[:]),
                      reads=[b_xt], writes=[b_x2s], sem_buf=b_xt, eng="pool")

        pipeline(8, loadm, compm, 1)
        S.barrier()
    mw.close()
    hst2 = ExitStack()
    hT2 = hst2.enter_context(nc.sbuf_tensor("hT2", [128, 8, S_], BF16)); b_hT2 = Buf("hT2")
    norm_phase(x2s, b_x2s, hT2, b_hT2)

    with ExitStack() as fst:
        wd2 = fst.enter_context(nc.sbuf_tensor("wd2", [128, NFF, D_], BF16)); b_wd2 = Buf()
        ffn_gateup(w_g2, w_u2, 2, hT2, b_hT2, w_d2, wd2, b_wd2)
        ffn_down(w_d2, x2s, b_x2s, y_out, b_yout, None, wd2, b_wd2)
    S.emit(nc)
    return nc


def _fm(v, kc):
    return np.ascontiguousarray(np.asarray(v, np.float32).reshape(kc, 128).T)


_CACHE = {}


def consts():
    ii = np.arange(128)
    U = (ii[:, None] <= ii[None, :]).astype(np.float32)
    Us = (ii[:, None] < ii[None, :]).astype(np.float32)
    ones = np.ones((128, 128), np.float32)
    I = np.eye(128, dtype=np.float32)
    mats = np.ascontiguousarray(np.stack([U, Us, ones, I], axis=1))
    negf = np.where(ii[:, None] > ii[None, :], -30000.0, 0.0).astype(np.float32)
    posb = np.where(ii[:, None] < ii[None, :], 30000.0, 0.0).astype(np.float32)
    neg = np.ascontiguousarray(np.stack([negf, posb], axis=1)).astype(ml_dtypes.bfloat16)
    invf = (1.0 / (10000.0 ** (np.arange(0, 32, 2, dtype=np.float32) / 32.0))).astype(np.float32)[None, :]
    return dict(c_identb=I.astype(ml_dtypes.bfloat16), c_mats=mats, c_neg=neg, c_invf=invf)


def make_shared(inp):
    f = lambda k: np.asarray(inp[k], np.float32)[0]
    d = {}
    d["gfm"] = np.ascontiguousarray(np.concatenate([_fm(f("ffn1_norm"), 8), _fm(f("mix_norm"), 8), _fm(f("ffn2_norm"), 8)], axis=1))
    d["gqa"] = _fm(f("q_a_norm"), 3)
    d["gkva"] = _fm(f("kv_a_norm"), 2)
    d["gssm"] = _fm(f("ssm_norm"), 16)
    d["w_g1"] = f("ffn1_w_gate"); d["w_u1"] = f("ffn1_w_up"); d["w_d1"] = f("ffn1_w_down")
    d["w_g2"] = f("ffn2_w_gate"); d["w_u2"] = f("ffn2_w_up"); d["w_d2"] = f("ffn2_w_down")
    d["w_in"] = f("w_in"); d["w_qb"] = f("w_q_b"); d["w_kvb"] = f("w_kv_b")
    d["hn"] = np.concatenate([f("q_head_norm"), f("k_head_norm")])[None, :].astype(np.float32)
    cw = f("conv_w")[:, 0, :]
    d["convw"] = np.ascontiguousarray(cw.T.reshape(24, 128, 5).transpose(1, 0, 2))
    d["convb"] = _fm(f("conv_b"), 24)
    d["ssp"] = np.concatenate([f("a_log_fwd"), f("a_log_bwd"), f("dt_bias_fwd"), f("dt_bias_bwd"), f("d_skip")])[None, :].astype(np.float32)
    d["w_pa"] = f("w_attn_branch"); d["w_pb"] = f("w_ssm_branch"); d["w_o"] = f("w_out")
    d.update(consts())
    return d


def make_inmap(inp, shared, b):
    d = dict(shared)
    d["x"] = np.ascontiguousarray(np.asarray(inp["x"], np.float32)[b])
    p = np.asarray(inp["positions"], np.int32)[b]
    d["pos"] = np.ascontiguousarray(p.reshape(32, 128).T)
    return d


def kernel(**inputs):
    nb = int(np.asarray(inputs["x"]).shape[0])
    nc = build(dbg=False)
    shared = make_shared(inputs)
    in_maps = [make_inmap(inputs, shared, b) for b in range(nb)]
    res = run_bass_kernel_spmd(nc, in_maps, core_ids=list(range(nb)))
    out = np.stack([np.asarray(res.results[b]["y"], dtype=np.float32) for b in range(nb)], axis=0)
    return out
```

```python
import math
from contextlib import ExitStack
import numpy as np
import ml_dtypes
import concourse.bass as bass
import concourse.mybir as mybir
from concourse.bass_utils import run_bass_kernel_spmd

F32 = mybir.dt.float32
BF16 = mybir.dt.bfloat16
I32 = mybir.dt.int32
AF = mybir.ActivationFunctionType
ALU = mybir.AluOpType
AX = mybir.AxisListType

S_ = 4096
D_ = 1024
FF = 2816
NFF = 22
NH = 16
EPS = 1e-6
C_Q, C_KV, C_PE, C_Z, C_XBC, C_DTF, C_DTB, C_GA, C_GB = 0, 384, 640, 672, 2720, 5792, 5824, 5856, 6880
IN_DIM = 7904
ENGS = ("pe", "act", "dve", "pool", "sp")
FUSE_WAITS = True


class DSem:
    def __init__(self):
        self.count = 0
        self.handle = None


class Buf:
    __slots__ = ("name", "lw", "rd", "dsem", "ep")

    def __init__(self, name=""):
        self.name = name
        self.lw = None
        self.rd = []
        self.dsem = None
        self.ep = -1


class Op:
    __slots__ = ("eng", "fn", "idx", "waits", "dwaits", "inc", "dsem", "know", "seq", "multi")


class Sched:
    def __init__(self):
        self.ops = {e: [] for e in ENGS}
        self.know = {e: {} for e in ENGS}
        self.dsems = []
        self.free = []
        self.epoch = 0

    def _add(self, eng, fn, reads, writes, dsem=None, extra=(), extra_ds=()):
        op = Op()
        op.eng = eng
        op.fn = fn
        op.idx = len(self.ops[eng])
        op.waits = {}
        op.dwaits = {}
        op.inc = False
        op.dsem = dsem
        op.seq = None
        op.multi = False
        know = self.know[eng]
        deps = list(extra)
        for b in reads:
            if b.lw is not None:
                deps.append(b.lw)
        for b in writes:
            if b.lw is not None:
                deps.append(b.lw)
            deps.extend(b.rd)
        for a in deps:
            if a is op:
                continue
            if a.dsem is None:
                if a.eng == "pe" and eng == "pe":
                    continue
                if know.get(a.eng, -1) >= a.idx:
                    continue
                a.inc = True
                cur = op.waits.get(a.eng)
                if cur is None or cur.idx < a.idx:
                    op.waits[a.eng] = a
                for k, v in a.know.items():
                    if know.get(k, -1) < v:
                        know[k] = v
                know[a.eng] = max(know.get(a.eng, -1), a.idx)
            else:
                ds = a.dsem
                v = ds.count
                if know.get(ds, -1) >= v:
                    continue
                op.dwaits[ds] = v
                for k, vv in a.know.items():
                    if know.get(k, -1) < vv:
                        know[k] = vv
                know[ds] = v
        for ds in extra_ds:
            v = ds.count
            if know.get(ds, -1) < v:
                op.dwaits[ds] = v
                know[ds] = v
        if dsem is not None:
            dsem.count += 16
        op.know = dict(know)
        for b in reads:
            b.rd.append(op)
        for b in writes:
            b.lw = op
            b.rd = []
        self.ops[eng].append(op)
        return op

    def op(self, eng, fn, reads=(), writes=(), multi=False):
        o = self._add(eng, fn, reads, writes)
        o.multi = multi
        return o

    def dma(self, fn, reads=(), writes=(), sem_buf=None, eng="sp"):
        if sem_buf.dsem is None or sem_buf.ep != self.epoch:
            if self.free:
                sem_buf.dsem = self.free.pop()
            else:
                sem_buf.dsem = DSem()
                self.dsems.append(sem_buf.dsem)
            sem_buf.ep = self.epoch
        return self._add(eng, fn, reads, writes, dsem=sem_buf.dsem)

    def barrier(self):
        lasts = []
        for e in ENGS:
            if e == "sp":
                continue
            for o in reversed(self.ops[e]):
                if o.dsem is None:
                    lasts.append(o)
                    break
        spop = self._add("sp", lambda e: e.nop(), (), (), extra=lasts, extra_ds=list(self.dsems))
        self.epoch += 1
        self.free = list(self.dsems)
        for e in ENGS:
            if e == "sp":
                continue
            self._add(e, lambda eh: eh.nop(), (), (), extra=[spop])

    def emit(self, nc):
        with ExitStack() as st:
            esem = {e: st.enter_context(nc.semaphore("es_" + e)) for e in ENGS}
            for i, d in enumerate(self.dsems):
                d.handle = st.enter_context(nc.semaphore("ds%d" % i))
            for e in ENGS:
                c = 0
                for o in self.ops[e]:
                    if o.dsem is None and o.inc:
                        c += 1
                        o.seq = c
            block = st.enter_context(nc.Block())

            def run(e, eh):
                for o in self.ops[e]:
                    wl = [(esem[se], a.seq) for se, a in o.waits.items()] + [(ds.handle, v) for ds, v in o.dwaits.items()]
                    attach = None
                    if wl and o.dsem is None and not o.multi and e != "sp" and FUSE_WAITS:
                        attach = wl.pop()
                    for hh_, vv_ in wl:
                        eh.wait_ge(hh_, vv_)
                    n0 = nc.n_instructions()
                    ins = o.fn(eh)
                    if attach is not None:
                        if nc.n_instructions() - n0 != 1:
                            raise RuntimeError("multi-instruction op with fused wait on %s (%d)" % (e, nc.n_instructions() - n0))
                        ins._wait_ge(attach[0], attach[1])
                    if o.dsem is not None:
                        ins.then_inc(o.dsem.handle, 16)
                    elif o.inc:
                        ins.then_inc(esem[e], 1)
                if e == "sp":
                    for ds in self.dsems:
                        eh.wait_ge(ds.handle, ds.count)

            @block.tensor
            def _(eh):
                run("pe", eh)

            @block.scalar
            def _(eh):
                run("act", eh)

            @block.vector
            def _(eh):
                run("dve", eh)

            @block.gpsimd
            def _(eh):
                run("pool", eh)

            @block.sync
            def _(eh):
                run("sp", eh)


class Ring:
    def __init__(self, nc, st, name, shape, dtype, n, psum=False):
        self.items = []
        for i in range(n):
            if psum:
                t = st.enter_context(nc.psum_tensor("%s%d" % (name, i), shape, dtype))
            else:
                t = st.enter_context(nc.sbuf_tensor("%s%d" % (name, i), shape, dtype))
            self.items.append((t, Buf("%s%d" % (name, i))))
        self.i = 0

    def next(self):
        r = self.items[self.i % len(self.items)]
        self.i += 1
        return r


def pipeline(n, load_fn, compute_fn, depth):
    hs = {}
    for i in range(n + depth):
        if i < n:
            hs[i] = load_fn(i)
        if i >= depth:
            compute_fn(i - depth, hs.pop(i - depth))


class K:
    pass


def build(dbg=False, stop_after=99):
    nc = bass.Bass("TRN2", target_bir_lowering=False)
    S = Sched()
    uid = [0]

    def un(p):
        uid[0] += 1
        return "%s_%d" % (p, uid[0])

    def inp(name, shape, dt=F32):
        return nc.dram_tensor(name, shape, dt, kind="ExternalInput").ap()

    def scratch(name, shape, dt, out=False):
        kind = "ExternalOutput" if (out or dbg) else "Internal"
        return nc.dram_tensor(name, shape, dt, kind=kind).ap(), Buf(name)

    x = inp("x", [S_, D_])
    pos = inp("pos", [128, 32], I32)
    gfm = inp("gfm", [128, 24])
    gqa = inp("gqa", [128, 3])
    gkva = inp("gkva", [128, 2])
    gssm = inp("gssm", [128, 16])
    w_g1 = inp("w_g1", [D_, FF]); w_u1 = inp("w_u1", [D_, FF]); w_d1 = inp("w_d1", [FF, D_])
    w_g2 = inp("w_g2", [D_, FF]); w_u2 = inp("w_u2", [D_, FF]); w_d2 = inp("w_d2", [FF, D_])
    w_in = inp("w_in", [D_, IN_DIM])
    w_qb = inp("w_qb", [384, 1536]); w_kvb = inp("w_kvb", [256, 2048])
    hn = inp("hn", [1, 192])
    convw = inp("convw", [128, 24, 5]); convb = inp("convb", [128, 24])
    ssp = inp("ssp", [1, 160])
    w_pa = inp("w_pa", [D_, D_]); w_pb = inp("w_pb", [2048, D_]); w_o = inp("w_o", [D_, D_])
    c_identb = inp("c_identb", [128, 128], BF16)
    c_mats = inp("c_mats", [128, 4, 128])
    c_neg = inp("c_neg", [128, 2, 128], BF16)
    c_invf = inp("c_invf", [1, 16])

    y_out, b_yout = scratch("y", [S_, D_], F32, out=True)
    x1s, b_x1s = scratch("x1s", [S_, D_], F32)
    x2s, b_x2s = scratch("x2s", [S_, D_], F32)
    hmid, b_hmid = scratch("hmid", [FF, S_], BF16)
    QT, b_QT = scratch("QT", [NH, 96, S_], BF16)
    KT, b_KT = scratch("KT", [NH, 96, S_], BF16)
    Vs, b_Vs = scratch("Vs", [NH, 128, 32, 65], BF16)
    zs, b_zs = scratch("zs", [S_, 2048], BF16)
    xcs, b_xcs = scratch("xcs", [32, 128, 24, 128], BF16)
    gts, b_gts = scratch("gts", [2048, S_], BF16)
    yfs, b_yfs = scratch("yfs", [S_, 2048], F32)
    mTs, b_mTs = scratch("mTs", [2048, S_], BF16)
    aTs, b_aTs = scratch("aTs", [D_, S_], BF16)
    if dbg:
        ybs, b_ybs = scratch("ybs", [S_, 2048], F32)

    top = ExitStack()
    A = lambda name, shape, dt: top.enter_context(nc.sbuf_tensor(name, shape, dt))
    identb = A("identb", [128, 128], BF16); b_identb = Buf()
    mats = A("mats", [128, 4, 128], F32); b_mats = Buf()
    negm = A("negm", [128, 2, 128], BF16); b_negm = Buf()
    gfm_t = A("gfm_t", [128, 24], F32); b_gfm = Buf()
    gqa_t = A("gqa_t", [128, 3], F32); b_gqa = Buf()
    gkva_t = A("gkva_t", [128, 2], F32); b_gkva = Buf()
    gssm_t = A("gssm_t", [128, 16], F32); b_gssm = Buf()
    hn_t = A("hn_t", [128, 192], F32); b_hn = Buf()
    ssp_t = A("ssp_t", [128, 160], F32); b_ssp = Buf()
    convw_t = A("convw_t", [128, 24, 5], F32); b_convw = Buf()
    convb_t = A("convb_t", [128, 24], F32); b_convb = Buf()
    cos_t = A("cos_t", [128, 32, 16], F32); b_cos = Buf()
    sin_t = A("sin_t", [128, 32, 16], F32); b_sin = Buf()
    dtraw = A("dtraw", [128, 32, 64], F32); b_dtraw = Buf()

    def ld(dst, src, b):
        S.dma(lambda e: e.dma_start(out=dst, in_=src), writes=[b], sem_buf=b)

    ld(identb[:], c_identb[:, :], b_identb)
    ld(mats[:], c_mats[:, :, :], b_mats)
    ld(negm[:], c_neg[:, :, :], b_negm)
    ld(gfm_t[:], gfm[:, :], b_gfm)
    ld(gqa_t[:], gqa[:, :], b_gqa)
    ld(gkva_t[:], gkva[:, :], b_gkva)
    ld(gssm_t[:], gssm[:, :], b_gssm)
    ld(hn_t[:], hn.partition_broadcast(128), b_hn)
    ld(ssp_t[:], ssp.partition_broadcast(128), b_ssp)
    ld(convw_t[:], convw[:, :, :], b_convw)
    ld(convb_t[:], convb[:, :], b_convb)
    Umat = mats[:, 0, :]
    Ustr = mats[:, 1, :]
    onesf = mats[:, 2, :]
    identf = mats[:, 3, :]

    with ExitStack() as st:
        post = st.enter_context(nc.sbuf_tensor("post", [128, 32], I32)); b_post = Buf()
        posf = st.enter_context(nc.sbuf_tensor("posf", [128, 32], F32)); b_posf = Buf()
        invf = st.enter_context(nc.sbuf_tensor("invf", [128, 16], F32)); b_invf = Buf()
        ang = st.enter_context(nc.sbuf_tensor("ang", [128, 32, 16], F32)); b_ang = Buf()
        ang2 = st.enter_context(nc.sbuf_tensor("ang2", [128, 32, 16], F32)); b_ang2 = Buf()
        ld(post[:], pos[:, :], b_post)
        ld(invf[:], c_invf.partition_broadcast(128), b_invf)
        S.op("dve", lambda e: e.tensor_copy(out=posf[:], in_=post[:]), reads=[b_post], writes=[b_posf])
        S.op("dve", lambda e: e.tensor_tensor(out=ang[:], in0=posf[:].unsqueeze(2).to_broadcast([128, 32, 16]),
                                              in1=invf[:].unsqueeze(1).to_broadcast([128, 32, 16]), op=ALU.mult),
             reads=[b_posf, b_invf], writes=[b_ang])
        PI = math.pi
        angi = st.enter_context(nc.sbuf_tensor("angi", [128, 32, 16], I32)); b_angi = Buf()
        ang3 = st.enter_context(nc.sbuf_tensor("ang3", [128, 32, 16], F32)); b_ang3 = Buf()

        def rr(shift, dst, b_dst):
            S.op("dve", lambda e: e.tensor_scalar(out=ang2[:], in0=ang[:], scalar1=shift, scalar2=None, op0=ALU.add),
                 reads=[b_ang], writes=[b_ang2])
            S.op("dve", lambda e: e.tensor_scalar(out=ang3[:], in0=ang2[:], scalar1=1.0 / (2 * PI), scalar2=None,
                                                  op0=ALU.mult), reads=[b_ang2], writes=[b_ang3])
            S.op("dve", lambda e: e.tensor_copy(out=angi[:], in_=ang3[:]), reads=[b_ang3], writes=[b_angi])
            S.op("dve", lambda e: e.tensor_copy(out=ang3[:], in_=angi[:]), reads=[b_angi], writes=[b_ang3])
            S.op("dve", lambda e: e.scalar_tensor_tensor(out=ang2[:], in0=ang3[:], scalar=-2 * PI, in1=ang2[:],
                                                         op0=ALU.mult, op1=ALU.add), reads=[b_ang3, b_ang2], writes=[b_ang2])
            S.op("dve", lambda e: e.tensor_scalar(out=ang3[:], in0=ang2[:], scalar1=-PI, scalar2=1e9,
                                                  op0=ALU.add, op1=ALU.mult), reads=[b_ang2], writes=[b_ang3])
            S.op("dve", lambda e: e.tensor_scalar(out=ang3[:], in0=ang3[:], scalar1=0.0, scalar2=1.0,
                                                  op0=ALU.max, op1=ALU.min), reads=[b_ang3], writes=[b_ang3])
            S.op("dve", lambda e: e.scalar_tensor_tensor(out=ang2[:], in0=ang3[:], scalar=-2 * PI, in1=ang2[:],
                                                         op0=ALU.mult, op1=ALU.add), reads=[b_ang3, b_ang2], writes=[b_ang2])
            S.op("dve", lambda e: e.tensor_scalar(out=ang3[:], in0=ang2[:], scalar1=PI, scalar2=-1e9,
                                                  op0=ALU.add, op1=ALU.mult), reads=[b_ang2], writes=[b_ang3])
            S.op("dve", lambda e: e.tensor_scalar(out=ang3[:], in0=ang3[:], scalar1=0.0, scalar2=1.0,
                                                  op0=ALU.max, op1=ALU.min), reads=[b_ang3], writes=[b_ang3])
            S.op("dve", lambda e: e.scalar_tensor_tensor(out=ang2[:], in0=ang3[:], scalar=2 * PI, in1=ang2[:],
                                                         op0=ALU.mult, op1=ALU.add), reads=[b_ang3, b_ang2], writes=[b_ang2])
            S.op("dve", lambda e: e.tensor_scalar(out=ang2[:], in0=ang2[:], scalar1=PI * (1 - 1e-6),
                                                  scalar2=-PI * (1 - 1e-6), op0=ALU.min, op1=ALU.max),
                 reads=[b_ang2], writes=[b_ang2])
            S.op("act", lambda e: e.activation(out=dst, in_=ang2[:], func=AF.Sin), reads=[b_ang2], writes=[b_dst])

        rr(0.0, sin_t[:], b_sin)
        rr(0.5 * PI, cos_t[:], b_cos)
        S.barrier()

    def wload(stage_ring, w_ring, wsrc, kc, n, gain=None, cast_eng="pool"):
        stg, b_stg = stage_ring.next()
        wt, b_wt = w_ring.next()
        S.dma(lambda e: e.dma_start(out=stg[:, 0:kc, 0:n], in_=wsrc.rearrange("(kc p) n -> p kc n", p=128)),
              writes=[b_stg], sem_buf=b_stg)
        if gain is None:
            S.op(cast_eng, lambda e: e.tensor_copy(out=wt[:, 0:kc, 0:n], in_=stg[:, 0:kc, 0:n]),
                 reads=[b_stg], writes=[b_wt])
        else:
            g_ap, b_g = gain
            S.op(cast_eng, lambda e: e.tensor_tensor(out=wt[:, 0:kc, 0:n], in0=stg[:, 0:kc, 0:n],
                                                     in1=g_ap.unsqueeze(2).to_broadcast([128, kc, n]), op=ALU.mult),
                 reads=[b_stg, b_g], writes=[b_wt])
        return wt, b_wt

    def norm_block(src, b_src, tb, rings, hT, b_hT, ncols=D_):
        junk, b_junk = rings["junk"].next()
        stt, b_stt = rings["st"].next()
        hb, b_hb = rings["hb"].next()
        ptr, b_ptr = rings["ptr"].next()
        S.op("act", lambda e: e.activation(out=junk[:], in_=src, func=AF.Square, scale=1.0 / math.sqrt(ncols),
                                           accum_out=stt[:, 0:1]), reads=[b_src], writes=[b_junk, b_stt])
        S.op("act", lambda e: e.activation(out=stt[:, 1:2], in_=stt[:, 0:1], func=AF.Sqrt, bias=EPS_AP[:, 0:1]),
             reads=[b_stt], writes=[b_stt])
        S.op("dve", lambda e: e.reciprocal(out=stt[:, 2:3], in_=stt[:, 1:2]), reads=[b_stt], writes=[b_stt])
        S.op("dve", lambda e: e.tensor_scalar(out=hb[:], in0=src, scalar1=stt[:, 2:3], scalar2=None, op0=ALU.mult),
             reads=[b_src, b_stt], writes=[b_hb])
        pv = ptr[:, :].bitcast(BF16)
        for kc in range(8):
            S.op("pe", lambda e, kc=kc: e.transpose(out=pv[:, kc * 128:(kc + 1) * 128],
                                                     in_=hb[:, kc * 128:(kc + 1) * 128], identity=identb[:]),
                 reads=[b_hb, b_identb], writes=[b_ptr])
        S.op("act", lambda e: e.copy(out=hT[:, :, tb * 128:(tb + 1) * 128],
                                     in_=pv.rearrange("p (k t) -> p k t", k=8)),
             reads=[b_ptr], writes=[b_hT])

    eps_t = A("eps_t", [128, 1], F32); b_eps = Buf()
    S.op("pool", lambda e: e.memset(eps_t[:], EPS), writes=[b_eps])
    EPS_AP = eps_t
    S.barrier()

    def ffn_gateup(w_g, w_u, gain_col, hT, b_hT, w_d=None, wd=None, b_wd=None):
        with ExitStack() as st:
            stg = Ring(nc, st, un("gu_stg"), [128, 8, 128], F32, 6)
            wr = Ring(nc, st, un("gu_w"), [128, 8, 128], BF16, 6)
            psg = Ring(nc, st, un("gu_pg"), [128, 512], F32, 3, psum=True)
            psu = Ring(nc, st, un("gu_pu"), [128, 512], F32, 3, psum=True)
            sil = Ring(nc, st, un("gu_sil"), [128, 512], F32, 3)
            hm = Ring(nc, st, un("gu_hm"), [128, S_], BF16, 2)
            gain = (gfm_t[:, gain_col * 8:(gain_col + 1) * 8], b_gfm)

            dstg = Ring(nc, st, un("gu_dstg"), [128, 1, D_], F32, 3)

            def load(j):
                wg = wload(stg, wr, w_g[:, j * 128:(j + 1) * 128], 8, 128, gain)
                wu = wload(stg, wr, w_u[:, j * 128:(j + 1) * 128], 8, 128, gain)
                if w_d is not None:
                    sg, b_sg = dstg.next()
                    S.dma(lambda e, sg=sg, j=j: e.dma_start(out=sg[:, 0, :], in_=w_d[j * 128:(j + 1) * 128, :]),
                          writes=[b_sg], sem_buf=b_sg)
                    S.op("pool", lambda e, sg=sg, j=j: e.tensor_copy(out=wd[:, j, :], in_=sg[:, 0, :]),
                         reads=[b_sg], writes=[b_wd])
                return wg, wu

            def comp(j, h):
                (wg, b_wg), (wu, b_wu) = h
                hmt, b_hm = hm.next()
                for tb in range(8):
                    pg, b_pg = psg.next()
                    pu, b_pu = psu.next()
                    sl, b_sl = sil.next()
                    ts = slice(tb * 512, (tb + 1) * 512)
                    for kc in range(8):
                        S.op("pe", lambda e, kc=kc, pg=pg, wg=wg, ts=ts: e.matmul(
                            out=pg[:], lhsT=wg[:, kc, :], rhs=hT[:, kc, ts], start=(kc == 0), stop=(kc == 7)),
                            reads=[b_wg, b_hT], writes=[b_pg])
                    for kc in range(8):
                        S.op("pe", lambda e, kc=kc, pu=pu, wu=wu, ts=ts: e.matmul(
                            out=pu[:], lhsT=wu[:, kc, :], rhs=hT[:, kc, ts], start=(kc == 0), stop=(kc == 7)),
                            reads=[b_wu, b_hT], writes=[b_pu])
                    S.op("act", lambda e, sl=sl, pg=pg: e.activation(out=sl[:], in_=pg[:], func=AF.Silu),
                         reads=[b_pg], writes=[b_sl])
                    S.op("dve", lambda e, sl=sl, pu=pu, hmt=hmt, ts=ts: e.tensor_tensor(
                        out=hmt[:, ts], in0=sl[:], in1=pu[:], op=ALU.mult), reads=[b_sl, b_pu], writes=[b_hm])
                S.dma(lambda e, hmt=hmt, j=j: e.dma_start(out=hmid[j * 128:(j + 1) * 128, :], in_=hmt[:]),
                      reads=[b_hm], writes=[b_hmid], sem_buf=b_hm, eng="pool")

            pipeline(NFF, load, comp, 2)
        S.barrier()

    def ffn_down(w_d, xsrc, b_xsrc, xdst, b_xdst, next_norm, wd=None, b_wd=None):
        with ExitStack() as st:
            pre = wd is not None
            if not pre:
                wd = st.enter_context(nc.sbuf_tensor(un("wd"), [128, NFF, D_], BF16)); b_wd = Buf()
            stg = Ring(nc, st, un("dn_stg"), [128, 1, D_], F32, 3)
            for j in range(NFF if not pre else 0):
                sg, b_sg = stg.next()
                S.dma(lambda e, sg=sg, j=j: e.dma_start(out=sg[:, 0, :], in_=w_d[j * 128:(j + 1) * 128, :]),
                      writes=[b_sg], sem_buf=b_sg)
                S.op("pool", lambda e, sg=sg, j=j: e.tensor_copy(out=wd[:, j, :], in_=sg[:, 0, :]),
                     reads=[b_sg], writes=[b_wd])
            hmr = Ring(nc, st, un("dn_hm"), [128, NFF, 512], BF16, 2)
            xr = Ring(nc, st, un("dn_x"), [128, D_], F32, 3)
            ps = Ring(nc, st, un("dn_ps"), [128, 512], F32, 4, psum=True)
            rings = None
            if next_norm:
                rings = dict(junk=Ring(nc, st, un("nj"), [128, D_], BF16, 2),
                             st=Ring(nc, st, un("nst"), [128, 4], F32, 3),
                             hb=Ring(nc, st, un("nhb"), [128, D_], BF16, 2),
                             ptr=Ring(nc, st, un("nptr"), [128, 512], F32, 2, psum=True))

            def load(t):
                hmt, b_hm = hmr.next()
                S.dma(lambda e: e.dma_start(out=hmt[:], in_=hmid[:, t * 512:(t + 1) * 512].rearrange(
                    "(j p) t -> p j t", p=128)), reads=[b_hmid], writes=[b_hm], sem_buf=b_hm)
                return hmt, b_hm

            def comp(t, h):
                hmt, b_hm = h
                for sb in range(4):
                    tb = t * 4 + sb
                    xt, b_xt = xr.next()
                    S.dma(lambda e, xt=xt, tb=tb: e.dma_start(out=xt[:], in_=xsrc[tb * 128:(tb + 1) * 128, :]),
                          reads=[b_xsrc], writes=[b_xt], sem_buf=b_xt)
                    for half in range(2):
                        p, b_p = ps.next()
                        for j in range(NFF):
                            S.op("pe", lambda e, j=j, p=p, sb=sb, half=half, hmt=hmt: e.matmul(
                                out=p[:], lhsT=hmt[:, j, sb * 128:(sb + 1) * 128],
                                rhs=wd[:, j, half * 512:(half + 1) * 512], start=(j == 0), stop=(j == NFF - 1)),
                                reads=[b_hm, b_wd], writes=[b_p])
                        S.op("dve", lambda e, p=p, xt=xt, half=half: e.scalar_tensor_tensor(
                            out=xt[:, half * 512:(half + 1) * 512], in0=p[:], scalar=0.5,
                            in1=xt[:, half * 512:(half + 1) * 512], op0=ALU.mult, op1=ALU.add),
                            reads=[b_p, b_xt], writes=[b_xt])
                    S.dma(lambda e, xt=xt, tb=tb: e.dma_start(out=xdst[tb * 128:(tb + 1) * 128, :], in_=xt[:]),
                          reads=[b_xt], writes=[b_xdst], sem_buf=b_xt, eng="pool")
                    if next_norm:
                        if pendn:
                            norm_block(*pendn.pop())
                        pendn.append((xt[:], b_xt, tb, rings, next_norm[0], next_norm[1]))

            pendn = []
            pipeline(8, load, comp, 1)
            if pendn:
                norm_block(*pendn.pop())
        S.barrier()

    def norm_phase(xsrc, b_xsrc, hT, b_hT):
        with ExitStack() as st:
            xr = Ring(nc, st, un("np_x"), [128, D_], F32, 3)
            rings = dict(junk=Ring(nc, st, un("nj"), [128, D_], BF16, 2),
                         st=Ring(nc, st, un("nst"), [128, 4], F32, 3),
                         hb=Ring(nc, st, un("nhb"), [128, D_], BF16, 2),
                         ptr=Ring(nc, st, un("nptr"), [128, 512], F32, 2, psum=True))

            def load(tb):
                xt, b_xt = xr.next()
                S.dma(lambda e: e.dma_start(out=xt[:], in_=xsrc[tb * 128:(tb + 1) * 128, :]),
                      reads=[b_xsrc], writes=[b_xt], sem_buf=b_xt)
                return xt, b_xt

            def comp(tb, h):
                norm_block(h[0][:], h[1], tb, rings, hT, b_hT)

            pipeline(32, load, comp, 2)
        S.barrier()

    b_x = Buf("x")
    hst = ExitStack()
    hT = hst.enter_context(nc.sbuf_tensor("hT", [128, 8, S_], BF16)); b_hT = Buf("hT")
    norm_phase(x, b_x, hT, b_hT)
    with ExitStack() as fst:
        wd1 = fst.enter_context(nc.sbuf_tensor("wd1", [128, NFF, D_], BF16)); b_wd1 = Buf()
        ffn_gateup(w_g1, w_u1, 0, hT, b_hT, w_d1, wd1, b_wd1)
        ffn_down(w_d1, x, b_x, x1s, b_x1s, (hT, b_hT), wd1, b_wd1)
    if stop_after <= 1:
        S.emit(nc)
        return nc

    gmix = (gfm_t[:, 8:16], b_gfm)

    def rope(src3, H, tb, dst3, tmp_ring):
        ta, b_ta = tmp_ring.next()
        tb_, b_tb = tmp_ring.next()
        cb = cos_t[:, tb, :].unsqueeze(1).to_broadcast([128, H, 16])
        sb = sin_t[:, tb, :].unsqueeze(1).to_broadcast([128, H, 16])
        t1 = src3[:, :, 0:16]
        t2 = src3[:, :, 16:32]
        a_ = ta[:, 0:H, :]
        b_ = tb_[:, 0:H, :]
        return [
            (lambda e: e.tensor_tensor(out=a_, in0=t1, in1=cb, op=ALU.mult), [b_cos], [b_ta]),
            (lambda e: e.tensor_tensor(out=b_, in0=t2, in1=sb, op=ALU.mult), [b_sin], [b_tb]),
            (lambda e: e.tensor_tensor(out=dst3[:, :, 0:16], in0=a_, in1=b_, op=ALU.subtract), [b_ta, b_tb], []),
            (lambda e: e.tensor_tensor(out=a_, in0=t2, in1=cb, op=ALU.mult), [b_cos], [b_ta]),
            (lambda e: e.tensor_tensor(out=b_, in0=t1, in1=sb, op=ALU.mult), [b_sin], [b_tb]),
            (lambda e: e.tensor_tensor(out=dst3[:, :, 16:32], in0=a_, in1=b_, op=ALU.add), [b_ta, b_tb], []),
        ]

    with ExitStack() as st:
        wAr = Ring(nc, st, un("a_w"), [128, 8, 672], BF16, 1)
        wqr = Ring(nc, st, un("q_w"), [128, 3, 1536], BF16, 1)
        wkr = Ring(nc, st, un("kv_w"), [128, 2, 2048], BF16, 1)
        with ExitStack() as st2:
            stgA = Ring(nc, st2, un("a_stg"), [128, 8, 672], F32, 1)
            stgq = Ring(nc, st2, un("q_stg"), [128, 3, 1536], F32, 1)
            stgk = Ring(nc, st2, un("kv_stg"), [128, 2, 2048], F32, 1)
            wA, b_wA = wload(stgA, wAr, w_in[:, 0:672], 8, 672, gmix)
            wq, b_wq = wload(stgq, wqr, w_qb[:, :], 3, 1536, (gqa_t[:, :], b_gqa))
            wkv, b_wkv = wload(stgk, wkr, w_kvb[:, :], 2, 2048, (gkva_t[:, :], b_gkva))
            S.barrier()
        psA = Ring(nc, st, un("a_ps"), [128, 512], F32, 2, psum=True)
        psT = Ring(nc, st, un("a_pt"), [128, 512], F32, 2, psum=True)
        psQ = Ring(nc, st, un("a_pq"), [128, 512], F32, 3, psum=True)
        junk = Ring(nc, st, un("a_junk"), [128, 1536], F32, 1)
        stt = Ring(nc, st, un("a_st"), [128, 8], F32, 2)
        sst = Ring(nc, st, un("a_ss"), [128, 100], F32, 2)
        cnr = Ring(nc, st, un("a_cn"), [128, 640], BF16, 2)
        cTr = Ring(nc, st, un("a_cT"), [128, 5, 128], BF16, 2)
        kper = Ring(nc, st, un("a_kpe"), [128, 1, 32], F32, 2)
        kpgr = Ring(nc, st, un("a_kpg"), [128, 1, 32], F32, 2)
        krr = Ring(nc, st, un("a_kr"), [128, 1, 32], F32, 2)
        qsbr = Ring(nc, st, un("a_qsb"), [128, 1536], F32, 1)
        kvsbr = Ring(nc, st, un("a_kvsb"), [128, 2048], F32, 1)
        tmpkr = Ring(nc, st, un("a_tmpk"), [128, 16, 64], F32, 1)
        qbr = Ring(nc, st, un("a_qb"), [128, 16, 96], BF16, 2)
        kbr = Ring(nc, st, un("a_kb"), [128, 16, 96], BF16, 2)
        ropet = Ring(nc, st, un("a_rt"), [128, 16, 16], F32, 4)
        vbr = Ring(nc, st, un("a_vb"), [128, 16, 4, 65], BF16, 1)
        qTr = Ring(nc, st, un("a_qT"), [128, 16, 256], BF16, 2)
        kTr = Ring(nc, st, un("a_kT"), [128, 16, 256], BF16, 2)
        for (vt, b_v) in vbr.items:
            S.op("pool", lambda e, vt=vt: e.memset(vt[:], 1.0), writes=[b_v])
        gq = hn_t[:, 0:96]
        gk = hn_t[:, 96:192]
        sh = {}

        def block(tb):
            tsl = slice(tb * 128, (tb + 1) * 128)
            pA1, b_pA1 = psA.next()
            pA2, b_pA2 = psA.next()
            for kc in range(8):
                S.op("pe", lambda e, kc=kc, pA1=pA1, tsl=tsl: e.matmul(out=pA1[:, 0:384], lhsT=hT[:, kc, tsl], rhs=wA[:, kc, 0:384],
                                                     start=(kc == 0), stop=(kc == 7)), reads=[b_hT, b_wA], writes=[b_pA1])
            for kc in range(8):
                S.op("pe", lambda e, kc=kc, pA2=pA2, tsl=tsl: e.matmul(out=pA2[:, 0:288], lhsT=hT[:, kc, tsl], rhs=wA[:, kc, 384:672],
                                                     start=(kc == 0), stop=(kc == 7)), reads=[b_hT, b_wA], writes=[b_pA2])
            jk, b_jk = junk.next()
            s8, b_s8 = stt.next()
            S.op("act", lambda e, jk=jk, pA1=pA1, s8=s8: e.activation(out=jk[:, 0:384], in_=pA1[:, 0:384], func=AF.Square,
                                               scale=1.0 / math.sqrt(384.0), accum_out=s8[:, 0:1]),
                 reads=[b_pA1], writes=[b_jk, b_s8])
            S.op("act", lambda e, jk=jk, pA2=pA2, s8=s8: e.activation(out=jk[:, 0:256], in_=pA2[:, 0:256], func=AF.Square,
                                               scale=1.0 / 16.0, accum_out=s8[:, 1:2]),
                 reads=[b_pA2], writes=[b_jk, b_s8])
            S.op("act", lambda e, s8=s8: e.activation(out=s8[:, 2:4], in_=s8[:, 0:2], func=AF.Sqrt, bias=EPS_AP[:, 0:1]),
                 reads=[b_s8, b_eps], writes=[b_s8])
            S.op("dve", lambda e, s8=s8: e.reciprocal(out=s8[:, 4:6], in_=s8[:, 2:4]), reads=[b_s8], writes=[b_s8])
            cn, b_cn = cnr.next()
            S.op("dve", lambda e, cn=cn, pA1=pA1, s8=s8: e.tensor_scalar(out=cn[:, 0:384], in0=pA1[:, 0:384], scalar1=s8[:, 4:5],
                                                  scalar2=None, op0=ALU.mult), reads=[b_pA1, b_s8], writes=[b_cn])
            S.op("dve", lambda e, cn=cn, pA2=pA2, s8=s8: e.tensor_scalar(out=cn[:, 384:640], in0=pA2[:, 0:256], scalar1=s8[:, 5:6],
                                                  scalar2=None, op0=ALU.mult), reads=[b_pA2, b_s8], writes=[b_cn])
            kpe, b_kpe = kper.next()
            S.op("act", lambda e, kpe=kpe, pA2=pA2: e.copy(out=kpe[:, 0, :], in_=pA2[:, 256:288]), reads=[b_pA2], writes=[b_kpe])
            yield
            ptr, b_ptr = psT.next()
            pv = ptr[:, :].bitcast(BF16)
            for kc in range(5):
                S.op("pe", lambda e, kc=kc, pv=pv, cn=cn: e.transpose(out=pv[:, kc * 128:(kc + 1) * 128],
                                                         in_=cn[:, kc * 128:(kc + 1) * 128], identity=identb[:]),
                     reads=[b_cn, b_identb], writes=[b_ptr])
            cT, b_cT = cTr.next()
            S.op("dve", lambda e, cT=cT, pv=pv: e.tensor_copy(out=cT[:], in_=pv[:, 0:640].rearrange("p (k t) -> p k t", k=5)),
                 reads=[b_ptr], writes=[b_cT])
            yield
            qsb, b_qsb = qsbr.next()
            kvsb, b_kvsb = kvsbr.next()
            for nb in range(3):
                pq, b_pq = psQ.next()
                for kc in range(3):
                    S.op("pe", lambda e, kc=kc, nb=nb, pq=pq, cT=cT: e.matmul(out=pq[:], lhsT=cT[:, kc, :],
                                                                 rhs=wq[:, kc, nb * 512:(nb + 1) * 512],
                                                                 start=(kc == 0), stop=(kc == 2)),
                         reads=[b_cT, b_wq], writes=[b_pq])
                S.op("act", lambda e, nb=nb, pq=pq, qsb=qsb: e.copy(out=qsb[:, nb * 512:(nb + 1) * 512], in_=pq[:]),
                     reads=[b_pq], writes=[b_qsb])
            for nb in range(4):
                pq, b_pq = psQ.next()
                for kc in range(2):
                    S.op("pe", lambda e, kc=kc, nb=nb, pq=pq, cT=cT: e.matmul(out=pq[:], lhsT=cT[:, 3 + kc, :],
                                                                 rhs=wkv[:, kc, nb * 512:(nb + 1) * 512],
                                                                 start=(kc == 0), stop=(kc == 1)),
                         reads=[b_cT, b_wkv], writes=[b_pq])
                eng = "act" if nb % 2 == 0 else "dve"
                if eng == "act":
                    S.op("act", lambda e, nb=nb, pq=pq, kvsb=kvsb: e.copy(out=kvsb[:, nb * 512:(nb + 1) * 512], in_=pq[:]),
                         reads=[b_pq], writes=[b_kvsb])
                else:
                    S.op("dve", lambda e, nb=nb, pq=pq, kvsb=kvsb: e.tensor_copy(out=kvsb[:, nb * 512:(nb + 1) * 512], in_=pq[:]),
                         reads=[b_pq], writes=[b_kvsb])
            q3 = qsb[:, :].rearrange("p (h d) -> p h d", h=16)
            kv3 = kvsb[:, :].rearrange("p (h d) -> p h d", h=16)
            ss, b_ss = sst.next()
            S.op("act", lambda e, jk=jk, qsb=qsb: e.activation(out=jk[:, :], in_=qsb[:, :], func=AF.Square),
                 reads=[b_qsb], writes=[b_jk])
            S.op("dve", lambda e, jk=jk, ss=ss: e.tensor_reduce(out=ss[:, 0:16], in_=jk[:, :].rearrange("p (h d) -> p h d", h=16),
                                                  axis=AX.X, op=ALU.add), reads=[b_jk], writes=[b_ss])
            tk, b_tk = tmpkr.next()
            S.op("act", lambda e, tk=tk, kv3=kv3: e.activation(out=tk[:], in_=kv3[:, :, 0:64], func=AF.Square),
                 reads=[b_kvsb], writes=[b_tk])
            S.op("dve", lambda e, tk=tk, ss=ss: e.tensor_reduce(out=ss[:, 16:32], in_=tk[:], axis=AX.X, op=ALU.add),
                 reads=[b_tk], writes=[b_ss])
            kpg, b_kpg = kpgr.next()
            S.op("act", lambda e, kpg=kpg, kpe=kpe, ss=ss: e.activation(out=kpg[:, 0, :], in_=kpe[:, 0, :], func=AF.Square,
                                               accum_out=ss[:, 96:97]), reads=[b_kpe], writes=[b_kpg, b_ss])
            S.op("dve", lambda e, ss=ss: e.tensor_scalar(out=ss[:, 16:32], in0=ss[:, 16:32], scalar1=ss[:, 96:97], scalar2=None,
                                                  op0=ALU.add), reads=[b_ss], writes=[b_ss])
            S.op("act", lambda e, ss=ss: e.activation(out=ss[:, 32:64], in_=ss[:, 0:32], func=AF.Sqrt, bias=EPS_AP[:, 0:1],
                                               scale=1.0 / 96.0), reads=[b_ss, b_eps], writes=[b_ss])
            S.op("dve", lambda e, ss=ss: e.reciprocal(out=ss[:, 64:96], in_=ss[:, 32:64]), reads=[b_ss], writes=[b_ss])
            rsq = ss[:, 64:80]
            rsk = ss[:, 80:96]
            S.op("dve", lambda e, q3=q3, rsq=rsq: e.tensor_tensor(out=q3, in0=q3, in1=rsq.unsqueeze(2).to_broadcast([128, 16, 96]),
                                                  op=ALU.mult), reads=[b_qsb, b_ss], writes=[b_qsb])
            S.op("dve", lambda e, q3=q3: e.tensor_tensor(out=q3, in0=q3, in1=gq.unsqueeze(1).to_broadcast([128, 16, 96]),
                                                   op=ALU.mult), reads=[b_qsb, b_hn], writes=[b_qsb])
            qb, b_qb = qbr.next()
            S.op("act", lambda e, qb=qb, q3=q3: e.copy(out=qb[:, :, 0:64], in_=q3[:, :, 0:64]), reads=[b_qsb], writes=[b_qb])
            for fn, rd, wr in rope(q3[:, :, 64:96], 16, tb, qb[:, :, 64:96], ropet):
                S.op("dve", fn, reads=[b_qsb] + rd, writes=wr + ([b_qb] if not wr else []))
            S.op("dve", lambda e, tk=tk, kv3=kv3, rsk=rsk: e.tensor_tensor(out=tk[:], in0=kv3[:, :, 0:64],
                                                  in1=rsk.unsqueeze(2).to_broadcast([128, 16, 64]), op=ALU.mult),
                 reads=[b_kvsb, b_ss], writes=[b_tk])
            kb, b_kb = kbr.next()
            S.op("dve", lambda e, tk=tk, kb=kb: e.tensor_tensor(out=kb[:, :, 0:64], in0=tk[:],
                                                   in1=gk[:, 0:64].unsqueeze(1).to_broadcast([128, 16, 64]), op=ALU.mult),
                 reads=[b_tk, b_hn], writes=[b_kb])
            S.op("dve", lambda e, kpg=kpg, kpe=kpe: e.tensor_tensor(out=kpg[:, 0, :], in0=kpe[:, 0, :], in1=gk[:, 64:96], op=ALU.mult),
                 reads=[b_kpe, b_hn], writes=[b_kpg])
            kr, b_kr = krr.next()
            for fn, rd, wr in rope(kpg[:, :, :], 1, tb, kr[:, :, :], ropet):
                S.op("dve", fn, reads=[b_kpg] + rd, writes=wr + ([b_kr] if not wr else []))
            S.op("dve", lambda e, kb=kb, kr=kr, rsk=rsk: e.tensor_tensor(out=kb[:, :, 64:96],
                                                  in0=kr[:, 0, :].unsqueeze(1).to_broadcast([128, 16, 32]),
                                                  in1=rsk.unsqueeze(2).to_broadcast([128, 16, 32]), op=ALU.mult),
                 reads=[b_kr, b_ss], writes=[b_kb])
            if tb % 4 == 0:
                sh['vb'] = vbr.next()
            vb, b_vb = sh['vb']
            S.op("pool", lambda e, vb=vb, kv3=kv3, tb=tb: e.tensor_copy(out=vb[:, :, tb % 4, 0:64], in_=kv3[:, :, 64:128]),
                 reads=[b_kvsb], writes=[b_vb])
            yield
            if tb % 2 == 0:
                sh['qT'] = qTr.next()
                sh['kT'] = kTr.next()
            qTs, b_qTs = sh['qT']
            kTs, b_kTs = sh['kT']
            for (src, b_src, dstT, b_dstT) in ((qb, b_qb, qTs, b_qTs), (kb, b_kb, kTs, b_kTs)):
                for half in range(2):
                    ptr, b_ptr = psT.next()
                    pv = ptr[:, :].bitcast(BF16)
                    for hh in range(8):
                        S.op("pe", lambda e, hh=hh, pv=pv, src=src, half=half: e.transpose(
                            out=pv[0:96, hh * 128:(hh + 1) * 128], in_=src[:, half * 8 + hh, :], identity=identb[:]),
                            reads=[b_src, b_identb], writes=[b_ptr])
                    off = (tb % 2) * 128
                    eng = "act" if half == 0 else "dve"
                    if eng == "act":
                        S.op("act", lambda e, pv=pv, dstT=dstT, half=half, off=off: e.copy(
                            out=dstT[0:96, half * 8:(half + 1) * 8, off:off + 128],
                            in_=pv[0:96, :].rearrange("p (h t) -> p h t", h=8)), reads=[b_ptr], writes=[b_dstT])
                    else:
                        S.op("dve", lambda e, pv=pv, dstT=dstT, half=half, off=off: e.tensor_copy(
                            out=dstT[0:96, half * 8:(half + 1) * 8, off:off + 128],
                            in_=pv[0:96, :].rearrange("p (h t) -> p h t", h=8)), reads=[b_ptr], writes=[b_dstT])
            if tb % 2 == 1:
                t0 = (tb - 1) * 128
                S.dma(lambda e, qTs=qTs, t0=t0: e.dma_start(out=QT[:, :, t0:t0 + 256].rearrange("h p t -> p h t"),
                                                             in_=qTs[0:96, :, :]),
                      reads=[b_qTs], writes=[b_QT], sem_buf=b_qTs, eng="pool")
                S.dma(lambda e, kTs=kTs, t0=t0: e.dma_start(out=KT[:, :, t0:t0 + 256].rearrange("h p t -> p h t"),
                                                             in_=kTs[0:96, :, :]),
                      reads=[b_kTs], writes=[b_KT], sem_buf=b_kTs, eng="pool")
            if tb % 4 == 3:
                c0 = tb - 3
                S.dma(lambda e, vb=vb, c0=c0: e.dma_start(out=Vs[:, :, c0:c0 + 4, :].rearrange("h p t c -> p h (t c)"),
                                                           in_=vb[:, :, :, :].rearrange("p h t c -> p h (t c)")),
                      reads=[b_vb], writes=[b_Vs], sem_buf=b_vb, eng="pool")
        gens = {}
        for i in range(32 + 2):
            if i < 32:
                gens[i] = block(i)
                next(gens[i])
            if 0 <= i - 1 < 32:
                next(gens[i - 1])
            if 0 <= i - 2 < 32:
                next(gens[i - 2], None)
            if 0 <= i - 1 < 32:
                next(gens[i - 1])
        S.barrier()
    if stop_after <= 2:
        S.emit(nc)
        return nc

    with ExitStack() as st:
        wzr = Ring(nc, st, un("z_w"), [128, 8, 512], BF16, 4)
        wdr = Ring(nc, st, un("dt_w"), [128, 8, 64], BF16, 1)
        with ExitStack() as st2:
            stgz = Ring(nc, st2, un("z_stg"), [128, 8, 512], F32, 2)
            stgd = Ring(nc, st2, un("dt_stg"), [128, 8, 64], F32, 1)
            wz = [wload(stgz, wzr, w_in[:, C_Z + cb * 512:C_Z + (cb + 1) * 512], 8, 512, gmix) for cb in range(4)]
            wdt, b_wdt = wload(stgd, wdr, w_in[:, C_DTF:C_DTF + 64], 8, 64, gmix)
            S.barrier()
        psz = Ring(nc, st, un("z_ps"), [128, 512], F32, 4, psum=True)
        psd = Ring(nc, st, un("dt_ps"), [128, 512], F32, 2, psum=True)
        zst = Ring(nc, st, un("z_st"), [128, 2048], BF16, 3)
        for tb in range(32):
            tsl = slice(tb * 128, (tb + 1) * 128)
            zt, b_zt = zst.next()
            for cb in range(4):
                pz, b_pz = psz.next()
                w_, b_w = wz[cb]
                for kc in range(8):
                    S.op("pe", lambda e, kc=kc, pz=pz, w_=w_, tsl=tsl: e.matmul(out=pz[:], lhsT=hT[:, kc, tsl], rhs=w_[:, kc, :],
                                                                  start=(kc == 0), stop=(kc == 7)),
                         reads=[b_hT, b_w], writes=[b_pz])
                S.op("act", lambda e, pz=pz, zt=zt, cb=cb: e.activation(out=zt[:, cb * 512:(cb + 1) * 512], in_=pz[:], func=AF.Silu),
                     reads=[b_pz], writes=[b_zt])
            pd, b_pd = psd.next()
            for kc in range(8):
                S.op("pe", lambda e, kc=kc, pd=pd, tsl=tsl: e.matmul(out=pd[:, 0:64], lhsT=hT[:, kc, tsl], rhs=wdt[:, kc, :],
                                                       start=(kc == 0), stop=(kc == 7)), reads=[b_hT, b_wdt], writes=[b_pd])
            S.op("dve", lambda e, pd=pd, tb=tb: e.tensor_copy(out=dtraw[:, tb, :], in_=pd[:, 0:64]), reads=[b_pd], writes=[b_dtraw])
            S.dma(lambda e, zt=zt, tsl=tsl: e.dma_start(out=zs[tsl, :], in_=zt[:]), reads=[b_zt], writes=[b_zs],
                  sem_buf=b_zt, eng="pool")
        S.barrier()

    with ExitStack() as st:
        stg = Ring(nc, st, un("x_stg"), [128, 8, 128], F32, 3)
        wr = Ring(nc, st, un("x_w"), [128, 8, 128], BF16, 3)
        psx = Ring(nc, st, un("x_ps"), [128, 512], F32, 3, psum=True)
        psc = Ring(nc, st, un("x_pc"), [128, 512], F32, 3, psum=True)
        xpre = Ring(nc, st, un("x_pre"), [128, S_ + 4], BF16, 2)
        dgr = Ring(nc, st, un("x_dg"), [128, 5, 128], BF16, 2)
        xcst = Ring(nc, st, un("x_cst"), [128, S_], BF16, 2)
        for (t_, b_) in xpre.items:
            S.op("pool", lambda e, t_=t_: e.memset(t_[:], 0.0), writes=[b_])

        def loadx(j):
            return wload(stg, wr, w_in[:, C_XBC + j * 128:C_XBC + (j + 1) * 128], 8, 128, gmix)

        def compx(j, h):
            w_, b_w = h
            xp, b_xp = xpre.next()
            dg, b_dg = dgr.next()
            for k in range(5):
                S.op("dve", lambda e, k=k, dg=dg, j=j: e.tensor_scalar(out=dg[:, k, :], in0=identf, scalar1=convw_t[:, j, k:k + 1],
                                                           scalar2=None, op0=ALU.mult),
                     reads=[b_mats, b_convw], writes=[b_dg])
            for tb in range(8):
                px, b_px = psx.next()
                ts = slice(tb * 512, (tb + 1) * 512)
                for kc in range(8):
                    S.op("pe", lambda e, kc=kc, px=px, w_=w_, ts=ts: e.matmul(out=px[:], lhsT=w_[:, kc, :], rhs=hT[:, kc, ts],
                                                                start=(kc == 0), stop=(kc == 7)),
                         reads=[b_w, b_hT], writes=[b_px])
                if tb % 2 == 0:
                    S.op("act", lambda e, px=px, xp=xp, tb=tb: e.copy(out=xp[:, 2 + tb * 512:2 + (tb + 1) * 512], in_=px[:]),
                         reads=[b_px], writes=[b_xp])
                else:
                    S.op("dve", lambda e, px=px, xp=xp, tb=tb: e.tensor_copy(out=xp[:, 2 + tb * 512:2 + (tb + 1) * 512], in_=px[:]),
                         reads=[b_px], writes=[b_xp])
            xc_, b_xc = xcst.next()
            for tb in range(8):
                pc, b_pc = psc.next()
                for k in range(5):
                    S.op("pe", lambda e, k=k, pc=pc, dg=dg, xp=xp, tb=tb: e.matmul(
                        out=pc[:], lhsT=dg[:, k, :], rhs=xp[:, tb * 512 + k:tb * 512 + k + 512],
                        start=(k == 0), stop=(k == 4)), reads=[b_dg, b_xp], writes=[b_pc])
                S.op("act", lambda e, pc=pc, xc_=xc_, tb=tb, j=j: e.activation(out=xc_[:, tb * 512:(tb + 1) * 512], in_=pc[:],
                                                               func=AF.Silu, bias=convb_t[:, j:j + 1]),
                     reads=[b_pc, b_convb], writes=[b_xc])
            for q4 in range(4):
                S.dma(lambda e, xc_=xc_, j=j, q4=q4: e.dma_start(
                    out=xcs[q4 * 8:(q4 + 1) * 8, :, j, :].rearrange("c p t -> p c t"),
                    in_=xc_[:, q4 * 1024:(q4 + 1) * 1024].rearrange("p (c t) -> p c t", c=8)),
                    reads=[b_xc], writes=[b_xcs], sem_buf=b_xc, eng="pool")

        pipeline(24, loadx, compx, 2)

        def loadg(j):
            return wload(stg, wr, w_in[:, C_GA + j * 128:C_GA + (j + 1) * 128], 8, 128, gmix)

        def compg(j, h):
            w_, b_w = h
            gt_, b_gt = xcst.next()
            for tb in range(8):
                px, b_px = psx.next()
                ts = slice(tb * 512, (tb + 1) * 512)
                for kc in range(8):
                    S.op("pe", lambda e, kc=kc, px=px, w_=w_, ts=ts: e.matmul(out=px[:], lhsT=w_[:, kc, :], rhs=hT[:, kc, ts],
                                                                start=(kc == 0), stop=(kc == 7)),
                         reads=[b_w, b_hT], writes=[b_px])
                S.op("act", lambda e, px=px, gt_=gt_, ts=ts: e.activation(out=gt_[:, ts], in_=px[:], func=AF.Sigmoid),
                     reads=[b_px], writes=[b_gt])
            S.dma(lambda e, gt_=gt_, j=j: e.dma_start(out=gts[j * 128:(j + 1) * 128, :], in_=gt_[:]),
                  reads=[b_gt], writes=[b_gts], sem_buf=b_gt, eng="pool")

        pipeline(16, loadg, compg, 2)
        S.barrier()
    if stop_after <= 3:
        S.emit(nc)
        return nc
    hst.close()

    def ssd_pass(fwd):
        with ExitStack() as st:
            AT = lambda n, shp, dt=F32: st.enter_context(nc.sbuf_tensor(un(n), shp, dt))
            dt_all = AT("dt_all", [128, 32, 32]); b_dt = Buf()
            da_all = AT("da_all", [128, 32, 32]); b_da = Buf()
            P_all = AT("P_all", [128, 32, 32]); b_P = Buf()
            bias_all = AT("bias_all", [128, 32, 32]); b_bias = Buf()
            wgt = AT("wgt", [128, 32, 32]); b_wgt = Buf()
            scl = AT("scl", [128, 32, 32]); b_scl = Buf()
            cdc = AT("cdc", [128, 32, 32]); b_cdc = Buf()
            tot = AT("tot", [128, 32, 32]); b_tot = Buf()
            nega = AT("nega", [128, 32]); b_nega = Buf()
            tmpa = AT("tmpa", [128, 32, 32]); b_tmpa = Buf()
            Sf = AT("Sf", [128, 2048]); b_Sf = [Buf() for _ in range(4)]
            Sbf = AT("Sbf", [128, 2048], BF16); b_Sbf = [Buf() for _ in range(4)]
            off = 0 if fwd else 32
            alog = ssp_t[:, off:off + 32]
            dtb = ssp_t[:, 64 + off:96 + off]
            dsk = ssp_t[:, 128:160]
            Uc = Umat if fwd else Ustr
            midx = 0 if fwd else 1
            ptA = Ring(nc, st, un("s_ptA"), [128, 512], F32, 1, psum=True)
            segb = Ring(nc, st, un("s_seg"), [128, 512], F32, 3, psum=True)
            pyr = Ring(nc, st, un("s_py"), [128, 512], F32, 2, psum=True)
            por = Ring(nc, st, un("s_po"), [128, 512], F32, 1, psum=True)
            pstr = Ring(nc, st, un("s_pst"), [128, 512], F32, 1, psum=True)
            segs = []
            for (t_, _b) in segb.items:
                for q in range(4):
                    segs.append((t_[:, q * 128:(q + 1) * 128], Buf()))
            segi = [0]
            flat = lambda t_: t_[:, :, :].rearrange("p a b -> p (a b)")
            S.op("pool", lambda e: e.memset(Sf[:], 0.0), writes=b_Sf)
            S.op("pool", lambda e: e.memset(Sbf[:], 0.0), writes=b_Sbf)
            S.op("act", lambda e: e.activation(out=nega[:], in_=alog, func=AF.Exp), reads=[b_ssp], writes=[b_nega])
            S.op("dve", lambda e: e.tensor_scalar(out=nega[:], in0=nega[:], scalar1=-1.0, scalar2=None, op0=ALU.mult),
                 reads=[b_nega], writes=[b_nega])
            S.op("dve", lambda e: e.tensor_tensor(out=tmpa[:], in0=dtraw[:, :, off:off + 32],
                                                  in1=dtb.unsqueeze(1).to_broadcast([128, 32, 32]), op=ALU.add),
                 reads=[b_dtraw, b_ssp], writes=[b_tmpa])
            S.op("act", lambda e: e.activation(out=tmpa[:], in_=tmpa[:], func=AF.Exp), reads=[b_tmpa], writes=[b_tmpa])
            S.op("act", lambda e: e.activation(out=dt_all[:], in_=tmpa[:], func=AF.Ln, bias=1.0), reads=[b_tmpa], writes=[b_dt])
            S.op("dve", lambda e: e.tensor_tensor(out=da_all[:], in0=dt_all[:], in1=nega[:].unsqueeze(1).to_broadcast([128, 32, 32]),
                                                  op=ALU.mult), reads=[b_dt, b_nega], writes=[b_da])
            for half in range(2):
                pp, b_pp = ptA.next()
                S.op("pe", lambda e, pp=pp, half=half: e.matmul(out=pp[:], lhsT=Uc, rhs=flat(da_all)[:, half * 512:(half + 1) * 512],
                                                                start=True, stop=True), reads=[b_mats, b_da], writes=[b_pp])
                S.op("dve", lambda e, pp=pp, half=half: e.tensor_copy(out=flat(P_all)[:, half * 512:(half + 1) * 512], in_=pp[:]),
                     reads=[b_pp], writes=[b_P])
            for half in range(2):
                pp, b_pp = ptA.next()
                S.op("pe", lambda e, pp=pp, half=half: e.matmul(out=pp[:], lhsT=onesf, rhs=flat(da_all)[:, half * 512:(half + 1) * 512],
                                                                start=True, stop=True), reads=[b_mats, b_da], writes=[b_pp])
                S.op("dve", lambda e, pp=pp, half=half: e.tensor_copy(out=flat(tot)[:, half * 512:(half + 1) * 512], in_=pp[:]),
                     reads=[b_pp], writes=[b_tot])
            S.op("dve", lambda e: e.tensor_tensor(out=tmpa[:], in0=tot[:], in1=P_all[:], op=ALU.subtract),
                 reads=[b_tot, b_P, b_dt], writes=[b_tmpa])
            e1, b_e1 = (scl, b_scl) if fwd else (wgt, b_wgt)
            e2, b_e2 = (wgt, b_wgt) if fwd else (scl, b_scl)
            S.op("act", lambda e: e.activation(out=e1[:], in_=P_all[:], func=AF.Exp), reads=[b_P], writes=[b_e1])
            S.op("act", lambda e: e.activation(out=e2[:], in_=tmpa[:], func=AF.Exp), reads=[b_tmpa], writes=[b_e2])
            S.op("act", lambda e: e.activation(out=cdc[:], in_=tot[:], func=AF.Exp), reads=[b_tot], writes=[b_cdc])
            S.op("dve", lambda e: e.tensor_scalar(out=bias_all[:], in0=P_all[:], scalar1=(-1.0 if fwd else 1.0), scalar2=None,
                                                  op0=ALU.mult), reads=[b_P], writes=[b_bias])

            xcr = Ring(nc, st, un("s_xc"), [128, 24, 128], BF16, 3)
            xsr = Ring(nc, st, un("s_xs"), [128, 2048], BF16, 2)
            Btr = Ring(nc, st, un("s_Bt"), [128, 512], BF16, 2)
            cbr = Ring(nc, st, un("s_cb"), [128, 512], F32, 2)
            xdtr = Ring(nc, st, un("s_xdt"), [128, 2048], BF16, 2)
            xwr = Ring(nc, st, un("s_xw"), [128, 2048], BF16, 2)
            decr = Ring(nc, st, un("s_dec"), [128, 128], F32, 12)
            MTr = Ring(nc, st, un("s_MT"), [128, 128], BF16, 12)
            yaccr = Ring(nc, st, un("s_ya"), [128, 2048], F32, 2)
            tmpr = Ring(nc, st, un("s_tmp"), [128, 512], F32, 2)
            if fwd:
                dskr = Ring(nc, st, un("s_dsk"), [128, 2048], F32, 1)
            else:
                zr = Ring(nc, st, un("s_z"), [128, 2048], BF16, 4)
                yfr = Ring(nc, st, un("s_yf"), [128, 2048], F32, 4)
                jkr = Ring(nc, st, un("s_jk"), [128, 512], BF16, 1)
                st4r = Ring(nc, st, un("s_st4"), [128, 12], F32, 2)
                mbr = Ring(nc, st, un("s_mb"), [128, 2048], BF16, 2)
                mstr = Ring(nc, st, un("s_mst"), [128, 16, 128], BF16, 2)
            order = list(range(32)) if fwd else list(range(31, -1, -1))

            def load(ci):
                c = order[ci]
                xc_, b_xc = xcr.next()
                S.dma(lambda e: e.dma_start(out=xc_[:], in_=xcs[c, :, :, :]), reads=[b_xcs], writes=[b_xc], sem_buf=b_xc)
                if fwd:
                    return (xc_, b_xc)
                z_, b_z = zr.next()
                yf_, b_yf = yfr.next()
                S.dma(lambda e: e.dma_start(out=z_[:], in_=zs[c * 128:(c + 1) * 128, :]), reads=[b_zs], writes=[b_z], sem_buf=b_z)
                S.dma(lambda e: e.dma_start(out=yf_[:], in_=yfs[c * 128:(c + 1) * 128, :]), reads=[b_yfs], writes=[b_yf], sem_buf=b_yf)
                return (xc_, b_xc, z_, b_z, yf_, b_yf)

            def prologue(ci, h):
                c = order[ci]
                xc_, b_xc = h[0], h[1]
                xs, b_xs = xsr.next()
                for half in range(2):
                    pt, b_pt = ptA.next()
                    pv = pt[:, :].bitcast(BF16)
                    for jj in range(8):
                        S.op("pe", lambda e, pv=pv, jj=jj, half=half: e.transpose(out=pv[:, jj * 128:(jj + 1) * 128],
                                                                                 in_=xc_[:, half * 8 + jj, :], identity=identb[:]),
                             reads=[b_xc, b_identb], writes=[b_pt])
                    if half == 0:
                        S.op("act", lambda e, pv=pv: e.copy(out=xs[:, 0:1024], in_=pv[:, :]), reads=[b_pt], writes=[b_xs])
                    else:
                        S.op("dve", lambda e, pv=pv: e.tensor_copy(out=xs[:, 1024:2048], in_=pv[:, :]), reads=[b_pt], writes=[b_xs])
                pt, b_pt = ptA.next()
                pvb = pt[:, :].bitcast(BF16)
                for g in range(4):
                    S.op("pe", lambda e, g=g: e.transpose(out=pvb[:, g * 128:(g + 1) * 128], in_=xc_[:, 16 + g, :], identity=identb[:]),
                         reads=[b_xc, b_identb], writes=[b_pt])
                Bt, b_Bt = Btr.next()
                S.op("dve", lambda e: e.tensor_copy(out=Bt[:], in_=pvb[:, 0:512]), reads=[b_pt], writes=[b_Bt])
                pcb, b_pcb = ptA.next()
                for g in range(4):
                    S.op("pe", lambda e, g=g: e.matmul(out=pcb[:, g * 128:(g + 1) * 128], lhsT=xc_[:, 16 + g, :], rhs=xc_[:, 20 + g, :],
                                                       start=True, stop=True), reads=[b_xc], writes=[b_pcb])
                cbT, b_cbT = cbr.next()
                S.op("act", lambda e: e.copy(out=cbT[:], in_=pcb[:]), reads=[b_pcb], writes=[b_cbT])
                xdt, b_xdt = xdtr.next()
                xw, b_xw = xwr.next()
                v3 = lambda t_: t_[:, :].rearrange("p (h d) -> p h d", h=32)
                S.op("dve", lambda e: e.tensor_tensor(out=v3(xdt), in0=v3(xs), in1=dt_all[:, c, :].unsqueeze(2).to_broadcast([128, 32, 64]),
                                                      op=ALU.mult), reads=[b_xs, b_dt], writes=[b_xdt])
                S.op("pool", lambda e: e.tensor_tensor(out=v3(xw), in0=v3(xdt), in1=wgt[:, c, :].unsqueeze(2).to_broadcast([128, 32, 64]),
                                                       op=ALU.mult), reads=[b_xdt, b_wgt], writes=[b_xw])
                return (xs, b_xs, Bt, b_Bt, cbT, b_cbT, xdt, b_xdt, xw, b_xw)

            def comp(ci, h, pr):
                c = order[ci]
                xc_, b_xc = h[0], h[1]
                xs, b_xs, Bt, b_Bt, cbT, b_cbT, xdt, b_xdt, xw, b_xw = pr
                v3 = lambda t_: t_[:, :].rearrange("p (h d) -> p h d", h=32)
                g8 = lambda t_: t_.rearrange("p (h d) -> p h d", h=8)
                ya, b_ya = yaccr.next()
                LAGH = 4
                mts = {}
                cur = {}

                def stageA(bi):
                    sb_, b_sb = segb.next()
                    for q in range(4):
                        h_ = bi * 4 + q
                        seg = sb_[:, q * 128:(q + 1) * 128]
                        S.op("pe", lambda e, seg=seg, h_=h_: e.matmul(out=seg, lhsT=da_all[:, c, h_:h_ + 1].to_broadcast([128, 128]),
                                                                      rhs=Uc, start=True, stop=False),
                             reads=[b_da, b_mats], writes=[b_sb])
                        S.op("pe", lambda e, seg=seg: e.matmul(out=seg, lhsT=identb[:], rhs=negm[:, midx, :], start=False, stop=True),
                             reads=[b_identb, b_negm], writes=[b_sb])
                    for q in range(4):
                        h_ = bi * 4 + q
                        g = h_ // 8
                        seg = sb_[:, q * 128:(q + 1) * 128]
                        dec, b_dec = decr.next()
                        S.op("act", lambda e, seg=seg, dec=dec, h_=h_: e.activation(out=dec[:], in_=seg, func=AF.Exp,
                                                                                   bias=bias_all[:, c, h_:h_ + 1],
                                                                                   scale=(1.0 if fwd else -1.0)),
                             reads=[b_sb, b_bias], writes=[b_dec])
                        MT, b_MT = MTr.next()
                        S.op("dve" if h_ % 2 == 0 else "pool", lambda e, dec=dec, MT=MT, g=g: e.tensor_tensor(
                            out=MT[:], in0=dec[:], in1=cbT[:, g * 128:(g + 1) * 128], op=ALU.mult),
                            reads=[b_dec, b_cbT], writes=[b_MT])
                        mts[h_] = (MT, b_MT)

                def stageB(h_):
                    g = h_ // 8
                    hh = h_ % 8
                    if hh == 0:
                        cur[0] = pyr.next()
                    py, b_py = cur[0]
                    MT, b_MT = mts.pop(h_)
                    S.op("pe", lambda e, MT=MT, py=py, hh=hh, h_=h_: e.matmul(out=py[:, hh * 64:(hh + 1) * 64], lhsT=MT[:],
                                                                             rhs=xdt[:, h_ * 64:(h_ + 1) * 64], start=True, stop=True),
                         reads=[b_MT, b_xdt], writes=[b_py])
                    if hh != 7:
                        return
                    po, b_po = por.next()
                    S.op("pe", lambda e, po=po, g=g: e.matmul(out=po[:], lhsT=xc_[:, 20 + g, :], rhs=Sbf[:, g * 512:(g + 1) * 512],
                                                              start=True, stop=True), reads=[b_xc, b_Sbf[g]], writes=[b_po])
                    pst, b_pst = pstr.next()
                    S.op("pe", lambda e, pst=pst, g=g: e.matmul(out=pst[:], lhsT=Bt[:, g * 128:(g + 1) * 128], rhs=xw[:, g * 512:(g + 1) * 512],
                                                                start=True, stop=True), reads=[b_Bt, b_xw], writes=[b_pst])
                    tmp, b_tmp = tmpr.next()
                    S.op("dve", lambda e, po=po, tmp=tmp, g=g: e.tensor_tensor(
                        out=g8(tmp[:, :]), in0=g8(po[:, :]), in1=scl[:, c, g * 8:(g + 1) * 8].unsqueeze(2).to_broadcast([128, 8, 64]),
                        op=ALU.mult), reads=[b_po, b_scl], writes=[b_tmp])
                    S.op("dve", lambda e, py=py, tmp=tmp, g=g: e.tensor_tensor(out=ya[:, g * 512:(g + 1) * 512], in0=py[:], in1=tmp[:],
                                                                              op=ALU.add), reads=[b_py, b_tmp], writes=[b_ya])
                    S.op("pool", lambda e, g=g: e.tensor_tensor(
                        out=g8(Sf[:, g * 512:(g + 1) * 512]), in0=g8(Sf[:, g * 512:(g + 1) * 512]),
                        in1=cdc[:, c, g * 8:(g + 1) * 8].unsqueeze(2).to_broadcast([128, 8, 64]), op=ALU.mult),
                        reads=[b_Sf[g], b_cdc], writes=[b_Sf[g]])
                    S.op("dve", lambda e, pst=pst, g=g: e.tensor_tensor(out=Sf[:, g * 512:(g + 1) * 512], in0=pst[:],
                                                                       in1=Sf[:, g * 512:(g + 1) * 512], op=ALU.add),
                         reads=[b_pst, b_Sf[g]], writes=[b_Sf[g]])
                    S.op("act", lambda e, g=g: e.copy(out=Sbf[:, g * 512:(g + 1) * 512], in_=Sf[:, g * 512:(g + 1) * 512]),
                         reads=[b_Sf[g]], writes=[b_Sbf[g]])

                for k in range(8 + 2):
                    if k < 8:
                        stageA(k)
                    if k >= 2:
                        for q in range(4):
                            stageB((k - 2) * 4 + q)
                if fwd:
                    dk, b_dk = dskr.next()
                    S.op("pool", lambda e: e.tensor_tensor(out=v3(dk), in0=v3(xs), in1=dsk.unsqueeze(2).to_broadcast([128, 32, 64]),
                                                           op=ALU.mult), reads=[b_xs, b_ssp], writes=[b_dk])
                    S.op("pool", lambda e: e.tensor_tensor(out=ya[:], in0=ya[:], in1=dk[:], op=ALU.add),
                         reads=[b_ya, b_dk], writes=[b_ya])
                    S.dma(lambda e: e.dma_start(out=yfs[c * 128:(c + 1) * 128, :], in_=ya[:]), reads=[b_ya], writes=[b_yfs],
                          sem_buf=b_ya, eng="pool")
                    return
                def epi():
                    z_, b_z, yf_, b_yf = h[2], h[3], h[4], h[5]
                    if dbg:
                        S.dma(lambda e: e.dma_start(out=ybs[c * 128:(c + 1) * 128, :], in_=ya[:]), reads=[b_ya], writes=[b_ybs],
                              sem_buf=b_ya, eng="pool")
                    S.op("pool", lambda e: e.tensor_tensor(out=ya[:], in0=ya[:], in1=yf_[:], op=ALU.add), reads=[b_ya, b_yf], writes=[b_ya])
                    S.op("dve", lambda e: e.tensor_tensor(out=ya[:], in0=ya[:], in1=z_[:], op=ALU.mult), reads=[b_ya, b_z], writes=[b_ya])
                    jk, b_jk = jkr.next()
                    s4, b_s4 = st4r.next()
                    for g in range(4):
                        S.op("act", lambda e, g=g: e.activation(out=jk[:], in_=ya[:, g * 512:(g + 1) * 512], func=AF.Square,
                                                                scale=1.0 / math.sqrt(512.0), accum_out=s4[:, g:g + 1]),
                             reads=[b_ya], writes=[b_jk, b_s4])
                    S.op("act", lambda e: e.activation(out=s4[:, 4:8], in_=s4[:, 0:4], func=AF.Sqrt, bias=EPS_AP[:, 0:1]),
                         reads=[b_s4, b_eps], writes=[b_s4])
                    S.op("dve", lambda e: e.reciprocal(out=s4[:, 8:12], in_=s4[:, 4:8]), reads=[b_s4], writes=[b_s4])
                    mb, b_mb = mbr.next()
                    for g in range(4):
                        S.op("dve", lambda e, g=g: e.tensor_scalar(out=mb[:, g * 512:(g + 1) * 512], in0=ya[:, g * 512:(g + 1) * 512],
                                                                   scalar1=s4[:, 8 + g:9 + g], scalar2=None, op0=ALU.mult),
                             reads=[b_ya, b_s4], writes=[b_mb])
                    mst, b_mst = mstr.next()
                    for half in range(2):
                        pt, b_pt = ptA.next()
                        pv = pt[:, :].bitcast(BF16)
                        for jj in range(8):
                            j = half * 8 + jj
                            S.op("pe", lambda e, pv=pv, jj=jj, j=j: e.transpose(out=pv[:, jj * 128:(jj + 1) * 128],
                                                                               in_=mb[:, j * 128:(j + 1) * 128], identity=identb[:]),
                                 reads=[b_mb, b_identb], writes=[b_pt])
                        S.op("act", lambda e, pv=pv, half=half: e.copy(out=mst[:, half * 8:(half + 1) * 8, :],
                                                                       in_=pv[:, :].rearrange("p (j t) -> p j t", j=8)),
                             reads=[b_pt], writes=[b_mst])
                    for half in range(2):
                        S.dma(lambda e, half=half: e.dma_start(
                            out=mTs[half * 1024:(half + 1) * 1024, c * 128:(c + 1) * 128].rearrange("(j p) t -> p j t", p=128),
                            in_=mst[:, half * 8:(half + 1) * 8, :]), reads=[b_mst], writes=[b_mTs], sem_buf=b_mst, eng="pool")

                if pend_epi:
                    pend_epi.pop()()
                pend_epi.append(epi)

            pend_epi = []
            hs = {}
            prs = {}
            for i in range(32 + 2):
                if i < 32:
                    hs[i] = load(i)
                if 1 <= i <= 32:
                    prs[i - 1] = prologue(i - 1, hs[i - 1])
                if i >= 2:
                    comp(i - 2, hs.pop(i - 2), prs.pop(i - 2))
            if pend_epi:
                pend_epi.pop()()
        S.barrier()

    ssd_pass(True)
    if stop_after <= 4 and stop_after == 4:
        pass
    ssd_pass(False)
    if stop_after <= 4:
        S.emit(nc)
        return nc

    mw = ExitStack()
    wpa = mw.enter_context(nc.sbuf_tensor("m_wpa", [128, 8, D_], BF16)); b_wpa = Buf()
    wpb = mw.enter_context(nc.sbuf_tensor("m_wpb", [128, 16, D_], BF16)); b_wpb = Buf()
    wo = mw.enter_context(nc.sbuf_tensor("m_wo", [128, 8, D_], BF16)); b_wo = Buf()
    mws = ExitStack()
    ms8 = mws.enter_context(nc.sbuf_tensor("m_s8", [128, 8, D_], F32)); b_ms8 = Buf()

    def preload_merge_weights():
        jobs = [(w_pa[:, :], wpa[:, :, :], b_wpa, None), (w_pb[0:1024, :], wpb[:, 0:8, :], b_wpb, gssm_t[:, 0:8]),
                (w_pb[1024:2048, :], wpb[:, 8:16, :], b_wpb, gssm_t[:, 8:16]), (w_o[:, :], wo[:, :, :], b_wo, None)]
        for src, dst, b_dst, g_ in jobs:
            S.dma(lambda e, src=src: e.dma_start(out=ms8[:], in_=src.rearrange("(kc p) n -> p kc n", p=128)),
                  writes=[b_ms8], sem_buf=b_ms8)
            if g_ is None:
                S.op("pool", lambda e, dst=dst: e.tensor_copy(out=dst, in_=ms8[:]), reads=[b_ms8], writes=[b_dst])
            else:
                S.op("pool", lambda e, dst=dst, g_=g_: e.tensor_tensor(out=dst, in0=ms8[:], in1=g_.unsqueeze(2).to_broadcast([128, 8, D_]),
                                                                      op=ALU.mult), reads=[b_ms8, b_gssm], writes=[b_dst])

    with ExitStack() as st:
        ktr = Ring(nc, st, un("t_k"), [128, S_], BF16, 2)
        qtr = Ring(nc, st, un("t_q"), [128, S_], BF16, 2)
        vtr = Ring(nc, st, un("t_v"), [128, 32, 65], BF16, 2)
        psS = Ring(nc, st, un("t_ps"), [128, 1024], F32, 3, psum=True)
        psO = Ring(nc, st, un("t_po"), [128, 1024], F32, 1, psum=True)
        pTr = Ring(nc, st, un("t_pT"), [128, 1024], BF16, 4)
        rdr = Ring(nc, st, un("t_rd"), [128, 1024], F32, 2)
        osr = Ring(nc, st, un("t_os"), [128, 1024], F32, 2)
        aor = Ring(nc, st, un("t_ao"), [128, S_], BF16, 2)
        sc = 1.0 / math.sqrt(96.0)
        LAG = 2
        tiles = {}

        def ensure(h_):
            if h_ >= NH or h_ in tiles:
                return
            kt, b_kt = ktr.next()
            qt, b_qt = qtr.next()
            vt, b_vt = vtr.next()
            S.dma(lambda e: e.dma_start(out=kt[0:96, :], in_=KT[h_, :, :]), reads=[b_KT], writes=[b_kt], sem_buf=b_kt)
            S.dma(lambda e: e.dma_start(out=qt[0:96, :], in_=QT[h_, :, :]), reads=[b_QT], writes=[b_qt], sem_buf=b_qt)
            S.dma(lambda e: e.dma_start(out=vt[:], in_=Vs[h_, :, :, :]), reads=[b_Vs], writes=[b_vt], sem_buf=b_vt)
            tiles[h_] = (kt, b_kt, qt, b_qt, vt, b_vt)

        steps = [(h_, sb, kc) for h_ in range(NH) for sb in range(4) for kc in range(32)]
        pend = {}
        cur_po = {}
        cur_ao = {}
        ensure(0)
        preload_merge_weights()
        for i in range(len(steps) + LAG):
            if i < len(steps):
                h_, sb, kc = steps[i]
                kt, b_kt, qt, b_qt, vt, b_vt = tiles[h_]
                ps, b_ps = psS.next()
                for u in range(2):
                    S.op("pe", lambda e, ps=ps, kc=kc, sb=sb, u=u, kt=kt, qt=qt: e.matmul(
                        out=ps[:, u * 512:(u + 1) * 512], lhsT=kt[0:96, kc * 128:(kc + 1) * 128],
                        rhs=qt[0:96, sb * 1024 + u * 512:sb * 1024 + (u + 1) * 512], start=True, stop=True),
                        reads=[b_kt, b_qt], writes=[b_ps])
                pT, b_pT = pTr.next()
                S.op("act", lambda e, ps=ps, pT=pT: e.activation(out=pT[:], in_=ps[:], func=AF.Exp, scale=sc),
                     reads=[b_ps], writes=[b_pT])
                pend[i] = (pT, b_pT)
            if i >= LAG:
                h_, sb, kc = steps[i - LAG]
                kt, b_kt, qt, b_qt, vt, b_vt = tiles[h_]
                pT, b_pT = pend.pop(i - LAG)
                if kc == 0:
                    cur_po[0] = psO.next()
                    if sb == 0:
                        cur_ao[0] = aor.next()
                        ensure(h_ + 1)
                po, b_po = cur_po[0]
                ao, b_ao = cur_ao[0]
                for u in range(2):
                    S.op("pe", lambda e, po=po, pT=pT, kc=kc, u=u, vt=vt: e.matmul(
                        out=po[0:65, u * 512:(u + 1) * 512], lhsT=vt[:, kc, :], rhs=pT[:, u * 512:(u + 1) * 512],
                        start=(kc == 0), stop=(kc == 31)), reads=[b_vt, b_pT], writes=[b_po])
                if kc == 31:
                    osb, b_osb = osr.next()
                    S.op("dve", lambda e, po=po, osb=osb: e.tensor_copy(out=osb[0:65, :], in_=po[0:65, :]), reads=[b_po], writes=[b_osb])
                    rd, b_rd = rdr.next()
                    S.op("dve", lambda e, osb=osb, rd=rd: e.reciprocal(out=rd[64:65, :], in_=osb[64:65, :]), reads=[b_osb], writes=[b_rd])
                    pb, b_pb = psS.next()
                    for u in range(2):
                        S.op("pe", lambda e, pb=pb, rd=rd, u=u: e.matmul(out=pb[0:64, u * 512:(u + 1) * 512], lhsT=mats[64:65, 2, 0:64],
                                                                         rhs=rd[64:65, u * 512:(u + 1) * 512], start=True, stop=True),
                             reads=[b_mats, b_rd], writes=[b_pb])
                    qs = slice(sb * 1024, (sb + 1) * 1024)
                    S.op("dve", lambda e, pb=pb, osb=osb, qs=qs, ao=ao: e.tensor_tensor(
                        out=ao[0:64, qs], in0=pb[0:64, :], in1=osb[0:64, :], op=ALU.mult),
                        reads=[b_pb, b_osb], writes=[b_ao])
                    if sb == 3:
                        S.dma(lambda e, h_=h_, ao=ao: e.dma_start(out=aTs[h_ * 64:(h_ + 1) * 64, :], in_=ao[0:64, :]),
                              reads=[b_ao], writes=[b_aTs], sem_buf=b_ao, eng="pool")
        S.barrier()
    if stop_after <= 5:
        S.emit(nc)
        return nc

    mws.close()
    with ExitStack() as st:
        atr = Ring(nc, st, un("m_at"), [128, 8, 512], BF16, 2)
        mtr = Ring(nc, st, un("m_mt"), [128, 16, 512], BF16, 2)
        gtr = Ring(nc, st, un("m_gt"), [128, 16, 512], BF16, 2)
        mgr = Ring(nc, st, un("m_mg"), [128, 8, 512], BF16, 2)
        t1r = Ring(nc, st, un("m_t1"), [128, 512], F32, 2)
        t2r = Ring(nc, st, un("m_t2"), [128, 512], F32, 2)
        xr = Ring(nc, st, un("m_x"), [128, D_], F32, 3)
        ps = Ring(nc, st, un("m_ps"), [128, 512], F32, 6, psum=True)
        def loadm(t):
            at, b_at = atr.next()
            mt, b_mt = mtr.next()
            gt_, b_gt = gtr.next()
            ts = slice(t * 512, (t + 1) * 512)
            S.dma(lambda e: e.dma_start(out=at[:], in_=aTs[:, ts].rearrange("(k p) t -> p k t", p=128)), reads=[b_aTs], writes=[b_at], sem_buf=b_at)
            S.dma(lambda e: e.dma_start(out=mt[:], in_=mTs[:, ts].rearrange("(k p) t -> p k t", p=128)), reads=[b_mTs], writes=[b_mt], sem_buf=b_mt)
            S.dma(lambda e: e.dma_start(out=gt_[:], in_=gts[:, ts].rearrange("(k p) t -> p k t", p=128)), reads=[b_gts], writes=[b_gt], sem_buf=b_gt)
            return (at, b_at, mt, b_mt, gt_, b_gt)

        def compm(t, hd):
            at, b_at, mt, b_mt, gt_, b_gt = hd
            mg, b_mg = mgr.next()
            for dc in range(8):
                pa, b_pa = ps.next()
                pb, b_pb = ps.next()
                for kc in range(8):
                    S.op("pe", lambda e, pa=pa, kc=kc, dc=dc: e.matmul(out=pa[:], lhsT=wpa[:, kc, dc * 128:(dc + 1) * 128], rhs=at[:, kc, :],
                                                                       start=(kc == 0), stop=(kc == 7)), reads=[b_wpa, b_at], writes=[b_pa])
                for kc in range(16):
                    S.op("pe", lambda e, pb=pb, kc=kc, dc=dc: e.matmul(out=pb[:], lhsT=wpb[:, kc, dc * 128:(dc + 1) * 128], rhs=mt[:, kc, :],
                                                                       start=(kc == 0), stop=(kc == 15)), reads=[b_wpb, b_mt], writes=[b_pb])
                t1, b_t1 = t1r.next()
                t2, b_t2 = t2r.next()
                S.op("dve", lambda e, pa=pa, t1=t1, dc=dc: e.tensor_tensor(out=t1[:], in0=pa[:], in1=gt_[:, dc, :], op=ALU.mult),
                     reads=[b_pa, b_gt], writes=[b_t1])
                S.op("dve", lambda e, pb=pb, t2=t2, dc=dc: e.tensor_tensor(out=t2[:], in0=pb[:], in1=gt_[:, 8 + dc, :], op=ALU.mult),
                     reads=[b_pb, b_gt], writes=[b_t2])
                S.op("pool", lambda e, t1=t1, t2=t2, dc=dc: e.tensor_tensor(out=mg[:, dc, :], in0=t1[:], in1=t2[:], op=ALU.add),
                     reads=[b_t1, b_t2], writes=[b_mg])
            for sb in range(4):
                tb = t * 4 + sb
                xt, b_xt = xr.next()
                S.dma(lambda e, xt=xt, tb=tb: e.dma_start(out=xt[:], in_=x1s[tb * 128:(tb + 1) * 128, :]),
                      reads=[b_x1s], writes=[b_xt], sem_buf=b_xt)
                for half in range(2):
                    p, b_p = ps.next()
                    for kc in range(8):
                        S.op("pe", lambda e, p=p, kc=kc, sb=sb, half=half: e.matmul(
                            out=p[:], lhsT=mg[:, kc, sb * 128:(sb + 1) * 128], rhs=wo[:, kc, half * 512:(half + 1) * 512],
                            start=(kc == 0), stop=(kc == 7)), reads=[b_mg, b_wo], writes=[b_p])
                    S.op("dve", lambda e, p=p, xt=xt, half=half: e.tensor_tensor(out=xt[:, half * 512:(half + 1) * 512], in0=p[:],
                                                                                in1=xt[:, half * 512:(half + 1) * 512], op=ALU.add),
                         reads=[b_p, b_xt], writes=[b_xt])
                S.dma(lambda e, xt=xt, tb=tb: e.dma_start(out=x2s[tb * 128:(tb + 1) * 128, :], in_=xt[:]),
                      reads=[b_xt], writes=[b_x2s], sem_buf=b_xt, eng="pool")

        pipeline(8, loadm, compm, 1)
        S.barrier()
    mw.close()
    hst2 = ExitStack()
    hT2 = hst2.enter_context(nc.sbuf_tensor("hT2", [128, 8, S_], BF16)); b_hT2 = Buf("hT2")
    norm_phase(x2s, b_x2s, hT2, b_hT2)

    with ExitStack() as fst:
        wd2 = fst.enter_context(nc.sbuf_tensor("wd2", [128, NFF, D_], BF16)); b_wd2 = Buf()
        ffn_gateup(w_g2, w_u2, 2, hT2, b_hT2, w_d2, wd2, b_wd2)
        ffn_down(w_d2, x2s, b_x2s, y_out, b_yout, None, wd2, b_wd2)
    S.emit(nc)
    return nc


def _fm(v, kc):
    return np.ascontiguousarray(np.asarray(v, np.float32).reshape(kc, 128).T)


_CACHE = {}


def consts():
    ii = np.arange(128)
    U = (ii[:, None] <= ii[None, :]).astype(np.float32)
    Us = (ii[:, None] < ii[None, :]).astype(np.float32)
    ones = np.ones((128, 128), np.float32)
    I = np.eye(128, dtype=np.float32)
    mats = np.ascontiguousarray(np.stack([U, Us, ones, I], axis=1))
    negf = np.where(ii[:, None] > ii[None, :], -30000.0, 0.0).astype(np.float32)
    posb = np.where(ii[:, None] < ii[None, :], 30000.0, 0.0).astype(np.float32)
    neg = np.ascontiguousarray(np.stack([negf, posb], axis=1)).astype(ml_dtypes.bfloat16)
    invf = (1.0 / (10000.0 ** (np.arange(0, 32, 2, dtype=np.float32) / 32.0))).astype(np.float32)[None, :]
    return dict(c_identb=I.astype(ml_dtypes.bfloat16), c_mats=mats, c_neg=neg, c_invf=invf)


def make_shared(inp):
    f = lambda k: np.asarray(inp[k], np.float32)[0]
    d = {}
    d["gfm"] = np.ascontiguousarray(np.concatenate([_fm(f("ffn1_norm"), 8), _fm(f("mix_norm"), 8), _fm(f("ffn2_norm"), 8)], axis=1))
    d["gqa"] = _fm(f("q_a_norm"), 3)
    d["gkva"] = _fm(f("kv_a_norm"), 2)
    d["gssm"] = _fm(f("ssm_norm"), 16)
    d["w_g1"] = f("ffn1_w_gate"); d["w_u1"] = f("ffn1_w_up"); d["w_d1"] = f("ffn1_w_down")
    d["w_g2"] = f("ffn2_w_gate"); d["w_u2"] = f("ffn2_w_up"); d["w_d2"] = f("ffn2_w_down")
    d["w_in"] = f("w_in"); d["w_qb"] = f("w_q_b"); d["w_kvb"] = f("w_kv_b")
    d["hn"] = np.concatenate([f("q_head_norm"), f("k_head_norm")])[None, :].astype(np.float32)
    cw = f("conv_w")[:, 0, :]
    d["convw"] = np.ascontiguousarray(cw.T.reshape(24, 128, 5).transpose(1, 0, 2))
    d["convb"] = _fm(f("conv_b"), 24)
    d["ssp"] = np.concatenate([f("a_log_fwd"), f("a_log_bwd"), f("dt_bias_fwd"), f("dt_bias_bwd"), f("d_skip")])[None, :].astype(np.float32)
    d["w_pa"] = f("w_attn_branch"); d["w_pb"] = f("w_ssm_branch"); d["w_o"] = f("w_out")
    d.update(consts())
    return d


def make_inmap(inp, shared, b):
    d = dict(shared)
    d["x"] = np.ascontiguousarray(np.asarray(inp["x"], np.float32)[b])
    p = np.asarray(inp["positions"], np.int32)[b]
    d["pos"] = np.ascontiguousarray(p.reshape(32, 128).T)
    return d


def kernel(**inputs):
    nb = int(np.asarray(inputs["x"]).shape[0])
    nc = build(dbg=False)
    shared = make_shared(inputs)
    in_maps = [make_inmap(inputs, shared, b) for b in range(nb)]
    res = run_bass_kernel_spmd(nc, in_maps, core_ids=list(range(nb)))
    out = np.stack([np.asarray(res.results[b]["y"], dtype=np.float32) for b in range(nb)], axis=0)
    return out
```

```python
import math
from contextlib import ExitStack
import numpy as np
import ml_dtypes
import concourse.bass as bass
import concourse.mybir as mybir
from concourse.bass_utils import run_bass_kernel_spmd

F32 = mybir.dt.float32
BF16 = mybir.dt.bfloat16
I32 = mybir.dt.int32
AF = mybir.ActivationFunctionType
ALU = mybir.AluOpType
AX = mybir.AxisListType

S_ = 4096
D_ = 1024
FF = 2816
NFF = 22
NH = 16
EPS = 1e-6
C_Q, C_KV, C_PE, C_Z, C_XBC, C_DTF, C_DTB, C_GA, C_GB = 0, 384, 640, 672, 2720, 5792, 5824, 5856, 6880
IN_DIM = 7904
ENGS = ("pe", "act", "dve", "pool", "sp")
FUSE_WAITS = True


class DSem:
    def __init__(self):
        self.count = 0
        self.handle = None


class Buf:
    __slots__ = ("name", "lw", "rd", "dsem", "ep")

    def __init__(self, name=""):
        self.name = name
        self.lw = None
        self.rd = []
        self.dsem = None
        self.ep = -1


class Op:
    __slots__ = ("eng", "fn", "idx", "waits", "dwaits", "inc", "dsem", "know", "seq", "multi")


class Sched:
    def __init__(self):
        self.ops = {e: [] for e in ENGS}
        self.know = {e: {} for e in ENGS}
        self.dsems = []
        self.free = []
        self.epoch = 0

    def _add(self, eng, fn, reads, writes, dsem=None, extra=(), extra_ds=()):
        op = Op()
        op.eng = eng
        op.fn = fn
        op.idx = len(self.ops[eng])
        op.waits = {}
        op.dwaits = {}
        op.inc = False
        op.dsem = dsem
        op.seq = None
        op.multi = False
        know = self.know[eng]
        deps = list(extra)
        for b in reads:
            if b.lw is not None:
                deps.append(b.lw)
        for b in writes:
            if b.lw is not None:
                deps.append(b.lw)
            deps.extend(b.rd)
        for a in deps:
            if a is op:
                continue
            if a.dsem is None:
                if a.eng == "pe" and eng == "pe":
                    continue
                if know.get(a.eng, -1) >= a.idx:
                    continue
                a.inc = True
                cur = op.waits.get(a.eng)
                if cur is None or cur.idx < a.idx:
                    op.waits[a.eng] = a
                for k, v in a.know.items():
                    if know.get(k, -1) < v:
                        know[k] = v
                know[a.eng] = max(know.get(a.eng, -1), a.idx)
            else:
                ds = a.dsem
                v = ds.count
                if know.get(ds, -1) >= v:
                    continue
                op.dwaits[ds] = v
                for k, vv in a.know.items():
                    if know.get(k, -1) < vv:
                        know[k] = vv
                know[ds] = v
        for ds in extra_ds:
            v = ds.count
            if know.get(ds, -1) < v:
                op.dwaits[ds] = v
                know[ds] = v
        if dsem is not None:
            dsem.count += 16
        op.know = dict(know)
        for b in reads:
            b.rd.append(op)
        for b in writes:
            b.lw = op
            b.rd = []
        self.ops[eng].append(op)
        return op

    def op(self, eng, fn, reads=(), writes=(), multi=False):
        o = self._add(eng, fn, reads, writes)
        o.multi = multi
        return o

    def dma(self, fn, reads=(), writes=(), sem_buf=None, eng="sp"):
        if sem_buf.dsem is None or sem_buf.ep != self.epoch:
            if self.free:
                sem_buf.dsem = self.free.pop()
            else:
                sem_buf.dsem = DSem()
                self.dsems.append(sem_buf.dsem)
            sem_buf.ep = self.epoch
        return self._add(eng, fn, reads, writes, dsem=sem_buf.dsem)

    def barrier(self):
        lasts = []
        for e in ENGS:
            if e == "sp":
                continue
            for o in reversed(self.ops[e]):
                if o.dsem is None:
                    lasts.append(o)
                    break
        spop = self._add("sp", lambda e: e.nop(), (), (), extra=lasts, extra_ds=list(self.dsems))
        self.epoch += 1
        self.free = list(self.dsems)
        for e in ENGS:
            if e == "sp":
                continue
            self._add(e, lambda eh: eh.nop(), (), (), extra=[spop])

    def emit(self, nc):
        with ExitStack() as st:
            esem = {e: st.enter_context(nc.semaphore("es_" + e)) for e in ENGS}
            for i, d in enumerate(self.dsems):
                d.handle = st.enter_context(nc.semaphore("ds%d" % i))
            for e in ENGS:
                c = 0
                for o in self.ops[e]:
                    if o.dsem is None and o.inc:
                        c += 1
                        o.seq = c
            block = st.enter_context(nc.Block())

            def run(e, eh):
                for o in self.ops[e]:
                    wl = [(esem[se], a.seq) for se, a in o.waits.items()] + [(ds.handle, v) for ds, v in o.dwaits.items()]
                    attach = None
                    if wl and o.dsem is None and not o.multi and e != "sp" and FUSE_WAITS:
                        attach = wl.pop()
                    for hh_, vv_ in wl:
                        eh.wait_ge(hh_, vv_)
                    n0 = nc.n_instructions()
                    ins = o.fn(eh)
                    if attach is not None:
                        if nc.n_instructions() - n0 != 1:
                            raise RuntimeError("multi-instruction op with fused wait on %s (%d)" % (e, nc.n_instructions() - n0))
                        ins._wait_ge(attach[0], attach[1])
                    if o.dsem is not None:
                        ins.then_inc(o.dsem.handle, 16)
                    elif o.inc:
                        ins.then_inc(esem[e], 1)
                if e == "sp":
                    for ds in self.dsems:
                        eh.wait_ge(ds.handle, ds.count)

            @block.tensor
            def _(eh):
                run("pe", eh)

            @block.scalar
            def _(eh):
                run("act", eh)

            @block.vector
            def _(eh):
                run("dve", eh)

            @block.gpsimd
            def _(eh):
                run("pool", eh)

            @block.sync
            def _(eh):
                run("sp", eh)


class Ring:
    def __init__(self, nc, st, name, shape, dtype, n, psum=False):
        self.items = []
        for i in range(n):
            if psum:
                t = st.enter_context(nc.psum_tensor("%s%d" % (name, i), shape, dtype))
            else:
                t = st.enter_context(nc.sbuf_tensor("%s%d" % (name, i), shape, dtype))
            self.items.append((t, Buf("%s%d" % (name, i))))
        self.i = 0

    def next(self):
        r = self.items[self.i % len(self.items)]
        self.i += 1
        return r


def pipeline(n, load_fn, compute_fn, depth):
    hs = {}
    for i in range(n + depth):
        if i < n:
            hs[i] = load_fn(i)
        if i >= depth:
            compute_fn(i - depth, hs.pop(i - depth))


class K:
    pass


def build(dbg=False, stop_after=99):
    nc = bass.Bass("TRN2", target_bir_lowering=False)
    S = Sched()
    uid = [0]

    def un(p):
        uid[0] += 1
        return "%s_%d" % (p, uid[0])

    def inp(name, shape, dt=F32):
        return nc.dram_tensor(name, shape, dt, kind="ExternalInput").ap()

    def scratch(name, shape, dt, out=False):
        kind = "ExternalOutput" if (out or dbg) else "Internal"
        return nc.dram_tensor(name, shape, dt, kind=kind).ap(), Buf(name)

    x = inp("x", [S_, D_])
    pos = inp("pos", [128, 32], I32)
    gfm = inp("gfm", [128, 24])
    gqa = inp("gqa", [128, 3])
    gkva = inp("gkva", [128, 2])
    gssm = inp("gssm", [128, 16])
    w_g1 = inp("w_g1", [D_, FF]); w_u1 = inp("w_u1", [D_, FF]); w_d1 = inp("w_d1", [FF, D_])
    w_g2 = inp("w_g2", [D_, FF]); w_u2 = inp("w_u2", [D_, FF]); w_d2 = inp("w_d2", [FF, D_])
    w_in = inp("w_in", [D_, IN_DIM])
    w_qb = inp("w_qb", [384, 1536]); w_kvb = inp("w_kvb", [256, 2048])
    hn = inp("hn", [1, 192])
    convw = inp("convw", [128, 24, 5]); convb = inp("convb", [128, 24])
    ssp = inp("ssp", [1, 160])
    w_pa = inp("w_pa", [D_, D_]); w_pb = inp("w_pb", [2048, D_]); w_o = inp("w_o", [D_, D_])
    c_identb = inp("c_identb", [128, 128], BF16)
    c_mats = inp("c_mats", [128, 4, 128])
    c_neg = inp("c_neg", [128, 2, 128], BF16)
    c_invf = inp("c_invf", [1, 16])

    y_out, b_yout = scratch("y", [S_, D_], F32, out=True)
    x1s, b_x1s = scratch("x1s", [S_, D_], F32)
    x2s, b_x2s = scratch("x2s", [S_, D_], F32)
    hmid, b_hmid = scratch("hmid", [FF, S_], BF16)
    QT, b_QT = scratch("QT", [NH, 96, S_], BF16)
    KT, b_KT = scratch("KT", [NH, 96, S_], BF16)
    Vs, b_Vs = scratch("Vs", [NH, 128, 32, 65], BF16)
    zs, b_zs = scratch("zs", [S_, 2048], BF16)
    xcs, b_xcs = scratch("xcs", [32, 128, 24, 128], BF16)
    gts, b_gts = scratch("gts", [2048, S_], BF16)
    yfs, b_yfs = scratch("yfs", [S_, 2048], F32)
    mTs, b_mTs = scratch("mTs", [2048, S_], BF16)
    aTs, b_aTs = scratch("aTs", [D_, S_], BF16)
    if dbg:
        ybs, b_ybs = scratch("ybs", [S_, 2048], F32)

    top = ExitStack()
    A = lambda name, shape, dt: top.enter_context(nc.sbuf_tensor(name, shape, dt))
    identb = A("identb", [128, 128], BF16); b_identb = Buf()
    mats = A("mats", [128, 4, 128], F32); b_mats = Buf()
    negm = A("negm", [128, 2, 128], BF16); b_negm = Buf()
    gfm_t = A("gfm_t", [128, 24], F32); b_gfm = Buf()
    gqa_t = A("gqa_t", [128, 3], F32); b_gqa = Buf()
    gkva_t = A("gkva_t", [128, 2], F32); b_gkva = Buf()
    gssm_t = A("gssm_t", [128, 16], F32); b_gssm = Buf()
    hn_t = A("hn_t", [128, 192], F32); b_hn = Buf()
    ssp_t = A("ssp_t", [128, 160], F32); b_ssp = Buf()
    convw_t = A("convw_t", [128, 24, 5], F32); b_convw = Buf()
    convb_t = A("convb_t", [128, 24], F32); b_convb = Buf()
    cos_t = A("cos_t", [128, 32, 16], F32); b_cos = Buf()
    sin_t = A("sin_t", [128, 32, 16], F32); b_sin = Buf()
    dtraw = A("dtraw", [128, 32, 64], F32); b_dtraw = Buf()

    def ld(dst, src, b):
        S.dma(lambda e: e.dma_start(out=dst, in_=src), writes=[b], sem_buf=b)

    ld(identb[:], c_identb[:, :], b_identb)
    ld(mats[:], c_mats[:, :, :], b_mats)
    ld(negm[:], c_neg[:, :, :], b_negm)
    ld(gfm_t[:], gfm[:, :], b_gfm)
    ld(gqa_t[:], gqa[:, :], b_gqa)
    ld(gkva_t[:], gkva[:, :], b_gkva)
    ld(gssm_t[:], gssm[:, :], b_gssm)
    ld(hn_t[:], hn.partition_broadcast(128), b_hn)
    ld(ssp_t[:], ssp.partition_broadcast(128), b_ssp)
    ld(convw_t[:], convw[:, :, :], b_convw)
    ld(convb_t[:], convb[:, :], b_convb)
    Umat = mats[:, 0, :]
    Ustr = mats[:, 1, :]
    onesf = mats[:, 2, :]
    identf = mats[:, 3, :]

    with ExitStack() as st:
        post = st.enter_context(nc.sbuf_tensor("post", [128, 32], I32)); b_post = Buf()
        posf = st.enter_context(nc.sbuf_tensor("posf", [128, 32], F32)); b_posf = Buf()
        invf = st.enter_context(nc.sbuf_tensor("invf", [128, 16], F32)); b_invf = Buf()
        ang = st.enter_context(nc.sbuf_tensor("ang", [128, 32, 16], F32)); b_ang = Buf()
        ang2 = st.enter_context(nc.sbuf_tensor("ang2", [128, 32, 16], F32)); b_ang2 = Buf()
        ld(post[:], pos[:, :], b_post)
        ld(invf[:], c_invf.partition_broadcast(128), b_invf)
        S.op("dve", lambda e: e.tensor_copy(out=posf[:], in_=post[:]), reads=[b_post], writes=[b_posf])
        S.op("dve", lambda e: e.tensor_tensor(out=ang[:], in0=posf[:].unsqueeze(2).to_broadcast([128, 32, 16]),
                                              in1=invf[:].unsqueeze(1).to_broadcast([128, 32, 16]), op=ALU.mult),
             reads=[b_posf, b_invf], writes=[b_ang])
        PI = math.pi
        angi = st.enter_context(nc.sbuf_tensor("angi", [128, 32, 16], I32)); b_angi = Buf()
        ang3 = st.enter_context(nc.sbuf_tensor("ang3", [128, 32, 16], F32)); b_ang3 = Buf()

        def rr(shift, dst, b_dst):
            S.op("dve", lambda e: e.tensor_scalar(out=ang2[:], in0=ang[:], scalar1=shift, scalar2=None, op0=ALU.add),
                 reads=[b_ang], writes=[b_ang2])
            S.op("dve", lambda e: e.tensor_scalar(out=ang3[:], in0=ang2[:], scalar1=1.0 / (2 * PI), scalar2=None,
                                                  op0=ALU.mult), reads=[b_ang2], writes=[b_ang3])
            S.op("dve", lambda e: e.tensor_copy(out=angi[:], in_=ang3[:]), reads=[b_ang3], writes=[b_angi])
            S.op("dve", lambda e: e.tensor_copy(out=ang3[:], in_=angi[:]), reads=[b_angi], writes=[b_ang3])
            S.op("dve", lambda e: e.scalar_tensor_tensor(out=ang2[:], in0=ang3[:], scalar=-2 * PI, in1=ang2[:],
                                                         op0=ALU.mult, op1=ALU.add), reads=[b_ang3, b_ang2], writes=[b_ang2])
            S.op("dve", lambda e: e.tensor_scalar(out=ang3[:], in0=ang2[:], scalar1=-PI, scalar2=1e9,
                                                  op0=ALU.add, op1=ALU.mult), reads=[b_ang2], writes=[b_ang3])
            S.op("dve", lambda e: e.tensor_scalar(out=ang3[:], in0=ang3[:], scalar1=0.0, scalar2=1.0,
                                                  op0=ALU.max, op1=ALU.min), reads=[b_ang3], writes=[b_ang3])
            S.op("dve", lambda e: e.scalar_tensor_tensor(out=ang2[:], in0=ang3[:], scalar=-2 * PI, in1=ang2[:],
                                                         op0=ALU.mult, op1=ALU.add), reads=[b_ang3, b_ang2], writes=[b_ang2])
            S.op("dve", lambda e: e.tensor_scalar(out=ang3[:], in0=ang2[:], scalar1=PI, scalar2=-1e9,
                                                  op0=ALU.add, op1=ALU.mult), reads=[b_ang2], writes=[b_ang3])
            S.op("dve", lambda e: e.tensor_scalar(out=ang3[:], in0=ang3[:], scalar1=0.0, scalar2=1.0,
                                                  op0=ALU.max, op1=ALU.min), reads=[b_ang3], writes=[b_ang3])
            S.op("dve", lambda e: e.scalar_tensor_tensor(out=ang2[:], in0=ang3[:], scalar=2 * PI, in1=ang2[:],
                                                         op0=ALU.mult, op1=ALU.add), reads=[b_ang3, b_ang2], writes=[b_ang2])
            S.op("dve", lambda e: e.tensor_scalar(out=ang2[:], in0=ang2[:], scalar1=PI * (1 - 1e-6),
                                                  scalar2=-PI * (1 - 1e-6), op0=ALU.min, op1=ALU.max),
                 reads=[b_ang2], writes=[b_ang2])
            S.op("act", lambda e: e.activation(out=dst, in_=ang2[:], func=AF.Sin), reads=[b_ang2], writes=[b_dst])

        rr(0.0, sin_t[:], b_sin)
        rr(0.5 * PI, cos_t[:], b_cos)
        S.barrier()

    def wload(stage_ring, w_ring, wsrc, kc, n, gain=None, cast_eng="pool"):
        stg, b_stg = stage_ring.next()
        wt, b_wt = w_ring.next()
        S.dma(lambda e: e.dma_start(out=stg[:, 0:kc, 0:n], in_=wsrc.rearrange("(kc p) n -> p kc n", p=128)),
              writes=[b_stg], sem_buf=b_stg)
        if gain is None:
            S.op(cast_eng, lambda e: e.tensor_copy(out=wt[:, 0:kc, 0:n], in_=stg[:, 0:kc, 0:n]),
                 reads=[b_stg], writes=[b_wt])
        else:
            g_ap, b_g = gain
            S.op(cast_eng, lambda e: e.tensor_tensor(out=wt[:, 0:kc, 0:n], in0=stg[:, 0:kc, 0:n],
                                                     in1=g_ap.unsqueeze(2).to_broadcast([128, kc, n]), op=ALU.mult),
                 reads=[b_stg, b_g], writes=[b_wt])
        return wt, b_wt

    def norm_block(src, b_src, tb, rings, hT, b_hT, ncols=D_):
        junk, b_junk = rings["junk"].next()
        stt, b_stt = rings["st"].next()
        hb, b_hb = rings["hb"].next()
        ptr, b_ptr = rings["ptr"].next()
        S.op("act", lambda e: e.activation(out=junk[:], in_=src, func=AF.Square, scale=1.0 / math.sqrt(ncols),
                                           accum_out=stt[:, 0:1]), reads=[b_src], writes=[b_junk, b_stt])
        S.op("act", lambda e: e.activation(out=stt[:, 1:2], in_=stt[:, 0:1], func=AF.Sqrt, bias=EPS_AP[:, 0:1]),
             reads=[b_stt], writes=[b_stt])
        S.op("dve", lambda e: e.reciprocal(out=stt[:, 2:3], in_=stt[:, 1:2]), reads=[b_stt], writes=[b_stt])
        S.op("dve", lambda e: e.tensor_scalar(out=hb[:], in0=src, scalar1=stt[:, 2:3], scalar2=None, op0=ALU.mult),
             reads=[b_src, b_stt], writes=[b_hb])
        pv = ptr[:, :].bitcast(BF16)
        for kc in range(8):
            S.op("pe", lambda e, kc=kc: e.transpose(out=pv[:, kc * 128:(kc + 1) * 128],
                                                     in_=hb[:, kc * 128:(kc + 1) * 128], identity=identb[:]),
                 reads=[b_hb, b_identb], writes=[b_ptr])
        S.op("dve", lambda e: e.tensor_copy(out=hT[:, :, tb * 128:(tb + 1) * 128],
                                            in_=pv.rearrange("p (k t) -> p k t", k=8)),
             reads=[b_ptr], writes=[b_hT])

    eps_t = A("eps_t", [128, 1], F32); b_eps = Buf()
    S.op("pool", lambda e: e.memset(eps_t[:], EPS), writes=[b_eps])
    EPS_AP = eps_t
    S.barrier()

    def ffn_gateup(w_g, w_u, gain_col, hT, b_hT, w_d=None, wd=None, b_wd=None):
        with ExitStack() as st:
            stg = Ring(nc, st, un("gu_stg"), [128, 8, 128], F32, 6)
            wr = Ring(nc, st, un("gu_w"), [128, 8, 128], BF16, 6)
            psg = Ring(nc, st, un("gu_pg"), [128, 512], F32, 3, psum=True)
            psu = Ring(nc, st, un("gu_pu"), [128, 512], F32, 3, psum=True)
            sil = Ring(nc, st, un("gu_sil"), [128, 512], F32, 3)
            hm = Ring(nc, st, un("gu_hm"), [128, S_], BF16, 2)
            gain = (gfm_t[:, gain_col * 8:(gain_col + 1) * 8], b_gfm)

            dstg = Ring(nc, st, un("gu_dstg"), [128, 1, D_], F32, 3)

            def load(j):
                wg = wload(stg, wr, w_g[:, j * 128:(j + 1) * 128], 8, 128, gain)
                wu = wload(stg, wr, w_u[:, j * 128:(j + 1) * 128], 8, 128, gain)
                if w_d is not None:
                    sg, b_sg = dstg.next()
                    S.dma(lambda e, sg=sg, j=j: e.dma_start(out=sg[:, 0, :], in_=w_d[j * 128:(j + 1) * 128, :]),
                          writes=[b_sg], sem_buf=b_sg)
                    S.op("pool", lambda e, sg=sg, j=j: e.tensor_copy(out=wd[:, j, :], in_=sg[:, 0, :]),
                         reads=[b_sg], writes=[b_wd])
                return wg, wu

            def comp(j, h):
                (wg, b_wg), (wu, b_wu) = h
                hmt, b_hm = hm.next()
                for tb in range(8):
                    pg, b_pg = psg.next()
                    pu, b_pu = psu.next()
                    sl, b_sl = sil.next()
                    ts = slice(tb * 512, (tb + 1) * 512)
                    for kc in range(8):
                        S.op("pe", lambda e, kc=kc, pg=pg, wg=wg, ts=ts: e.matmul(
                            out=pg[:], lhsT=wg[:, kc, :], rhs=hT[:, kc, ts], start=(kc == 0), stop=(kc == 7)),
                            reads=[b_wg, b_hT], writes=[b_pg])
                    for kc in range(8):
                        S.op("pe", lambda e, kc=kc, pu=pu, wu=wu, ts=ts: e.matmul(
                            out=pu[:], lhsT=wu[:, kc, :], rhs=hT[:, kc, ts], start=(kc == 0), stop=(kc == 7)),
                            reads=[b_wu, b_hT], writes=[b_pu])
                    S.op("act", lambda e, sl=sl, pg=pg: e.activation(out=sl[:], in_=pg[:], func=AF.Silu),
                         reads=[b_pg], writes=[b_sl])
                    S.op("dve", lambda e, sl=sl, pu=pu, hmt=hmt, ts=ts: e.tensor_tensor(
                        out=hmt[:, ts], in0=sl[:], in1=pu[:], op=ALU.mult), reads=[b_sl, b_pu], writes=[b_hm])
                S.dma(lambda e, hmt=hmt, j=j: e.dma_start(out=hmid[j * 128:(j + 1) * 128, :], in_=hmt[:]),
                      reads=[b_hm], writes=[b_hmid], sem_buf=b_hm, eng="pool")

            pipeline(NFF, load, comp, 2)
        S.barrier()

    def ffn_down(w_d, xsrc, b_xsrc, xdst, b_xdst, next_norm, wd=None, b_wd=None):
        with ExitStack() as st:
            pre = wd is not None
            if not pre:
                wd = st.enter_context(nc.sbuf_tensor(un("wd"), [128, NFF, D_], BF16)); b_wd = Buf()
            stg = Ring(nc, st, un("dn_stg"), [128, 1, D_], F32, 3)
            for j in range(NFF if not pre else 0):
                sg, b_sg = stg.next()
                S.dma(lambda e, sg=sg, j=j: e.dma_start(out=sg[:, 0, :], in_=w_d[j * 128:(j + 1) * 128, :]),
                      writes=[b_sg], sem_buf=b_sg)
                S.op("pool", lambda e, sg=sg, j=j: e.tensor_copy(out=wd[:, j, :], in_=sg[:, 0, :]),
                     reads=[b_sg], writes=[b_wd])
            hmr = Ring(nc, st, un("dn_hm"), [128, NFF, 512], BF16, 2)
            xr = Ring(nc, st, un("dn_x"), [128, D_], F32, 3)
            ps = Ring(nc, st, un("dn_ps"), [128, 512], F32, 4, psum=True)
            rings = None
            if next_norm:
                rings = dict(junk=Ring(nc, st, un("nj"), [128, D_], BF16, 2),
                             st=Ring(nc, st, un("nst"), [128, 4], F32, 3),
                             hb=Ring(nc, st, un("nhb"), [128, D_], BF16, 2),
                             ptr=Ring(nc, st, un("nptr"), [128, 512], F32, 2, psum=True))

            def load(t):
                hmt, b_hm = hmr.next()
                S.dma(lambda e: e.dma_start(out=hmt[:], in_=hmid[:, t * 512:(t + 1) * 512].rearrange(
                    "(j p) t -> p j t", p=128)), reads=[b_hmid], writes=[b_hm], sem_buf=b_hm)
                return hmt, b_hm

            def comp(t, h):
                hmt, b_hm = h
                for sb in range(4):
                    tb = t * 4 + sb
                    xt, b_xt = xr.next()
                    S.dma(lambda e, xt=xt, tb=tb: e.dma_start(out=xt[:], in_=xsrc[tb * 128:(tb + 1) * 128, :]),
                          reads=[b_xsrc], writes=[b_xt], sem_buf=b_xt)
                    for half in range(2):
                        p, b_p = ps.next()
                        for j in range(NFF):
                            S.op("pe", lambda e, j=j, p=p, sb=sb, half=half, hmt=hmt: e.matmul(
                                out=p[:], lhsT=hmt[:, j, sb * 128:(sb + 1) * 128],
                                rhs=wd[:, j, half * 512:(half + 1) * 512], start=(j == 0), stop=(j == NFF - 1)),
                                reads=[b_hm, b_wd], writes=[b_p])
                        S.op("dve", lambda e, p=p, xt=xt, half=half: e.scalar_tensor_tensor(
                            out=xt[:, half * 512:(half + 1) * 512], in0=p[:], scalar=0.5,
                            in1=xt[:, half * 512:(half + 1) * 512], op0=ALU.mult, op1=ALU.add),
                            reads=[b_p, b_xt], writes=[b_xt])
                    S.dma(lambda e, xt=xt, tb=tb: e.dma_start(out=xdst[tb * 128:(tb + 1) * 128, :], in_=xt[:]),
                          reads=[b_xt], writes=[b_xdst], sem_buf=b_xt, eng="pool")
                    if next_norm:
                        if pendn:
                            norm_block(*pendn.pop())
                        pendn.append((xt[:], b_xt, tb, rings, next_norm[0], next_norm[1]))

            pendn = []
            pipeline(8, load, comp, 1)
            if pendn:
                norm_block(*pendn.pop())
        S.barrier()

    def norm_phase(xsrc, b_xsrc, hT, b_hT):
        with ExitStack() as st:
            xr = Ring(nc, st, un("np_x"), [128, D_], F32, 3)
            rings = dict(junk=Ring(nc, st, un("nj"), [128, D_], BF16, 2),
                         st=Ring(nc, st, un("nst"), [128, 4], F32, 3),
                         hb=Ring(nc, st, un("nhb"), [128, D_], BF16, 2),
                         ptr=Ring(nc, st, un("nptr"), [128, 512], F32, 2, psum=True))

            def load(tb):
                xt, b_xt = xr.next()
                S.dma(lambda e: e.dma_start(out=xt[:], in_=xsrc[tb * 128:(tb + 1) * 128, :]),
                      reads=[b_xsrc], writes=[b_xt], sem_buf=b_xt)
                return xt, b_xt

            def comp(tb, h):
                norm_block(h[0][:], h[1], tb, rings, hT, b_hT)

            pipeline(32, load, comp, 2)
        S.barrier()

    b_x = Buf("x")
    hst = ExitStack()
    hT = hst.enter_context(nc.sbuf_tensor("hT", [128, 8, S_], BF16)); b_hT = Buf("hT")
    norm_phase(x, b_x, hT, b_hT)
    with ExitStack() as fst:
        wd1 = fst.enter_context(nc.sbuf_tensor("wd1", [128, NFF, D_], BF16)); b_wd1 = Buf()
        ffn_gateup(w_g1, w_u1, 0, hT, b_hT, w_d1, wd1, b_wd1)
        ffn_down(w_d1, x, b_x, x1s, b_x1s, (hT, b_hT), wd1, b_wd1)
    if stop_after <= 1:
        S.emit(nc)
        return nc

    gmix = (gfm_t[:, 8:16], b_gfm)

    def rope(src3, H, tb, dst3, tmp_ring):
        ta, b_ta = tmp_ring.next()
        tb_, b_tb = tmp_ring.next()
        cb = cos_t[:, tb, :].unsqueeze(1).to_broadcast([128, H, 16])
        sb = sin_t[:, tb, :].unsqueeze(1).to_broadcast([128, H, 16])
        t1 = src3[:, :, 0:16]
        t2 = src3[:, :, 16:32]
        a_ = ta[:, 0:H, :]
        b_ = tb_[:, 0:H, :]
        return [
            (lambda e: e.tensor_tensor(out=a_, in0=t1, in1=cb, op=ALU.mult), [b_cos], [b_ta]),
            (lambda e: e.tensor_tensor(out=b_, in0=t2, in1=sb, op=ALU.mult), [b_sin], [b_tb]),
            (lambda e: e.tensor_tensor(out=dst3[:, :, 0:16], in0=a_, in1=b_, op=ALU.subtract), [b_ta, b_tb], []),
            (lambda e: e.tensor_tensor(out=a_, in0=t2, in1=cb, op=ALU.mult), [b_cos], [b_ta]),
            (lambda e: e.tensor_tensor(out=b_, in0=t1, in1=sb, op=ALU.mult), [b_sin], [b_tb]),
            (lambda e: e.tensor_tensor(out=dst3[:, :, 16:32], in0=a_, in1=b_, op=ALU.add), [b_ta, b_tb], []),
        ]

    with ExitStack() as st:
        wAr = Ring(nc, st, un("a_w"), [128, 8, 672], BF16, 1)
        wqr = Ring(nc, st, un("q_w"), [128, 3, 1536], BF16, 1)
        wkr = Ring(nc, st, un("kv_w"), [128, 2, 2048], BF16, 1)
        with ExitStack() as st2:
            stgA = Ring(nc, st2, un("a_stg"), [128, 8, 672], F32, 1)
            stgq = Ring(nc, st2, un("q_stg"), [128, 3, 1536], F32, 1)
            stgk = Ring(nc, st2, un("kv_stg"), [128, 2, 2048], F32, 1)
            wA, b_wA = wload(stgA, wAr, w_in[:, 0:672], 8, 672, gmix)
            wq, b_wq = wload(stgq, wqr, w_qb[:, :], 3, 1536, (gqa_t[:, :], b_gqa))
            wkv, b_wkv = wload(stgk, wkr, w_kvb[:, :], 2, 2048, (gkva_t[:, :], b_gkva))
            S.barrier()
        psA = Ring(nc, st, un("a_ps"), [128, 512], F32, 2, psum=True)
        psT = Ring(nc, st, un("a_pt"), [128, 512], F32, 2, psum=True)
        psQ = Ring(nc, st, un("a_pq"), [128, 512], F32, 3, psum=True)
        junk = Ring(nc, st, un("a_junk"), [128, 1536], F32, 1)
        stt = Ring(nc, st, un("a_st"), [128, 8], F32, 2)
        sst = Ring(nc, st, un("a_ss"), [128, 100], F32, 2)
        cnr = Ring(nc, st, un("a_cn"), [128, 640], BF16, 2)
        cTr = Ring(nc, st, un("a_cT"), [128, 5, 128], BF16, 2)
        kper = Ring(nc, st, un("a_kpe"), [128, 1, 32], F32, 2)
        kpgr = Ring(nc, st, un("a_kpg"), [128, 1, 32], F32, 2)
        krr = Ring(nc, st, un("a_kr"), [128, 1, 32], F32, 2)
        qsbr = Ring(nc, st, un("a_qsb"), [128, 1536], F32, 1)
        kvsbr = Ring(nc, st, un("a_kvsb"), [128, 2048], F32, 1)
        tmpkr = Ring(nc, st, un("a_tmpk"), [128, 16, 64], F32, 1)
        qbr = Ring(nc, st, un("a_qb"), [128, 16, 96], BF16, 2)
        kbr = Ring(nc, st, un("a_kb"), [128, 16, 96], BF16, 2)
        ropet = Ring(nc, st, un("a_rt"), [128, 16, 16], F32, 4)
        vbr = Ring(nc, st, un("a_vb"), [128, 16, 4, 65], BF16, 1)
        qTr = Ring(nc, st, un("a_qT"), [128, 16, 256], BF16, 2)
        kTr = Ring(nc, st, un("a_kT"), [128, 16, 256], BF16, 2)
        for (vt, b_v) in vbr.items:
            S.op("pool", lambda e, vt=vt: e.memset(vt[:], 1.0), writes=[b_v])
        gq = hn_t[:, 0:96]
        gk = hn_t[:, 96:192]
        sh = {}

        def block(tb):
            tsl = slice(tb * 128, (tb + 1) * 128)
            pA1, b_pA1 = psA.next()
            pA2, b_pA2 = psA.next()
            for kc in range(8):
                S.op("pe", lambda e, kc=kc, pA1=pA1, tsl=tsl: e.matmul(out=pA1[:, 0:384], lhsT=hT[:, kc, tsl], rhs=wA[:, kc, 0:384],
                                                     start=(kc == 0), stop=(kc == 7)), reads=[b_hT, b_wA], writes=[b_pA1])
            for kc in range(8):
                S.op("pe", lambda e, kc=kc, pA2=pA2, tsl=tsl: e.matmul(out=pA2[:, 0:288], lhsT=hT[:, kc, tsl], rhs=wA[:, kc, 384:672],
                                                     start=(kc == 0), stop=(kc == 7)), reads=[b_hT, b_wA], writes=[b_pA2])
            jk, b_jk = junk.next()
            s8, b_s8 = stt.next()
            S.op("act", lambda e, jk=jk, pA1=pA1, s8=s8: e.activation(out=jk[:, 0:384], in_=pA1[:, 0:384], func=AF.Square,
                                               scale=1.0 / math.sqrt(384.0), accum_out=s8[:, 0:1]),
                 reads=[b_pA1], writes=[b_jk, b_s8])
            S.op("act", lambda e, jk=jk, pA2=pA2, s8=s8: e.activation(out=jk[:, 0:256], in_=pA2[:, 0:256], func=AF.Square,
                                               scale=1.0 / 16.0, accum_out=s8[:, 1:2]),
                 reads=[b_pA2], writes=[b_jk, b_s8])
            S.op("act", lambda e, s8=s8: e.activation(out=s8[:, 2:4], in_=s8[:, 0:2], func=AF.Sqrt, bias=EPS_AP[:, 0:1]),
                 reads=[b_s8, b_eps], writes=[b_s8])
            S.op("dve", lambda e, s8=s8: e.reciprocal(out=s8[:, 4:6], in_=s8[:, 2:4]), reads=[b_s8], writes=[b_s8])
            cn, b_cn = cnr.next()
            S.op("dve", lambda e, cn=cn, pA1=pA1, s8=s8: e.tensor_scalar(out=cn[:, 0:384], in0=pA1[:, 0:384], scalar1=s8[:, 4:5],
                                                  scalar2=None, op0=ALU.mult), reads=[b_pA1, b_s8], writes=[b_cn])
            S.op("dve", lambda e, cn=cn, pA2=pA2, s8=s8: e.tensor_scalar(out=cn[:, 384:640], in0=pA2[:, 0:256], scalar1=s8[:, 5:6],
                                                  scalar2=None, op0=ALU.mult), reads=[b_pA2, b_s8], writes=[b_cn])
            kpe, b_kpe = kper.next()
            S.op("act", lambda e, kpe=kpe, pA2=pA2: e.copy(out=kpe[:, 0, :], in_=pA2[:, 256:288]), reads=[b_pA2], writes=[b_kpe])
            yield
            ptr, b_ptr = psT.next()
            pv = ptr[:, :].bitcast(BF16)
            for kc in range(5):
                S.op("pe", lambda e, kc=kc, pv=pv, cn=cn: e.transpose(out=pv[:, kc * 128:(kc + 1) * 128],
                                                         in_=cn[:, kc * 128:(kc + 1) * 128], identity=identb[:]),
                     reads=[b_cn, b_identb], writes=[b_ptr])
            cT, b_cT = cTr.next()
            S.op("dve", lambda e, cT=cT, pv=pv: e.tensor_copy(out=cT[:], in_=pv[:, 0:640].rearrange("p (k t) -> p k t", k=5)),
                 reads=[b_ptr], writes=[b_cT])
            yield
            qsb, b_qsb = qsbr.next()
            kvsb, b_kvsb = kvsbr.next()
            for nb in range(3):
                pq, b_pq = psQ.next()
                for kc in range(3):
                    S.op("pe", lambda e, kc=kc, nb=nb, pq=pq, cT=cT: e.matmul(out=pq[:], lhsT=cT[:, kc, :],
                                                                 rhs=wq[:, kc, nb * 512:(nb + 1) * 512],
                                                                 start=(kc == 0), stop=(kc == 2)),
                         reads=[b_cT, b_wq], writes=[b_pq])
                S.op("act", lambda e, nb=nb, pq=pq, qsb=qsb: e.copy(out=qsb[:, nb * 512:(nb + 1) * 512], in_=pq[:]),
                     reads=[b_pq], writes=[b_qsb])
            for nb in range(4):
                pq, b_pq = psQ.next()
                for kc in range(2):
                    S.op("pe", lambda e, kc=kc, nb=nb, pq=pq, cT=cT: e.matmul(out=pq[:], lhsT=cT[:, 3 + kc, :],
                                                                 rhs=wkv[:, kc, nb * 512:(nb + 1) * 512],
                                                                 start=(kc == 0), stop=(kc == 1)),
                         reads=[b_cT, b_wkv], writes=[b_pq])
                eng = "act" if nb % 2 == 0 else "dve"
                if eng == "act":
                    S.op("act", lambda e, nb=nb, pq=pq, kvsb=kvsb: e.copy(out=kvsb[:, nb * 512:(nb + 1) * 512], in_=pq[:]),
                         reads=[b_pq], writes=[b_kvsb])
                else:
                    S.op("dve", lambda e, nb=nb, pq=pq, kvsb=kvsb: e.tensor_copy(out=kvsb[:, nb * 512:(nb + 1) * 512], in_=pq[:]),
                         reads=[b_pq], writes=[b_kvsb])
            q3 = qsb[:, :].rearrange("p (h d) -> p h d", h=16)
            kv3 = kvsb[:, :].rearrange("p (h d) -> p h d", h=16)
            ss, b_ss = sst.next()
            S.op("act", lambda e, jk=jk, qsb=qsb: e.activation(out=jk[:, :], in_=qsb[:, :], func=AF.Square),
                 reads=[b_qsb], writes=[b_jk])
            S.op("dve", lambda e, jk=jk, ss=ss: e.tensor_reduce(out=ss[:, 0:16], in_=jk[:, :].rearrange("p (h d) -> p h d", h=16),
                                                  axis=AX.X, op=ALU.add), reads=[b_jk], writes=[b_ss])
            tk, b_tk = tmpkr.next()
            S.op("act", lambda e, tk=tk, kv3=kv3: e.activation(out=tk[:], in_=kv3[:, :, 0:64], func=AF.Square),
                 reads=[b_kvsb], writes=[b_tk])
            S.op("dve", lambda e, tk=tk, ss=ss: e.tensor_reduce(out=ss[:, 16:32], in_=tk[:], axis=AX.X, op=ALU.add),
                 reads=[b_tk], writes=[b_ss])
            kpg, b_kpg = kpgr.next()
            S.op("act", lambda e, kpg=kpg, kpe=kpe, ss=ss: e.activation(out=kpg[:, 0, :], in_=kpe[:, 0, :], func=AF.Square,
                                               accum_out=ss[:, 96:97]), reads=[b_kpe], writes=[b_kpg, b_ss])
            S.op("dve", lambda e, ss=ss: e.tensor_scalar(out=ss[:, 16:32], in0=ss[:, 16:32], scalar1=ss[:, 96:97], scalar2=None,
                                                  op0=ALU.add), reads=[b_ss], writes=[b_ss])
            S.op("act", lambda e, ss=ss: e.activation(out=ss[:, 32:64], in_=ss[:, 0:32], func=AF.Sqrt, bias=EPS_AP[:, 0:1],
                                               scale=1.0 / 96.0), reads=[b_ss, b_eps], writes=[b_ss])
            S.op("dve", lambda e, ss=ss: e.reciprocal(out=ss[:, 64:96], in_=ss[:, 32:64]), reads=[b_ss], writes=[b_ss])
            rsq = ss[:, 64:80]
            rsk = ss[:, 80:96]
            S.op("dve", lambda e, q3=q3, rsq=rsq: e.tensor_tensor(out=q3, in0=q3, in1=rsq.unsqueeze(2).to_broadcast([128, 16, 96]),
                                                  op=ALU.mult), reads=[b_qsb, b_ss], writes=[b_qsb])
            S.op("dve", lambda e, q3=q3: e.tensor_tensor(out=q3, in0=q3, in1=gq.unsqueeze(1).to_broadcast([128, 16, 96]),
                                                   op=ALU.mult), reads=[b_qsb, b_hn], writes=[b_qsb])
            qb, b_qb = qbr.next()
            S.op("act", lambda e, qb=qb, q3=q3: e.copy(out=qb[:, :, 0:64], in_=q3[:, :, 0:64]), reads=[b_qsb], writes=[b_qb])
            for fn, rd, wr in rope(q3[:, :, 64:96], 16, tb, qb[:, :, 64:96], ropet):
                S.op("dve", fn, reads=[b_qsb] + rd, writes=wr + ([b_qb] if not wr else []))
            S.op("dve", lambda e, tk=tk, kv3=kv3, rsk=rsk: e.tensor_tensor(out=tk[:], in0=kv3[:, :, 0:64],
                                                  in1=rsk.unsqueeze(2).to_broadcast([128, 16, 64]), op=ALU.mult),
                 reads=[b_kvsb, b_ss], writes=[b_tk])
            kb, b_kb = kbr.next()
            S.op("dve", lambda e, tk=tk, kb=kb: e.tensor_tensor(out=kb[:, :, 0:64], in0=tk[:],
                                                   in1=gk[:, 0:64].unsqueeze(1).to_broadcast([128, 16, 64]), op=ALU.mult),
                 reads=[b_tk, b_hn], writes=[b_kb])
            S.op("dve", lambda e, kpg=kpg, kpe=kpe: e.tensor_tensor(out=kpg[:, 0, :], in0=kpe[:, 0, :], in1=gk[:, 64:96], op=ALU.mult),
                 reads=[b_kpe, b_hn], writes=[b_kpg])
            kr, b_kr = krr.next()
            for fn, rd, wr in rope(kpg[:, :, :], 1, tb, kr[:, :, :], ropet):
                S.op("dve", fn, reads=[b_kpg] + rd, writes=wr + ([b_kr] if not wr else []))
            S.op("dve", lambda e, kb=kb, kr=kr, rsk=rsk: e.tensor_tensor(out=kb[:, :, 64:96],
                                                  in0=kr[:, 0, :].unsqueeze(1).to_broadcast([128, 16, 32]),
                                                  in1=rsk.unsqueeze(2).to_broadcast([128, 16, 32]), op=ALU.mult),
                 reads=[b_kr, b_ss], writes=[b_kb])
            if tb % 4 == 0:
                sh['vb'] = vbr.next()
            vb, b_vb = sh['vb']
            S.op("pool", lambda e, vb=vb, kv3=kv3, tb=tb: e.tensor_copy(out=vb[:, :, tb % 4, 0:64], in_=kv3[:, :, 64:128]),
                 reads=[b_kvsb], writes=[b_vb])
            yield
            if tb % 2 == 0:
                sh['qT'] = qTr.next()
                sh['kT'] = kTr.next()
            qTs, b_qTs = sh['qT']
            kTs, b_kTs = sh['kT']
            for (src, b_src, dstT, b_dstT) in ((qb, b_qb, qTs, b_qTs), (kb, b_kb, kTs, b_kTs)):
                for half in range(2):
                    ptr, b_ptr = psT.next()
                    pv = ptr[:, :].bitcast(BF16)
                    for hh in range(8):
                        S.op("pe", lambda e, hh=hh, pv=pv, src=src, half=half: e.transpose(
                            out=pv[0:96, hh * 128:(hh + 1) * 128], in_=src[:, half * 8 + hh, :], identity=identb[:]),
                            reads=[b_src, b_identb], writes=[b_ptr])
                    off = (tb % 2) * 128
                    eng = "act" if half == 0 else "dve"
                    if eng == "act":
                        S.op("act", lambda e, pv=pv, dstT=dstT, half=half, off=off: e.copy(
                            out=dstT[0:96, half * 8:(half + 1) * 8, off:off + 128],
                            in_=pv[0:96, :].rearrange("p (h t) -> p h t", h=8)), reads=[b_ptr], writes=[b_dstT])
                    else:
                        S.op("dve", lambda e, pv=pv, dstT=dstT, half=half, off=off: e.tensor_copy(
                            out=dstT[0:96, half * 8:(half + 1) * 8, off:off + 128],
                            in_=pv[0:96, :].rearrange("p (h t) -> p h t", h=8)), reads=[b_ptr], writes=[b_dstT])
            if tb % 2 == 1:
                t0 = (tb - 1) * 128
                S.dma(lambda e, qTs=qTs, t0=t0: e.dma_start(out=QT[:, :, t0:t0 + 256].rearrange("h p t -> p h t"),
                                                             in_=qTs[0:96, :, :]),
                      reads=[b_qTs], writes=[b_QT], sem_buf=b_qTs, eng="pool")
                S.dma(lambda e, kTs=kTs, t0=t0: e.dma_start(out=KT[:, :, t0:t0 + 256].rearrange("h p t -> p h t"),
                                                             in_=kTs[0:96, :, :]),
                      reads=[b_kTs], writes=[b_KT], sem_buf=b_kTs, eng="pool")
            if tb % 4 == 3:
                c0 = tb - 3
                S.dma(lambda e, vb=vb, c0=c0: e.dma_start(out=Vs[:, :, c0:c0 + 4, :].rearrange("h p t c -> p h (t c)"),
                                                           in_=vb[:, :, :, :].rearrange("p h t c -> p h (t c)")),
                      reads=[b_vb], writes=[b_Vs], sem_buf=b_vb, eng="pool")
        gens = {}
        for i in range(32 + 2):
            if i < 32:
                gens[i] = block(i)
                next(gens[i])
            if 0 <= i - 1 < 32:
                next(gens[i - 1])
            if 0 <= i - 2 < 32:
                next(gens[i - 2], None)
            if 0 <= i - 1 < 32:
                next(gens[i - 1])
        S.barrier()
    if stop_after <= 2:
        S.emit(nc)
        return nc

    with ExitStack() as st:
        wzr = Ring(nc, st, un("z_w"), [128, 8, 512], BF16, 4)
        wdr = Ring(nc, st, un("dt_w"), [128, 8, 64], BF16, 1)
        with ExitStack() as st2:
            stgz = Ring(nc, st2, un("z_stg"), [128, 8, 512], F32, 2)
            stgd = Ring(nc, st2, un("dt_stg"), [128, 8, 64], F32, 1)
            wz = [wload(stgz, wzr, w_in[:, C_Z + cb * 512:C_Z + (cb + 1) * 512], 8, 512, gmix) for cb in range(4)]
            wdt, b_wdt = wload(stgd, wdr, w_in[:, C_DTF:C_DTF + 64], 8, 64, gmix)
            S.barrier()
        psz = Ring(nc, st, un("z_ps"), [128, 512], F32, 4, psum=True)
        psd = Ring(nc, st, un("dt_ps"), [128, 512], F32, 2, psum=True)
        zst = Ring(nc, st, un("z_st"), [128, 2048], BF16, 3)
        for tb in range(32):
            tsl = slice(tb * 128, (tb + 1) * 128)
            zt, b_zt = zst.next()
            for cb in range(4):
                pz, b_pz = psz.next()
                w_, b_w = wz[cb]
                for kc in range(8):
                    S.op("pe", lambda e, kc=kc, pz=pz, w_=w_, tsl=tsl: e.matmul(out=pz[:], lhsT=hT[:, kc, tsl], rhs=w_[:, kc, :],
                                                                  start=(kc == 0), stop=(kc == 7)),
                         reads=[b_hT, b_w], writes=[b_pz])
                S.op("act", lambda e, pz=pz, zt=zt, cb=cb: e.activation(out=zt[:, cb * 512:(cb + 1) * 512], in_=pz[:], func=AF.Silu),
                     reads=[b_pz], writes=[b_zt])
            pd, b_pd = psd.next()
            for kc in range(8):
                S.op("pe", lambda e, kc=kc, pd=pd, tsl=tsl: e.matmul(out=pd[:, 0:64], lhsT=hT[:, kc, tsl], rhs=wdt[:, kc, :],
                                                       start=(kc == 0), stop=(kc == 7)), reads=[b_hT, b_wdt], writes=[b_pd])
            S.op("dve", lambda e, pd=pd, tb=tb: e.tensor_copy(out=dtraw[:, tb, :], in_=pd[:, 0:64]), reads=[b_pd], writes=[b_dtraw])
            S.dma(lambda e, zt=zt, tsl=tsl: e.dma_start(out=zs[tsl, :], in_=zt[:]), reads=[b_zt], writes=[b_zs],
                  sem_buf=b_zt, eng="pool")
        S.barrier()

    with ExitStack() as st:
        stg = Ring(nc, st, un("x_stg"), [128, 8, 128], F32, 3)
        wr = Ring(nc, st, un("x_w"), [128, 8, 128], BF16, 3)
        psx = Ring(nc, st, un("x_ps"), [128, 512], F32, 3, psum=True)
        psc = Ring(nc, st, un("x_pc"), [128, 512], F32, 3, psum=True)
        xpre = Ring(nc, st, un("x_pre"), [128, S_ + 4], BF16, 2)
        dgr = Ring(nc, st, un("x_dg"), [128, 5, 128], BF16, 2)
        xcst = Ring(nc, st, un("x_cst"), [128, S_], BF16, 2)
        for (t_, b_) in xpre.items:
            S.op("pool", lambda e, t_=t_: e.memset(t_[:], 0.0), writes=[b_])

        def loadx(j):
            return wload(stg, wr, w_in[:, C_XBC + j * 128:C_XBC + (j + 1) * 128], 8, 128, gmix)

        def compx(j, h):
            w_, b_w = h
            xp, b_xp = xpre.next()
            dg, b_dg = dgr.next()
            for k in range(5):
                S.op("dve", lambda e, k=k, dg=dg, j=j: e.tensor_scalar(out=dg[:, k, :], in0=identf, scalar1=convw_t[:, j, k:k + 1],
                                                           scalar2=None, op0=ALU.mult),
                     reads=[b_mats, b_convw], writes=[b_dg])
            for tb in range(8):
                px, b_px = psx.next()
                ts = slice(tb * 512, (tb + 1) * 512)
                for kc in range(8):
                    S.op("pe", lambda e, kc=kc, px=px, w_=w_, ts=ts: e.matmul(out=px[:], lhsT=w_[:, kc, :], rhs=hT[:, kc, ts],
                                                                start=(kc == 0), stop=(kc == 7)),
                         reads=[b_w, b_hT], writes=[b_px])
                if tb % 2 == 0:
                    S.op("act", lambda e, px=px, xp=xp, tb=tb: e.copy(out=xp[:, 2 + tb * 512:2 + (tb + 1) * 512], in_=px[:]),
                         reads=[b_px], writes=[b_xp])
                else:
                    S.op("dve", lambda e, px=px, xp=xp, tb=tb: e.tensor_copy(out=xp[:, 2 + tb * 512:2 + (tb + 1) * 512], in_=px[:]),
                         reads=[b_px], writes=[b_xp])
            xc_, b_xc = xcst.next()
            for tb in range(8):
                pc, b_pc = psc.next()
                for k in range(5):
                    S.op("pe", lambda e, k=k, pc=pc, dg=dg, xp=xp, tb=tb: e.matmul(
                        out=pc[:], lhsT=dg[:, k, :], rhs=xp[:, tb * 512 + k:tb * 512 + k + 512],
                        start=(k == 0), stop=(k == 4)), reads=[b_dg, b_xp], writes=[b_pc])
                S.op("act", lambda e, pc=pc, xc_=xc_, tb=tb, j=j: e.activation(out=xc_[:, tb * 512:(tb + 1) * 512], in_=pc[:],
                                                               func=AF.Silu, bias=convb_t[:, j:j + 1]),
                     reads=[b_pc, b_convb], writes=[b_xc])
            for q4 in range(4):
                S.dma(lambda e, xc_=xc_, j=j, q4=q4: e.dma_start(
                    out=xcs[q4 * 8:(q4 + 1) * 8, :, j, :].rearrange("c p t -> p c t"),
                    in_=xc_[:, q4 * 1024:(q4 + 1) * 1024].rearrange("p (c t) -> p c t", c=8)),
                    reads=[b_xc], writes=[b_xcs], sem_buf=b_xc, eng="pool")

        pipeline(24, loadx, compx, 2)

        def loadg(j):
            return wload(stg, wr, w_in[:, C_GA + j * 128:C_GA + (j + 1) * 128], 8, 128, gmix)

        def compg(j, h):
            w_, b_w = h
            gt_, b_gt = xcst.next()
            for tb in range(8):
                px, b_px = psx.next()
                ts = slice(tb * 512, (tb + 1) * 512)
                for kc in range(8):
                    S.op("pe", lambda e, kc=kc, px=px, w_=w_, ts=ts: e.matmul(out=px[:], lhsT=w_[:, kc, :], rhs=hT[:, kc, ts],
                                                                start=(kc == 0), stop=(kc == 7)),
                         reads=[b_w, b_hT], writes=[b_px])
                S.op("act", lambda e, px=px, gt_=gt_, ts=ts: e.activation(out=gt_[:, ts], in_=px[:], func=AF.Sigmoid),
                     reads=[b_px], writes=[b_gt])
            S.dma(lambda e, gt_=gt_, j=j: e.dma_start(out=gts[j * 128:(j + 1) * 128, :], in_=gt_[:]),
                  reads=[b_gt], writes=[b_gts], sem_buf=b_gt, eng="pool")

        pipeline(16, loadg, compg, 2)
        S.barrier()
    if stop_after <= 3:
        S.emit(nc)
        return nc
    hst.close()

    def ssd_pass(fwd):
        with ExitStack() as st:
            AT = lambda n, shp, dt=F32: st.enter_context(nc.sbuf_tensor(un(n), shp, dt))
            dt_all = AT("dt_all", [128, 32, 32]); b_dt = Buf()
            da_all = AT("da_all", [128, 32, 32]); b_da = Buf()
            P_all = AT("P_all", [128, 32, 32]); b_P = Buf()
            bias_all = AT("bias_all", [128, 32, 32]); b_bias = Buf()
            wgt = AT("wgt", [128, 32, 32]); b_wgt = Buf()
            scl = AT("scl", [128, 32, 32]); b_scl = Buf()
            cdc = AT("cdc", [128, 32, 32]); b_cdc = Buf()
            tot = AT("tot", [128, 32, 32]); b_tot = Buf()
            nega = AT("nega", [128, 32]); b_nega = Buf()
            tmpa = AT("tmpa", [128, 32, 32]); b_tmpa = Buf()
            Sf = AT("Sf", [128, 2048]); b_Sf = [Buf() for _ in range(4)]
            Sbf = AT("Sbf", [128, 2048], BF16); b_Sbf = [Buf() for _ in range(4)]
            off = 0 if fwd else 32
            alog = ssp_t[:, off:off + 32]
            dtb = ssp_t[:, 64 + off:96 + off]
            dsk = ssp_t[:, 128:160]
            Uc = Umat if fwd else Ustr
            midx = 0 if fwd else 1
            ptA = Ring(nc, st, un("s_ptA"), [128, 512], F32, 1, psum=True)
            segb = Ring(nc, st, un("s_seg"), [128, 512], F32, 3, psum=True)
            pyr = Ring(nc, st, un("s_py"), [128, 512], F32, 2, psum=True)
            por = Ring(nc, st, un("s_po"), [128, 512], F32, 1, psum=True)
            pstr = Ring(nc, st, un("s_pst"), [128, 512], F32, 1, psum=True)
            segs = []
            for (t_, _b) in segb.items:
                for q in range(4):
                    segs.append((t_[:, q * 128:(q + 1) * 128], Buf()))
            segi = [0]
            flat = lambda t_: t_[:, :, :].rearrange("p a b -> p (a b)")
            S.op("pool", lambda e: e.memset(Sf[:], 0.0), writes=b_Sf)
            S.op("pool", lambda e: e.memset(Sbf[:], 0.0), writes=b_Sbf)
            S.op("act", lambda e: e.activation(out=nega[:], in_=alog, func=AF.Exp), reads=[b_ssp], writes=[b_nega])
            S.op("dve", lambda e: e.tensor_scalar(out=nega[:], in0=nega[:], scalar1=-1.0, scalar2=None, op0=ALU.mult),
                 reads=[b_nega], writes=[b_nega])
            S.op("dve", lambda e: e.tensor_tensor(out=tmpa[:], in0=dtraw[:, :, off:off + 32],
                                                  in1=dtb.unsqueeze(1).to_broadcast([128, 32, 32]), op=ALU.add),
                 reads=[b_dtraw, b_ssp], writes=[b_tmpa])
            S.op("act", lambda e: e.activation(out=tmpa[:], in_=tmpa[:], func=AF.Exp), reads=[b_tmpa], writes=[b_tmpa])
            S.op("act", lambda e: e.activation(out=dt_all[:], in_=tmpa[:], func=AF.Ln, bias=1.0), reads=[b_tmpa], writes=[b_dt])
            S.op("dve", lambda e: e.tensor_tensor(out=da_all[:], in0=dt_all[:], in1=nega[:].unsqueeze(1).to_broadcast([128, 32, 32]),
                                                  op=ALU.mult), reads=[b_dt, b_nega], writes=[b_da])
            for half in range(2):
                pp, b_pp = ptA.next()
                S.op("pe", lambda e, pp=pp, half=half: e.matmul(out=pp[:], lhsT=Uc, rhs=flat(da_all)[:, half * 512:(half + 1) * 512],
                                                                start=True, stop=True), reads=[b_mats, b_da], writes=[b_pp])
                S.op("dve", lambda e, pp=pp, half=half: e.tensor_copy(out=flat(P_all)[:, half * 512:(half + 1) * 512], in_=pp[:]),
                     reads=[b_pp], writes=[b_P])
            for half in range(2):
                pp, b_pp = ptA.next()
                S.op("pe", lambda e, pp=pp, half=half: e.matmul(out=pp[:], lhsT=onesf, rhs=flat(da_all)[:, half * 512:(half + 1) * 512],
                                                                start=True, stop=True), reads=[b_mats, b_da], writes=[b_pp])
                S.op("dve", lambda e, pp=pp, half=half: e.tensor_copy(out=flat(tot)[:, half * 512:(half + 1) * 512], in_=pp[:]),
                     reads=[b_pp], writes=[b_tot])
            S.op("dve", lambda e: e.tensor_tensor(out=tmpa[:], in0=tot[:], in1=P_all[:], op=ALU.subtract),
                 reads=[b_tot, b_P, b_dt], writes=[b_tmpa])
            e1, b_e1 = (scl, b_scl) if fwd else (wgt, b_wgt)
            e2, b_e2 = (wgt, b_wgt) if fwd else (scl, b_scl)
            S.op("act", lambda e: e.activation(out=e1[:], in_=P_all[:], func=AF.Exp), reads=[b_P], writes=[b_e1])
            S.op("act", lambda e: e.activation(out=e2[:], in_=tmpa[:], func=AF.Exp), reads=[b_tmpa], writes=[b_e2])
            S.op("act", lambda e: e.activation(out=cdc[:], in_=tot[:], func=AF.Exp), reads=[b_tot], writes=[b_cdc])
            S.op("dve", lambda e: e.tensor_scalar(out=bias_all[:], in0=P_all[:], scalar1=(-1.0 if fwd else 1.0), scalar2=None,
                                                  op0=ALU.mult), reads=[b_P], writes=[b_bias])

            xcr = Ring(nc, st, un("s_xc"), [128, 24, 128], BF16, 3)
            xsr = Ring(nc, st, un("s_xs"), [128, 2048], BF16, 2)
            Btr = Ring(nc, st, un("s_Bt"), [128, 512], BF16, 2)
            cbr = Ring(nc, st, un("s_cb"), [128, 512], F32, 2)
            xdtr = Ring(nc, st, un("s_xdt"), [128, 2048], BF16, 2)
            xwr = Ring(nc, st, un("s_xw"), [128, 2048], BF16, 2)
            decr = Ring(nc, st, un("s_dec"), [128, 128], F32, 12)
            MTr = Ring(nc, st, un("s_MT"), [128, 128], BF16, 12)
            yaccr = Ring(nc, st, un("s_ya"), [128, 2048], F32, 2)
            tmpr = Ring(nc, st, un("s_tmp"), [128, 512], F32, 2)
            if fwd:
                dskr = Ring(nc, st, un("s_dsk"), [128, 2048], F32, 1)
            else:
                zr = Ring(nc, st, un("s_z"), [128, 2048], BF16, 3)
                yfr = Ring(nc, st, un("s_yf"), [128, 2048], F32, 3)
                jkr = Ring(nc, st, un("s_jk"), [128, 512], BF16, 1)
                st4r = Ring(nc, st, un("s_st4"), [128, 12], F32, 2)
                mbr = Ring(nc, st, un("s_mb"), [128, 2048], BF16, 2)
                mstr = Ring(nc, st, un("s_mst"), [128, 16, 128], BF16, 2)
            order = list(range(32)) if fwd else list(range(31, -1, -1))

            def load(ci):
                c = order[ci]
                xc_, b_xc = xcr.next()
                S.dma(lambda e: e.dma_start(out=xc_[:], in_=xcs[c, :, :, :]), reads=[b_xcs], writes=[b_xc], sem_buf=b_xc)
                if fwd:
                    return (xc_, b_xc)
                z_, b_z = zr.next()
                yf_, b_yf = yfr.next()
                S.dma(lambda e: e.dma_start(out=z_[:], in_=zs[c * 128:(c + 1) * 128, :]), reads=[b_zs], writes=[b_z], sem_buf=b_z)
                S.dma(lambda e: e.dma_start(out=yf_[:], in_=yfs[c * 128:(c + 1) * 128, :]), reads=[b_yfs], writes=[b_yf], sem_buf=b_yf)
                return (xc_, b_xc, z_, b_z, yf_, b_yf)

            def prologue(ci, h):
                c = order[ci]
                xc_, b_xc = h[0], h[1]
                xs, b_xs = xsr.next()
                for half in range(2):
                    pt, b_pt = ptA.next()
                    pv = pt[:, :].bitcast(BF16)
                    for jj in range(8):
                        S.op("pe", lambda e, pv=pv, jj=jj, half=half: e.transpose(out=pv[:, jj * 128:(jj + 1) * 128],
                                                                                 in_=xc_[:, half * 8 + jj, :], identity=identb[:]),
                             reads=[b_xc, b_identb], writes=[b_pt])
                    if half == 0:
                        S.op("act", lambda e, pv=pv: e.copy(out=xs[:, 0:1024], in_=pv[:, :]), reads=[b_pt], writes=[b_xs])
                    else:
                        S.op("dve", lambda e, pv=pv: e.tensor_copy(out=xs[:, 1024:2048], in_=pv[:, :]), reads=[b_pt], writes=[b_xs])
                pt, b_pt = ptA.next()
                pvb = pt[:, :].bitcast(BF16)
                for g in range(4):
                    S.op("pe", lambda e, g=g: e.transpose(out=pvb[:, g * 128:(g + 1) * 128], in_=xc_[:, 16 + g, :], identity=identb[:]),
                         reads=[b_xc, b_identb], writes=[b_pt])
                Bt, b_Bt = Btr.next()
                S.op("dve", lambda e: e.tensor_copy(out=Bt[:], in_=pvb[:, 0:512]), reads=[b_pt], writes=[b_Bt])
                pcb, b_pcb = ptA.next()
                for g in range(4):
                    S.op("pe", lambda e, g=g: e.matmul(out=pcb[:, g * 128:(g + 1) * 128], lhsT=xc_[:, 16 + g, :], rhs=xc_[:, 20 + g, :],
                                                       start=True, stop=True), reads=[b_xc], writes=[b_pcb])
                cbT, b_cbT = cbr.next()
                S.op("act", lambda e: e.copy(out=cbT[:], in_=pcb[:]), reads=[b_pcb], writes=[b_cbT])
                xdt, b_xdt = xdtr.next()
                xw, b_xw = xwr.next()
                v3 = lambda t_: t_[:, :].rearrange("p (h d) -> p h d", h=32)
                S.op("dve", lambda e: e.tensor_tensor(out=v3(xdt), in0=v3(xs), in1=dt_all[:, c, :].unsqueeze(2).to_broadcast([128, 32, 64]),
                                                      op=ALU.mult), reads=[b_xs, b_dt], writes=[b_xdt])
                S.op("pool", lambda e: e.tensor_tensor(out=v3(xw), in0=v3(xdt), in1=wgt[:, c, :].unsqueeze(2).to_broadcast([128, 32, 64]),
                                                       op=ALU.mult), reads=[b_xdt, b_wgt], writes=[b_xw])
                return (xs, b_xs, Bt, b_Bt, cbT, b_cbT, xdt, b_xdt, xw, b_xw)

            def comp(ci, h, pr):
                c = order[ci]
                xc_, b_xc = h[0], h[1]
                xs, b_xs, Bt, b_Bt, cbT, b_cbT, xdt, b_xdt, xw, b_xw = pr
                v3 = lambda t_: t_[:, :].rearrange("p (h d) -> p h d", h=32)
                g8 = lambda t_: t_.rearrange("p (h d) -> p h d", h=8)
                ya, b_ya = yaccr.next()
                LAGH = 4
                mts = {}
                cur = {}

                def stageA(bi):
                    sb_, b_sb = segb.next()
                    for q in range(4):
                        h_ = bi * 4 + q
                        seg = sb_[:, q * 128:(q + 1) * 128]
                        S.op("pe", lambda e, seg=seg, h_=h_: e.matmul(out=seg, lhsT=da_all[:, c, h_:h_ + 1].to_broadcast([128, 128]),
                                                                      rhs=Uc, start=True, stop=False),
                             reads=[b_da, b_mats], writes=[b_sb])
                        S.op("pe", lambda e, seg=seg: e.matmul(out=seg, lhsT=identb[:], rhs=negm[:, midx, :], start=False, stop=True),
                             reads=[b_identb, b_negm], writes=[b_sb])
                    for q in range(4):
                        h_ = bi * 4 + q
                        g = h_ // 8
                        seg = sb_[:, q * 128:(q + 1) * 128]
                        dec, b_dec = decr.next()
                        S.op("act", lambda e, seg=seg, dec=dec, h_=h_: e.activation(out=dec[:], in_=seg, func=AF.Exp,
                                                                                   bias=bias_all[:, c, h_:h_ + 1],
                                                                                   scale=(1.0 if fwd else -1.0)),
                             reads=[b_sb, b_bias], writes=[b_dec])
                        MT, b_MT = MTr.next()
                        S.op("dve" if h_ % 2 == 0 else "pool", lambda e, dec=dec, MT=MT, g=g: e.tensor_tensor(
                            out=MT[:], in0=dec[:], in1=cbT[:, g * 128:(g + 1) * 128], op=ALU.mult),
                            reads=[b_dec, b_cbT], writes=[b_MT])
                        mts[h_] = (MT, b_MT)

                def stageB(h_):
                    g = h_ // 8
                    hh = h_ % 8
                    if hh == 0:
                        cur[0] = pyr.next()
                    py, b_py = cur[0]
                    MT, b_MT = mts.pop(h_)
                    S.op("pe", lambda e, MT=MT, py=py, hh=hh, h_=h_: e.matmul(out=py[:, hh * 64:(hh + 1) * 64], lhsT=MT[:],
                                                                             rhs=xdt[:, h_ * 64:(h_ + 1) * 64], start=True, stop=True),
                         reads=[b_MT, b_xdt], writes=[b_py])
                    if hh != 7:
                        return
                    po, b_po = por.next()
                    S.op("pe", lambda e, po=po, g=g: e.matmul(out=po[:], lhsT=xc_[:, 20 + g, :], rhs=Sbf[:, g * 512:(g + 1) * 512],
                                                              start=True, stop=True), reads=[b_xc, b_Sbf[g]], writes=[b_po])
                    pst, b_pst = pstr.next()
                    S.op("pe", lambda e, pst=pst, g=g: e.matmul(out=pst[:], lhsT=Bt[:, g * 128:(g + 1) * 128], rhs=xw[:, g * 512:(g + 1) * 512],
                                                                start=True, stop=True), reads=[b_Bt, b_xw], writes=[b_pst])
                    tmp, b_tmp = tmpr.next()
                    S.op("dve", lambda e, po=po, tmp=tmp, g=g: e.tensor_tensor(
                        out=g8(tmp[:, :]), in0=g8(po[:, :]), in1=scl[:, c, g * 8:(g + 1) * 8].unsqueeze(2).to_broadcast([128, 8, 64]),
                        op=ALU.mult), reads=[b_po, b_scl], writes=[b_tmp])
                    S.op("dve", lambda e, py=py, tmp=tmp, g=g: e.tensor_tensor(out=ya[:, g * 512:(g + 1) * 512], in0=py[:], in1=tmp[:],
                                                                              op=ALU.add), reads=[b_py, b_tmp], writes=[b_ya])
                    S.op("pool", lambda e, g=g: e.tensor_tensor(
                        out=g8(Sf[:, g * 512:(g + 1) * 512]), in0=g8(Sf[:, g * 512:(g + 1) * 512]),
                        in1=cdc[:, c, g * 8:(g + 1) * 8].unsqueeze(2).to_broadcast([128, 8, 64]), op=ALU.mult),
                        reads=[b_Sf[g], b_cdc], writes=[b_Sf[g]])
                    S.op("dve", lambda e, pst=pst, g=g: e.tensor_tensor(out=Sf[:, g * 512:(g + 1) * 512], in0=pst[:],
                                                                       in1=Sf[:, g * 512:(g + 1) * 512], op=ALU.add),
                         reads=[b_pst, b_Sf[g]], writes=[b_Sf[g]])
                    S.op("act", lambda e, g=g: e.copy(out=Sbf[:, g * 512:(g + 1) * 512], in_=Sf[:, g * 512:(g + 1) * 512]),
                         reads=[b_Sf[g]], writes=[b_Sbf[g]])

                for k in range(8 + 2):
                    if k < 8:
                        stageA(k)
                    if k >= 2:
                        for q in range(4):
                            stageB((k - 2) * 4 + q)
                if fwd:
                    dk, b_dk = dskr.next()
                    S.op("pool", lambda e: e.tensor_tensor(out=v3(dk), in0=v3(xs), in1=dsk.unsqueeze(2).to_broadcast([128, 32, 64]),
                                                           op=ALU.mult), reads=[b_xs, b_ssp], writes=[b_dk])
                    S.op("pool", lambda e: e.tensor_tensor(out=ya[:], in0=ya[:], in1=dk[:], op=ALU.add),
                         reads=[b_ya, b_dk], writes=[b_ya])
                    S.dma(lambda e: e.dma_start(out=yfs[c * 128:(c + 1) * 128, :], in_=ya[:]), reads=[b_ya], writes=[b_yfs],
                          sem_buf=b_ya, eng="pool")
                    return
                z_, b_z, yf_, b_yf = h[2], h[3], h[4], h[5]
                if dbg:
                    S.dma(lambda e: e.dma_start(out=ybs[c * 128:(c + 1) * 128, :], in_=ya[:]), reads=[b_ya], writes=[b_ybs],
                          sem_buf=b_ya, eng="pool")
                S.op("pool", lambda e: e.tensor_tensor(out=ya[:], in0=ya[:], in1=yf_[:], op=ALU.add), reads=[b_ya, b_yf], writes=[b_ya])
                S.op("pool", lambda e: e.tensor_tensor(out=ya[:], in0=ya[:], in1=z_[:], op=ALU.mult), reads=[b_ya, b_z], writes=[b_ya])

                def epi():
                    jk, b_jk = jkr.next()
                    s4, b_s4 = st4r.next()
                    for g in range(4):
                        S.op("act", lambda e, g=g: e.activation(out=jk[:], in_=ya[:, g * 512:(g + 1) * 512], func=AF.Square,
                                                                scale=1.0 / math.sqrt(512.0), accum_out=s4[:, g:g + 1]),
                             reads=[b_ya], writes=[b_jk, b_s4])
                    S.op("act", lambda e: e.activation(out=s4[:, 4:8], in_=s4[:, 0:4], func=AF.Sqrt, bias=EPS_AP[:, 0:1]),
                         reads=[b_s4, b_eps], writes=[b_s4])
                    S.op("dve", lambda e: e.reciprocal(out=s4[:, 8:12], in_=s4[:, 4:8]), reads=[b_s4], writes=[b_s4])
                    mb, b_mb = mbr.next()
                    for g in range(4):
                        S.op("dve", lambda e, g=g: e.tensor_scalar(out=mb[:, g * 512:(g + 1) * 512], in0=ya[:, g * 512:(g + 1) * 512],
                                                                   scalar1=s4[:, 8 + g:9 + g], scalar2=None, op0=ALU.mult),
                             reads=[b_ya, b_s4], writes=[b_mb])
                    mst, b_mst = mstr.next()
                    for half in range(2):
                        pt, b_pt = ptA.next()
                        pv = pt[:, :].bitcast(BF16)
                        for jj in range(8):
                            j = half * 8 + jj
                            S.op("pe", lambda e, pv=pv, jj=jj, j=j: e.transpose(out=pv[:, jj * 128:(jj + 1) * 128],
                                                                               in_=mb[:, j * 128:(j + 1) * 128], identity=identb[:]),
                                 reads=[b_mb, b_identb], writes=[b_pt])
                        S.op("act", lambda e, pv=pv, half=half: e.copy(out=mst[:, half * 8:(half + 1) * 8, :],
                                                                       in_=pv[:, :].rearrange("p (j t) -> p j t", j=8)),
                             reads=[b_pt], writes=[b_mst])
                    for half in range(2):
                        S.dma(lambda e, half=half: e.dma_start(
                            out=mTs[half * 1024:(half + 1) * 1024, c * 128:(c + 1) * 128].rearrange("(j p) t -> p j t", p=128),
                            in_=mst[:, half * 8:(half + 1) * 8, :]), reads=[b_mst], writes=[b_mTs], sem_buf=b_mst, eng="pool")

                if pend_epi:
                    pend_epi.pop()()
                pend_epi.append(epi)

            pend_epi = []
            hs = {}
            prs = {}
            for i in range(32 + 2):
                if i < 32:
                    hs[i] = load(i)
                if 1 <= i <= 32:
                    prs[i - 1] = prologue(i - 1, hs[i - 1])
                if i >= 2:
                    comp(i - 2, hs.pop(i - 2), prs.pop(i - 2))
            if pend_epi:
                pend_epi.pop()()
        S.barrier()

    ssd_pass(True)
    if stop_after <= 4 and stop_after == 4:
        pass
    ssd_pass(False)
    if stop_after <= 4:
        S.emit(nc)
        return nc

    mw = ExitStack()
    wpa = mw.enter_context(nc.sbuf_tensor("m_wpa", [128, 8, D_], BF16)); b_wpa = Buf()
    wpb = mw.enter_context(nc.sbuf_tensor("m_wpb", [128, 16, D_], BF16)); b_wpb = Buf()
    wo = mw.enter_context(nc.sbuf_tensor("m_wo", [128, 8, D_], BF16)); b_wo = Buf()
    mws = ExitStack()
    ms8 = mws.enter_context(nc.sbuf_tensor("m_s8", [128, 8, D_], F32)); b_ms8 = Buf()

    def preload_merge_weights():
        jobs = [(w_pa[:, :], wpa[:, :, :], b_wpa, None), (w_pb[0:1024, :], wpb[:, 0:8, :], b_wpb, gssm_t[:, 0:8]),
                (w_pb[1024:2048, :], wpb[:, 8:16, :], b_wpb, gssm_t[:, 8:16]), (w_o[:, :], wo[:, :, :], b_wo, None)]
        for src, dst, b_dst, g_ in jobs:
            S.dma(lambda e, src=src: e.dma_start(out=ms8[:], in_=src.rearrange("(kc p) n -> p kc n", p=128)),
                  writes=[b_ms8], sem_buf=b_ms8)
            if g_ is None:
                S.op("pool", lambda e, dst=dst: e.tensor_copy(out=dst, in_=ms8[:]), reads=[b_ms8], writes=[b_dst])
            else:
                S.op("pool", lambda e, dst=dst, g_=g_: e.tensor_tensor(out=dst, in0=ms8[:], in1=g_.unsqueeze(2).to_broadcast([128, 8, D_]),
                                                                      op=ALU.mult), reads=[b_ms8, b_gssm], writes=[b_dst])

    with ExitStack() as st:
        ktr = Ring(nc, st, un("t_k"), [128, S_], BF16, 2)
        qtr = Ring(nc, st, un("t_q"), [128, S_], BF16, 2)
        vtr = Ring(nc, st, un("t_v"), [128, 32, 65], BF16, 2)
        psS = Ring(nc, st, un("t_ps"), [128, 1024], F32, 3, psum=True)
        psO = Ring(nc, st, un("t_po"), [128, 1024], F32, 1, psum=True)
        pTr = Ring(nc, st, un("t_pT"), [128, 1024], BF16, 4)
        rdr = Ring(nc, st, un("t_rd"), [128, 1024], F32, 2)
        osr = Ring(nc, st, un("t_os"), [128, 1024], F32, 2)
        aor = Ring(nc, st, un("t_ao"), [128, S_], BF16, 2)
        sc = 1.0 / math.sqrt(96.0)
        LAG = 2
        tiles = {}

        def ensure(h_):
            if h_ >= NH or h_ in tiles:
                return
            kt, b_kt = ktr.next()
            qt, b_qt = qtr.next()
            vt, b_vt = vtr.next()
            S.dma(lambda e: e.dma_start(out=kt[0:96, :], in_=KT[h_, :, :]), reads=[b_KT], writes=[b_kt], sem_buf=b_kt)
            S.dma(lambda e: e.dma_start(out=qt[0:96, :], in_=QT[h_, :, :]), reads=[b_QT], writes=[b_qt], sem_buf=b_qt)
            S.dma(lambda e: e.dma_start(out=vt[:], in_=Vs[h_, :, :, :]), reads=[b_Vs], writes=[b_vt], sem_buf=b_vt)
            tiles[h_] = (kt, b_kt, qt, b_qt, vt, b_vt)

        steps = [(h_, sb, kc) for h_ in range(NH) for sb in range(4) for kc in range(32)]
        pend = {}
        cur_po = {}
        cur_ao = {}
        ensure(0)
        preload_merge_weights()
        for i in range(len(steps) + LAG):
            if i < len(steps):
                h_, sb, kc = steps[i]
                kt, b_kt, qt, b_qt, vt, b_vt = tiles[h_]
                ps, b_ps = psS.next()
                for u in range(2):
                    S.op("pe", lambda e, ps=ps, kc=kc, sb=sb, u=u, kt=kt, qt=qt: e.matmul(
                        out=ps[:, u * 512:(u + 1) * 512], lhsT=kt[0:96, kc * 128:(kc + 1) * 128],
                        rhs=qt[0:96, sb * 1024 + u * 512:sb * 1024 + (u + 1) * 512], start=True, stop=True),
                        reads=[b_kt, b_qt], writes=[b_ps])
                pT, b_pT = pTr.next()
                S.op("act", lambda e, ps=ps, pT=pT: e.activation(out=pT[:], in_=ps[:], func=AF.Exp, scale=sc),
                     reads=[b_ps], writes=[b_pT])
                pend[i] = (pT, b_pT)
            if i >= LAG:
                h_, sb, kc = steps[i - LAG]
                kt, b_kt, qt, b_qt, vt, b_vt = tiles[h_]
                pT, b_pT = pend.pop(i - LAG)
                if kc == 0:
                    cur_po[0] = psO.next()
                    if sb == 0:
                        cur_ao[0] = aor.next()
                        ensure(h_ + 1)
                po, b_po = cur_po[0]
                ao, b_ao = cur_ao[0]
                for u in range(2):
                    S.op("pe", lambda e, po=po, pT=pT, kc=kc, u=u, vt=vt: e.matmul(
                        out=po[0:65, u * 512:(u + 1) * 512], lhsT=vt[:, kc, :], rhs=pT[:, u * 512:(u + 1) * 512],
                        start=(kc == 0), stop=(kc == 31)), reads=[b_vt, b_pT], writes=[b_po])
                if kc == 31:
                    osb, b_osb = osr.next()
                    S.op("dve", lambda e, po=po, osb=osb: e.tensor_copy(out=osb[0:65, :], in_=po[0:65, :]), reads=[b_po], writes=[b_osb])
                    rd, b_rd = rdr.next()
                    S.op("dve", lambda e, osb=osb, rd=rd: e.reciprocal(out=rd[64:65, :], in_=osb[64:65, :]), reads=[b_osb], writes=[b_rd])
                    pb, b_pb = psS.next()
                    for u in range(2):
                        S.op("pe", lambda e, pb=pb, rd=rd, u=u: e.matmul(out=pb[0:64, u * 512:(u + 1) * 512], lhsT=mats[64:65, 2, 0:64],
                                                                         rhs=rd[64:65, u * 512:(u + 1) * 512], start=True, stop=True),
                             reads=[b_mats, b_rd], writes=[b_pb])
                    qs = slice(sb * 1024, (sb + 1) * 1024)
                    S.op("dve", lambda e, pb=pb, osb=osb, qs=qs, ao=ao: e.tensor_tensor(
                        out=ao[0:64, qs], in0=pb[0:64, :], in1=osb[0:64, :], op=ALU.mult),
                        reads=[b_pb, b_osb], writes=[b_ao])
                    if sb == 3:
                        S.dma(lambda e, h_=h_, ao=ao: e.dma_start(out=aTs[h_ * 64:(h_ + 1) * 64, :], in_=ao[0:64, :]),
                              reads=[b_ao], writes=[b_aTs], sem_buf=b_ao, eng="pool")
        S.barrier()
    if stop_after <= 5:
        S.emit(nc)
        return nc

    mws.close()
    with ExitStack() as st:
        atr = Ring(nc, st, un("m_at"), [128, 8, 512], BF16, 2)
        mtr = Ring(nc, st, un("m_mt"), [128, 16, 512], BF16, 2)
        gtr = Ring(nc, st, un("m_gt"), [128, 16, 512], BF16, 2)
        mgr = Ring(nc, st, un("m_mg"), [128, 8, 512], BF16, 2)
        t1r = Ring(nc, st, un("m_t1"), [128, 512], F32, 2)
        t2r = Ring(nc, st, un("m_t2"), [128, 512], F32, 2)
        xr = Ring(nc, st, un("m_x"), [128, D_], F32, 3)
        ps = Ring(nc, st, un("m_ps"), [128, 512], F32, 6, psum=True)
        def loadm(t):
            at, b_at = atr.next()
            mt, b_mt = mtr.next()
            gt_, b_gt = gtr.next()
            ts = slice(t * 512, (t + 1) * 512)
            S.dma(lambda e: e.dma_start(out=at[:], in_=aTs[:, ts].rearrange("(k p) t -> p k t", p=128)), reads=[b_aTs], writes=[b_at], sem_buf=b_at)
            S.dma(lambda e: e.dma_start(out=mt[:], in_=mTs[:, ts].rearrange("(k p) t -> p k t", p=128)), reads=[b_mTs], writes=[b_mt], sem_buf=b_mt)
            S.dma(lambda e: e.dma_start(out=gt_[:], in_=gts[:, ts].rearrange("(k p) t -> p k t", p=128)), reads=[b_gts], writes=[b_gt], sem_buf=b_gt)
            return (at, b_at, mt, b_mt, gt_, b_gt)

        def compm(t, hd):
            at, b_at, mt, b_mt, gt_, b_gt = hd
            mg, b_mg = mgr.next()
            for dc in range(8):
                pa, b_pa = ps.next()
                pb, b_pb = ps.next()
                for kc in range(8):
                    S.op("pe", lambda e, pa=pa, kc=kc, dc=dc: e.matmul(out=pa[:], lhsT=wpa[:, kc, dc * 128:(dc + 1) * 128], rhs=at[:, kc, :],
                                                                       start=(kc == 0), stop=(kc == 7)), reads=[b_wpa, b_at], writes=[b_pa])
                for kc in range(16):
                    S.op("pe", lambda e, pb=pb, kc=kc, dc=dc: e.matmul(out=pb[:], lhsT=wpb[:, kc, dc * 128:(dc + 1) * 128], rhs=mt[:, kc, :],
                                                                       start=(kc == 0), stop=(kc == 15)), reads=[b_wpb, b_mt], writes=[b_pb])
                t1, b_t1 = t1r.next()
                t2, b_t2 = t2r.next()
                S.op("dve", lambda e, pa=pa, t1=t1, dc=dc: e.tensor_tensor(out=t1[:], in0=pa[:], in1=gt_[:, dc, :], op=ALU.mult),
                     reads=[b_pa, b_gt], writes=[b_t1])
                S.op("dve", lambda e, pb=pb, t2=t2, dc=dc: e.tensor_tensor(out=t2[:], in0=pb[:], in1=gt_[:, 8 + dc, :], op=ALU.mult),
                     reads=[b_pb, b_gt], writes=[b_t2])
                S.op("pool", lambda e, t1=t1, t2=t2, dc=dc: e.tensor_tensor(out=mg[:, dc, :], in0=t1[:], in1=t2[:], op=ALU.add),
                     reads=[b_t1, b_t2], writes=[b_mg])
            for sb in range(4):
                tb = t * 4 + sb
                xt, b_xt = xr.next()
                S.dma(lambda e, xt=xt, tb=tb: e.dma_start(out=xt[:], in_=x1s[tb * 128:(tb + 1) * 128, :]),
                      reads=[b_x1s], writes=[b_xt], sem_buf=b_xt)
                for half in range(2):
                    p, b_p = ps.next()
                    for kc in range(8):
                        S.op("pe", lambda e, p=p, kc=kc, sb=sb, half=half: e.matmul(
                            out=p[:], lhsT=mg[:, kc, sb * 128:(sb + 1) * 128], rhs=wo[:, kc, half * 512:(half + 1) * 512],
                            start=(kc == 0), stop=(kc == 7)), reads=[b_mg, b_wo], writes=[b_p])
                    S.op("dve", lambda e, p=p, xt=xt, half=half: e.tensor_tensor(out=xt[:, half * 512:(half + 1) * 512], in0=p[:],
                                                                                in1=xt[:, half * 512:(half + 1) * 512], op=ALU.add),
                         reads=[b_p, b_xt], writes=[b_xt])
                S.dma(lambda e, xt=xt, tb=tb: e.dma_start(out=x2s[tb * 128:(tb + 1) * 128, :], in_=xt[:]),
                      reads=[b_xt], writes=[b_x2s], sem_buf=b_xt, eng="pool")

        pipeline(8, loadm, compm, 1)
        S.barrier()
    mw.close()
    hst2 = ExitStack()
    hT2 = hst2.enter_context(nc.sbuf_tensor("hT2", [128, 8, S_], BF16)); b_hT2 = Buf("hT2")
    norm_phase(x2s, b_x2s, hT2, b_hT2)

    with ExitStack() as fst:
        wd2 = fst.enter_context(nc.sbuf_tensor("wd2", [128, NFF, D_], BF16)); b_wd2 = Buf()
        ffn_gateup(w_g2, w_u2, 2, hT2, b_hT2, w_d2, wd2, b_wd2)
        ffn_down(w_d2, x2s, b_x2s, y_out, b_yout, None, wd2, b_wd2)
    S.emit(nc)
    return nc


def _fm(v, kc):
    return np.ascontiguousarray(np.asarray(v, np.float32).reshape(kc, 128).T)


_CACHE = {}


def consts():
    ii = np.arange(128)
    U = (ii[:, None] <= ii[None, :]).astype(np.float32)
    Us = (ii[:, None] < ii[None, :]).astype(np.float32)
    ones = np.ones((128, 128), np.float32)
    I = np.eye(128, dtype=np.float32)
    mats = np.ascontiguousarray(np.stack([U, Us, ones, I], axis=1))
    negf = np.where(ii[:, None] > ii[None, :], -30000.0, 0.0).astype(np.float32)
    posb = np.where(ii[:, None] < ii[None, :], 30000.0, 0.0).astype(np.float32)
    neg = np.ascontiguousarray(np.stack([negf, posb], axis=1)).astype(ml_dtypes.bfloat16)
    invf = (1.0 / (10000.0 ** (np.arange(0, 32, 2, dtype=np.float32) / 32.0))).astype(np.float32)[None, :]
    return dict(c_identb=I.astype(ml_dtypes.bfloat16), c_mats=mats, c_neg=neg, c_invf=invf)


def make_shared(inp):
    f = lambda k: np.asarray(inp[k], np.float32)[0]
    d = {}
    d["gfm"] = np.ascontiguousarray(np.concatenate([_fm(f("ffn1_norm"), 8), _fm(f("mix_norm"), 8), _fm(f("ffn2_norm"), 8)], axis=1))
    d["gqa"] = _fm(f("q_a_norm"), 3)
    d["gkva"] = _fm(f("kv_a_norm"), 2)
    d["gssm"] = _fm(f("ssm_norm"), 16)
    d["w_g1"] = f("ffn1_w_gate"); d["w_u1"] = f("ffn1_w_up"); d["w_d1"] = f("ffn1_w_down")
    d["w_g2"] = f("ffn2_w_gate"); d["w_u2"] = f("ffn2_w_up"); d["w_d2"] = f("ffn2_w_down")
    d["w_in"] = f("w_in"); d["w_qb"] = f("w_q_b"); d["w_kvb"] = f("w_kv_b")
    d["hn"] = np.concatenate([f("q_head_norm"), f("k_head_norm")])[None, :].astype(np.float32)
    cw = f("conv_w")[:, 0, :]
    d["convw"] = np.ascontiguousarray(cw.T.reshape(24, 128, 5).transpose(1, 0, 2))
    d["convb"] = _fm(f("conv_b"), 24)
    d["ssp"] = np.concatenate([f("a_log_fwd"), f("a_log_bwd"), f("dt_bias_fwd"), f("dt_bias_bwd"), f("d_skip")])[None, :].astype(np.float32)
    d["w_pa"] = f("w_attn_branch"); d["w_pb"] = f("w_ssm_branch"); d["w_o"] = f("w_out")
    d.update(consts())
    return d


def make_inmap(inp, shared, b):
    d = dict(shared)
    d["x"] = np.ascontiguousarray(np.asarray(inp["x"], np.float32)[b])
    p = np.asarray(inp["positions"], np.int32)[b]
    d["pos"] = np.ascontiguousarray(p.reshape(32, 128).T)
    return d


def kernel(**inputs):
    nb = int(np.asarray(inputs["x"]).shape[0])
    nc = build(dbg=False)
    shared = make_shared(inputs)
    in_maps = [make_inmap(inputs, shared, b) for b in range(nb)]
    res = run_bass_kernel_spmd(nc, in_maps, core_ids=list(range(nb)))
    out = np.stack([np.asarray(res.results[b]["y"], dtype=np.float32) for b in range(nb)], axis=0)
    return out
```

```python
import math
from contextlib import ExitStack
import numpy as np
import ml_dtypes
import concourse.bass as bass
import concourse.mybir as mybir
from concourse.bass_utils import run_bass_kernel_spmd

F32 = mybir.dt.float32
BF16 = mybir.dt.bfloat16
I32 = mybir.dt.int32
AF = mybir.ActivationFunctionType
ALU = mybir.AluOpType
AX = mybir.AxisListType

S_ = 4096
D_ = 1024
FF = 2816
NFF = 22
NH = 16
EPS = 1e-6
C_Q, C_KV, C_PE, C_Z, C_XBC, C_DTF, C_DTB, C_GA, C_GB = 0, 384, 640, 672, 2720, 5792, 5824, 5856, 6880
IN_DIM = 7904
ENGS = ("pe", "act", "dve", "pool", "sp")
FUSE_WAITS = True


class DSem:
    def __init__(self):
        self.count = 0
        self.handle = None


class Buf:
    __slots__ = ("name", "lw", "rd", "dsem", "ep")

    def __init__(self, name=""):
        self.name = name
        self.lw = None
        self.rd = []
        self.dsem = None
        self.ep = -1


class Op:
    __slots__ = ("eng", "fn", "idx", "waits", "dwaits", "inc", "dsem", "know", "seq", "multi")


class Sched:
    def __init__(self):
        self.ops = {e: [] for e in ENGS}
        self.know = {e: {} for e in ENGS}
        self.dsems = []
        self.free = []
        self.epoch = 0

    def _add(self, eng, fn, reads, writes, dsem=None, extra=(), extra_ds=()):
        op = Op()
        op.eng = eng
        op.fn = fn
        op.idx = len(self.ops[eng])
        op.waits = {}
        op.dwaits = {}
        op.inc = False
        op.dsem = dsem
        op.seq = None
        op.multi = False
        know = self.know[eng]
        deps = list(extra)
        for b in reads:
            if b.lw is not None:
                deps.append(b.lw)
        for b in writes:
            if b.lw is not None:
                deps.append(b.lw)
            deps.extend(b.rd)
        for a in deps:
            if a is op:
                continue
            if a.dsem is None:
                if a.eng == "pe" and eng == "pe":
                    continue
                if know.get(a.eng, -1) >= a.idx:
                    continue
                a.inc = True
                cur = op.waits.get(a.eng)
                if cur is None or cur.idx < a.idx:
                    op.waits[a.eng] = a
                for k, v in a.know.items():
                    if know.get(k, -1) < v:
                        know[k] = v
                know[a.eng] = max(know.get(a.eng, -1), a.idx)
            else:
                ds = a.dsem
                v = ds.count
                if know.get(ds, -1) >= v:
                    continue
                op.dwaits[ds] = v
                for k, vv in a.know.items():
                    if know.get(k, -1) < vv:
                        know[k] = vv
                know[ds] = v
        for ds in extra_ds:
            v = ds.count
            if know.get(ds, -1) < v:
                op.dwaits[ds] = v
                know[ds] = v
        if dsem is not None:
            dsem.count += 16
        op.know = dict(know)
        for b in reads:
            b.rd.append(op)
        for b in writes:
            b.lw = op
            b.rd = []
        self.ops[eng].append(op)
        return op

    def op(self, eng, fn, reads=(), writes=(), multi=False):
        o = self._add(eng, fn, reads, writes)
        o.multi = multi
        return o

    def dma(self, fn, reads=(), writes=(), sem_buf=None, eng="sp"):
        if sem_buf.dsem is None or sem_buf.ep != self.epoch:
            if self.free:
                sem_buf.dsem = self.free.pop()
            else:
                sem_buf.dsem = DSem()
                self.dsems.append(sem_buf.dsem)
            sem_buf.ep = self.epoch
        return self._add(eng, fn, reads, writes, dsem=sem_buf.dsem)

    def barrier(self):
        lasts = []
        for e in ENGS:
            if e == "sp":
                continue
            for o in reversed(self.ops[e]):
                if o.dsem is None:
                    lasts.append(o)
                    break
        spop = self._add("sp", lambda e: e.nop(), (), (), extra=lasts, extra_ds=list(self.dsems))
        self.epoch += 1
        self.free = list(self.dsems)
        for e in ENGS:
            if e == "sp":
                continue
            self._add(e, lambda eh: eh.nop(), (), (), extra=[spop])

    def emit(self, nc):
        with ExitStack() as st:
            esem = {e: st.enter_context(nc.semaphore("es_" + e)) for e in ENGS}
            for i, d in enumerate(self.dsems):
                d.handle = st.enter_context(nc.semaphore("ds%d" % i))
            for e in ENGS:
                c = 0
                for o in self.ops[e]:
                    if o.dsem is None and o.inc:
                        c += 1
                        o.seq = c
            block = st.enter_context(nc.Block())

            def run(e, eh):
                for o in self.ops[e]:
                    wl = [(esem[se], a.seq) for se, a in o.waits.items()] + [(ds.handle, v) for ds, v in o.dwaits.items()]
                    attach = None
                    if wl and o.dsem is None and not o.multi and e != "sp" and FUSE_WAITS:
                        attach = wl.pop()
                    for hh_, vv_ in wl:
                        eh.wait_ge(hh_, vv_)
                    n0 = nc.n_instructions()
                    ins = o.fn(eh)
                    if attach is not None:
                        if nc.n_instructions() - n0 != 1:
                            raise RuntimeError("multi-instruction op with fused wait on %s (%d)" % (e, nc.n_instructions() - n0))
                        ins._wait_ge(attach[0], attach[1])
                    if o.dsem is not None:
                        ins.then_inc(o.dsem.handle, 16)
                    elif o.inc:
                        ins.then_inc(esem[e], 1)
                if e == "sp":
                    for ds in self.dsems:
                        eh.wait_ge(ds.handle, ds.count)

            @block.tensor
            def _(eh):
                run("pe", eh)

            @block.scalar
            def _(eh):
                run("act", eh)

            @block.vector
            def _(eh):
                run("dve", eh)

            @block.gpsimd
            def _(eh):
                run("pool", eh)

            @block.sync
            def _(eh):
                run("sp", eh)


class Ring:
    def __init__(self, nc, st, name, shape, dtype, n, psum=False):
        self.items = []
        for i in range(n):
            if psum:
                t = st.enter_context(nc.psum_tensor("%s%d" % (name, i), shape, dtype))
            else:
                t = st.enter_context(nc.sbuf_tensor("%s%d" % (name, i), shape, dtype))
            self.items.append((t, Buf("%s%d" % (name, i))))
        self.i = 0

    def next(self):
        r = self.items[self.i % len(self.items)]
        self.i += 1
        return r


def pipeline(n, load_fn, compute_fn, depth):
    hs = {}
    for i in range(n + depth):
        if i < n:
            hs[i] = load_fn(i)
        if i >= depth:
            compute_fn(i - depth, hs.pop(i - depth))


class K:
    pass


def build(dbg=False, stop_after=99):
    nc = bass.Bass("TRN2", target_bir_lowering=False)
    S = Sched()
    uid = [0]

    def un(p):
        uid[0] += 1
        return "%s_%d" % (p, uid[0])

    def inp(name, shape, dt=F32):
        return nc.dram_tensor(name, shape, dt, kind="ExternalInput").ap()

    def scratch(name, shape, dt, out=False):
        kind = "ExternalOutput" if (out or dbg) else "Internal"
        return nc.dram_tensor(name, shape, dt, kind=kind).ap(), Buf(name)

    x = inp("x", [S_, D_])
    pos = inp("pos", [128, 32], I32)
    gfm = inp("gfm", [128, 24])
    gqa = inp("gqa", [128, 3])
    gkva = inp("gkva", [128, 2])
    gssm = inp("gssm", [128, 16])
    w_g1 = inp("w_g1", [D_, FF]); w_u1 = inp("w_u1", [D_, FF]); w_d1 = inp("w_d1", [FF, D_])
    w_g2 = inp("w_g2", [D_, FF]); w_u2 = inp("w_u2", [D_, FF]); w_d2 = inp("w_d2", [FF, D_])
    w_in = inp("w_in", [D_, IN_DIM])
    w_qb = inp("w_qb", [384, 1536]); w_kvb = inp("w_kvb", [256, 2048])
    hn = inp("hn", [1, 192])
    convw = inp("convw", [128, 24, 5]); convb = inp("convb", [128, 24])
    ssp = inp("ssp", [1, 160])
    w_pa = inp("w_pa", [D_, D_]); w_pb = inp("w_pb", [2048, D_]); w_o = inp("w_o", [D_, D_])
    c_identb = inp("c_identb", [128, 128], BF16)
    c_mats = inp("c_mats", [128, 4, 128])
    c_neg = inp("c_neg", [128, 2, 128], BF16)
    c_invf = inp("c_invf", [1, 16])

    y_out, b_yout = scratch("y", [S_, D_], F32, out=True)
    x1s, b_x1s = scratch("x1s", [S_, D_], F32)
    x2s, b_x2s = scratch("x2s", [S_, D_], F32)
    hmid, b_hmid = scratch("hmid", [FF, S_], BF16)
    QT, b_QT = scratch("QT", [NH, 96, S_], BF16)
    KT, b_KT = scratch("KT", [NH, 96, S_], BF16)
    Vs, b_Vs = scratch("Vs", [NH, 128, 32, 65], BF16)
    zs, b_zs = scratch("zs", [S_, 2048], BF16)
    xcs, b_xcs = scratch("xcs", [32, 128, 24, 128], BF16)
    gts, b_gts = scratch("gts", [2048, S_], BF16)
    yfs, b_yfs = scratch("yfs", [S_, 2048], F32)
    mTs, b_mTs = scratch("mTs", [2048, S_], BF16)
    aTs, b_aTs = scratch("aTs", [D_, S_], BF16)
    if dbg:
        ybs, b_ybs = scratch("ybs", [S_, 2048], F32)

    top = ExitStack()
    A = lambda name, shape, dt: top.enter_context(nc.sbuf_tensor(name, shape, dt))
    identb = A("identb", [128, 128], BF16); b_identb = Buf()
    mats = A("mats", [128, 4, 128], F32); b_mats = Buf()
    negm = A("negm", [128, 2, 128], BF16); b_negm = Buf()
    gfm_t = A("gfm_t", [128, 24], F32); b_gfm = Buf()
    gqa_t = A("gqa_t", [128, 3], F32); b_gqa = Buf()
    gkva_t = A("gkva_t", [128, 2], F32); b_gkva = Buf()
    gssm_t = A("gssm_t", [128, 16], F32); b_gssm = Buf()
    hn_t = A("hn_t", [128, 192], F32); b_hn = Buf()
    ssp_t = A("ssp_t", [128, 160], F32); b_ssp = Buf()
    convw_t = A("convw_t", [128, 24, 5], F32); b_convw = Buf()
    convb_t = A("convb_t", [128, 24], F32); b_convb = Buf()
    cos_t = A("cos_t", [128, 32, 16], F32); b_cos = Buf()
    sin_t = A("sin_t", [128, 32, 16], F32); b_sin = Buf()
    dtraw = A("dtraw", [128, 32, 64], F32); b_dtraw = Buf()

    def ld(dst, src, b):
        S.dma(lambda e: e.dma_start(out=dst, in_=src), writes=[b], sem_buf=b)

    ld(identb[:], c_identb[:, :], b_identb)
    ld(mats[:], c_mats[:, :, :], b_mats)
    ld(negm[:], c_neg[:, :, :], b_negm)
    ld(gfm_t[:], gfm[:, :], b_gfm)
    ld(gqa_t[:], gqa[:, :], b_gqa)
    ld(gkva_t[:], gkva[:, :], b_gkva)
    ld(gssm_t[:], gssm[:, :], b_gssm)
    ld(hn_t[:], hn.partition_broadcast(128), b_hn)
    ld(ssp_t[:], ssp.partition_broadcast(128), b_ssp)
    ld(convw_t[:], convw[:, :, :], b_convw)
    ld(convb_t[:], convb[:, :], b_convb)
    Umat = mats[:, 0, :]
    Ustr = mats[:, 1, :]
    onesf = mats[:, 2, :]
    identf = mats[:, 3, :]

    with ExitStack() as st:
        post = st.enter_context(nc.sbuf_tensor("post", [128, 32], I32)); b_post = Buf()
        posf = st.enter_context(nc.sbuf_tensor("posf", [128, 32], F32)); b_posf = Buf()
        invf = st.enter_context(nc.sbuf_tensor("invf", [128, 16], F32)); b_invf = Buf()
        ang = st.enter_context(nc.sbuf_tensor("ang", [128, 32, 16], F32)); b_ang = Buf()
        ang2 = st.enter_context(nc.sbuf_tensor("ang2", [128, 32, 16], F32)); b_ang2 = Buf()
        ld(post[:], pos[:, :], b_post)
        ld(invf[:], c_invf.partition_broadcast(128), b_invf)
        S.op("dve", lambda e: e.tensor_copy(out=posf[:], in_=post[:]), reads=[b_post], writes=[b_posf])
        S.op("dve", lambda e: e.tensor_tensor(out=ang[:], in0=posf[:].unsqueeze(2).to_broadcast([128, 32, 16]),
                                              in1=invf[:].unsqueeze(1).to_broadcast([128, 32, 16]), op=ALU.mult),
             reads=[b_posf, b_invf], writes=[b_ang])
        PI = math.pi
        angi = st.enter_context(nc.sbuf_tensor("angi", [128, 32, 16], I32)); b_angi = Buf()
        ang3 = st.enter_context(nc.sbuf_tensor("ang3", [128, 32, 16], F32)); b_ang3 = Buf()

        def rr(shift, dst, b_dst):
            S.op("dve", lambda e: e.tensor_scalar(out=ang2[:], in0=ang[:], scalar1=shift, scalar2=None, op0=ALU.add),
                 reads=[b_ang], writes=[b_ang2])
            S.op("dve", lambda e: e.tensor_scalar(out=ang3[:], in0=ang2[:], scalar1=1.0 / (2 * PI), scalar2=None,
                                                  op0=ALU.mult), reads=[b_ang2], writes=[b_ang3])
            S.op("dve", lambda e: e.tensor_copy(out=angi[:], in_=ang3[:]), reads=[b_ang3], writes=[b_angi])
            S.op("dve", lambda e: e.tensor_copy(out=ang3[:], in_=angi[:]), reads=[b_angi], writes=[b_ang3])
            S.op("dve", lambda e: e.scalar_tensor_tensor(out=ang2[:], in0=ang3[:], scalar=-2 * PI, in1=ang2[:],
                                                         op0=ALU.mult, op1=ALU.add), reads=[b_ang3, b_ang2], writes=[b_ang2])
            S.op("dve", lambda e: e.tensor_scalar(out=ang3[:], in0=ang2[:], scalar1=-PI, scalar2=1e9,
                                                  op0=ALU.add, op1=ALU.mult), reads=[b_ang2], writes=[b_ang3])
            S.op("dve", lambda e: e.tensor_scalar(out=ang3[:], in0=ang3[:], scalar1=0.0, scalar2=1.0,
                                                  op0=ALU.max, op1=ALU.min), reads=[b_ang3], writes=[b_ang3])
            S.op("dve", lambda e: e.scalar_tensor_tensor(out=ang2[:], in0=ang3[:], scalar=-2 * PI, in1=ang2[:],
                                                         op0=ALU.mult, op1=ALU.add), reads=[b_ang3, b_ang2], writes=[b_ang2])
            S.op("dve", lambda e: e.tensor_scalar(out=ang3[:], in0=ang2[:], scalar1=PI, scalar2=-1e9,
                                                  op0=ALU.add, op1=ALU.mult), reads=[b_ang2], writes=[b_ang3])
            S.op("dve", lambda e: e.tensor_scalar(out=ang3[:], in0=ang3[:], scalar1=0.0, scalar2=1.0,
                                                  op0=ALU.max, op1=ALU.min), reads=[b_ang3], writes=[b_ang3])
            S.op("dve", lambda e: e.scalar_tensor_tensor(out=ang2[:], in0=ang3[:], scalar=2 * PI, in1=ang2[:],
                                                         op0=ALU.mult, op1=ALU.add), reads=[b_ang3, b_ang2], writes=[b_ang2])
            S.op("dve", lambda e: e.tensor_scalar(out=ang2[:], in0=ang2[:], scalar1=PI * (1 - 1e-6),
                                                  scalar2=-PI * (1 - 1e-6), op0=ALU.min, op1=ALU.max),
                 reads=[b_ang2], writes=[b_ang2])
            S.op("act", lambda e: e.activation(out=dst, in_=ang2[:], func=AF.Sin), reads=[b_ang2], writes=[b_dst])

        rr(0.0, sin_t[:], b_sin)
        rr(0.5 * PI, cos_t[:], b_cos)
        S.barrier()

    def wload(stage_ring, w_ring, wsrc, kc, n, gain=None, cast_eng="pool"):
        stg, b_stg = stage_ring.next()
        wt, b_wt = w_ring.next()
        S.dma(lambda e: e.dma_start(out=stg[:, 0:kc, 0:n], in_=wsrc.rearrange("(kc p) n -> p kc n", p=128)),
              writes=[b_stg], sem_buf=b_stg)
        if gain is None:
            S.op(cast_eng, lambda e: e.tensor_copy(out=wt[:, 0:kc, 0:n], in_=stg[:, 0:kc, 0:n]),
                 reads=[b_stg], writes=[b_wt])
        else:
            g_ap, b_g = gain
            S.op(cast_eng, lambda e: e.tensor_tensor(out=wt[:, 0:kc, 0:n], in0=stg[:, 0:kc, 0:n],
                                                     in1=g_ap.unsqueeze(2).to_broadcast([128, kc, n]), op=ALU.mult),
                 reads=[b_stg, b_g], writes=[b_wt])
        return wt, b_wt

    def norm_block(src, b_src, tb, rings, hT, b_hT, ncols=D_):
        junk, b_junk = rings["junk"].next()
        stt, b_stt = rings["st"].next()
        hb, b_hb = rings["hb"].next()
        ptr, b_ptr = rings["ptr"].next()
        S.op("act", lambda e: e.activation(out=junk[:], in_=src, func=AF.Square, scale=1.0 / math.sqrt(ncols),
                                           accum_out=stt[:, 0:1]), reads=[b_src], writes=[b_junk, b_stt])
        S.op("act", lambda e: e.activation(out=stt[:, 1:2], in_=stt[:, 0:1], func=AF.Sqrt, bias=EPS_AP[:, 0:1]),
             reads=[b_stt], writes=[b_stt])
        S.op("dve", lambda e: e.reciprocal(out=stt[:, 2:3], in_=stt[:, 1:2]), reads=[b_stt], writes=[b_stt])
        S.op("dve", lambda e: e.tensor_scalar(out=hb[:], in0=src, scalar1=stt[:, 2:3], scalar2=None, op0=ALU.mult),
             reads=[b_src, b_stt], writes=[b_hb])
        pv = ptr[:, :].bitcast(BF16)
        for kc in range(8):
            S.op("pe", lambda e, kc=kc: e.transpose(out=pv[:, kc * 128:(kc + 1) * 128],
                                                     in_=hb[:, kc * 128:(kc + 1) * 128], identity=identb[:]),
                 reads=[b_hb, b_identb], writes=[b_ptr])
        S.op("dve", lambda e: e.tensor_copy(out=hT[:, :, tb * 128:(tb + 1) * 128],
                                            in_=pv.rearrange("p (k t) -> p k t", k=8)),
             reads=[b_ptr], writes=[b_hT])

    eps_t = A("eps_t", [128, 1], F32); b_eps = Buf()
    S.op("pool", lambda e: e.memset(eps_t[:], EPS), writes=[b_eps])
    EPS_AP = eps_t
    S.barrier()

    def ffn_gateup(w_g, w_u, gain_col, hT, b_hT, w_d=None, wd=None, b_wd=None):
        with ExitStack() as st:
            stg = Ring(nc, st, un("gu_stg"), [128, 8, 128], F32, 6)
            wr = Ring(nc, st, un("gu_w"), [128, 8, 128], BF16, 6)
            psg = Ring(nc, st, un("gu_pg"), [128, 512], F32, 3, psum=True)
            psu = Ring(nc, st, un("gu_pu"), [128, 512], F32, 3, psum=True)
            sil = Ring(nc, st, un("gu_sil"), [128, 512], F32, 3)
            hm = Ring(nc, st, un("gu_hm"), [128, S_], BF16, 2)
            gain = (gfm_t[:, gain_col * 8:(gain_col + 1) * 8], b_gfm)

            dstg = Ring(nc, st, un("gu_dstg"), [128, 1, D_], F32, 3)

            def load(j):
                wg = wload(stg, wr, w_g[:, j * 128:(j + 1) * 128], 8, 128, gain)
                wu = wload(stg, wr, w_u[:, j * 128:(j + 1) * 128], 8, 128, gain)
                if w_d is not None:
                    sg, b_sg = dstg.next()
                    S.dma(lambda e, sg=sg, j=j: e.dma_start(out=sg[:, 0, :], in_=w_d[j * 128:(j + 1) * 128, :]),
                          writes=[b_sg], sem_buf=b_sg)
                    S.op("pool", lambda e, sg=sg, j=j: e.tensor_copy(out=wd[:, j, :], in_=sg[:, 0, :]),
                         reads=[b_sg], writes=[b_wd])
                return wg, wu

            def comp(j, h):
                (wg, b_wg), (wu, b_wu) = h
                hmt, b_hm = hm.next()
                for tb in range(8):
                    pg, b_pg = psg.next()
                    pu, b_pu = psu.next()
                    sl, b_sl = sil.next()
                    ts = slice(tb * 512, (tb + 1) * 512)
                    for kc in range(8):
                        S.op("pe", lambda e, kc=kc, pg=pg, wg=wg, ts=ts: e.matmul(
                            out=pg[:], lhsT=wg[:, kc, :], rhs=hT[:, kc, ts], start=(kc == 0), stop=(kc == 7)),
                            reads=[b_wg, b_hT], writes=[b_pg])
                    for kc in range(8):
                        S.op("pe", lambda e, kc=kc, pu=pu, wu=wu, ts=ts: e.matmul(
                            out=pu[:], lhsT=wu[:, kc, :], rhs=hT[:, kc, ts], start=(kc == 0), stop=(kc == 7)),
                            reads=[b_wu, b_hT], writes=[b_pu])
                    S.op("act", lambda e, sl=sl, pg=pg: e.activation(out=sl[:], in_=pg[:], func=AF.Silu),
                         reads=[b_pg], writes=[b_sl])
                    S.op("dve", lambda e, sl=sl, pu=pu, hmt=hmt, ts=ts: e.tensor_tensor(
                        out=hmt[:, ts], in0=sl[:], in1=pu[:], op=ALU.mult), reads=[b_sl, b_pu], writes=[b_hm])
                S.dma(lambda e, hmt=hmt, j=j: e.dma_start(out=hmid[j * 128:(j + 1) * 128, :], in_=hmt[:]),
                      reads=[b_hm], writes=[b_hmid], sem_buf=b_hm, eng="pool")

            pipeline(NFF, load, comp, 2)
        S.barrier()

    def ffn_down(w_d, xsrc, b_xsrc, xdst, b_xdst, next_norm, wd=None, b_wd=None):
        with ExitStack() as st:
            pre = wd is not None
            if not pre:
                wd = st.enter_context(nc.sbuf_tensor(un("wd"), [128, NFF, D_], BF16)); b_wd = Buf()
            stg = Ring(nc, st, un("dn_stg"), [128, 1, D_], F32, 3)
            for j in range(NFF if not pre else 0):
                sg, b_sg = stg.next()
                S.dma(lambda e, sg=sg, j=j: e.dma_start(out=sg[:, 0, :], in_=w_d[j * 128:(j + 1) * 128, :]),
                      writes=[b_sg], sem_buf=b_sg)
                S.op("pool", lambda e, sg=sg, j=j: e.tensor_copy(out=wd[:, j, :], in_=sg[:, 0, :]),
                     reads=[b_sg], writes=[b_wd])
            hmr = Ring(nc, st, un("dn_hm"), [128, NFF, 512], BF16, 2)
            xr = Ring(nc, st, un("dn_x"), [128, D_], F32, 3)
            ps = Ring(nc, st, un("dn_ps"), [128, 512], F32, 4, psum=True)
            rings = None
            if next_norm:
                rings = dict(junk=Ring(nc, st, un("nj"), [128, D_], BF16, 2),
                             st=Ring(nc, st, un("nst"), [128, 4], F32, 3),
                             hb=Ring(nc, st, un("nhb"), [128, D_], BF16, 2),
                             ptr=Ring(nc, st, un("nptr"), [128, 512], F32, 2, psum=True))

            def load(t):
                hmt, b_hm = hmr.next()
                S.dma(lambda e: e.dma_start(out=hmt[:], in_=hmid[:, t * 512:(t + 1) * 512].rearrange(
                    "(j p) t -> p j t", p=128)), reads=[b_hmid], writes=[b_hm], sem_buf=b_hm)
                return hmt, b_hm

            def comp(t, h):
                hmt, b_hm = h
                for sb in range(4):
                    tb = t * 4 + sb
                    xt, b_xt = xr.next()
                    S.dma(lambda e, xt=xt, tb=tb: e.dma_start(out=xt[:], in_=xsrc[tb * 128:(tb + 1) * 128, :]),
                          reads=[b_xsrc], writes=[b_xt], sem_buf=b_xt)
                    for half in range(2):
                        p, b_p = ps.next()
                        for j in range(NFF):
                            S.op("pe", lambda e, j=j, p=p, sb=sb, half=half, hmt=hmt: e.matmul(
                                out=p[:], lhsT=hmt[:, j, sb * 128:(sb + 1) * 128],
                                rhs=wd[:, j, half * 512:(half + 1) * 512], start=(j == 0), stop=(j == NFF - 1)),
                                reads=[b_hm, b_wd], writes=[b_p])
                        S.op("dve", lambda e, p=p, xt=xt, half=half: e.scalar_tensor_tensor(
                            out=xt[:, half * 512:(half + 1) * 512], in0=p[:], scalar=0.5,
                            in1=xt[:, half * 512:(half + 1) * 512], op0=ALU.mult, op1=ALU.add),
                            reads=[b_p, b_xt], writes=[b_xt])
                    S.dma(lambda e, xt=xt, tb=tb: e.dma_start(out=xdst[tb * 128:(tb + 1) * 128, :], in_=xt[:]),
                          reads=[b_xt], writes=[b_xdst], sem_buf=b_xt, eng="pool")
                    if next_norm:
                        if pendn:
                            norm_block(*pendn.pop())
                        pendn.append((xt[:], b_xt, tb, rings, next_norm[0], next_norm[1]))

            pendn = []
            pipeline(8, load, comp, 1)
            if pendn:
                norm_block(*pendn.pop())
        S.barrier()

    def norm_phase(xsrc, b_xsrc, hT, b_hT):
        with ExitStack() as st:
            xr = Ring(nc, st, un("np_x"), [128, D_], F32, 3)
            rings = dict(junk=Ring(nc, st, un("nj"), [128, D_], BF16, 2),
                         st=Ring(nc, st, un("nst"), [128, 4], F32, 3),
                         hb=Ring(nc, st, un("nhb"), [128, D_], BF16, 2),
                         ptr=Ring(nc, st, un("nptr"), [128, 512], F32, 2, psum=True))

            def load(tb):
                xt, b_xt = xr.next()
                S.dma(lambda e: e.dma_start(out=xt[:], in_=xsrc[tb * 128:(tb + 1) * 128, :]),
                      reads=[b_xsrc], writes=[b_xt], sem_buf=b_xt)
                return xt, b_xt

            def comp(tb, h):
                norm_block(h[0][:], h[1], tb, rings, hT, b_hT)

            pipeline(32, load, comp, 2)
        S.barrier()

    b_x = Buf("x")
    hst = ExitStack()
    hT = hst.enter_context(nc.sbuf_tensor("hT", [128, 8, S_], BF16)); b_hT = Buf("hT")
    norm_phase(x, b_x, hT, b_hT)
    with ExitStack() as fst:
        wd1 = fst.enter_context(nc.sbuf_tensor("wd1", [128, NFF, D_], BF16)); b_wd1 = Buf()
        ffn_gateup(w_g1, w_u1, 0, hT, b_hT, w_d1, wd1, b_wd1)
        ffn_down(w_d1, x, b_x, x1s, b_x1s, (hT, b_hT), wd1, b_wd1)
    if stop_after <= 1:
        S.emit(nc)
        return nc

    gmix = (gfm_t[:, 8:16], b_gfm)

    def rope(src3, H, tb, dst3, tmp_ring):
        ta, b_ta = tmp_ring.next()
        tb_, b_tb = tmp_ring.next()
        cb = cos_t[:, tb, :].unsqueeze(1).to_broadcast([128, H, 16])
        sb = sin_t[:, tb, :].unsqueeze(1).to_broadcast([128, H, 16])
        t1 = src3[:, :, 0:16]
        t2 = src3[:, :, 16:32]
        a_ = ta[:, 0:H, :]
        b_ = tb_[:, 0:H, :]
        return [
            (lambda e: e.tensor_tensor(out=a_, in0=t1, in1=cb, op=ALU.mult), [b_cos], [b_ta]),
            (lambda e: e.tensor_tensor(out=b_, in0=t2, in1=sb, op=ALU.mult), [b_sin], [b_tb]),
            (lambda e: e.tensor_tensor(out=dst3[:, :, 0:16], in0=a_, in1=b_, op=ALU.subtract), [b_ta, b_tb], []),
            (lambda e: e.tensor_tensor(out=a_, in0=t2, in1=cb, op=ALU.mult), [b_cos], [b_ta]),
            (lambda e: e.tensor_tensor(out=b_, in0=t1, in1=sb, op=ALU.mult), [b_sin], [b_tb]),
            (lambda e: e.tensor_tensor(out=dst3[:, :, 16:32], in0=a_, in1=b_, op=ALU.add), [b_ta, b_tb], []),
        ]

    with ExitStack() as st:
        wAr = Ring(nc, st, un("a_w"), [128, 8, 672], BF16, 1)
        wqr = Ring(nc, st, un("q_w"), [128, 3, 1536], BF16, 1)
        wkr = Ring(nc, st, un("kv_w"), [128, 2, 2048], BF16, 1)
        with ExitStack() as st2:
            stgA = Ring(nc, st2, un("a_stg"), [128, 8, 672], F32, 1)
            stgq = Ring(nc, st2, un("q_stg"), [128, 3, 1536], F32, 1)
            stgk = Ring(nc, st2, un("kv_stg"), [128, 2, 2048], F32, 1)
            wA, b_wA = wload(stgA, wAr, w_in[:, 0:672], 8, 672, gmix)
            wq, b_wq = wload(stgq, wqr, w_qb[:, :], 3, 1536, (gqa_t[:, :], b_gqa))
            wkv, b_wkv = wload(stgk, wkr, w_kvb[:, :], 2, 2048, (gkva_t[:, :], b_gkva))
            S.barrier()
        psA = Ring(nc, st, un("a_ps"), [128, 512], F32, 2, psum=True)
        psT = Ring(nc, st, un("a_pt"), [128, 512], F32, 2, psum=True)
        psQ = Ring(nc, st, un("a_pq"), [128, 512], F32, 3, psum=True)
        junk = Ring(nc, st, un("a_junk"), [128, 1536], F32, 1)
        stt = Ring(nc, st, un("a_st"), [128, 8], F32, 2)
        sst = Ring(nc, st, un("a_ss"), [128, 100], F32, 2)
        cnr = Ring(nc, st, un("a_cn"), [128, 640], BF16, 2)
        cTr = Ring(nc, st, un("a_cT"), [128, 5, 128], BF16, 2)
        kper = Ring(nc, st, un("a_kpe"), [128, 1, 32], F32, 2)
        kpgr = Ring(nc, st, un("a_kpg"), [128, 1, 32], F32, 2)
        krr = Ring(nc, st, un("a_kr"), [128, 1, 32], F32, 2)
        qsbr = Ring(nc, st, un("a_qsb"), [128, 1536], F32, 1)
        kvsbr = Ring(nc, st, un("a_kvsb"), [128, 2048], F32, 1)
        tmpkr = Ring(nc, st, un("a_tmpk"), [128, 16, 64], F32, 1)
        qbr = Ring(nc, st, un("a_qb"), [128, 16, 96], BF16, 2)
        kbr = Ring(nc, st, un("a_kb"), [128, 16, 96], BF16, 2)
        ropet = Ring(nc, st, un("a_rt"), [128, 16, 16], F32, 4)
        vbr = Ring(nc, st, un("a_vb"), [128, 16, 4, 65], BF16, 1)
        qTr = Ring(nc, st, un("a_qT"), [128, 16, 256], BF16, 2)
        kTr = Ring(nc, st, un("a_kT"), [128, 16, 256], BF16, 2)
        for (vt, b_v) in vbr.items:
            S.op("pool", lambda e, vt=vt: e.memset(vt[:], 1.0), writes=[b_v])
        gq = hn_t[:, 0:96]
        gk = hn_t[:, 96:192]
        sh = {}

        def block(tb):
            tsl = slice(tb * 128, (tb + 1) * 128)
            pA1, b_pA1 = psA.next()
            pA2, b_pA2 = psA.next()
            for kc in range(8):
                S.op("pe", lambda e, kc=kc, pA1=pA1, tsl=tsl: e.matmul(out=pA1[:, 0:384], lhsT=hT[:, kc, tsl], rhs=wA[:, kc, 0:384],
                                                     start=(kc == 0), stop=(kc == 7)), reads=[b_hT, b_wA], writes=[b_pA1])
            for kc in range(8):
                S.op("pe", lambda e, kc=kc, pA2=pA2, tsl=tsl: e.matmul(out=pA2[:, 0:288], lhsT=hT[:, kc, tsl], rhs=wA[:, kc, 384:672],
                                                     start=(kc == 0), stop=(kc == 7)), reads=[b_hT, b_wA], writes=[b_pA2])
            jk, b_jk = junk.next()
            s8, b_s8 = stt.next()
            S.op("act", lambda e, jk=jk, pA1=pA1, s8=s8: e.activation(out=jk[:, 0:384], in_=pA1[:, 0:384], func=AF.Square,
                                               scale=1.0 / math.sqrt(384.0), accum_out=s8[:, 0:1]),
                 reads=[b_pA1], writes=[b_jk, b_s8])
            S.op("act", lambda e, jk=jk, pA2=pA2, s8=s8: e.activation(out=jk[:, 0:256], in_=pA2[:, 0:256], func=AF.Square,
                                               scale=1.0 / 16.0, accum_out=s8[:, 1:2]),
                 reads=[b_pA2], writes=[b_jk, b_s8])
            S.op("act", lambda e, s8=s8: e.activation(out=s8[:, 2:4], in_=s8[:, 0:2], func=AF.Sqrt, bias=EPS_AP[:, 0:1]),
                 reads=[b_s8, b_eps], writes=[b_s8])
            S.op("dve", lambda e, s8=s8: e.reciprocal(out=s8[:, 4:6], in_=s8[:, 2:4]), reads=[b_s8], writes=[b_s8])
            cn, b_cn = cnr.next()
            S.op("dve", lambda e, cn=cn, pA1=pA1, s8=s8: e.tensor_scalar(out=cn[:, 0:384], in0=pA1[:, 0:384], scalar1=s8[:, 4:5],
                                                  scalar2=None, op0=ALU.mult), reads=[b_pA1, b_s8], writes=[b_cn])
            S.op("dve", lambda e, cn=cn, pA2=pA2, s8=s8: e.tensor_scalar(out=cn[:, 384:640], in0=pA2[:, 0:256], scalar1=s8[:, 5:6],
                                                  scalar2=None, op0=ALU.mult), reads=[b_pA2, b_s8], writes=[b_cn])
            kpe, b_kpe = kper.next()
            S.op("act", lambda e, kpe=kpe, pA2=pA2: e.copy(out=kpe[:, 0, :], in_=pA2[:, 256:288]), reads=[b_pA2], writes=[b_kpe])
            yield
            ptr, b_ptr = psT.next()
            pv = ptr[:, :].bitcast(BF16)
            for kc in range(5):
                S.op("pe", lambda e, kc=kc, pv=pv, cn=cn: e.transpose(out=pv[:, kc * 128:(kc + 1) * 128],
                                                         in_=cn[:, kc * 128:(kc + 1) * 128], identity=identb[:]),
                     reads=[b_cn, b_identb], writes=[b_ptr])
            cT, b_cT = cTr.next()
            S.op("dve", lambda e, cT=cT, pv=pv: e.tensor_copy(out=cT[:], in_=pv[:, 0:640].rearrange("p (k t) -> p k t", k=5)),
                 reads=[b_ptr], writes=[b_cT])
            yield
            qsb, b_qsb = qsbr.next()
            kvsb, b_kvsb = kvsbr.next()
            for nb in range(3):
                pq, b_pq = psQ.next()
                for kc in range(3):
                    S.op("pe", lambda e, kc=kc, nb=nb, pq=pq, cT=cT: e.matmul(out=pq[:], lhsT=cT[:, kc, :],
                                                                 rhs=wq[:, kc, nb * 512:(nb + 1) * 512],
                                                                 start=(kc == 0), stop=(kc == 2)),
                         reads=[b_cT, b_wq], writes=[b_pq])
                S.op("act", lambda e, nb=nb, pq=pq, qsb=qsb: e.copy(out=qsb[:, nb * 512:(nb + 1) * 512], in_=pq[:]),
                     reads=[b_pq], writes=[b_qsb])
            for nb in range(4):
                pq, b_pq = psQ.next()
                for kc in range(2):
                    S.op("pe", lambda e, kc=kc, nb=nb, pq=pq, cT=cT: e.matmul(out=pq[:], lhsT=cT[:, 3 + kc, :],
                                                                 rhs=wkv[:, kc, nb * 512:(nb + 1) * 512],
                                                                 start=(kc == 0), stop=(kc == 1)),
                         reads=[b_cT, b_wkv], writes=[b_pq])
                eng = "act" if nb % 2 == 0 else "dve"
                if eng == "act":
                    S.op("act", lambda e, nb=nb, pq=pq, kvsb=kvsb: e.copy(out=kvsb[:, nb * 512:(nb + 1) * 512], in_=pq[:]),
                         reads=[b_pq], writes=[b_kvsb])
                else:
                    S.op("dve", lambda e, nb=nb, pq=pq, kvsb=kvsb: e.tensor_copy(out=kvsb[:, nb * 512:(nb + 1) * 512], in_=pq[:]),
                         reads=[b_pq], writes=[b_kvsb])
            q3 = qsb[:, :].rearrange("p (h d) -> p h d", h=16)
            kv3 = kvsb[:, :].rearrange("p (h d) -> p h d", h=16)
            ss, b_ss = sst.next()
            S.op("act", lambda e, jk=jk, qsb=qsb: e.activation(out=jk[:, :], in_=qsb[:, :], func=AF.Square),
                 reads=[b_qsb], writes=[b_jk])
            S.op("dve", lambda e, jk=jk, ss=ss: e.tensor_reduce(out=ss[:, 0:16], in_=jk[:, :].rearrange("p (h d) -> p h d", h=16),
                                                  axis=AX.X, op=ALU.add), reads=[b_jk], writes=[b_ss])
            tk, b_tk = tmpkr.next()
            S.op("act", lambda e, tk=tk, kv3=kv3: e.activation(out=tk[:], in_=kv3[:, :, 0:64], func=AF.Square),
                 reads=[b_kvsb], writes=[b_tk])
            S.op("dve", lambda e, tk=tk, ss=ss: e.tensor_reduce(out=ss[:, 16:32], in_=tk[:], axis=AX.X, op=ALU.add),
                 reads=[b_tk], writes=[b_ss])
            kpg, b_kpg = kpgr.next()
            S.op("act", lambda e, kpg=kpg, kpe=kpe, ss=ss: e.activation(out=kpg[:, 0, :], in_=kpe[:, 0, :], func=AF.Square,
                                               accum_out=ss[:, 96:97]), reads=[b_kpe], writes=[b_kpg, b_ss])
            S.op("dve", lambda e, ss=ss: e.tensor_scalar(out=ss[:, 16:32], in0=ss[:, 16:32], scalar1=ss[:, 96:97], scalar2=None,
                                                  op0=ALU.add), reads=[b_ss], writes=[b_ss])
            S.op("act", lambda e, ss=ss: e.activation(out=ss[:, 32:64], in_=ss[:, 0:32], func=AF.Sqrt, bias=EPS_AP[:, 0:1],
                                               scale=1.0 / 96.0), reads=[b_ss, b_eps], writes=[b_ss])
            S.op("dve", lambda e, ss=ss: e.reciprocal(out=ss[:, 64:96], in_=ss[:, 32:64]), reads=[b_ss], writes=[b_ss])
            rsq = ss[:, 64:80]
            rsk = ss[:, 80:96]
            S.op("dve", lambda e, q3=q3, rsq=rsq: e.tensor_tensor(out=q3, in0=q3, in1=rsq.unsqueeze(2).to_broadcast([128, 16, 96]),
                                                  op=ALU.mult), reads=[b_qsb, b_ss], writes=[b_qsb])
            S.op("dve", lambda e, q3=q3: e.tensor_tensor(out=q3, in0=q3, in1=gq.unsqueeze(1).to_broadcast([128, 16, 96]),
                                                   op=ALU.mult), reads=[b_qsb, b_hn], writes=[b_qsb])
            qb, b_qb = qbr.next()
            S.op("act", lambda e, qb=qb, q3=q3: e.copy(out=qb[:, :, 0:64], in_=q3[:, :, 0:64]), reads=[b_qsb], writes=[b_qb])
            for fn, rd, wr in rope(q3[:, :, 64:96], 16, tb, qb[:, :, 64:96], ropet):
                S.op("dve", fn, reads=[b_qsb] + rd, writes=wr + ([b_qb] if not wr else []))
            S.op("dve", lambda e, tk=tk, kv3=kv3, rsk=rsk: e.tensor_tensor(out=tk[:], in0=kv3[:, :, 0:64],
                                                  in1=rsk.unsqueeze(2).to_broadcast([128, 16, 64]), op=ALU.mult),
                 reads=[b_kvsb, b_ss], writes=[b_tk])
            kb, b_kb = kbr.next()
            S.op("dve", lambda e, tk=tk, kb=kb: e.tensor_tensor(out=kb[:, :, 0:64], in0=tk[:],
                                                   in1=gk[:, 0:64].unsqueeze(1).to_broadcast([128, 16, 64]), op=ALU.mult),
                 reads=[b_tk, b_hn], writes=[b_kb])
            S.op("dve", lambda e, kpg=kpg, kpe=kpe: e.tensor_tensor(out=kpg[:, 0, :], in0=kpe[:, 0, :], in1=gk[:, 64:96], op=ALU.mult),
                 reads=[b_kpe, b_hn], writes=[b_kpg])
            kr, b_kr = krr.next()
            for fn, rd, wr in rope(kpg[:, :, :], 1, tb, kr[:, :, :], ropet):
                S.op("dve", fn, reads=[b_kpg] + rd, writes=wr + ([b_kr] if not wr else []))
            S.op("dve", lambda e, kb=kb, kr=kr, rsk=rsk: e.tensor_tensor(out=kb[:, :, 64:96],
                                                  in0=kr[:, 0, :].unsqueeze(1).to_broadcast([128, 16, 32]),
                                                  in1=rsk.unsqueeze(2).to_broadcast([128, 16, 32]), op=ALU.mult),
                 reads=[b_kr, b_ss], writes=[b_kb])
            if tb % 4 == 0:
                sh['vb'] = vbr.next()
            vb, b_vb = sh['vb']
            S.op("pool", lambda e, vb=vb, kv3=kv3, tb=tb: e.tensor_copy(out=vb[:, :, tb % 4, 0:64], in_=kv3[:, :, 64:128]),
                 reads=[b_kvsb], writes=[b_vb])
            yield
            if tb % 2 == 0:
                sh['qT'] = qTr.next()
                sh['kT'] = kTr.next()
            qTs, b_qTs = sh['qT']
            kTs, b_kTs = sh['kT']
            for (src, b_src, dstT, b_dstT) in ((qb, b_qb, qTs, b_qTs), (kb, b_kb, kTs, b_kTs)):
                for half in range(2):
                    ptr, b_ptr = psT.next()
                    pv = ptr[:, :].bitcast(BF16)
                    for hh in range(8):
                        S.op("pe", lambda e, hh=hh, pv=pv, src=src, half=half: e.transpose(
                            out=pv[0:96, hh * 128:(hh + 1) * 128], in_=src[:, half * 8 + hh, :], identity=identb[:]),
                            reads=[b_src, b_identb], writes=[b_ptr])
                    off = (tb % 2) * 128
                    eng = "act" if half == 0 else "dve"
                    if eng == "act":
                        S.op("act", lambda e, pv=pv, dstT=dstT, half=half, off=off: e.copy(
                            out=dstT[0:96, half * 8:(half + 1) * 8, off:off + 128],
                            in_=pv[0:96, :].rearrange("p (h t) -> p h t", h=8)), reads=[b_ptr], writes=[b_dstT])
                    else:
                        S.op("dve", lambda e, pv=pv, dstT=dstT, half=half, off=off: e.tensor_copy(
                            out=dstT[0:96, half * 8:(half + 1) * 8, off:off + 128],
                            in_=pv[0:96, :].rearrange("p (h t) -> p h t", h=8)), reads=[b_ptr], writes=[b_dstT])
            if tb % 2 == 1:
                t0 = (tb - 1) * 128
                S.dma(lambda e, qTs=qTs, t0=t0: e.dma_start(out=QT[:, :, t0:t0 + 256].rearrange("h p t -> p h t"),
                                                             in_=qTs[0:96, :, :]),
                      reads=[b_qTs], writes=[b_QT], sem_buf=b_qTs, eng="pool")
                S.dma(lambda e, kTs=kTs, t0=t0: e.dma_start(out=KT[:, :, t0:t0 + 256].rearrange("h p t -> p h t"),
                                                             in_=kTs[0:96, :, :]),
                      reads=[b_kTs], writes=[b_KT], sem_buf=b_kTs, eng="pool")
            if tb % 4 == 3:
                c0 = tb - 3
                S.dma(lambda e, vb=vb, c0=c0: e.dma_start(out=Vs[:, :, c0:c0 + 4, :].rearrange("h p t c -> p h (t c)"),
                                                           in_=vb[:, :, :, :].rearrange("p h t c -> p h (t c)")),
                      reads=[b_vb], writes=[b_Vs], sem_buf=b_vb, eng="pool")
        gens = {}
        for i in range(32 + 2):
            if i < 32:
                gens[i] = block(i)
                next(gens[i])
            if 0 <= i - 1 < 32:
                next(gens[i - 1])
            if 0 <= i - 2 < 32:
                next(gens[i - 2], None)
            if 0 <= i - 1 < 32:
                next(gens[i - 1])
        S.barrier()
    if stop_after <= 2:
        S.emit(nc)
        return nc

    with ExitStack() as st:
        wzr = Ring(nc, st, un("z_w"), [128, 8, 512], BF16, 4)
        wdr = Ring(nc, st, un("dt_w"), [128, 8, 64], BF16, 1)
        with ExitStack() as st2:
            stgz = Ring(nc, st2, un("z_stg"), [128, 8, 512], F32, 2)
            stgd = Ring(nc, st2, un("dt_stg"), [128, 8, 64], F32, 1)
            wz = [wload(stgz, wzr, w_in[:, C_Z + cb * 512:C_Z + (cb + 1) * 512], 8, 512, gmix) for cb in range(4)]
            wdt, b_wdt = wload(stgd, wdr, w_in[:, C_DTF:C_DTF + 64], 8, 64, gmix)
            S.barrier()
        psz = Ring(nc, st, un("z_ps"), [128, 512], F32, 4, psum=True)
        psd = Ring(nc, st, un("dt_ps"), [128, 512], F32, 2, psum=True)
        zst = Ring(nc, st, un("z_st"), [128, 2048], BF16, 3)
        for tb in range(32):
            tsl = slice(tb * 128, (tb + 1) * 128)
            zt, b_zt = zst.next()
            for cb in range(4):
                pz, b_pz = psz.next()
                w_, b_w = wz[cb]
                for kc in range(8):
                    S.op("pe", lambda e, kc=kc, pz=pz, w_=w_, tsl=tsl: e.matmul(out=pz[:], lhsT=hT[:, kc, tsl], rhs=w_[:, kc, :],
                                                                  start=(kc == 0), stop=(kc == 7)),
                         reads=[b_hT, b_w], writes=[b_pz])
                S.op("act", lambda e, pz=pz, zt=zt, cb=cb: e.activation(out=zt[:, cb * 512:(cb + 1) * 512], in_=pz[:], func=AF.Silu),
                     reads=[b_pz], writes=[b_zt])
            pd, b_pd = psd.next()
            for kc in range(8):
                S.op("pe", lambda e, kc=kc, pd=pd, tsl=tsl: e.matmul(out=pd[:, 0:64], lhsT=hT[:, kc, tsl], rhs=wdt[:, kc, :],
                                                       start=(kc == 0), stop=(kc == 7)), reads=[b_hT, b_wdt], writes=[b_pd])
            S.op("dve", lambda e, pd=pd, tb=tb: e.tensor_copy(out=dtraw[:, tb, :], in_=pd[:, 0:64]), reads=[b_pd], writes=[b_dtraw])
            S.dma(lambda e, zt=zt, tsl=tsl: e.dma_start(out=zs[tsl, :], in_=zt[:]), reads=[b_zt], writes=[b_zs],
                  sem_buf=b_zt, eng="pool")
        S.barrier()

    with ExitStack() as st:
        stg = Ring(nc, st, un("x_stg"), [128, 8, 128], F32, 3)
        wr = Ring(nc, st, un("x_w"), [128, 8, 128], BF16, 3)
        psx = Ring(nc, st, un("x_ps"), [128, 512], F32, 3, psum=True)
        psc = Ring(nc, st, un("x_pc"), [128, 512], F32, 3, psum=True)
        xpre = Ring(nc, st, un("x_pre"), [128, S_ + 4], BF16, 2)
        dgr = Ring(nc, st, un("x_dg"), [128, 5, 128], BF16, 2)
        xcst = Ring(nc, st, un("x_cst"), [128, S_], BF16, 3)
        for (t_, b_) in xpre.items:
            S.op("pool", lambda e, t_=t_: e.memset(t_[:], 0.0), writes=[b_])

        def loadx(j):
            return wload(stg, wr, w_in[:, C_XBC + j * 128:C_XBC + (j + 1) * 128], 8, 128, gmix)

        def compx(j, h):
            w_, b_w = h
            xp, b_xp = xpre.next()
            dg, b_dg = dgr.next()
            for k in range(5):
                S.op("dve", lambda e, k=k, dg=dg, j=j: e.tensor_scalar(out=dg[:, k, :], in0=identf, scalar1=convw_t[:, j, k:k + 1],
                                                           scalar2=None, op0=ALU.mult),
                     reads=[b_mats, b_convw], writes=[b_dg])
            for tb in range(8):
                px, b_px = psx.next()
                ts = slice(tb * 512, (tb + 1) * 512)
                for kc in range(8):
                    S.op("pe", lambda e, kc=kc, px=px, w_=w_, ts=ts: e.matmul(out=px[:], lhsT=w_[:, kc, :], rhs=hT[:, kc, ts],
                                                                start=(kc == 0), stop=(kc == 7)),
                         reads=[b_w, b_hT], writes=[b_px])
                if tb % 2 == 0:
                    S.op("act", lambda e, px=px, xp=xp, tb=tb: e.copy(out=xp[:, 2 + tb * 512:2 + (tb + 1) * 512], in_=px[:]),
                         reads=[b_px], writes=[b_xp])
                else:
                    S.op("dve", lambda e, px=px, xp=xp, tb=tb: e.tensor_copy(out=xp[:, 2 + tb * 512:2 + (tb + 1) * 512], in_=px[:]),
                         reads=[b_px], writes=[b_xp])
            xc_, b_xc = xcst.next()
            for tb in range(8):
                pc, b_pc = psc.next()
                for k in range(5):
                    S.op("pe", lambda e, k=k, pc=pc, dg=dg, xp=xp, tb=tb: e.matmul(
                        out=pc[:], lhsT=dg[:, k, :], rhs=xp[:, tb * 512 + k:tb * 512 + k + 512],
                        start=(k == 0), stop=(k == 4)), reads=[b_dg, b_xp], writes=[b_pc])
                S.op("act", lambda e, pc=pc, xc_=xc_, tb=tb, j=j: e.activation(out=xc_[:, tb * 512:(tb + 1) * 512], in_=pc[:],
                                                               func=AF.Silu, bias=convb_t[:, j:j + 1]),
                     reads=[b_pc, b_convb], writes=[b_xc])
            for q4 in range(4):
                S.dma(lambda e, xc_=xc_, j=j, q4=q4: e.dma_start(
                    out=xcs[q4 * 8:(q4 + 1) * 8, :, j, :].rearrange("c p t -> p c t"),
                    in_=xc_[:, q4 * 1024:(q4 + 1) * 1024].rearrange("p (c t) -> p c t", c=8)),
                    reads=[b_xc], writes=[b_xcs], sem_buf=b_xc, eng="sp")

        pipeline(24, loadx, compx, 2)

        def loadg(j):
            return wload(stg, wr, w_in[:, C_GA + j * 128:C_GA + (j + 1) * 128], 8, 128, gmix)

        def compg(j, h):
            w_, b_w = h
            gt_, b_gt = xcst.next()
            for tb in range(8):
                px, b_px = psx.next()
                ts = slice(tb * 512, (tb + 1) * 512)
                for kc in range(8):
                    S.op("pe", lambda e, kc=kc, px=px, w_=w_, ts=ts: e.matmul(out=px[:], lhsT=w_[:, kc, :], rhs=hT[:, kc, ts],
                                                                start=(kc == 0), stop=(kc == 7)),
                         reads=[b_w, b_hT], writes=[b_px])
                S.op("act", lambda e, px=px, gt_=gt_, ts=ts: e.activation(out=gt_[:, ts], in_=px[:], func=AF.Sigmoid),
                     reads=[b_px], writes=[b_gt])
            S.dma(lambda e, gt_=gt_, j=j: e.dma_start(out=gts[j * 128:(j + 1) * 128, :], in_=gt_[:]),
                  reads=[b_gt], writes=[b_gts], sem_buf=b_gt, eng="pool")

        pipeline(16, loadg, compg, 2)
        S.barrier()
    if stop_after <= 3:
        S.emit(nc)
        return nc
    hst.close()

    def ssd_pass(fwd):
        with ExitStack() as st:
            AT = lambda n, shp, dt=F32: st.enter_context(nc.sbuf_tensor(un(n), shp, dt))
            dt_all = AT("dt_all", [128, 32, 32]); b_dt = Buf()
            da_all = AT("da_all", [128, 32, 32]); b_da = Buf()
            P_all = AT("P_all", [128, 32, 32]); b_P = Buf()
            bias_all = AT("bias_all", [128, 32, 32]); b_bias = Buf()
            wgt = AT("wgt", [128, 32, 32]); b_wgt = Buf()
            scl = AT("scl", [128, 32, 32]); b_scl = Buf()
            cdc = AT("cdc", [128, 32, 32]); b_cdc = Buf()
            tot = AT("tot", [128, 32, 32]); b_tot = Buf()
            nega = AT("nega", [128, 32]); b_nega = Buf()
            tmpa = AT("tmpa", [128, 32, 32]); b_tmpa = Buf()
            Sf = AT("Sf", [128, 2048]); b_Sf = [Buf() for _ in range(4)]
            Sbf = AT("Sbf", [128, 2048], BF16); b_Sbf = [Buf() for _ in range(4)]
            off = 0 if fwd else 32
            alog = ssp_t[:, off:off + 32]
            dtb = ssp_t[:, 64 + off:96 + off]
            dsk = ssp_t[:, 128:160]
            Uc = Umat if fwd else Ustr
            midx = 0 if fwd else 1
            ptA = Ring(nc, st, un("s_ptA"), [128, 512], F32, 1, psum=True)
            segb = Ring(nc, st, un("s_seg"), [128, 512], F32, 3, psum=True)
            pyr = Ring(nc, st, un("s_py"), [128, 512], F32, 2, psum=True)
            por = Ring(nc, st, un("s_po"), [128, 512], F32, 1, psum=True)
            pstr = Ring(nc, st, un("s_pst"), [128, 512], F32, 1, psum=True)
            segs = []
            for (t_, _b) in segb.items:
                for q in range(4):
                    segs.append((t_[:, q * 128:(q + 1) * 128], Buf()))
            segi = [0]
            flat = lambda t_: t_[:, :, :].rearrange("p a b -> p (a b)")
            S.op("pool", lambda e: e.memset(Sf[:], 0.0), writes=b_Sf)
            S.op("pool", lambda e: e.memset(Sbf[:], 0.0), writes=b_Sbf)
            S.op("act", lambda e: e.activation(out=nega[:], in_=alog, func=AF.Exp), reads=[b_ssp], writes=[b_nega])
            S.op("dve", lambda e: e.tensor_scalar(out=nega[:], in0=nega[:], scalar1=-1.0, scalar2=None, op0=ALU.mult),
                 reads=[b_nega], writes=[b_nega])
            S.op("dve", lambda e: e.tensor_tensor(out=tmpa[:], in0=dtraw[:, :, off:off + 32],
                                                  in1=dtb.unsqueeze(1).to_broadcast([128, 32, 32]), op=ALU.add),
                 reads=[b_dtraw, b_ssp], writes=[b_tmpa])
            S.op("act", lambda e: e.activation(out=tmpa[:], in_=tmpa[:], func=AF.Exp), reads=[b_tmpa], writes=[b_tmpa])
            S.op("act", lambda e: e.activation(out=dt_all[:], in_=tmpa[:], func=AF.Ln, bias=1.0), reads=[b_tmpa], writes=[b_dt])
            S.op("dve", lambda e: e.tensor_tensor(out=da_all[:], in0=dt_all[:], in1=nega[:].unsqueeze(1).to_broadcast([128, 32, 32]),
                                                  op=ALU.mult), reads=[b_dt, b_nega], writes=[b_da])
            for half in range(2):
                pp, b_pp = ptA.next()
                S.op("pe", lambda e, pp=pp, half=half: e.matmul(out=pp[:], lhsT=Uc, rhs=flat(da_all)[:, half * 512:(half + 1) * 512],
                                                                start=True, stop=True), reads=[b_mats, b_da], writes=[b_pp])
                S.op("dve", lambda e, pp=pp, half=half: e.tensor_copy(out=flat(P_all)[:, half * 512:(half + 1) * 512], in_=pp[:]),
                     reads=[b_pp], writes=[b_P])
            for half in range(2):
                pp, b_pp = ptA.next()
                S.op("pe", lambda e, pp=pp, half=half: e.matmul(out=pp[:], lhsT=onesf, rhs=flat(da_all)[:, half * 512:(half + 1) * 512],
                                                                start=True, stop=True), reads=[b_mats, b_da], writes=[b_pp])
                S.op("dve", lambda e, pp=pp, half=half: e.tensor_copy(out=flat(tot)[:, half * 512:(half + 1) * 512], in_=pp[:]),
                     reads=[b_pp], writes=[b_tot])
            S.op("dve", lambda e: e.tensor_tensor(out=tmpa[:], in0=tot[:], in1=P_all[:], op=ALU.subtract),
                 reads=[b_tot, b_P, b_dt], writes=[b_tmpa])
            e1, b_e1 = (scl, b_scl) if fwd else (wgt, b_wgt)
            e2, b_e2 = (wgt, b_wgt) if fwd else (scl, b_scl)
            S.op("act", lambda e: e.activation(out=e1[:], in_=P_all[:], func=AF.Exp), reads=[b_P], writes=[b_e1])
            S.op("act", lambda e: e.activation(out=e2[:], in_=tmpa[:], func=AF.Exp), reads=[b_tmpa], writes=[b_e2])
            S.op("act", lambda e: e.activation(out=cdc[:], in_=tot[:], func=AF.Exp), reads=[b_tot], writes=[b_cdc])
            S.op("dve", lambda e: e.tensor_scalar(out=bias_all[:], in0=P_all[:], scalar1=(-1.0 if fwd else 1.0), scalar2=None,
                                                  op0=ALU.mult), reads=[b_P], writes=[b_bias])

            xcr = Ring(nc, st, un("s_xc"), [128, 24, 128], BF16, 3)
            xsr = Ring(nc, st, un("s_xs"), [128, 2048], BF16, 2)
            Btr = Ring(nc, st, un("s_Bt"), [128, 512], BF16, 2)
            cbr = Ring(nc, st, un("s_cb"), [128, 512], F32, 2)
            xdtr = Ring(nc, st, un("s_xdt"), [128, 2048], BF16, 2)
            xwr = Ring(nc, st, un("s_xw"), [128, 2048], BF16, 2)
            decr = Ring(nc, st, un("s_dec"), [128, 128], F32, 12)
            MTr = Ring(nc, st, un("s_MT"), [128, 128], BF16, 12)
            yaccr = Ring(nc, st, un("s_ya"), [128, 2048], F32, 2)
            tmpr = Ring(nc, st, un("s_tmp"), [128, 512], F32, 2)
            if fwd:
                dskr = Ring(nc, st, un("s_dsk"), [128, 2048], F32, 1)
            else:
                zr = Ring(nc, st, un("s_z"), [128, 2048], BF16, 3)
                yfr = Ring(nc, st, un("s_yf"), [128, 2048], F32, 3)
                jkr = Ring(nc, st, un("s_jk"), [128, 512], BF16, 1)
                st4r = Ring(nc, st, un("s_st4"), [128, 12], F32, 2)
                mbr = Ring(nc, st, un("s_mb"), [128, 2048], BF16, 2)
                mstr = Ring(nc, st, un("s_mst"), [128, 16, 128], BF16, 2)
            order = list(range(32)) if fwd else list(range(31, -1, -1))

            def load(ci):
                c = order[ci]
                xc_, b_xc = xcr.next()
                S.dma(lambda e: e.dma_start(out=xc_[:], in_=xcs[c, :, :, :]), reads=[b_xcs], writes=[b_xc], sem_buf=b_xc)
                if fwd:
                    return (xc_, b_xc)
                z_, b_z = zr.next()
                yf_, b_yf = yfr.next()
                S.dma(lambda e: e.dma_start(out=z_[:], in_=zs[c * 128:(c + 1) * 128, :]), reads=[b_zs], writes=[b_z], sem_buf=b_z)
                S.dma(lambda e: e.dma_start(out=yf_[:], in_=yfs[c * 128:(c + 1) * 128, :]), reads=[b_yfs], writes=[b_yf], sem_buf=b_yf)
                return (xc_, b_xc, z_, b_z, yf_, b_yf)

            def prologue(ci, h):
                c = order[ci]
                xc_, b_xc = h[0], h[1]
                xs, b_xs = xsr.next()
                for half in range(2):
                    pt, b_pt = ptA.next()
                    pv = pt[:, :].bitcast(BF16)
                    for jj in range(8):
                        S.op("pe", lambda e, pv=pv, jj=jj, half=half: e.transpose(out=pv[:, jj * 128:(jj + 1) * 128],
                                                                                 in_=xc_[:, half * 8 + jj, :], identity=identb[:]),
                             reads=[b_xc, b_identb], writes=[b_pt])
                    if half == 0:
                        S.op("act", lambda e, pv=pv: e.copy(out=xs[:, 0:1024], in_=pv[:, :]), reads=[b_pt], writes=[b_xs])
                    else:
                        S.op("dve", lambda e, pv=pv: e.tensor_copy(out=xs[:, 1024:2048], in_=pv[:, :]), reads=[b_pt], writes=[b_xs])
                pt, b_pt = ptA.next()
                pvb = pt[:, :].bitcast(BF16)
                for g in range(4):
                    S.op("pe", lambda e, g=g: e.transpose(out=pvb[:, g * 128:(g + 1) * 128], in_=xc_[:, 16 + g, :], identity=identb[:]),
                         reads=[b_xc, b_identb], writes=[b_pt])
                Bt, b_Bt = Btr.next()
                S.op("dve", lambda e: e.tensor_copy(out=Bt[:], in_=pvb[:, 0:512]), reads=[b_pt], writes=[b_Bt])
                pcb, b_pcb = ptA.next()
                for g in range(4):
                    S.op("pe", lambda e, g=g: e.matmul(out=pcb[:, g * 128:(g + 1) * 128], lhsT=xc_[:, 16 + g, :], rhs=xc_[:, 20 + g, :],
                                                       start=True, stop=True), reads=[b_xc], writes=[b_pcb])
                cbT, b_cbT = cbr.next()
                S.op("act", lambda e: e.copy(out=cbT[:], in_=pcb[:]), reads=[b_pcb], writes=[b_cbT])
                xdt, b_xdt = xdtr.next()
                xw, b_xw = xwr.next()
                v3 = lambda t_: t_[:, :].rearrange("p (h d) -> p h d", h=32)
                S.op("dve", lambda e: e.tensor_tensor(out=v3(xdt), in0=v3(xs), in1=dt_all[:, c, :].unsqueeze(2).to_broadcast([128, 32, 64]),
                                                      op=ALU.mult), reads=[b_xs, b_dt], writes=[b_xdt])
                S.op("pool", lambda e: e.tensor_tensor(out=v3(xw), in0=v3(xdt), in1=wgt[:, c, :].unsqueeze(2).to_broadcast([128, 32, 64]),
                                                       op=ALU.mult), reads=[b_xdt, b_wgt], writes=[b_xw])
                return (xs, b_xs, Bt, b_Bt, cbT, b_cbT, xdt, b_xdt, xw, b_xw)

            def comp(ci, h, pr):
                c = order[ci]
                xc_, b_xc = h[0], h[1]
                xs, b_xs, Bt, b_Bt, cbT, b_cbT, xdt, b_xdt, xw, b_xw = pr
                v3 = lambda t_: t_[:, :].rearrange("p (h d) -> p h d", h=32)
                g8 = lambda t_: t_.rearrange("p (h d) -> p h d", h=8)
                ya, b_ya = yaccr.next()
                LAGH = 4
                mts = {}
                cur = {}

                def stageA(bi):
                    sb_, b_sb = segb.next()
                    for q in range(4):
                        h_ = bi * 4 + q
                        seg = sb_[:, q * 128:(q + 1) * 128]
                        S.op("pe", lambda e, seg=seg, h_=h_: e.matmul(out=seg, lhsT=da_all[:, c, h_:h_ + 1].to_broadcast([128, 128]),
                                                                      rhs=Uc, start=True, stop=False),
                             reads=[b_da, b_mats], writes=[b_sb])
                        S.op("pe", lambda e, seg=seg: e.matmul(out=seg, lhsT=identb[:], rhs=negm[:, midx, :], start=False, stop=True),
                             reads=[b_identb, b_negm], writes=[b_sb])
                    for q in range(4):
                        h_ = bi * 4 + q
                        g = h_ // 8
                        seg = sb_[:, q * 128:(q + 1) * 128]
                        dec, b_dec = decr.next()
                        S.op("act", lambda e, seg=seg, dec=dec, h_=h_: e.activation(out=dec[:], in_=seg, func=AF.Exp,
                                                                                   bias=bias_all[:, c, h_:h_ + 1],
                                                                                   scale=(1.0 if fwd else -1.0)),
                             reads=[b_sb, b_bias], writes=[b_dec])
                        MT, b_MT = MTr.next()
                        S.op("dve" if h_ % 2 == 0 else "pool", lambda e, dec=dec, MT=MT, g=g: e.tensor_tensor(
                            out=MT[:], in0=dec[:], in1=cbT[:, g * 128:(g + 1) * 128], op=ALU.mult),
                            reads=[b_dec, b_cbT], writes=[b_MT])
                        mts[h_] = (MT, b_MT)

                def stageB(h_):
                    g = h_ // 8
                    hh = h_ % 8
                    if hh == 0:
                        cur[0] = pyr.next()
                    py, b_py = cur[0]
                    MT, b_MT = mts.pop(h_)
                    S.op("pe", lambda e, MT=MT, py=py, hh=hh, h_=h_: e.matmul(out=py[:, hh * 64:(hh + 1) * 64], lhsT=MT[:],
                                                                             rhs=xdt[:, h_ * 64:(h_ + 1) * 64], start=True, stop=True),
                         reads=[b_MT, b_xdt], writes=[b_py])
                    if hh != 7:
                        return
                    po, b_po = por.next()
                    S.op("pe", lambda e, po=po, g=g: e.matmul(out=po[:], lhsT=xc_[:, 20 + g, :], rhs=Sbf[:, g * 512:(g + 1) * 512],
                                                              start=True, stop=True), reads=[b_xc, b_Sbf[g]], writes=[b_po])
                    pst, b_pst = pstr.next()
                    S.op("pe", lambda e, pst=pst, g=g: e.matmul(out=pst[:], lhsT=Bt[:, g * 128:(g + 1) * 128], rhs=xw[:, g * 512:(g + 1) * 512],
                                                                start=True, stop=True), reads=[b_Bt, b_xw], writes=[b_pst])
                    tmp, b_tmp = tmpr.next()
                    S.op("dve", lambda e, po=po, tmp=tmp, g=g: e.tensor_tensor(
                        out=g8(tmp[:, :]), in0=g8(po[:, :]), in1=scl[:, c, g * 8:(g + 1) * 8].unsqueeze(2).to_broadcast([128, 8, 64]),
                        op=ALU.mult), reads=[b_po, b_scl], writes=[b_tmp])
                    S.op("dve", lambda e, py=py, tmp=tmp, g=g: e.tensor_tensor(out=ya[:, g * 512:(g + 1) * 512], in0=py[:], in1=tmp[:],
                                                                              op=ALU.add), reads=[b_py, b_tmp], writes=[b_ya])
                    S.op("pool", lambda e, g=g: e.tensor_tensor(
                        out=g8(Sf[:, g * 512:(g + 1) * 512]), in0=g8(Sf[:, g * 512:(g + 1) * 512]),
                        in1=cdc[:, c, g * 8:(g + 1) * 8].unsqueeze(2).to_broadcast([128, 8, 64]), op=ALU.mult),
                        reads=[b_Sf[g], b_cdc], writes=[b_Sf[g]])
                    S.op("dve", lambda e, pst=pst, g=g: e.tensor_tensor(out=Sf[:, g * 512:(g + 1) * 512], in0=pst[:],
                                                                       in1=Sf[:, g * 512:(g + 1) * 512], op=ALU.add),
                         reads=[b_pst, b_Sf[g]], writes=[b_Sf[g]])
                    S.op("act", lambda e, g=g: e.copy(out=Sbf[:, g * 512:(g + 1) * 512], in_=Sf[:, g * 512:(g + 1) * 512]),
                         reads=[b_Sf[g]], writes=[b_Sbf[g]])

                for k in range(8 + 2):
                    if k < 8:
                        stageA(k)
                    if k >= 2:
                        for q in range(4):
                            stageB((k - 2) * 4 + q)
                if fwd:
                    dk, b_dk = dskr.next()
                    S.op("pool", lambda e: e.tensor_tensor(out=v3(dk), in0=v3(xs), in1=dsk.unsqueeze(2).to_broadcast([128, 32, 64]),
                                                           op=ALU.mult), reads=[b_xs, b_ssp], writes=[b_dk])
                    S.op("pool", lambda e: e.tensor_tensor(out=ya[:], in0=ya[:], in1=dk[:], op=ALU.add),
                         reads=[b_ya, b_dk], writes=[b_ya])
                    S.dma(lambda e: e.dma_start(out=yfs[c * 128:(c + 1) * 128, :], in_=ya[:]), reads=[b_ya], writes=[b_yfs],
                          sem_buf=b_ya, eng="pool")
                    return
                z_, b_z, yf_, b_yf = h[2], h[3], h[4], h[5]
                if dbg:
                    S.dma(lambda e: e.dma_start(out=ybs[c * 128:(c + 1) * 128, :], in_=ya[:]), reads=[b_ya], writes=[b_ybs],
                          sem_buf=b_ya, eng="pool")
                S.op("pool", lambda e: e.tensor_tensor(out=ya[:], in0=ya[:], in1=yf_[:], op=ALU.add), reads=[b_ya, b_yf], writes=[b_ya])
                S.op("pool", lambda e: e.tensor_tensor(out=ya[:], in0=ya[:], in1=z_[:], op=ALU.mult), reads=[b_ya, b_z], writes=[b_ya])

                def epi():
                    jk, b_jk = jkr.next()
                    s4, b_s4 = st4r.next()
                    for g in range(4):
                        S.op("act", lambda e, g=g: e.activation(out=jk[:], in_=ya[:, g * 512:(g + 1) * 512], func=AF.Square,
                                                                scale=1.0 / math.sqrt(512.0), accum_out=s4[:, g:g + 1]),
                             reads=[b_ya], writes=[b_jk, b_s4])
                    S.op("act", lambda e: e.activation(out=s4[:, 4:8], in_=s4[:, 0:4], func=AF.Sqrt, bias=EPS_AP[:, 0:1]),
                         reads=[b_s4, b_eps], writes=[b_s4])
                    S.op("dve", lambda e: e.reciprocal(out=s4[:, 8:12], in_=s4[:, 4:8]), reads=[b_s4], writes=[b_s4])
                    mb, b_mb = mbr.next()
                    for g in range(4):
                        S.op("dve", lambda e, g=g: e.tensor_scalar(out=mb[:, g * 512:(g + 1) * 512], in0=ya[:, g * 512:(g + 1) * 512],
                                                                   scalar1=s4[:, 8 + g:9 + g], scalar2=None, op0=ALU.mult),
                             reads=[b_ya, b_s4], writes=[b_mb])
                    mst, b_mst = mstr.next()
                    for half in range(2):
                        pt, b_pt = ptA.next()
                        pv = pt[:, :].bitcast(BF16)
                        for jj in range(8):
                            j = half * 8 + jj
                            S.op("pe", lambda e, pv=pv, jj=jj, j=j: e.transpose(out=pv[:, jj * 128:(jj + 1) * 128],
                                                                               in_=mb[:, j * 128:(j + 1) * 128], identity=identb[:]),
                                 reads=[b_mb, b_identb], writes=[b_pt])
                        S.op("act", lambda e, pv=pv, half=half: e.copy(out=mst[:, half * 8:(half + 1) * 8, :],
                                                                       in_=pv[:, :].rearrange("p (j t) -> p j t", j=8)),
                             reads=[b_pt], writes=[b_mst])
                    for half in range(2):
                        S.dma(lambda e, half=half: e.dma_start(
                            out=mTs[half * 1024:(half + 1) * 1024, c * 128:(c + 1) * 128].rearrange("(j p) t -> p j t", p=128),
                            in_=mst[:, half * 8:(half + 1) * 8, :]), reads=[b_mst], writes=[b_mTs], sem_buf=b_mst, eng="pool")

                if pend_epi:
                    pend_epi.pop()()
                pend_epi.append(epi)

            pend_epi = []
            hs = {}
            prs = {}
            for i in range(32 + 2):
                if i < 32:
                    hs[i] = load(i)
                if 1 <= i <= 32:
                    prs[i - 1] = prologue(i - 1, hs[i - 1])
                if i >= 2:
                    comp(i - 2, hs.pop(i - 2), prs.pop(i - 2))
            if pend_epi:
                pend_epi.pop()()
        S.barrier()

    ssd_pass(True)
    if stop_after <= 4 and stop_after == 4:
        pass
    ssd_pass(False)
    if stop_after <= 4:
        S.emit(nc)
        return nc

    mw = ExitStack()
    wpa = mw.enter_context(nc.sbuf_tensor("m_wpa", [128, 8, D_], BF16)); b_wpa = Buf()
    wpb = mw.enter_context(nc.sbuf_tensor("m_wpb", [128, 16, D_], BF16)); b_wpb = Buf()
    wo = mw.enter_context(nc.sbuf_tensor("m_wo", [128, 8, D_], BF16)); b_wo = Buf()
    mws = ExitStack()
    ms8 = mws.enter_context(nc.sbuf_tensor("m_s8", [128, 8, D_], F32)); b_ms8 = Buf()

    def preload_merge_weights():
        jobs = [(w_pa[:, :], wpa[:, :, :], b_wpa, None), (w_pb[0:1024, :], wpb[:, 0:8, :], b_wpb, gssm_t[:, 0:8]),
                (w_pb[1024:2048, :], wpb[:, 8:16, :], b_wpb, gssm_t[:, 8:16]), (w_o[:, :], wo[:, :, :], b_wo, None)]
        for src, dst, b_dst, g_ in jobs:
            S.dma(lambda e, src=src: e.dma_start(out=ms8[:], in_=src.rearrange("(kc p) n -> p kc n", p=128)),
                  writes=[b_ms8], sem_buf=b_ms8)
            if g_ is None:
                S.op("pool", lambda e, dst=dst: e.tensor_copy(out=dst, in_=ms8[:]), reads=[b_ms8], writes=[b_dst])
            else:
                S.op("pool", lambda e, dst=dst, g_=g_: e.tensor_tensor(out=dst, in0=ms8[:], in1=g_.unsqueeze(2).to_broadcast([128, 8, D_]),
                                                                      op=ALU.mult), reads=[b_ms8, b_gssm], writes=[b_dst])

    with ExitStack() as st:
        ktr = Ring(nc, st, un("t_k"), [128, S_], BF16, 2)
        qtr = Ring(nc, st, un("t_q"), [128, S_], BF16, 2)
        vtr = Ring(nc, st, un("t_v"), [128, 32, 65], BF16, 2)
        psS = Ring(nc, st, un("t_ps"), [128, 1024], F32, 3, psum=True)
        psO = Ring(nc, st, un("t_po"), [128, 1024], F32, 1, psum=True)
        pTr = Ring(nc, st, un("t_pT"), [128, 1024], BF16, 4)
        rdr = Ring(nc, st, un("t_rd"), [128, 1024], F32, 2)
        osr = Ring(nc, st, un("t_os"), [128, 1024], F32, 2)
        aor = Ring(nc, st, un("t_ao"), [128, S_], BF16, 2)
        sc = 1.0 / math.sqrt(96.0)
        LAG = 2
        tiles = {}

        def ensure(h_):
            if h_ >= NH or h_ in tiles:
                return
            kt, b_kt = ktr.next()
            qt, b_qt = qtr.next()
            vt, b_vt = vtr.next()
            S.dma(lambda e: e.dma_start(out=kt[0:96, :], in_=KT[h_, :, :]), reads=[b_KT], writes=[b_kt], sem_buf=b_kt)
            S.dma(lambda e: e.dma_start(out=qt[0:96, :], in_=QT[h_, :, :]), reads=[b_QT], writes=[b_qt], sem_buf=b_qt)
            S.dma(lambda e: e.dma_start(out=vt[:], in_=Vs[h_, :, :, :]), reads=[b_Vs], writes=[b_vt], sem_buf=b_vt)
            tiles[h_] = (kt, b_kt, qt, b_qt, vt, b_vt)

        steps = [(h_, sb, kc) for h_ in range(NH) for sb in range(4) for kc in range(32)]
        pend = {}
        cur_po = {}
        cur_ao = {}
        ensure(0)
        preload_merge_weights()
        for i in range(len(steps) + LAG):
            if i < len(steps):
                h_, sb, kc = steps[i]
                kt, b_kt, qt, b_qt, vt, b_vt = tiles[h_]
                ps, b_ps = psS.next()
                for u in range(2):
                    S.op("pe", lambda e, ps=ps, kc=kc, sb=sb, u=u, kt=kt, qt=qt: e.matmul(
                        out=ps[:, u * 512:(u + 1) * 512], lhsT=kt[0:96, kc * 128:(kc + 1) * 128],
                        rhs=qt[0:96, sb * 1024 + u * 512:sb * 1024 + (u + 1) * 512], start=True, stop=True),
                        reads=[b_kt, b_qt], writes=[b_ps])
                pT, b_pT = pTr.next()
                S.op("act", lambda e, ps=ps, pT=pT: e.activation(out=pT[:], in_=ps[:], func=AF.Exp, scale=sc),
                     reads=[b_ps], writes=[b_pT])
                pend[i] = (pT, b_pT)
            if i >= LAG:
                h_, sb, kc = steps[i - LAG]
                kt, b_kt, qt, b_qt, vt, b_vt = tiles[h_]
                pT, b_pT = pend.pop(i - LAG)
                if kc == 0:
                    cur_po[0] = psO.next()
                    if sb == 0:
                        cur_ao[0] = aor.next()
                        ensure(h_ + 1)
                po, b_po = cur_po[0]
                ao, b_ao = cur_ao[0]
                for u in range(2):
                    S.op("pe", lambda e, po=po, pT=pT, kc=kc, u=u, vt=vt: e.matmul(
                        out=po[0:65, u * 512:(u + 1) * 512], lhsT=vt[:, kc, :], rhs=pT[:, u * 512:(u + 1) * 512],
                        start=(kc == 0), stop=(kc == 31)), reads=[b_vt, b_pT], writes=[b_po])
                if kc == 31:
                    osb, b_osb = osr.next()
                    S.op("dve", lambda e, po=po, osb=osb: e.tensor_copy(out=osb[0:65, :], in_=po[0:65, :]), reads=[b_po], writes=[b_osb])
                    rd, b_rd = rdr.next()
                    S.op("dve", lambda e, osb=osb, rd=rd: e.reciprocal(out=rd[64:65, :], in_=osb[64:65, :]), reads=[b_osb], writes=[b_rd])
                    pb, b_pb = psS.next()
                    for u in range(2):
                        S.op("pe", lambda e, pb=pb, rd=rd, u=u: e.matmul(out=pb[0:64, u * 512:(u + 1) * 512], lhsT=mats[64:65, 2, 0:64],
                                                                         rhs=rd[64:65, u * 512:(u + 1) * 512], start=True, stop=True),
                             reads=[b_mats, b_rd], writes=[b_pb])
                    qs = slice(sb * 1024, (sb + 1) * 1024)
                    S.op("dve", lambda e, pb=pb, osb=osb, qs=qs, ao=ao: e.tensor_tensor(
                        out=ao[0:64, qs], in0=pb[0:64, :], in1=osb[0:64, :], op=ALU.mult),
                        reads=[b_pb, b_osb], writes=[b_ao])
                    if sb == 3:
                        S.dma(lambda e, h_=h_, ao=ao: e.dma_start(out=aTs[h_ * 64:(h_ + 1) * 64, :], in_=ao[0:64, :]),
                              reads=[b_ao], writes=[b_aTs], sem_buf=b_ao, eng="pool")
        S.barrier()
    if stop_after <= 5:
        S.emit(nc)
        return nc

    mws.close()
    with ExitStack() as st:
        atr = Ring(nc, st, un("m_at"), [128, 8, 512], BF16, 2)
        mtr = Ring(nc, st, un("m_mt"), [128, 16, 512], BF16, 2)
        gtr = Ring(nc, st, un("m_gt"), [128, 16, 512], BF16, 2)
        mgr = Ring(nc, st, un("m_mg"), [128, 8, 512], BF16, 2)
        t1r = Ring(nc, st, un("m_t1"), [128, 512], F32, 2)
        t2r = Ring(nc, st, un("m_t2"), [128, 512], F32, 2)
        xr = Ring(nc, st, un("m_x"), [128, D_], F32, 3)
        ps = Ring(nc, st, un("m_ps"), [128, 512], F32, 6, psum=True)
        def loadm(t):
            at, b_at = atr.next()
            mt, b_mt = mtr.next()
            gt_, b_gt = gtr.next()
            ts = slice(t * 512, (t + 1) * 512)
            S.dma(lambda e: e.dma_start(out=at[:], in_=aTs[:, ts].rearrange("(k p) t -> p k t", p=128)), reads=[b_aTs], writes=[b_at], sem_buf=b_at)
            S.dma(lambda e: e.dma_start(out=mt[:], in_=mTs[:, ts].rearrange("(k p) t -> p k t", p=128)), reads=[b_mTs], writes=[b_mt], sem_buf=b_mt)
            S.dma(lambda e: e.dma_start(out=gt_[:], in_=gts[:, ts].rearrange("(k p) t -> p k t", p=128)), reads=[b_gts], writes=[b_gt], sem_buf=b_gt)
            return (at, b_at, mt, b_mt, gt_, b_gt)

        def compm(t, hd):
            at, b_at, mt, b_mt, gt_, b_gt = hd
            mg, b_mg = mgr.next()
            for dc in range(8):
                pa, b_pa = ps.next()
                pb, b_pb = ps.next()
                for kc in range(8):
                    S.op("pe", lambda e, pa=pa, kc=kc, dc=dc: e.matmul(out=pa[:], lhsT=wpa[:, kc, dc * 128:(dc + 1) * 128], rhs=at[:, kc, :],
                                                                       start=(kc == 0), stop=(kc == 7)), reads=[b_wpa, b_at], writes=[b_pa])
                for kc in range(16):
                    S.op("pe", lambda e, pb=pb, kc=kc, dc=dc: e.matmul(out=pb[:], lhsT=wpb[:, kc, dc * 128:(dc + 1) * 128], rhs=mt[:, kc, :],
                                                                       start=(kc == 0), stop=(kc == 15)), reads=[b_wpb, b_mt], writes=[b_pb])
                t1, b_t1 = t1r.next()
                t2, b_t2 = t2r.next()
                S.op("dve", lambda e, pa=pa, t1=t1, dc=dc: e.tensor_tensor(out=t1[:], in0=pa[:], in1=gt_[:, dc, :], op=ALU.mult),
                     reads=[b_pa, b_gt], writes=[b_t1])
                S.op("dve", lambda e, pb=pb, t2=t2, dc=dc: e.tensor_tensor(out=t2[:], in0=pb[:], in1=gt_[:, 8 + dc, :], op=ALU.mult),
                     reads=[b_pb, b_gt], writes=[b_t2])
                S.op("pool", lambda e, t1=t1, t2=t2, dc=dc: e.tensor_tensor(out=mg[:, dc, :], in0=t1[:], in1=t2[:], op=ALU.add),
                     reads=[b_t1, b_t2], writes=[b_mg])
            for sb in range(4):
                tb = t * 4 + sb
                xt, b_xt = xr.next()
                S.dma(lambda e, xt=xt, tb=tb: e.dma_start(out=xt[:], in_=x1s[tb * 128:(tb + 1) * 128, :]),
                      reads=[b_x1s], writes=[b_xt], sem_buf=b_xt)
                for half in range(2):
                    p, b_p = ps.next()
                    for kc in range(8):
                        S.op("pe", lambda e, p=p, kc=kc, sb=sb, half=half: e.matmul(
                            out=p[:], lhsT=mg[:, kc, sb * 128:(sb + 1) * 128], rhs=wo[:, kc, half * 512:(half + 1) * 512],
                            start=(kc == 0), stop=(kc == 7)), reads=[b_mg, b_wo], writes=[b_p])
                    S.op("dve", lambda e, p=p, xt=xt, half=half: e.tensor_tensor(out=xt[:, half * 512:(half + 1) * 512], in0=p[:],
                                                                                in1=xt[:, half * 512:(half + 1) * 512], op=ALU.add),
                         reads=[b_p, b_xt], writes=[b_xt])
                S.dma(lambda e, xt=xt, tb=tb: e.dma_start(out=x2s[tb * 128:(tb + 1) * 128, :], in_=xt[:]),
                      reads=[b_xt], writes=[b_x2s], sem_buf=b_xt, eng="pool")

        pipeline(8, loadm, compm, 1)
        S.barrier()
    mw.close()
    hst2 = ExitStack()
    hT2 = hst2.enter_context(nc.sbuf_tensor("hT2", [128, 8, S_], BF16)); b_hT2 = Buf("hT2")
    norm_phase(x2s, b_x2s, hT2, b_hT2)

    with ExitStack() as fst:
        wd2 = fst.enter_context(nc.sbuf_tensor("wd2", [128, NFF, D_], BF16)); b_wd2 = Buf()
        ffn_gateup(w_g2, w_u2, 2, hT2, b_hT2, w_d2, wd2, b_wd2)
        ffn_down(w_d2, x2s, b_x2s, y_out, b_yout, None, wd2, b_wd2)
    S.emit(nc)
    return nc


def _fm(v, kc):
    return np.ascontiguousarray(np.asarray(v, np.float32).reshape(kc, 128).T)


_CACHE = {}


def consts():
    ii = np.arange(128)
    U = (ii[:, None] <= ii[None, :]).astype(np.float32)
    Us = (ii[:, None] < ii[None, :]).astype(np.float32)
    ones = np.ones((128, 128), np.float32)
    I = np.eye(128, dtype=np.float32)
    mats = np.ascontiguousarray(np.stack([U, Us, ones, I], axis=1))
    negf = np.where(ii[:, None] > ii[None, :], -30000.0, 0.0).astype(np.float32)
    posb = np.where(ii[:, None] < ii[None, :], 30000.0, 0.0).astype(np.float32)
    neg = np.ascontiguousarray(np.stack([negf, posb], axis=1)).astype(ml_dtypes.bfloat16)
    invf = (1.0 / (10000.0 ** (np.arange(0, 32, 2, dtype=np.float32) / 32.0))).astype(np.float32)[None, :]
    return dict(c_identb=I.astype(ml_dtypes.bfloat16), c_mats=mats, c_neg=neg, c_invf=invf)


def make_shared(inp):
    f = lambda k: np.asarray(inp[k], np.float32)[0]
    d = {}
    d["gfm"] = np.ascontiguousarray(np.concatenate([_fm(f("ffn1_norm"), 8), _fm(f("mix_norm"), 8), _fm(f("ffn2_norm"), 8)], axis=1))
    d["gqa"] = _fm(f("q_a_norm"), 3)
    d["gkva"] = _fm(f("kv_a_norm"), 2)
    d["gssm"] = _fm(f("ssm_norm"), 16)
    d["w_g1"] = f("ffn1_w_gate"); d["w_u1"] = f("ffn1_w_up"); d["w_d1"] = f("ffn1_w_down")
    d["w_g2"] = f("ffn2_w_gate"); d["w_u2"] = f("ffn2_w_up"); d["w_d2"] = f("ffn2_w_down")
    d["w_in"] = f("w_in"); d["w_qb"] = f("w_q_b"); d["w_kvb"] = f("w_kv_b")
    d["hn"] = np.concatenate([f("q_head_norm"), f("k_head_norm")])[None, :].astype(np.float32)
    cw = f("conv_w")[:, 0, :]
    d["convw"] = np.ascontiguousarray(cw.T.reshape(24, 128, 5).transpose(1, 0, 2))
    d["convb"] = _fm(f("conv_b"), 24)
    d["ssp"] = np.concatenate([f("a_log_fwd"), f("a_log_bwd"), f("dt_bias_fwd"), f("dt_bias_bwd"), f("d_skip")])[None, :].astype(np.float32)
    d["w_pa"] = f("w_attn_branch"); d["w_pb"] = f("w_ssm_branch"); d["w_o"] = f("w_out")
    d.update(consts())
    return d


def make_inmap(inp, shared, b):
    d = dict(shared)
    d["x"] = np.ascontiguousarray(np.asarray(inp["x"], np.float32)[b])
    p = np.asarray(inp["positions"], np.int32)[b]
    d["pos"] = np.ascontiguousarray(p.reshape(32, 128).T)
    return d


def kernel(**inputs):
    nb = int(np.asarray(inputs["x"]).shape[0])
    nc = build(dbg=False)
    shared = make_shared(inputs)
    in_maps = [make_inmap(inputs, shared, b) for b in range(nb)]
    res = run_bass_kernel_spmd(nc, in_maps, core_ids=list(range(nb)))
    out = np.stack([np.asarray(res.results[b]["y"], dtype=np.float32) for b in range(nb)], axis=0)
    return out
```

```python
import math
from contextlib import ExitStack
import numpy as np
import ml_dtypes
import concourse.bass as bass
import concourse.mybir as mybir
from concourse.bass_utils import run_bass_kernel_spmd

F32 = mybir.dt.float32
BF16 = mybir.dt.bfloat16
I32 = mybir.dt.int32
AF = mybir.ActivationFunctionType
ALU = mybir.AluOpType
AX = mybir.AxisListType

S_ = 4096
D_ = 1024
FF = 2816
NFF = 22
NH = 16
EPS = 1e-6
C_Q, C_KV, C_PE, C_Z, C_XBC, C_DTF, C_DTB, C_GA, C_GB = 0, 384, 640, 672, 2720, 5792, 5824, 5856, 6880
IN_DIM = 7904
ENGS = ("pe", "act", "dve", "pool", "sp")
FUSE_WAITS = True


class DSem:
    def __init__(self):
        self.count = 0
        self.handle = None


class Buf:
    __slots__ = ("name", "lw", "rd", "dsem", "ep")

    def __init__(self, name=""):
        self.name = name
        self.lw = None
        self.rd = []
        self.dsem = None
        self.ep = -1


class Op:
    __slots__ = ("eng", "fn", "idx", "waits", "dwaits", "inc", "dsem", "know", "seq", "multi")


class Sched:
    def __init__(self):
        self.ops = {e: [] for e in ENGS}
        self.know = {e: {} for e in ENGS}
        self.dsems = []
        self.free = []
        self.epoch = 0

    def _add(self, eng, fn, reads, writes, dsem=None, extra=(), extra_ds=()):
        op = Op()
        op.eng = eng
        op.fn = fn
        op.idx = len(self.ops[eng])
        op.waits = {}
        op.dwaits = {}
        op.inc = False
        op.dsem = dsem
        op.seq = None
        op.multi = False
        know = self.know[eng]
        deps = list(extra)
        for b in reads:
            if b.lw is not None:
                deps.append(b.lw)
        for b in writes:
            if b.lw is not None:
                deps.append(b.lw)
            deps.extend(b.rd)
        for a in deps:
            if a is op:
                continue
            if a.dsem is None:
                if a.eng == "pe" and eng == "pe":
                    continue
                if know.get(a.eng, -1) >= a.idx:
                    continue
                a.inc = True
                cur = op.waits.get(a.eng)
                if cur is None or cur.idx < a.idx:
                    op.waits[a.eng] = a
                for k, v in a.know.items():
                    if know.get(k, -1) < v:
                        know[k] = v
                know[a.eng] = max(know.get(a.eng, -1), a.idx)
            else:
                ds = a.dsem
                v = ds.count
                if know.get(ds, -1) >= v:
                    continue
                op.dwaits[ds] = v
                for k, vv in a.know.items():
                    if know.get(k, -1) < vv:
                        know[k] = vv
                know[ds] = v
        for ds in extra_ds:
            v = ds.count
            if know.get(ds, -1) < v:
                op.dwaits[ds] = v
                know[ds] = v
        if dsem is not None:
            dsem.count += 16
        op.know = dict(know)
        for b in reads:
            b.rd.append(op)
        for b in writes:
            b.lw = op
            b.rd = []
        self.ops[eng].append(op)
        return op

    def op(self, eng, fn, reads=(), writes=(), multi=False):
        o = self._add(eng, fn, reads, writes)
        o.multi = multi
        return o

    def dma(self, fn, reads=(), writes=(), sem_buf=None, eng="sp"):
        if sem_buf.dsem is None or sem_buf.ep != self.epoch:
            if self.free:
                sem_buf.dsem = self.free.pop()
            else:
                sem_buf.dsem = DSem()
                self.dsems.append(sem_buf.dsem)
            sem_buf.ep = self.epoch
        return self._add(eng, fn, reads, writes, dsem=sem_buf.dsem)

    def barrier(self):
        lasts = []
        for e in ENGS:
            if e == "sp":
                continue
            for o in reversed(self.ops[e]):
                if o.dsem is None:
                    lasts.append(o)
                    break
        spop = self._add("sp", lambda e: e.nop(), (), (), extra=lasts, extra_ds=list(self.dsems))
        self.epoch += 1
        self.free = list(self.dsems)
        for e in ENGS:
            if e == "sp":
                continue
            self._add(e, lambda eh: eh.nop(), (), (), extra=[spop])

    def emit(self, nc):
        with ExitStack() as st:
            esem = {e: st.enter_context(nc.semaphore("es_" + e)) for e in ENGS}
            for i, d in enumerate(self.dsems):
                d.handle = st.enter_context(nc.semaphore("ds%d" % i))
            for e in ENGS:
                c = 0
                for o in self.ops[e]:
                    if o.dsem is None and o.inc:
                        c += 1
                        o.seq = c
            block = st.enter_context(nc.Block())

            def run(e, eh):
                for o in self.ops[e]:
                    wl = [(esem[se], a.seq) for se, a in o.waits.items()] + [(ds.handle, v) for ds, v in o.dwaits.items()]
                    attach = None
                    if wl and o.dsem is None and not o.multi and e != "sp" and FUSE_WAITS:
                        attach = wl.pop()
                    for hh_, vv_ in wl:
                        eh.wait_ge(hh_, vv_)
                    n0 = nc.n_instructions()
                    ins = o.fn(eh)
                    if attach is not None:
                        if nc.n_instructions() - n0 != 1:
                            raise RuntimeError("multi-instruction op with fused wait on %s (%d)" % (e, nc.n_instructions() - n0))
                        ins._wait_ge(attach[0], attach[1])
                    if o.dsem is not None:
                        ins.then_inc(o.dsem.handle, 16)
                    elif o.inc:
                        ins.then_inc(esem[e], 1)
                if e == "sp":
                    for ds in self.dsems:
                        eh.wait_ge(ds.handle, ds.count)

            @block.tensor
            def _(eh):
                run("pe", eh)

            @block.scalar
            def _(eh):
                run("act", eh)

            @block.vector
            def _(eh):
                run("dve", eh)

            @block.gpsimd
            def _(eh):
                run("pool", eh)

            @block.sync
            def _(eh):
                run("sp", eh)


class Ring:
    def __init__(self, nc, st, name, shape, dtype, n, psum=False):
        self.items = []
        for i in range(n):
            if psum:
                t = st.enter_context(nc.psum_tensor("%s%d" % (name, i), shape, dtype))
            else:
                t = st.enter_context(nc.sbuf_tensor("%s%d" % (name, i), shape, dtype))
            self.items.append((t, Buf("%s%d" % (name, i))))
        self.i = 0

    def next(self):
        r = self.items[self.i % len(self.items)]
        self.i += 1
        return r


def pipeline(n, load_fn, compute_fn, depth):
    hs = {}
    for i in range(n + depth):
        if i < n:
            hs[i] = load_fn(i)
        if i >= depth:
            compute_fn(i - depth, hs.pop(i - depth))


class K:
    pass


def build(dbg=False, stop_after=99):
    nc = bass.Bass("TRN2", target_bir_lowering=False)
    S = Sched()
    uid = [0]

    def un(p):
        uid[0] += 1
        return "%s_%d" % (p, uid[0])

    def inp(name, shape, dt=F32):
        return nc.dram_tensor(name, shape, dt, kind="ExternalInput").ap()

    def scratch(name, shape, dt, out=False):
        kind = "ExternalOutput" if (out or dbg) else "Internal"
        return nc.dram_tensor(name, shape, dt, kind=kind).ap(), Buf(name)

    x = inp("x", [S_, D_])
    pos = inp("pos", [128, 32], I32)
    gfm = inp("gfm", [128, 24])
    gqa = inp("gqa", [128, 3])
    gkva = inp("gkva", [128, 2])
    gssm = inp("gssm", [128, 16])
    w_g1 = inp("w_g1", [D_, FF]); w_u1 = inp("w_u1", [D_, FF]); w_d1 = inp("w_d1", [FF, D_])
    w_g2 = inp("w_g2", [D_, FF]); w_u2 = inp("w_u2", [D_, FF]); w_d2 = inp("w_d2", [FF, D_])
    w_in = inp("w_in", [D_, IN_DIM])
    w_qb = inp("w_qb", [384, 1536]); w_kvb = inp("w_kvb", [256, 2048])
    hn = inp("hn", [1, 192])
    convw = inp("convw", [128, 24, 5]); convb = inp("convb", [128, 24])
    ssp = inp("ssp", [1, 160])
    w_pa = inp("w_pa", [D_, D_]); w_pb = inp("w_pb", [2048, D_]); w_o = inp("w_o", [D_, D_])
    c_identb = inp("c_identb", [128, 128], BF16)
    c_mats = inp("c_mats", [128, 4, 128])
    c_neg = inp("c_neg", [128, 2, 128], BF16)
    c_invf = inp("c_invf", [1, 16])

    y_out, b_yout = scratch("y", [S_, D_], F32, out=True)
    x1s, b_x1s = scratch("x1s", [S_, D_], F32)
    x2s, b_x2s = scratch("x2s", [S_, D_], F32)
    hmid, b_hmid = scratch("hmid", [FF, S_], BF16)
    QT, b_QT = scratch("QT", [NH, 96, S_], BF16)
    KT, b_KT = scratch("KT", [NH, 96, S_], BF16)
    Vs, b_Vs = scratch("Vs", [NH, 128, 32, 65], BF16)
    zs, b_zs = scratch("zs", [S_, 2048], BF16)
    xcs, b_xcs = scratch("xcs", [32, 128, 24, 128], BF16)
    gts, b_gts = scratch("gts", [2048, S_], BF16)
    yfs, b_yfs = scratch("yfs", [S_, 2048], F32)
    mTs, b_mTs = scratch("mTs", [2048, S_], BF16)
    aTs, b_aTs = scratch("aTs", [D_, S_], BF16)
    if dbg:
        ybs, b_ybs = scratch("ybs", [S_, 2048], F32)

    top = ExitStack()
    A = lambda name, shape, dt: top.enter_context(nc.sbuf_tensor(name, shape, dt))
    identb = A("identb", [128, 128], BF16); b_identb = Buf()
    mats = A("mats", [128, 4, 128], F32); b_mats = Buf()
    negm = A("negm", [128, 2, 128], BF16); b_negm = Buf()
    gfm_t = A("gfm_t", [128, 24], F32); b_gfm = Buf()
    gqa_t = A("gqa_t", [128, 3], F32); b_gqa = Buf()
    gkva_t = A("gkva_t", [128, 2], F32); b_gkva = Buf()
    gssm_t = A("gssm_t", [128, 16], F32); b_gssm = Buf()
    hn_t = A("hn_t", [128, 192], F32); b_hn = Buf()
    ssp_t = A("ssp_t", [128, 160], F32); b_ssp = Buf()
    convw_t = A("convw_t", [128, 24, 5], F32); b_convw = Buf()
    convb_t = A("convb_t", [128, 24], F32); b_convb = Buf()
    cos_t = A("cos_t", [128, 32, 16], F32); b_cos = Buf()
    sin_t = A("sin_t", [128, 32, 16], F32); b_sin = Buf()
    dtraw = A("dtraw", [128, 32, 64], F32); b_dtraw = Buf()

    def ld(dst, src, b):
        S.dma(lambda e: e.dma_start(out=dst, in_=src), writes=[b], sem_buf=b)

    ld(identb[:], c_identb[:, :], b_identb)
    ld(mats[:], c_mats[:, :, :], b_mats)
    ld(negm[:], c_neg[:, :, :], b_negm)
    ld(gfm_t[:], gfm[:, :], b_gfm)
    ld(gqa_t[:], gqa[:, :], b_gqa)
    ld(gkva_t[:], gkva[:, :], b_gkva)
    ld(gssm_t[:], gssm[:, :], b_gssm)
    ld(hn_t[:], hn.partition_broadcast(128), b_hn)
    ld(ssp_t[:], ssp.partition_broadcast(128), b_ssp)
    ld(convw_t[:], convw[:, :, :], b_convw)
    ld(convb_t[:], convb[:, :], b_convb)
    Umat = mats[:, 0, :]
    Ustr = mats[:, 1, :]
    onesf = mats[:, 2, :]
    identf = mats[:, 3, :]

    with ExitStack() as st:
        post = st.enter_context(nc.sbuf_tensor("post", [128, 32], I32)); b_post = Buf()
        posf = st.enter_context(nc.sbuf_tensor("posf", [128, 32], F32)); b_posf = Buf()
        invf = st.enter_context(nc.sbuf_tensor("invf", [128, 16], F32)); b_invf = Buf()
        ang = st.enter_context(nc.sbuf_tensor("ang", [128, 32, 16], F32)); b_ang = Buf()
        ang2 = st.enter_context(nc.sbuf_tensor("ang2", [128, 32, 16], F32)); b_ang2 = Buf()
        ld(post[:], pos[:, :], b_post)
        ld(invf[:], c_invf.partition_broadcast(128), b_invf)
        S.op("dve", lambda e: e.tensor_copy(out=posf[:], in_=post[:]), reads=[b_post], writes=[b_posf])
        S.op("dve", lambda e: e.tensor_tensor(out=ang[:], in0=posf[:].unsqueeze(2).to_broadcast([128, 32, 16]),
                                              in1=invf[:].unsqueeze(1).to_broadcast([128, 32, 16]), op=ALU.mult),
             reads=[b_posf, b_invf], writes=[b_ang])
        PI = math.pi
        angi = st.enter_context(nc.sbuf_tensor("angi", [128, 32, 16], I32)); b_angi = Buf()
        ang3 = st.enter_context(nc.sbuf_tensor("ang3", [128, 32, 16], F32)); b_ang3 = Buf()

        def rr(shift, dst, b_dst):
            S.op("dve", lambda e: e.tensor_scalar(out=ang2[:], in0=ang[:], scalar1=shift, scalar2=None, op0=ALU.add),
                 reads=[b_ang], writes=[b_ang2])
            S.op("dve", lambda e: e.tensor_scalar(out=ang3[:], in0=ang2[:], scalar1=1.0 / (2 * PI), scalar2=None,
                                                  op0=ALU.mult), reads=[b_ang2], writes=[b_ang3])
            S.op("dve", lambda e: e.tensor_copy(out=angi[:], in_=ang3[:]), reads=[b_ang3], writes=[b_angi])
            S.op("dve", lambda e: e.tensor_copy(out=ang3[:], in_=angi[:]), reads=[b_angi], writes=[b_ang3])
            S.op("dve", lambda e: e.scalar_tensor_tensor(out=ang2[:], in0=ang3[:], scalar=-2 * PI, in1=ang2[:],
                                                         op0=ALU.mult, op1=ALU.add), reads=[b_ang3, b_ang2], writes=[b_ang2])
            S.op("dve", lambda e: e.tensor_scalar(out=ang3[:], in0=ang2[:], scalar1=-PI, scalar2=1e9,
                                                  op0=ALU.add, op1=ALU.mult), reads=[b_ang2], writes=[b_ang3])
            S.op("dve", lambda e: e.tensor_scalar(out=ang3[:], in0=ang3[:], scalar1=0.0, scalar2=1.0,
                                                  op0=ALU.max, op1=ALU.min), reads=[b_ang3], writes=[b_ang3])
            S.op("dve", lambda e: e.scalar_tensor_tensor(out=ang2[:], in0=ang3[:], scalar=-2 * PI, in1=ang2[:],
                                                         op0=ALU.mult, op1=ALU.add), reads=[b_ang3, b_ang2], writes=[b_ang2])
            S.op("dve", lambda e: e.tensor_scalar(out=ang3[:], in0=ang2[:], scalar1=PI, scalar2=-1e9,
                                                  op0=ALU.add, op1=ALU.mult), reads=[b_ang2], writes=[b_ang3])
            S.op("dve", lambda e: e.tensor_scalar(out=ang3[:], in0=ang3[:], scalar1=0.0, scalar2=1.0,
                                                  op0=ALU.max, op1=ALU.min), reads=[b_ang3], writes=[b_ang3])
            S.op("dve", lambda e: e.scalar_tensor_tensor(out=ang2[:], in0=ang3[:], scalar=2 * PI, in1=ang2[:],
                                                         op0=ALU.mult, op1=ALU.add), reads=[b_ang3, b_ang2], writes=[b_ang2])
            S.op("dve", lambda e: e.tensor_scalar(out=ang2[:], in0=ang2[:], scalar1=PI * (1 - 1e-6),
                                                  scalar2=-PI * (1 - 1e-6), op0=ALU.min, op1=ALU.max),
                 reads=[b_ang2], writes=[b_ang2])
            S.op("act", lambda e: e.activation(out=dst, in_=ang2[:], func=AF.Sin), reads=[b_ang2], writes=[b_dst])

        rr(0.0, sin_t[:], b_sin)
        rr(0.5 * PI, cos_t[:], b_cos)
        S.barrier()

    def wload(stage_ring, w_ring, wsrc, kc, n, gain=None, cast_eng="pool"):
        stg, b_stg = stage_ring.next()
        wt, b_wt = w_ring.next()
        S.dma(lambda e: e.dma_start(out=stg[:, 0:kc, 0:n], in_=wsrc.rearrange("(kc p) n -> p kc n", p=128)),
              writes=[b_stg], sem_buf=b_stg)
        if gain is None:
            S.op(cast_eng, lambda e: e.tensor_copy(out=wt[:, 0:kc, 0:n], in_=stg[:, 0:kc, 0:n]),
                 reads=[b_stg], writes=[b_wt])
        else:
            g_ap, b_g = gain
            S.op(cast_eng, lambda e: e.tensor_tensor(out=wt[:, 0:kc, 0:n], in0=stg[:, 0:kc, 0:n],
                                                     in1=g_ap.unsqueeze(2).to_broadcast([128, kc, n]), op=ALU.mult),
                 reads=[b_stg, b_g], writes=[b_wt])
        return wt, b_wt

    def norm_block(src, b_src, tb, rings, hT, b_hT, ncols=D_):
        junk, b_junk = rings["junk"].next()
        stt, b_stt = rings["st"].next()
        hb, b_hb = rings["hb"].next()
        ptr, b_ptr = rings["ptr"].next()
        S.op("act", lambda e: e.activation(out=junk[:], in_=src, func=AF.Square, scale=1.0 / math.sqrt(ncols),
                                           accum_out=stt[:, 0:1]), reads=[b_src], writes=[b_junk, b_stt])
        S.op("act", lambda e: e.activation(out=stt[:, 1:2], in_=stt[:, 0:1], func=AF.Sqrt, bias=EPS_AP[:, 0:1]),
             reads=[b_stt], writes=[b_stt])
        S.op("dve", lambda e: e.reciprocal(out=stt[:, 2:3], in_=stt[:, 1:2]), reads=[b_stt], writes=[b_stt])
        S.op("dve", lambda e: e.tensor_scalar(out=hb[:], in0=src, scalar1=stt[:, 2:3], scalar2=None, op0=ALU.mult),
             reads=[b_src, b_stt], writes=[b_hb])
        pv = ptr[:, :].bitcast(BF16)
        for kc in range(8):
            S.op("pe", lambda e, kc=kc: e.transpose(out=pv[:, kc * 128:(kc + 1) * 128],
                                                     in_=hb[:, kc * 128:(kc + 1) * 128], identity=identb[:]),
                 reads=[b_hb, b_identb], writes=[b_ptr])
        S.op("dve", lambda e: e.tensor_copy(out=hT[:, :, tb * 128:(tb + 1) * 128],
                                            in_=pv.rearrange("p (k t) -> p k t", k=8)),
             reads=[b_ptr], writes=[b_hT])

    eps_t = A("eps_t", [128, 1], F32); b_eps = Buf()
    S.op("pool", lambda e: e.memset(eps_t[:], EPS), writes=[b_eps])
    EPS_AP = eps_t
    S.barrier()

    def ffn_gateup(w_g, w_u, gain_col, hT, b_hT, w_d=None, wd=None, b_wd=None):
        with ExitStack() as st:
            stg = Ring(nc, st, un("gu_stg"), [128, 8, 128], F32, 6)
            wr = Ring(nc, st, un("gu_w"), [128, 8, 128], BF16, 6)
            psg = Ring(nc, st, un("gu_pg"), [128, 512], F32, 3, psum=True)
            psu = Ring(nc, st, un("gu_pu"), [128, 512], F32, 3, psum=True)
            sil = Ring(nc, st, un("gu_sil"), [128, 512], F32, 3)
            hm = Ring(nc, st, un("gu_hm"), [128, S_], BF16, 2)
            gain = (gfm_t[:, gain_col * 8:(gain_col + 1) * 8], b_gfm)

            dstg = Ring(nc, st, un("gu_dstg"), [128, 1, D_], F32, 3)

            def load(j):
                wg = wload(stg, wr, w_g[:, j * 128:(j + 1) * 128], 8, 128, gain)
                wu = wload(stg, wr, w_u[:, j * 128:(j + 1) * 128], 8, 128, gain)
                if w_d is not None:
                    sg, b_sg = dstg.next()
                    S.dma(lambda e, sg=sg, j=j: e.dma_start(out=sg[:, 0, :], in_=w_d[j * 128:(j + 1) * 128, :]),
                          writes=[b_sg], sem_buf=b_sg)
                    S.op("pool", lambda e, sg=sg, j=j: e.tensor_copy(out=wd[:, j, :], in_=sg[:, 0, :]),
                         reads=[b_sg], writes=[b_wd])
                return wg, wu

            def comp(j, h):
                (wg, b_wg), (wu, b_wu) = h
                hmt, b_hm = hm.next()
                for tb in range(8):
                    pg, b_pg = psg.next()
                    pu, b_pu = psu.next()
                    sl, b_sl = sil.next()
                    ts = slice(tb * 512, (tb + 1) * 512)
                    for kc in range(8):
                        S.op("pe", lambda e, kc=kc, pg=pg, wg=wg, ts=ts: e.matmul(
                            out=pg[:], lhsT=wg[:, kc, :], rhs=hT[:, kc, ts], start=(kc == 0), stop=(kc == 7)),
                            reads=[b_wg, b_hT], writes=[b_pg])
                    for kc in range(8):
                        S.op("pe", lambda e, kc=kc, pu=pu, wu=wu, ts=ts: e.matmul(
                            out=pu[:], lhsT=wu[:, kc, :], rhs=hT[:, kc, ts], start=(kc == 0), stop=(kc == 7)),
                            reads=[b_wu, b_hT], writes=[b_pu])
                    S.op("act", lambda e, sl=sl, pg=pg: e.activation(out=sl[:], in_=pg[:], func=AF.Silu),
                         reads=[b_pg], writes=[b_sl])
                    S.op("dve", lambda e, sl=sl, pu=pu, hmt=hmt, ts=ts: e.tensor_tensor(
                        out=hmt[:, ts], in0=sl[:], in1=pu[:], op=ALU.mult), reads=[b_sl, b_pu], writes=[b_hm])
                S.dma(lambda e, hmt=hmt, j=j: e.dma_start(out=hmid[j * 128:(j + 1) * 128, :], in_=hmt[:]),
                      reads=[b_hm], writes=[b_hmid], sem_buf=b_hm, eng="pool")

            pipeline(NFF, load, comp, 2)
        S.barrier()

    def ffn_down(w_d, xsrc, b_xsrc, xdst, b_xdst, next_norm, wd=None, b_wd=None):
        with ExitStack() as st:
            pre = wd is not None
            if not pre:
                wd = st.enter_context(nc.sbuf_tensor(un("wd"), [128, NFF, D_], BF16)); b_wd = Buf()
            stg = Ring(nc, st, un("dn_stg"), [128, 1, D_], F32, 3)
            for j in range(NFF if not pre else 0):
                sg, b_sg = stg.next()
                S.dma(lambda e, sg=sg, j=j: e.dma_start(out=sg[:, 0, :], in_=w_d[j * 128:(j + 1) * 128, :]),
                      writes=[b_sg], sem_buf=b_sg)
                S.op("pool", lambda e, sg=sg, j=j: e.tensor_copy(out=wd[:, j, :], in_=sg[:, 0, :]),
                     reads=[b_sg], writes=[b_wd])
            hmr = Ring(nc, st, un("dn_hm"), [128, NFF, 512], BF16, 2)
            xr = Ring(nc, st, un("dn_x"), [128, D_], F32, 3)
            ps = Ring(nc, st, un("dn_ps"), [128, 512], F32, 4, psum=True)
            rings = None
            if next_norm:
                rings = dict(junk=Ring(nc, st, un("nj"), [128, D_], BF16, 2),
                             st=Ring(nc, st, un("nst"), [128, 4], F32, 3),
                             hb=Ring(nc, st, un("nhb"), [128, D_], BF16, 2),
                             ptr=Ring(nc, st, un("nptr"), [128, 512], F32, 2, psum=True))

            def load(t):
                hmt, b_hm = hmr.next()
                S.dma(lambda e: e.dma_start(out=hmt[:], in_=hmid[:, t * 512:(t + 1) * 512].rearrange(
                    "(j p) t -> p j t", p=128)), reads=[b_hmid], writes=[b_hm], sem_buf=b_hm)
                return hmt, b_hm

            def comp(t, h):
                hmt, b_hm = h
                for sb in range(4):
                    tb = t * 4 + sb
                    xt, b_xt = xr.next()
                    S.dma(lambda e, xt=xt, tb=tb: e.dma_start(out=xt[:], in_=xsrc[tb * 128:(tb + 1) * 128, :]),
                          reads=[b_xsrc], writes=[b_xt], sem_buf=b_xt)
                    for half in range(2):
                        p, b_p = ps.next()
                        for j in range(NFF):
                            S.op("pe", lambda e, j=j, p=p, sb=sb, half=half, hmt=hmt: e.matmul(
                                out=p[:], lhsT=hmt[:, j, sb * 128:(sb + 1) * 128],
                                rhs=wd[:, j, half * 512:(half + 1) * 512], start=(j == 0), stop=(j == NFF - 1)),
                                reads=[b_hm, b_wd], writes=[b_p])
                        S.op("dve", lambda e, p=p, xt=xt, half=half: e.scalar_tensor_tensor(
                            out=xt[:, half * 512:(half + 1) * 512], in0=p[:], scalar=0.5,
                            in1=xt[:, half * 512:(half + 1) * 512], op0=ALU.mult, op1=ALU.add),
                            reads=[b_p, b_xt], writes=[b_xt])
                    S.dma(lambda e, xt=xt, tb=tb: e.dma_start(out=xdst[tb * 128:(tb + 1) * 128, :], in_=xt[:]),
                          reads=[b_xt], writes=[b_xdst], sem_buf=b_xt, eng="pool")
                    if next_norm:
                        if pendn:
                            norm_block(*pendn.pop())
                        pendn.append((xt[:], b_xt, tb, rings, next_norm[0], next_norm[1]))

            pendn = []
            pipeline(8, load, comp, 1)
            if pendn:
                norm_block(*pendn.pop())
        S.barrier()

    def norm_phase(xsrc, b_xsrc, hT, b_hT):
        with ExitStack() as st:
            xr = Ring(nc, st, un("np_x"), [128, D_], F32, 3)
            rings = dict(junk=Ring(nc, st, un("nj"), [128, D_], BF16, 2),
                         st=Ring(nc, st, un("nst"), [128, 4], F32, 3),
                         hb=Ring(nc, st, un("nhb"), [128, D_], BF16, 2),
                         ptr=Ring(nc, st, un("nptr"), [128, 512], F32, 2, psum=True))

            def load(tb):
                xt, b_xt = xr.next()
                S.dma(lambda e: e.dma_start(out=xt[:], in_=xsrc[tb * 128:(tb + 1) * 128, :]),
                      reads=[b_xsrc], writes=[b_xt], sem_buf=b_xt)
                return xt, b_xt

            def comp(tb, h):
                norm_block(h[0][:], h[1], tb, rings, hT, b_hT)

            pipeline(32, load, comp, 2)
        S.barrier()

    b_x = Buf("x")
    hst = ExitStack()
    hT = hst.enter_context(nc.sbuf_tensor("hT", [128, 8, S_], BF16)); b_hT = Buf("hT")
    norm_phase(x, b_x, hT, b_hT)
    with ExitStack() as fst:
        wd1 = fst.enter_context(nc.sbuf_tensor("wd1", [128, NFF, D_], BF16)); b_wd1 = Buf()
        ffn_gateup(w_g1, w_u1, 0, hT, b_hT, w_d1, wd1, b_wd1)
        ffn_down(w_d1, x, b_x, x1s, b_x1s, (hT, b_hT), wd1, b_wd1)
    if stop_after <= 1:
        S.emit(nc)
        return nc

    gmix = (gfm_t[:, 8:16], b_gfm)

    def rope(src3, H, tb, dst3, tmp_ring):
        ta, b_ta = tmp_ring.next()
        tb_, b_tb = tmp_ring.next()
        cb = cos_t[:, tb, :].unsqueeze(1).to_broadcast([128, H, 16])
        sb = sin_t[:, tb, :].unsqueeze(1).to_broadcast([128, H, 16])
        t1 = src3[:, :, 0:16]
        t2 = src3[:, :, 16:32]
        a_ = ta[:, 0:H, :]
        b_ = tb_[:, 0:H, :]
        return [
            (lambda e: e.tensor_tensor(out=a_, in0=t1, in1=cb, op=ALU.mult), [b_cos], [b_ta]),
            (lambda e: e.tensor_tensor(out=b_, in0=t2, in1=sb, op=ALU.mult), [b_sin], [b_tb]),
            (lambda e: e.tensor_tensor(out=dst3[:, :, 0:16], in0=a_, in1=b_, op=ALU.subtract), [b_ta, b_tb], []),
            (lambda e: e.tensor_tensor(out=a_, in0=t2, in1=cb, op=ALU.mult), [b_cos], [b_ta]),
            (lambda e: e.tensor_tensor(out=b_, in0=t1, in1=sb, op=ALU.mult), [b_sin], [b_tb]),
            (lambda e: e.tensor_tensor(out=dst3[:, :, 16:32], in0=a_, in1=b_, op=ALU.add), [b_ta, b_tb], []),
        ]

    with ExitStack() as st:
        wAr = Ring(nc, st, un("a_w"), [128, 8, 672], BF16, 1)
        wqr = Ring(nc, st, un("q_w"), [128, 3, 1536], BF16, 1)
        wkr = Ring(nc, st, un("kv_w"), [128, 2, 2048], BF16, 1)
        with ExitStack() as st2:
            stgA = Ring(nc, st2, un("a_stg"), [128, 8, 672], F32, 1)
            stgq = Ring(nc, st2, un("q_stg"), [128, 3, 1536], F32, 1)
            stgk = Ring(nc, st2, un("kv_stg"), [128, 2, 2048], F32, 1)
            wA, b_wA = wload(stgA, wAr, w_in[:, 0:672], 8, 672, gmix)
            wq, b_wq = wload(stgq, wqr, w_qb[:, :], 3, 1536, (gqa_t[:, :], b_gqa))
            wkv, b_wkv = wload(stgk, wkr, w_kvb[:, :], 2, 2048, (gkva_t[:, :], b_gkva))
            S.barrier()
        psA = Ring(nc, st, un("a_ps"), [128, 512], F32, 2, psum=True)
        psT = Ring(nc, st, un("a_pt"), [128, 512], F32, 2, psum=True)
        psQ = Ring(nc, st, un("a_pq"), [128, 512], F32, 3, psum=True)
        junk = Ring(nc, st, un("a_junk"), [128, 1536], F32, 1)
        stt = Ring(nc, st, un("a_st"), [128, 8], F32, 2)
        sst = Ring(nc, st, un("a_ss"), [128, 100], F32, 2)
        cnr = Ring(nc, st, un("a_cn"), [128, 640], BF16, 2)
        cTr = Ring(nc, st, un("a_cT"), [128, 5, 128], BF16, 2)
        kper = Ring(nc, st, un("a_kpe"), [128, 1, 32], F32, 2)
        kpgr = Ring(nc, st, un("a_kpg"), [128, 1, 32], F32, 2)
        krr = Ring(nc, st, un("a_kr"), [128, 1, 32], F32, 2)
        qsbr = Ring(nc, st, un("a_qsb"), [128, 1536], F32, 1)
        kvsbr = Ring(nc, st, un("a_kvsb"), [128, 2048], F32, 1)
        tmpkr = Ring(nc, st, un("a_tmpk"), [128, 16, 64], F32, 1)
        qbr = Ring(nc, st, un("a_qb"), [128, 16, 96], BF16, 2)
        kbr = Ring(nc, st, un("a_kb"), [128, 16, 96], BF16, 2)
        ropet = Ring(nc, st, un("a_rt"), [128, 16, 16], F32, 4)
        vbr = Ring(nc, st, un("a_vb"), [128, 16, 4, 65], BF16, 1)
        qTr = Ring(nc, st, un("a_qT"), [128, 16, 256], BF16, 2)
        kTr = Ring(nc, st, un("a_kT"), [128, 16, 256], BF16, 2)
        for (vt, b_v) in vbr.items:
            S.op("pool", lambda e, vt=vt: e.memset(vt[:], 1.0), writes=[b_v])
        gq = hn_t[:, 0:96]
        gk = hn_t[:, 96:192]
        sh = {}

        def block(tb):
            tsl = slice(tb * 128, (tb + 1) * 128)
            pA1, b_pA1 = psA.next()
            pA2, b_pA2 = psA.next()
            for kc in range(8):
                S.op("pe", lambda e, kc=kc, pA1=pA1, tsl=tsl: e.matmul(out=pA1[:, 0:384], lhsT=hT[:, kc, tsl], rhs=wA[:, kc, 0:384],
                                                     start=(kc == 0), stop=(kc == 7)), reads=[b_hT, b_wA], writes=[b_pA1])
            for kc in range(8):
                S.op("pe", lambda e, kc=kc, pA2=pA2, tsl=tsl: e.matmul(out=pA2[:, 0:288], lhsT=hT[:, kc, tsl], rhs=wA[:, kc, 384:672],
                                                     start=(kc == 0), stop=(kc == 7)), reads=[b_hT, b_wA], writes=[b_pA2])
            jk, b_jk = junk.next()
            s8, b_s8 = stt.next()
            S.op("act", lambda e, jk=jk, pA1=pA1, s8=s8: e.activation(out=jk[:, 0:384], in_=pA1[:, 0:384], func=AF.Square,
                                               scale=1.0 / math.sqrt(384.0), accum_out=s8[:, 0:1]),
                 reads=[b_pA1], writes=[b_jk, b_s8])
            S.op("act", lambda e, jk=jk, pA2=pA2, s8=s8: e.activation(out=jk[:, 0:256], in_=pA2[:, 0:256], func=AF.Square,
                                               scale=1.0 / 16.0, accum_out=s8[:, 1:2]),
                 reads=[b_pA2], writes=[b_jk, b_s8])
            S.op("act", lambda e, s8=s8: e.activation(out=s8[:, 2:4], in_=s8[:, 0:2], func=AF.Sqrt, bias=EPS_AP[:, 0:1]),
                 reads=[b_s8, b_eps], writes=[b_s8])
            S.op("dve", lambda e, s8=s8: e.reciprocal(out=s8[:, 4:6], in_=s8[:, 2:4]), reads=[b_s8], writes=[b_s8])
            cn, b_cn = cnr.next()
            S.op("dve", lambda e, cn=cn, pA1=pA1, s8=s8: e.tensor_scalar(out=cn[:, 0:384], in0=pA1[:, 0:384], scalar1=s8[:, 4:5],
                                                  scalar2=None, op0=ALU.mult), reads=[b_pA1, b_s8], writes=[b_cn])
            S.op("dve", lambda e, cn=cn, pA2=pA2, s8=s8: e.tensor_scalar(out=cn[:, 384:640], in0=pA2[:, 0:256], scalar1=s8[:, 5:6],
                                                  scalar2=None, op0=ALU.mult), reads=[b_pA2, b_s8], writes=[b_cn])
            kpe, b_kpe = kper.next()
            S.op("act", lambda e, kpe=kpe, pA2=pA2: e.copy(out=kpe[:, 0, :], in_=pA2[:, 256:288]), reads=[b_pA2], writes=[b_kpe])
            yield
            ptr, b_ptr = psT.next()
            pv = ptr[:, :].bitcast(BF16)
            for kc in range(5):
                S.op("pe", lambda e, kc=kc, pv=pv, cn=cn: e.transpose(out=pv[:, kc * 128:(kc + 1) * 128],
                                                         in_=cn[:, kc * 128:(kc + 1) * 128], identity=identb[:]),
                     reads=[b_cn, b_identb], writes=[b_ptr])
            cT, b_cT = cTr.next()
            S.op("dve", lambda e, cT=cT, pv=pv: e.tensor_copy(out=cT[:], in_=pv[:, 0:640].rearrange("p (k t) -> p k t", k=5)),
                 reads=[b_ptr], writes=[b_cT])
            yield
            qsb, b_qsb = qsbr.next()
            kvsb, b_kvsb = kvsbr.next()
            for nb in range(3):
                pq, b_pq = psQ.next()
                for kc in range(3):
                    S.op("pe", lambda e, kc=kc, nb=nb, pq=pq, cT=cT: e.matmul(out=pq[:], lhsT=cT[:, kc, :],
                                                                 rhs=wq[:, kc, nb * 512:(nb + 1) * 512],
                                                                 start=(kc == 0), stop=(kc == 2)),
                         reads=[b_cT, b_wq], writes=[b_pq])
                S.op("act", lambda e, nb=nb, pq=pq, qsb=qsb: e.copy(out=qsb[:, nb * 512:(nb + 1) * 512], in_=pq[:]),
                     reads=[b_pq], writes=[b_qsb])
            for nb in range(4):
                pq, b_pq = psQ.next()
                for kc in range(2):
                    S.op("pe", lambda e, kc=kc, nb=nb, pq=pq, cT=cT: e.matmul(out=pq[:], lhsT=cT[:, 3 + kc, :],
                                                                 rhs=wkv[:, kc, nb * 512:(nb + 1) * 512],
                                                                 start=(kc == 0), stop=(kc == 1)),
                         reads=[b_cT, b_wkv], writes=[b_pq])
                eng = "act" if nb % 2 == 0 else "dve"
                if eng == "act":
                    S.op("act", lambda e, nb=nb, pq=pq, kvsb=kvsb: e.copy(out=kvsb[:, nb * 512:(nb + 1) * 512], in_=pq[:]),
                         reads=[b_pq], writes=[b_kvsb])
                else:
                    S.op("dve", lambda e, nb=nb, pq=pq, kvsb=kvsb: e.tensor_copy(out=kvsb[:, nb * 512:(nb + 1) * 512], in_=pq[:]),
                         reads=[b_pq], writes=[b_kvsb])
            q3 = qsb[:, :].rearrange("p (h d) -> p h d", h=16)
            kv3 = kvsb[:, :].rearrange("p (h d) -> p h d", h=16)
            ss, b_ss = sst.next()
            S.op("act", lambda e, jk=jk, qsb=qsb: e.activation(out=jk[:, :], in_=qsb[:, :], func=AF.Square),
                 reads=[b_qsb], writes=[b_jk])
            S.op("dve", lambda e, jk=jk, ss=ss: e.tensor_reduce(out=ss[:, 0:16], in_=jk[:, :].rearrange("p (h d) -> p h d", h=16),
                                                  axis=AX.X, op=ALU.add), reads=[b_jk], writes=[b_ss])
            tk, b_tk = tmpkr.next()
            S.op("act", lambda e, tk=tk, kv3=kv3: e.activation(out=tk[:], in_=kv3[:, :, 0:64], func=AF.Square),
                 reads=[b_kvsb], writes=[b_tk])
            S.op("dve", lambda e, tk=tk, ss=ss: e.tensor_reduce(out=ss[:, 16:32], in_=tk[:], axis=AX.X, op=ALU.add),
                 reads=[b_tk], writes=[b_ss])
            kpg, b_kpg = kpgr.next()
            S.op("act", lambda e, kpg=kpg, kpe=kpe, ss=ss: e.activation(out=kpg[:, 0, :], in_=kpe[:, 0, :], func=AF.Square,
                                               accum_out=ss[:, 96:97]), reads=[b_kpe], writes=[b_kpg, b_ss])
            S.op("dve", lambda e, ss=ss: e.tensor_scalar(out=ss[:, 16:32], in0=ss[:, 16:32], scalar1=ss[:, 96:97], scalar2=None,
                                                  op0=ALU.add), reads=[b_ss], writes=[b_ss])
            S.op("act", lambda e, ss=ss: e.activation(out=ss[:, 32:64], in_=ss[:, 0:32], func=AF.Sqrt, bias=EPS_AP[:, 0:1],
                                               scale=1.0 / 96.0), reads=[b_ss, b_eps], writes=[b_ss])
            S.op("dve", lambda e, ss=ss: e.reciprocal(out=ss[:, 64:96], in_=ss[:, 32:64]), reads=[b_ss], writes=[b_ss])
            rsq = ss[:, 64:80]
            rsk = ss[:, 80:96]
            S.op("dve", lambda e, q3=q3, rsq=rsq: e.tensor_tensor(out=q3, in0=q3, in1=rsq.unsqueeze(2).to_broadcast([128, 16, 96]),
                                                  op=ALU.mult), reads=[b_qsb, b_ss], writes=[b_qsb])
            S.op("dve", lambda e, q3=q3: e.tensor_tensor(out=q3, in0=q3, in1=gq.unsqueeze(1).to_broadcast([128, 16, 96]),
                                                   op=ALU.mult), reads=[b_qsb, b_hn], writes=[b_qsb])
            qb, b_qb = qbr.next()
            S.op("act", lambda e, qb=qb, q3=q3: e.copy(out=qb[:, :, 0:64], in_=q3[:, :, 0:64]), reads=[b_qsb], writes=[b_qb])
            for fn, rd, wr in rope(q3[:, :, 64:96], 16, tb, qb[:, :, 64:96], ropet):
                S.op("dve", fn, reads=[b_qsb] + rd, writes=wr + ([b_qb] if not wr else []))
            S.op("dve", lambda e, tk=tk, kv3=kv3, rsk=rsk: e.tensor_tensor(out=tk[:], in0=kv3[:, :, 0:64],
                                                  in1=rsk.unsqueeze(2).to_broadcast([128, 16, 64]), op=ALU.mult),
                 reads=[b_kvsb, b_ss], writes=[b_tk])
            kb, b_kb = kbr.next()
            S.op("dve", lambda e, tk=tk, kb=kb: e.tensor_tensor(out=kb[:, :, 0:64], in0=tk[:],
                                                   in1=gk[:, 0:64].unsqueeze(1).to_broadcast([128, 16, 64]), op=ALU.mult),
                 reads=[b_tk, b_hn], writes=[b_kb])
            S.op("dve", lambda e, kpg=kpg, kpe=kpe: e.tensor_tensor(out=kpg[:, 0, :], in0=kpe[:, 0, :], in1=gk[:, 64:96], op=ALU.mult),
                 reads=[b_kpe, b_hn], writes=[b_kpg])
            kr, b_kr = krr.next()
            for fn, rd, wr in rope(kpg[:, :, :], 1, tb, kr[:, :, :], ropet):
                S.op("dve", fn, reads=[b_kpg] + rd, writes=wr + ([b_kr] if not wr else []))
            S.op("dve", lambda e, kb=kb, kr=kr, rsk=rsk: e.tensor_tensor(out=kb[:, :, 64:96],
                                                  in0=kr[:, 0, :].unsqueeze(1).to_broadcast([128, 16, 32]),
                                                  in1=rsk.unsqueeze(2).to_broadcast([128, 16, 32]), op=ALU.mult),
                 reads=[b_kr, b_ss], writes=[b_kb])
            if tb % 4 == 0:
                sh['vb'] = vbr.next()
            vb, b_vb = sh['vb']
            S.op("pool", lambda e, vb=vb, kv3=kv3, tb=tb: e.tensor_copy(out=vb[:, :, tb % 4, 0:64], in_=kv3[:, :, 64:128]),
                 reads=[b_kvsb], writes=[b_vb])
            yield
            if tb % 2 == 0:
                sh['qT'] = qTr.next()
                sh['kT'] = kTr.next()
            qTs, b_qTs = sh['qT']
            kTs, b_kTs = sh['kT']
            for (src, b_src, dstT, b_dstT) in ((qb, b_qb, qTs, b_qTs), (kb, b_kb, kTs, b_kTs)):
                for half in range(2):
                    ptr, b_ptr = psT.next()
                    pv = ptr[:, :].bitcast(BF16)
                    for hh in range(8):
                        S.op("pe", lambda e, hh=hh, pv=pv, src=src, half=half: e.transpose(
                            out=pv[0:96, hh * 128:(hh + 1) * 128], in_=src[:, half * 8 + hh, :], identity=identb[:]),
                            reads=[b_src, b_identb], writes=[b_ptr])
                    off = (tb % 2) * 128
                    eng = "act" if half == 0 else "dve"
                    if eng == "act":
                        S.op("act", lambda e, pv=pv, dstT=dstT, half=half, off=off: e.copy(
                            out=dstT[0:96, half * 8:(half + 1) * 8, off:off + 128],
                            in_=pv[0:96, :].rearrange("p (h t) -> p h t", h=8)), reads=[b_ptr], writes=[b_dstT])
                    else:
                        S.op("dve", lambda e, pv=pv, dstT=dstT, half=half, off=off: e.tensor_copy(
                            out=dstT[0:96, half * 8:(half + 1) * 8, off:off + 128],
                            in_=pv[0:96, :].rearrange("p (h t) -> p h t", h=8)), reads=[b_ptr], writes=[b_dstT])
            if tb % 2 == 1:
                t0 = (tb - 1) * 128
                S.dma(lambda e, qTs=qTs, t0=t0: e.dma_start(out=QT[:, :, t0:t0 + 256].rearrange("h p t -> p h t"),
                                                             in_=qTs[0:96, :, :]),
                      reads=[b_qTs], writes=[b_QT], sem_buf=b_qTs, eng="sp")
                S.dma(lambda e, kTs=kTs, t0=t0: e.dma_start(out=KT[:, :, t0:t0 + 256].rearrange("h p t -> p h t"),
                                                             in_=kTs[0:96, :, :]),
                      reads=[b_kTs], writes=[b_KT], sem_buf=b_kTs, eng="sp")
            if tb % 4 == 3:
                c0 = tb - 3
                S.dma(lambda e, vb=vb, c0=c0: e.dma_start(out=Vs[:, :, c0:c0 + 4, :].rearrange("h p t c -> p h (t c)"),
                                                           in_=vb[:, :, :, :].rearrange("p h t c -> p h (t c)")),
                      reads=[b_vb], writes=[b_Vs], sem_buf=b_vb, eng="sp")
        gens = {}
        for i in range(32 + 2):
            if i < 32:
                gens[i] = block(i)
                next(gens[i])
            if 0 <= i - 1 < 32:
                next(gens[i - 1])
            if 0 <= i - 2 < 32:
                next(gens[i - 2], None)
            if 0 <= i - 1 < 32:
                next(gens[i - 1])
        S.barrier()
    if stop_after <= 2:
        S.emit(nc)
        return nc

    with ExitStack() as st:
        wzr = Ring(nc, st, un("z_w"), [128, 8, 512], BF16, 4)
        wdr = Ring(nc, st, un("dt_w"), [128, 8, 64], BF16, 1)
        with ExitStack() as st2:
            stgz = Ring(nc, st2, un("z_stg"), [128, 8, 512], F32, 2)
            stgd = Ring(nc, st2, un("dt_stg"), [128, 8, 64], F32, 1)
            wz = [wload(stgz, wzr, w_in[:, C_Z + cb * 512:C_Z + (cb + 1) * 512], 8, 512, gmix) for cb in range(4)]
            wdt, b_wdt = wload(stgd, wdr, w_in[:, C_DTF:C_DTF + 64], 8, 64, gmix)
            S.barrier()
        psz = Ring(nc, st, un("z_ps"), [128, 512], F32, 4, psum=True)
        psd = Ring(nc, st, un("dt_ps"), [128, 512], F32, 2, psum=True)
        zst = Ring(nc, st, un("z_st"), [128, 2048], BF16, 3)
        for tb in range(32):
            tsl = slice(tb * 128, (tb + 1) * 128)
            zt, b_zt = zst.next()
            for cb in range(4):
                pz, b_pz = psz.next()
                w_, b_w = wz[cb]
                for kc in range(8):
                    S.op("pe", lambda e, kc=kc, pz=pz, w_=w_, tsl=tsl: e.matmul(out=pz[:], lhsT=hT[:, kc, tsl], rhs=w_[:, kc, :],
                                                                  start=(kc == 0), stop=(kc == 7)),
                         reads=[b_hT, b_w], writes=[b_pz])
                S.op("act", lambda e, pz=pz, zt=zt, cb=cb: e.activation(out=zt[:, cb * 512:(cb + 1) * 512], in_=pz[:], func=AF.Silu),
                     reads=[b_pz], writes=[b_zt])
            pd, b_pd = psd.next()
            for kc in range(8):
                S.op("pe", lambda e, kc=kc, pd=pd, tsl=tsl: e.matmul(out=pd[:, 0:64], lhsT=hT[:, kc, tsl], rhs=wdt[:, kc, :],
                                                       start=(kc == 0), stop=(kc == 7)), reads=[b_hT, b_wdt], writes=[b_pd])
            S.op("dve", lambda e, pd=pd, tb=tb: e.tensor_copy(out=dtraw[:, tb, :], in_=pd[:, 0:64]), reads=[b_pd], writes=[b_dtraw])
            S.dma(lambda e, zt=zt, tsl=tsl: e.dma_start(out=zs[tsl, :], in_=zt[:]), reads=[b_zt], writes=[b_zs],
                  sem_buf=b_zt, eng="pool")
        S.barrier()

    with ExitStack() as st:
        stg = Ring(nc, st, un("x_stg"), [128, 8, 128], F32, 3)
        wr = Ring(nc, st, un("x_w"), [128, 8, 128], BF16, 3)
        psx = Ring(nc, st, un("x_ps"), [128, 512], F32, 3, psum=True)
        psc = Ring(nc, st, un("x_pc"), [128, 512], F32, 3, psum=True)
        xpre = Ring(nc, st, un("x_pre"), [128, S_ + 4], BF16, 2)
        dgr = Ring(nc, st, un("x_dg"), [128, 5, 128], BF16, 2)
        xcst = Ring(nc, st, un("x_cst"), [128, S_], BF16, 3)
        for (t_, b_) in xpre.items:
            S.op("pool", lambda e, t_=t_: e.memset(t_[:], 0.0), writes=[b_])

        def loadx(j):
            return wload(stg, wr, w_in[:, C_XBC + j * 128:C_XBC + (j + 1) * 128], 8, 128, gmix)

        def compx(j, h):
            w_, b_w = h
            xp, b_xp = xpre.next()
            dg, b_dg = dgr.next()
            for k in range(5):
                S.op("dve", lambda e, k=k, dg=dg, j=j: e.tensor_scalar(out=dg[:, k, :], in0=identf, scalar1=convw_t[:, j, k:k + 1],
                                                           scalar2=None, op0=ALU.mult),
                     reads=[b_mats, b_convw], writes=[b_dg])
            for tb in range(8):
                px, b_px = psx.next()
                ts = slice(tb * 512, (tb + 1) * 512)
                for kc in range(8):
                    S.op("pe", lambda e, kc=kc, px=px, w_=w_, ts=ts: e.matmul(out=px[:], lhsT=w_[:, kc, :], rhs=hT[:, kc, ts],
                                                                start=(kc == 0), stop=(kc == 7)),
                         reads=[b_w, b_hT], writes=[b_px])
                if tb % 2 == 0:
                    S.op("act", lambda e, px=px, xp=xp, tb=tb: e.copy(out=xp[:, 2 + tb * 512:2 + (tb + 1) * 512], in_=px[:]),
                         reads=[b_px], writes=[b_xp])
                else:
                    S.op("dve", lambda e, px=px, xp=xp, tb=tb: e.tensor_copy(out=xp[:, 2 + tb * 512:2 + (tb + 1) * 512], in_=px[:]),
                         reads=[b_px], writes=[b_xp])
            xc_, b_xc = xcst.next()
            for tb in range(8):
                pc, b_pc = psc.next()
                for k in range(5):
                    S.op("pe", lambda e, k=k, pc=pc, dg=dg, xp=xp, tb=tb: e.matmul(
                        out=pc[:], lhsT=dg[:, k, :], rhs=xp[:, tb * 512 + k:tb * 512 + k + 512],
                        start=(k == 0), stop=(k == 4)), reads=[b_dg, b_xp], writes=[b_pc])
                S.op("act", lambda e, pc=pc, xc_=xc_, tb=tb, j=j: e.activation(out=xc_[:, tb * 512:(tb + 1) * 512], in_=pc[:],
                                                               func=AF.Silu, bias=convb_t[:, j:j + 1]),
                     reads=[b_pc, b_convb], writes=[b_xc])
            for q4 in range(4):
                S.dma(lambda e, xc_=xc_, j=j, q4=q4: e.dma_start(
                    out=xcs[q4 * 8:(q4 + 1) * 8, :, j, :].rearrange("c p t -> p c t"),
                    in_=xc_[:, q4 * 1024:(q4 + 1) * 1024].rearrange("p (c t) -> p c t", c=8)),
                    reads=[b_xc], writes=[b_xcs], sem_buf=b_xc, eng="sp")

        pipeline(24, loadx, compx, 2)

        def loadg(j):
            return wload(stg, wr, w_in[:, C_GA + j * 128:C_GA + (j + 1) * 128], 8, 128, gmix)

        def compg(j, h):
            w_, b_w = h
            gt_, b_gt = xcst.next()
            for tb in range(8):
                px, b_px = psx.next()
                ts = slice(tb * 512, (tb + 1) * 512)
                for kc in range(8):
                    S.op("pe", lambda e, kc=kc, px=px, w_=w_, ts=ts: e.matmul(out=px[:], lhsT=w_[:, kc, :], rhs=hT[:, kc, ts],
                                                                start=(kc == 0), stop=(kc == 7)),
                         reads=[b_w, b_hT], writes=[b_px])
                S.op("act", lambda e, px=px, gt_=gt_, ts=ts: e.activation(out=gt_[:, ts], in_=px[:], func=AF.Sigmoid),
                     reads=[b_px], writes=[b_gt])
            S.dma(lambda e, gt_=gt_, j=j: e.dma_start(out=gts[j * 128:(j + 1) * 128, :], in_=gt_[:]),
                  reads=[b_gt], writes=[b_gts], sem_buf=b_gt, eng="pool")

        pipeline(16, loadg, compg, 2)
        S.barrier()
    if stop_after <= 3:
        S.emit(nc)
        return nc
    hst.close()

    def ssd_pass(fwd):
        with ExitStack() as st:
            AT = lambda n, shp, dt=F32: st.enter_context(nc.sbuf_tensor(un(n), shp, dt))
            dt_all = AT("dt_all", [128, 32, 32]); b_dt = Buf()
            da_all = AT("da_all", [128, 32, 32]); b_da = Buf()
            P_all = AT("P_all", [128, 32, 32]); b_P = Buf()
            bias_all = AT("bias_all", [128, 32, 32]); b_bias = Buf()
            wgt = AT("wgt", [128, 32, 32]); b_wgt = Buf()
            scl = AT("scl", [128, 32, 32]); b_scl = Buf()
            cdc = AT("cdc", [128, 32, 32]); b_cdc = Buf()
            tot = AT("tot", [128, 32, 32]); b_tot = Buf()
            nega = AT("nega", [128, 32]); b_nega = Buf()
            tmpa = AT("tmpa", [128, 32, 32]); b_tmpa = Buf()
            Sf = AT("Sf", [128, 2048]); b_Sf = [Buf() for _ in range(4)]
            Sbf = AT("Sbf", [128, 2048], BF16); b_Sbf = [Buf() for _ in range(4)]
            off = 0 if fwd else 32
            alog = ssp_t[:, off:off + 32]
            dtb = ssp_t[:, 64 + off:96 + off]
            dsk = ssp_t[:, 128:160]
            Uc = Umat if fwd else Ustr
            midx = 0 if fwd else 1
            ptA = Ring(nc, st, un("s_ptA"), [128, 512], F32, 1, psum=True)
            segb = Ring(nc, st, un("s_seg"), [128, 512], F32, 3, psum=True)
            pyr = Ring(nc, st, un("s_py"), [128, 512], F32, 2, psum=True)
            por = Ring(nc, st, un("s_po"), [128, 512], F32, 1, psum=True)
            pstr = Ring(nc, st, un("s_pst"), [128, 512], F32, 1, psum=True)
            segs = []
            for (t_, _b) in segb.items:
                for q in range(4):
                    segs.append((t_[:, q * 128:(q + 1) * 128], Buf()))
            segi = [0]
            flat = lambda t_: t_[:, :, :].rearrange("p a b -> p (a b)")
            S.op("pool", lambda e: e.memset(Sf[:], 0.0), writes=b_Sf)
            S.op("pool", lambda e: e.memset(Sbf[:], 0.0), writes=b_Sbf)
            S.op("act", lambda e: e.activation(out=nega[:], in_=alog, func=AF.Exp), reads=[b_ssp], writes=[b_nega])
            S.op("dve", lambda e: e.tensor_scalar(out=nega[:], in0=nega[:], scalar1=-1.0, scalar2=None, op0=ALU.mult),
                 reads=[b_nega], writes=[b_nega])
            S.op("dve", lambda e: e.tensor_tensor(out=tmpa[:], in0=dtraw[:, :, off:off + 32],
                                                  in1=dtb.unsqueeze(1).to_broadcast([128, 32, 32]), op=ALU.add),
                 reads=[b_dtraw, b_ssp], writes=[b_tmpa])
            S.op("act", lambda e: e.activation(out=tmpa[:], in_=tmpa[:], func=AF.Exp), reads=[b_tmpa], writes=[b_tmpa])
            S.op("act", lambda e: e.activation(out=dt_all[:], in_=tmpa[:], func=AF.Ln, bias=1.0), reads=[b_tmpa], writes=[b_dt])
            S.op("dve", lambda e: e.tensor_tensor(out=da_all[:], in0=dt_all[:], in1=nega[:].unsqueeze(1).to_broadcast([128, 32, 32]),
                                                  op=ALU.mult), reads=[b_dt, b_nega], writes=[b_da])
            for half in range(2):
                pp, b_pp = ptA.next()
                S.op("pe", lambda e, pp=pp, half=half: e.matmul(out=pp[:], lhsT=Uc, rhs=flat(da_all)[:, half * 512:(half + 1) * 512],
                                                                start=True, stop=True), reads=[b_mats, b_da], writes=[b_pp])
                S.op("dve", lambda e, pp=pp, half=half: e.tensor_copy(out=flat(P_all)[:, half * 512:(half + 1) * 512], in_=pp[:]),
                     reads=[b_pp], writes=[b_P])
            for half in range(2):
                pp, b_pp = ptA.next()
                S.op("pe", lambda e, pp=pp, half=half: e.matmul(out=pp[:], lhsT=onesf, rhs=flat(da_all)[:, half * 512:(half + 1) * 512],
                                                                start=True, stop=True), reads=[b_mats, b_da], writes=[b_pp])
                S.op("dve", lambda e, pp=pp, half=half: e.tensor_copy(out=flat(tot)[:, half * 512:(half + 1) * 512], in_=pp[:]),
                     reads=[b_pp], writes=[b_tot])
            S.op("dve", lambda e: e.tensor_tensor(out=tmpa[:], in0=tot[:], in1=P_all[:], op=ALU.subtract),
                 reads=[b_tot, b_P, b_dt], writes=[b_tmpa])
            e1, b_e1 = (scl, b_scl) if fwd else (wgt, b_wgt)
            e2, b_e2 = (wgt, b_wgt) if fwd else (scl, b_scl)
            S.op("act", lambda e: e.activation(out=e1[:], in_=P_all[:], func=AF.Exp), reads=[b_P], writes=[b_e1])
            S.op("act", lambda e: e.activation(out=e2[:], in_=tmpa[:], func=AF.Exp), reads=[b_tmpa], writes=[b_e2])
            S.op("act", lambda e: e.activation(out=cdc[:], in_=tot[:], func=AF.Exp), reads=[b_tot], writes=[b_cdc])
            S.op("dve", lambda e: e.tensor_scalar(out=bias_all[:], in0=P_all[:], scalar1=(-1.0 if fwd else 1.0), scalar2=None,
                                                  op0=ALU.mult), reads=[b_P], writes=[b_bias])

            xcr = Ring(nc, st, un("s_xc"), [128, 24, 128], BF16, 3)
            xsr = Ring(nc, st, un("s_xs"), [128, 2048], BF16, 2)
            Btr = Ring(nc, st, un("s_Bt"), [128, 512], BF16, 2)
            cbr = Ring(nc, st, un("s_cb"), [128, 512], F32, 2)
            xdtr = Ring(nc, st, un("s_xdt"), [128, 2048], BF16, 2)
            xwr = Ring(nc, st, un("s_xw"), [128, 2048], BF16, 2)
            decr = Ring(nc, st, un("s_dec"), [128, 128], F32, 12)
            MTr = Ring(nc, st, un("s_MT"), [128, 128], BF16, 12)
            yaccr = Ring(nc, st, un("s_ya"), [128, 2048], F32, 2)
            tmpr = Ring(nc, st, un("s_tmp"), [128, 512], F32, 2)
            if fwd:
                dskr = Ring(nc, st, un("s_dsk"), [128, 2048], F32, 1)
            else:
                zr = Ring(nc, st, un("s_z"), [128, 2048], BF16, 3)
                yfr = Ring(nc, st, un("s_yf"), [128, 2048], F32, 3)
                jkr = Ring(nc, st, un("s_jk"), [128, 512], BF16, 1)
                st4r = Ring(nc, st, un("s_st4"), [128, 12], F32, 2)
                mbr = Ring(nc, st, un("s_mb"), [128, 2048], BF16, 2)
                mstr = Ring(nc, st, un("s_mst"), [128, 16, 128], BF16, 2)
            order = list(range(32)) if fwd else list(range(31, -1, -1))

            def load(ci):
                c = order[ci]
                xc_, b_xc = xcr.next()
                S.dma(lambda e: e.dma_start(out=xc_[:], in_=xcs[c, :, :, :]), reads=[b_xcs], writes=[b_xc], sem_buf=b_xc)
                if fwd:
                    return (xc_, b_xc)
                z_, b_z = zr.next()
                yf_, b_yf = yfr.next()
                S.dma(lambda e: e.dma_start(out=z_[:], in_=zs[c * 128:(c + 1) * 128, :]), reads=[b_zs], writes=[b_z], sem_buf=b_z)
                S.dma(lambda e: e.dma_start(out=yf_[:], in_=yfs[c * 128:(c + 1) * 128, :]), reads=[b_yfs], writes=[b_yf], sem_buf=b_yf)
                return (xc_, b_xc, z_, b_z, yf_, b_yf)

            def prologue(ci, h):
                c = order[ci]
                xc_, b_xc = h[0], h[1]
                xs, b_xs = xsr.next()
                for half in range(2):
                    pt, b_pt = ptA.next()
                    pv = pt[:, :].bitcast(BF16)
                    for jj in range(8):
                        S.op("pe", lambda e, pv=pv, jj=jj, half=half: e.transpose(out=pv[:, jj * 128:(jj + 1) * 128],
                                                                                 in_=xc_[:, half * 8 + jj, :], identity=identb[:]),
                             reads=[b_xc, b_identb], writes=[b_pt])
                    if half == 0:
                        S.op("act", lambda e, pv=pv: e.copy(out=xs[:, 0:1024], in_=pv[:, :]), reads=[b_pt], writes=[b_xs])
                    else:
                        S.op("dve", lambda e, pv=pv: e.tensor_copy(out=xs[:, 1024:2048], in_=pv[:, :]), reads=[b_pt], writes=[b_xs])
                pt, b_pt = ptA.next()
                pvb = pt[:, :].bitcast(BF16)
                for g in range(4):
                    S.op("pe", lambda e, g=g: e.transpose(out=pvb[:, g * 128:(g + 1) * 128], in_=xc_[:, 16 + g, :], identity=identb[:]),
                         reads=[b_xc, b_identb], writes=[b_pt])
                Bt, b_Bt = Btr.next()
                S.op("dve", lambda e: e.tensor_copy(out=Bt[:], in_=pvb[:, 0:512]), reads=[b_pt], writes=[b_Bt])
                pcb, b_pcb = ptA.next()
                for g in range(4):
                    S.op("pe", lambda e, g=g: e.matmul(out=pcb[:, g * 128:(g + 1) * 128], lhsT=xc_[:, 16 + g, :], rhs=xc_[:, 20 + g, :],
                                                       start=True, stop=True), reads=[b_xc], writes=[b_pcb])
                cbT, b_cbT = cbr.next()
                S.op("act", lambda e: e.copy(out=cbT[:], in_=pcb[:]), reads=[b_pcb], writes=[b_cbT])
                xdt, b_xdt = xdtr.next()
                xw, b_xw = xwr.next()
                v3 = lambda t_: t_[:, :].rearrange("p (h d) -> p h d", h=32)
                S.op("dve", lambda e: e.tensor_tensor(out=v3(xdt), in0=v3(xs), in1=dt_all[:, c, :].unsqueeze(2).to_broadcast([128, 32, 64]),
                                                      op=ALU.mult), reads=[b_xs, b_dt], writes=[b_xdt])
                S.op("pool", lambda e: e.tensor_tensor(out=v3(xw), in0=v3(xdt), in1=wgt[:, c, :].unsqueeze(2).to_broadcast([128, 32, 64]),
                                                       op=ALU.mult), reads=[b_xdt, b_wgt], writes=[b_xw])
                return (xs, b_xs, Bt, b_Bt, cbT, b_cbT, xdt, b_xdt, xw, b_xw)

            def comp(ci, h, pr):
                c = order[ci]
                xc_, b_xc = h[0], h[1]
                xs, b_xs, Bt, b_Bt, cbT, b_cbT, xdt, b_xdt, xw, b_xw = pr
                v3 = lambda t_: t_[:, :].rearrange("p (h d) -> p h d", h=32)
                g8 = lambda t_: t_.rearrange("p (h d) -> p h d", h=8)
                ya, b_ya = yaccr.next()
                LAGH = 4
                mts = {}
                cur = {}

                def stageA(bi):
                    sb_, b_sb = segb.next()
                    for q in range(4):
                        h_ = bi * 4 + q
                        seg = sb_[:, q * 128:(q + 1) * 128]
                        S.op("pe", lambda e, seg=seg, h_=h_: e.matmul(out=seg, lhsT=da_all[:, c, h_:h_ + 1].to_broadcast([128, 128]),
                                                                      rhs=Uc, start=True, stop=False),
                             reads=[b_da, b_mats], writes=[b_sb])
                        S.op("pe", lambda e, seg=seg: e.matmul(out=seg, lhsT=identb[:], rhs=negm[:, midx, :], start=False, stop=True),
                             reads=[b_identb, b_negm], writes=[b_sb])
                    for q in range(4):
                        h_ = bi * 4 + q
                        g = h_ // 8
                        seg = sb_[:, q * 128:(q + 1) * 128]
                        dec, b_dec = decr.next()
                        S.op("act", lambda e, seg=seg, dec=dec, h_=h_: e.activation(out=dec[:], in_=seg, func=AF.Exp,
                                                                                   bias=bias_all[:, c, h_:h_ + 1],
                                                                                   scale=(1.0 if fwd else -1.0)),
                             reads=[b_sb, b_bias], writes=[b_dec])
                        MT, b_MT = MTr.next()
                        S.op("dve" if h_ % 2 == 0 else "pool", lambda e, dec=dec, MT=MT, g=g: e.tensor_tensor(
                            out=MT[:], in0=dec[:], in1=cbT[:, g * 128:(g + 1) * 128], op=ALU.mult),
                            reads=[b_dec, b_cbT], writes=[b_MT])
                        mts[h_] = (MT, b_MT)

                def stageB(h_):
                    g = h_ // 8
                    hh = h_ % 8
                    if hh == 0:
                        cur[0] = pyr.next()
                    py, b_py = cur[0]
                    MT, b_MT = mts.pop(h_)
                    S.op("pe", lambda e, MT=MT, py=py, hh=hh, h_=h_: e.matmul(out=py[:, hh * 64:(hh + 1) * 64], lhsT=MT[:],
                                                                             rhs=xdt[:, h_ * 64:(h_ + 1) * 64], start=True, stop=True),
                         reads=[b_MT, b_xdt], writes=[b_py])
                    if hh != 7:
                        return
                    po, b_po = por.next()
                    S.op("pe", lambda e, po=po, g=g: e.matmul(out=po[:], lhsT=xc_[:, 20 + g, :], rhs=Sbf[:, g * 512:(g + 1) * 512],
                                                              start=True, stop=True), reads=[b_xc, b_Sbf[g]], writes=[b_po])
                    pst, b_pst = pstr.next()
                    S.op("pe", lambda e, pst=pst, g=g: e.matmul(out=pst[:], lhsT=Bt[:, g * 128:(g + 1) * 128], rhs=xw[:, g * 512:(g + 1) * 512],
                                                                start=True, stop=True), reads=[b_Bt, b_xw], writes=[b_pst])
                    tmp, b_tmp = tmpr.next()
                    S.op("dve", lambda e, po=po, tmp=tmp, g=g: e.tensor_tensor(
                        out=g8(tmp[:, :]), in0=g8(po[:, :]), in1=scl[:, c, g * 8:(g + 1) * 8].unsqueeze(2).to_broadcast([128, 8, 64]),
                        op=ALU.mult), reads=[b_po, b_scl], writes=[b_tmp])
                    S.op("dve", lambda e, py=py, tmp=tmp, g=g: e.tensor_tensor(out=ya[:, g * 512:(g + 1) * 512], in0=py[:], in1=tmp[:],
                                                                              op=ALU.add), reads=[b_py, b_tmp], writes=[b_ya])
                    S.op("pool", lambda e, g=g: e.tensor_tensor(
                        out=g8(Sf[:, g * 512:(g + 1) * 512]), in0=g8(Sf[:, g * 512:(g + 1) * 512]),
                        in1=cdc[:, c, g * 8:(g + 1) * 8].unsqueeze(2).to_broadcast([128, 8, 64]), op=ALU.mult),
                        reads=[b_Sf[g], b_cdc], writes=[b_Sf[g]])
                    S.op("dve", lambda e, pst=pst, g=g: e.tensor_tensor(out=Sf[:, g * 512:(g + 1) * 512], in0=pst[:],
                                                                       in1=Sf[:, g * 512:(g + 1) * 512], op=ALU.add),
                         reads=[b_pst, b_Sf[g]], writes=[b_Sf[g]])
                    S.op("act", lambda e, g=g: e.copy(out=Sbf[:, g * 512:(g + 1) * 512], in_=Sf[:, g * 512:(g + 1) * 512]),
                         reads=[b_Sf[g]], writes=[b_Sbf[g]])

                for k in range(8 + 2):
                    if k < 8:
                        stageA(k)
                    if k >= 2:
                        for q in range(4):
                            stageB((k - 2) * 4 + q)
                if fwd:
                    dk, b_dk = dskr.next()
                    S.op("pool", lambda e: e.tensor_tensor(out=v3(dk), in0=v3(xs), in1=dsk.unsqueeze(2).to_broadcast([128, 32, 64]),
                                                           op=ALU.mult), reads=[b_xs, b_ssp], writes=[b_dk])
                    S.op("pool", lambda e: e.tensor_tensor(out=ya[:], in0=ya[:], in1=dk[:], op=ALU.add),
                         reads=[b_ya, b_dk], writes=[b_ya])
                    S.dma(lambda e: e.dma_start(out=yfs[c * 128:(c + 1) * 128, :], in_=ya[:]), reads=[b_ya], writes=[b_yfs],
                          sem_buf=b_ya, eng="pool")
                    return
                z_, b_z, yf_, b_yf = h[2], h[3], h[4], h[5]
                if dbg:
                    S.dma(lambda e: e.dma_start(out=ybs[c * 128:(c + 1) * 128, :], in_=ya[:]), reads=[b_ya], writes=[b_ybs],
                          sem_buf=b_ya, eng="pool")
                S.op("pool", lambda e: e.tensor_tensor(out=ya[:], in0=ya[:], in1=yf_[:], op=ALU.add), reads=[b_ya, b_yf], writes=[b_ya])
                S.op("pool", lambda e: e.tensor_tensor(out=ya[:], in0=ya[:], in1=z_[:], op=ALU.mult), reads=[b_ya, b_z], writes=[b_ya])

                def epi():
                    jk, b_jk = jkr.next()
                    s4, b_s4 = st4r.next()
                    for g in range(4):
                        S.op("act", lambda e, g=g: e.activation(out=jk[:], in_=ya[:, g * 512:(g + 1) * 512], func=AF.Square,
                                                                scale=1.0 / math.sqrt(512.0), accum_out=s4[:, g:g + 1]),
                             reads=[b_ya], writes=[b_jk, b_s4])
                    S.op("act", lambda e: e.activation(out=s4[:, 4:8], in_=s4[:, 0:4], func=AF.Sqrt, bias=EPS_AP[:, 0:1]),
                         reads=[b_s4, b_eps], writes=[b_s4])
                    S.op("dve", lambda e: e.reciprocal(out=s4[:, 8:12], in_=s4[:, 4:8]), reads=[b_s4], writes=[b_s4])
                    mb, b_mb = mbr.next()
                    for g in range(4):
                        S.op("dve", lambda e, g=g: e.tensor_scalar(out=mb[:, g * 512:(g + 1) * 512], in0=ya[:, g * 512:(g + 1) * 512],
                                                                   scalar1=s4[:, 8 + g:9 + g], scalar2=None, op0=ALU.mult),
                             reads=[b_ya, b_s4], writes=[b_mb])
                    mst, b_mst = mstr.next()
                    for half in range(2):
                        pt, b_pt = ptA.next()
                        pv = pt[:, :].bitcast(BF16)
                        for jj in range(8):
                            j = half * 8 + jj
                            S.op("pe", lambda e, pv=pv, jj=jj, j=j: e.transpose(out=pv[:, jj * 128:(jj + 1) * 128],
                                                                               in_=mb[:, j * 128:(j + 1) * 128], identity=identb[:]),
                                 reads=[b_mb, b_identb], writes=[b_pt])
                        S.op("act", lambda e, pv=pv, half=half: e.copy(out=mst[:, half * 8:(half + 1) * 8, :],
                                                                       in_=pv[:, :].rearrange("p (j t) -> p j t", j=8)),
                             reads=[b_pt], writes=[b_mst])
                    for half in range(2):
                        S.dma(lambda e, half=half: e.dma_start(
                            out=mTs[half * 1024:(half + 1) * 1024, c * 128:(c + 1) * 128].rearrange("(j p) t -> p j t", p=128),
                            in_=mst[:, half * 8:(half + 1) * 8, :]), reads=[b_mst], writes=[b_mTs], sem_buf=b_mst, eng="sp")

                if pend_epi:
                    pend_epi.pop()()
                pend_epi.append(epi)

            pend_epi = []
            hs = {}
            prs = {}
            for i in range(32 + 2):
                if i < 32:
                    hs[i] = load(i)
                if 1 <= i <= 32:
                    prs[i - 1] = prologue(i - 1, hs[i - 1])
                if i >= 2:
                    comp(i - 2, hs.pop(i - 2), prs.pop(i - 2))
            if pend_epi:
                pend_epi.pop()()
        S.barrier()

    ssd_pass(True)
    if stop_after <= 4 and stop_after == 4:
        pass
    ssd_pass(False)
    if stop_after <= 4:
        S.emit(nc)
        return nc

    mw = ExitStack()
    wpa = mw.enter_context(nc.sbuf_tensor("m_wpa", [128, 8, D_], BF16)); b_wpa = Buf()
    wpb = mw.enter_context(nc.sbuf_tensor("m_wpb", [128, 16, D_], BF16)); b_wpb = Buf()
    wo = mw.enter_context(nc.sbuf_tensor("m_wo", [128, 8, D_], BF16)); b_wo = Buf()
    mws = ExitStack()
    ms8 = mws.enter_context(nc.sbuf_tensor("m_s8", [128, 8, D_], F32)); b_ms8 = Buf()

    def preload_merge_weights():
        jobs = [(w_pa[:, :], wpa[:, :, :], b_wpa, None), (w_pb[0:1024, :], wpb[:, 0:8, :], b_wpb, gssm_t[:, 0:8]),
                (w_pb[1024:2048, :], wpb[:, 8:16, :], b_wpb, gssm_t[:, 8:16]), (w_o[:, :], wo[:, :, :], b_wo, None)]
        for src, dst, b_dst, g_ in jobs:
            S.dma(lambda e, src=src: e.dma_start(out=ms8[:], in_=src.rearrange("(kc p) n -> p kc n", p=128)),
                  writes=[b_ms8], sem_buf=b_ms8)
            if g_ is None:
                S.op("pool", lambda e, dst=dst: e.tensor_copy(out=dst, in_=ms8[:]), reads=[b_ms8], writes=[b_dst])
            else:
                S.op("pool", lambda e, dst=dst, g_=g_: e.tensor_tensor(out=dst, in0=ms8[:], in1=g_.unsqueeze(2).to_broadcast([128, 8, D_]),
                                                                      op=ALU.mult), reads=[b_ms8, b_gssm], writes=[b_dst])

    with ExitStack() as st:
        ktr = Ring(nc, st, un("t_k"), [128, S_], BF16, 2)
        qtr = Ring(nc, st, un("t_q"), [128, S_], BF16, 2)
        vtr = Ring(nc, st, un("t_v"), [128, 32, 65], BF16, 2)
        psS = Ring(nc, st, un("t_ps"), [128, 1024], F32, 3, psum=True)
        psO = Ring(nc, st, un("t_po"), [128, 1024], F32, 1, psum=True)
        pTr = Ring(nc, st, un("t_pT"), [128, 1024], BF16, 4)
        rdr = Ring(nc, st, un("t_rd"), [128, 1024], F32, 2)
        osr = Ring(nc, st, un("t_os"), [128, 1024], F32, 2)
        aor = Ring(nc, st, un("t_ao"), [128, S_], BF16, 2)
        sc = 1.0 / math.sqrt(96.0)
        LAG = 2
        tiles = {}

        def ensure(h_):
            if h_ >= NH or h_ in tiles:
                return
            kt, b_kt = ktr.next()
            qt, b_qt = qtr.next()
            vt, b_vt = vtr.next()
            S.dma(lambda e: e.dma_start(out=kt[0:96, :], in_=KT[h_, :, :]), reads=[b_KT], writes=[b_kt], sem_buf=b_kt)
            S.dma(lambda e: e.dma_start(out=qt[0:96, :], in_=QT[h_, :, :]), reads=[b_QT], writes=[b_qt], sem_buf=b_qt)
            S.dma(lambda e: e.dma_start(out=vt[:], in_=Vs[h_, :, :, :]), reads=[b_Vs], writes=[b_vt], sem_buf=b_vt)
            tiles[h_] = (kt, b_kt, qt, b_qt, vt, b_vt)

        steps = [(h_, sb, kc) for h_ in range(NH) for sb in range(4) for kc in range(32)]
        pend = {}
        cur_po = {}
        cur_ao = {}
        ensure(0)
        preload_merge_weights()
        for i in range(len(steps) + LAG):
            if i < len(steps):
                h_, sb, kc = steps[i]
                kt, b_kt, qt, b_qt, vt, b_vt = tiles[h_]
                ps, b_ps = psS.next()
                for u in range(2):
                    S.op("pe", lambda e, ps=ps, kc=kc, sb=sb, u=u, kt=kt, qt=qt: e.matmul(
                        out=ps[:, u * 512:(u + 1) * 512], lhsT=kt[0:96, kc * 128:(kc + 1) * 128],
                        rhs=qt[0:96, sb * 1024 + u * 512:sb * 1024 + (u + 1) * 512], start=True, stop=True),
                        reads=[b_kt, b_qt], writes=[b_ps])
                pT, b_pT = pTr.next()
                S.op("act", lambda e, ps=ps, pT=pT: e.activation(out=pT[:], in_=ps[:], func=AF.Exp, scale=sc),
                     reads=[b_ps], writes=[b_pT])
                pend[i] = (pT, b_pT)
            if i >= LAG:
                h_, sb, kc = steps[i - LAG]
                kt, b_kt, qt, b_qt, vt, b_vt = tiles[h_]
                pT, b_pT = pend.pop(i - LAG)
                if kc == 0:
                    cur_po[0] = psO.next()
                    if sb == 0:
                        cur_ao[0] = aor.next()
                        ensure(h_ + 1)
                po, b_po = cur_po[0]
                ao, b_ao = cur_ao[0]
                for u in range(2):
                    S.op("pe", lambda e, po=po, pT=pT, kc=kc, u=u, vt=vt: e.matmul(
                        out=po[0:65, u * 512:(u + 1) * 512], lhsT=vt[:, kc, :], rhs=pT[:, u * 512:(u + 1) * 512],
                        start=(kc == 0), stop=(kc == 31)), reads=[b_vt, b_pT], writes=[b_po])
                if kc == 31:
                    osb, b_osb = osr.next()
                    S.op("dve", lambda e, po=po, osb=osb: e.tensor_copy(out=osb[0:65, :], in_=po[0:65, :]), reads=[b_po], writes=[b_osb])
                    rd, b_rd = rdr.next()
                    S.op("dve", lambda e, osb=osb, rd=rd: e.reciprocal(out=rd[64:65, :], in_=osb[64:65, :]), reads=[b_osb], writes=[b_rd])
                    pb, b_pb = psS.next()
                    for u in range(2):
                        S.op("pe", lambda e, pb=pb, rd=rd, u=u: e.matmul(out=pb[0:64, u * 512:(u + 1) * 512], lhsT=mats[64:65, 2, 0:64],
                                                                         rhs=rd[64:65, u * 512:(u + 1) * 512], start=True, stop=True),
                             reads=[b_mats, b_rd], writes=[b_pb])
                    qs = slice(sb * 1024, (sb + 1) * 1024)
                    S.op("dve", lambda e, pb=pb, osb=osb, qs=qs, ao=ao: e.tensor_tensor(
                        out=ao[0:64, qs], in0=pb[0:64, :], in1=osb[0:64, :], op=ALU.mult),
                        reads=[b_pb, b_osb], writes=[b_ao])
                    if sb == 3:
                        S.dma(lambda e, h_=h_, ao=ao: e.dma_start(out=aTs[h_ * 64:(h_ + 1) * 64, :], in_=ao[0:64, :]),
                              reads=[b_ao], writes=[b_aTs], sem_buf=b_ao, eng="pool")
        S.barrier()
    if stop_after <= 5:
        S.emit(nc)
        return nc

    mws.close()
    with ExitStack() as st:
        atr = Ring(nc, st, un("m_at"), [128, 8, 512], BF16, 2)
        mtr = Ring(nc, st, un("m_mt"), [128, 16, 512], BF16, 2)
        gtr = Ring(nc, st, un("m_gt"), [128, 16, 512], BF16, 2)
        mgr = Ring(nc, st, un("m_mg"), [128, 8, 512], BF16, 2)
        t1r = Ring(nc, st, un("m_t1"), [128, 512], F32, 2)
        t2r = Ring(nc, st, un("m_t2"), [128, 512], F32, 2)
        xr = Ring(nc, st, un("m_x"), [128, D_], F32, 3)
        ps = Ring(nc, st, un("m_ps"), [128, 512], F32, 6, psum=True)
        def loadm(t):
            at, b_at = atr.next()
            mt, b_mt = mtr.next()
            gt_, b_gt = gtr.next()
            ts = slice(t * 512, (t + 1) * 512)
            S.dma(lambda e: e.dma_start(out=at[:], in_=aTs[:, ts].rearrange("(k p) t -> p k t", p=128)), reads=[b_aTs], writes=[b_at], sem_buf=b_at)
            S.dma(lambda e: e.dma_start(out=mt[:], in_=mTs[:, ts].rearrange("(k p) t -> p k t", p=128)), reads=[b_mTs], writes=[b_mt], sem_buf=b_mt)
            S.dma(lambda e: e.dma_start(out=gt_[:], in_=gts[:, ts].rearrange("(k p) t -> p k t", p=128)), reads=[b_gts], writes=[b_gt], sem_buf=b_gt)
            return (at, b_at, mt, b_mt, gt_, b_gt)

        def compm(t, hd):
            at, b_at, mt, b_mt, gt_, b_gt = hd
            mg, b_mg = mgr.next()
            for dc in range(8):
                pa, b_pa = ps.next()
                pb, b_pb = ps.next()
                for kc in range(8):
                    S.op("pe", lambda e, pa=pa, kc=kc, dc=dc: e.matmul(out=pa[:], lhsT=wpa[:, kc, dc * 128:(dc + 1) * 128], rhs=at[:, kc, :],
                                                                       start=(kc == 0), stop=(kc == 7)), reads=[b_wpa, b_at], writes=[b_pa])
                for kc in range(16):
                    S.op("pe", lambda e, pb=pb, kc=kc, dc=dc: e.matmul(out=pb[:], lhsT=wpb[:, kc, dc * 128:(dc + 1) * 128], rhs=mt[:, kc, :],
                                                                       start=(kc == 0), stop=(kc == 15)), reads=[b_wpb, b_mt], writes=[b_pb])
                t1, b_t1 = t1r.next()
                t2, b_t2 = t2r.next()
                S.op("dve", lambda e, pa=pa, t1=t1, dc=dc: e.tensor_tensor(out=t1[:], in0=pa[:], in1=gt_[:, dc, :], op=ALU.mult),
                     reads=[b_pa, b_gt], writes=[b_t1])
                S.op("dve", lambda e, pb=pb, t2=t2, dc=dc: e.tensor_tensor(out=t2[:], in0=pb[:], in1=gt_[:, 8 + dc, :], op=ALU.mult),
                     reads=[b_pb, b_gt], writes=[b_t2])
                S.op("pool", lambda e, t1=t1, t2=t2, dc=dc: e.tensor_tensor(out=mg[:, dc, :], in0=t1[:], in1=t2[:], op=ALU.add),
                     reads=[b_t1, b_t2], writes=[b_mg])
            for sb in range(4):
                tb = t * 4 + sb
                xt, b_xt = xr.next()
                S.dma(lambda e, xt=xt, tb=tb: e.dma_start(out=xt[:], in_=x1s[tb * 128:(tb + 1) * 128, :]),
                      reads=[b_x1s], writes=[b_xt], sem_buf=b_xt)
                for half in range(2):
                    p, b_p = ps.next()
                    for kc in range(8):
                        S.op("pe", lambda e, p=p, kc=kc, sb=sb, half=half: e.matmul(
                            out=p[:], lhsT=mg[:, kc, sb * 128:(sb + 1) * 128], rhs=wo[:, kc, half * 512:(half + 1) * 512],
                            start=(kc == 0), stop=(kc == 7)), reads=[b_mg, b_wo], writes=[b_p])
                    S.op("dve", lambda e, p=p, xt=xt, half=half: e.tensor_tensor(out=xt[:, half * 512:(half + 1) * 512], in0=p[:],
                                                                                in1=xt[:, half * 512:(half + 1) * 512], op=ALU.add),
                         reads=[b_p, b_xt], writes=[b_xt])
                S.dma(lambda e, xt=xt, tb=tb: e.dma_start(out=x2s[tb * 128:(tb + 1) * 128, :], in_=xt[:]),
                      reads=[b_xt], writes=[b_x2s], sem_buf=b_xt, eng="pool")

        pipeline(8, loadm, compm, 1)
        S.barrier()
    mw.close()
    hst2 = ExitStack()
    hT2 = hst2.enter_context(nc.sbuf_tensor("hT2", [128, 8, S_], BF16)); b_hT2 = Buf("hT2")
    norm_phase(x2s, b_x2s, hT2, b_hT2)

    with ExitStack() as fst:
        wd2 = fst.enter_context(nc.sbuf_tensor("wd2", [128, NFF, D_], BF16)); b_wd2 = Buf()
        ffn_gateup(w_g2, w_u2, 2, hT2, b_hT2, w_d2, wd2, b_wd2)
        ffn_down(w_d2, x2s, b_x2s, y_out, b_yout, None, wd2, b_wd2)
    S.emit(nc)
    return nc


def _fm(v, kc):
    return np.ascontiguousarray(np.asarray(v, np.float32).reshape(kc, 128).T)


_CACHE = {}


def consts():
    ii = np.arange(128)
    U = (ii[:, None] <= ii[None, :]).astype(np.float32)
    Us = (ii[:, None] < ii[None, :]).astype(np.float32)
    ones = np.ones((128, 128), np.float32)
    I = np.eye(128, dtype=np.float32)
    mats = np.ascontiguousarray(np.stack([U, Us, ones, I], axis=1))
    negf = np.where(ii[:, None] > ii[None, :], -30000.0, 0.0).astype(np.float32)
    posb = np.where(ii[:, None] < ii[None, :], 30000.0, 0.0).astype(np.float32)
    neg = np.ascontiguousarray(np.stack([negf, posb], axis=1)).astype(ml_dtypes.bfloat16)
    invf = (1.0 / (10000.0 ** (np.arange(0, 32, 2, dtype=np.float32) / 32.0))).astype(np.float32)[None, :]
    return dict(c_identb=I.astype(ml_dtypes.bfloat16), c_mats=mats, c_neg=neg, c_invf=invf)


def make_shared(inp):
    f = lambda k: np.asarray(inp[k], np.float32)[0]
    d = {}
    d["gfm"] = np.ascontiguousarray(np.concatenate([_fm(f("ffn1_norm"), 8), _fm(f("mix_norm"), 8), _fm(f("ffn2_norm"), 8)], axis=1))
    d["gqa"] = _fm(f("q_a_norm"), 3)
    d["gkva"] = _fm(f("kv_a_norm"), 2)
    d["gssm"] = _fm(f("ssm_norm"), 16)
    d["w_g1"] = f("ffn1_w_gate"); d["w_u1"] = f("ffn1_w_up"); d["w_d1"] = f("ffn1_w_down")
    d["w_g2"] = f("ffn2_w_gate"); d["w_u2"] = f("ffn2_w_up"); d["w_d2"] = f("ffn2_w_down")
    d["w_in"] = f("w_in"); d["w_qb"] = f("w_q_b"); d["w_kvb"] = f("w_kv_b")
    d["hn"] = np.concatenate([f("q_head_norm"), f("k_head_norm")])[None, :].astype(np.float32)
    cw = f("conv_w")[:, 0, :]
    d["convw"] = np.ascontiguousarray(cw.T.reshape(24, 128, 5).transpose(1, 0, 2))
    d["convb"] = _fm(f("conv_b"), 24)
    d["ssp"] = np.concatenate([f("a_log_fwd"), f("a_log_bwd"), f("dt_bias_fwd"), f("dt_bias_bwd"), f("d_skip")])[None, :].astype(np.float32)
    d["w_pa"] = f("w_attn_branch"); d["w_pb"] = f("w_ssm_branch"); d["w_o"] = f("w_out")
    d.update(consts())
    return d


def make_inmap(inp, shared, b):
    d = dict(shared)
    d["x"] = np.ascontiguousarray(np.asarray(inp["x"], np.float32)[b])
    p = np.asarray(inp["positions"], np.int32)[b]
    d["pos"] = np.ascontiguousarray(p.reshape(32, 128).T)
    return d


def kernel(**inputs):
    nb = int(np.asarray(inputs["x"]).shape[0])
    nc = build(dbg=False)
    shared = make_shared(inputs)
    in_maps = [make_inmap(inputs, shared, b) for b in range(nb)]
    res = run_bass_kernel_spmd(nc, in_maps, core_ids=list(range(nb)))
    out = np.stack([np.asarray(res.results[b]["y"], dtype=np.float32) for b in range(nb)], axis=0)
    return out
```
